# Optimizing a Trainium2 kernel written in Bass

```python
import math
import jax, jax.numpy as jnp
from jax import lax
import numpy as np

D_MODEL = 1024
BATCH = 4
SEQ = 4096
DEPTH = 4

N_MIXERS = 3
EXPAND = 2
D_INNER = EXPAND * D_MODEL
NORM_EPS = 1e-6

GMLP_CHUNK = 128
GMLP_GROUPS = 8

S5_GROUP = 16
S5_STATE = 64
S5_GROUPS = D_INNER // S5_GROUP
S5_DT_MIN = 1e-3
S5_DT_MAX = 1e-1

MLA_HEADS = 16
MLA_NOPE = 128
MLA_ROPE = 64
MLA_V = D_INNER // MLA_HEADS
MLA_QK_DIM = MLA_NOPE + MLA_ROPE
MLA_Q_RANK = 384
MLA_KV_RANK = 128
MLA_SCALE = MLA_QK_DIM ** -0.5
ROPE_THETA = 10000.0
ATTN_QBLOCK = 128
NEG_INF = -1e30

kernel_name = "hybrid_gmlp_s5_mla_gated"


def _rmsnorm(x, g):
    xf = x.astype(jnp.float32)
    y = xf * lax.rsqrt(jnp.mean(xf * xf, axis=-1, keepdims=True) + NORM_EPS)
    return (y * g.astype(jnp.float32)).astype(x.dtype)


def _layernorm(x, g, b):
    xf = x.astype(jnp.float32)
    mu = jnp.mean(xf, axis=-1, keepdims=True)
    xc = xf - mu
    var = jnp.mean(xc * xc, axis=-1, keepdims=True)
    y = xc * lax.rsqrt(var + NORM_EPS) * g.astype(jnp.float32) + b.astype(jnp.float32)
    return y.astype(x.dtype)


def _rope(x, cos, sin):
    half = x.shape[-1] // 2
    x1 = x[..., :half].astype(jnp.float32)
    x2 = x[..., half:].astype(jnp.float32)
    return jnp.concatenate([x1 * cos - x2 * sin, x2 * cos + x1 * sin], axis=-1).astype(x.dtype)


def _gmlp_mixer(h, w_in, ln_g, ln_b, w_s, b_s, w_out):
    bsz, seq, _ = h.shape
    u, v, z = jnp.split(h @ w_in, 3, axis=-1)
    u = jax.nn.gelu(u)
    v = _layernorm(jax.nn.gelu(v), ln_g, ln_b)
    v = v.reshape(bsz, seq // GMLP_CHUNK, GMLP_CHUNK, GMLP_GROUPS, D_INNER // GMLP_GROUPS)
    causal = jnp.tril(jnp.ones((GMLP_CHUNK, GMLP_CHUNK), dtype=bool))
    w = jnp.where(causal[None], w_s, jnp.zeros((), w_s.dtype))
    s = jnp.einsum('gts,bcsgd->bctgd', w, v) + b_s.T[:, :, None]
    s = s.reshape(bsz, seq, D_INNER)
    return (u * s * jax.nn.silu(z)) @ w_out


def _s5_combine(left, right):
    a_l, b_l = left
    a_r, b_r = right
    return a_r * a_l, a_r * b_l + b_r


def _s5_mixer(h, w_in, a_re, a_im, log_step, b_re, b_im, c_re, c_im, d_skip, w_glu, b_glu, w_out):
    bsz, seq, _ = h.shape
    u, z = jnp.split(h @ w_in, 2, axis=-1)
    uf = u.astype(jnp.float32).reshape(bsz, seq, S5_GROUPS, S5_GROUP)
    lam = lax.complex(a_re.astype(jnp.float32), a_im.astype(jnp.float32))
    step = jnp.exp(log_step.astype(jnp.float32))[:, None]
    lam_bar = jnp.exp(lam * step)
    bmat = lax.complex(b_re.astype(jnp.float32), b_im.astype(jnp.float32))
    b_bar = ((lam_bar - 1.0) / lam)[..., None] * bmat
    bu = lax.complex(jnp.einsum('blgh,gph->lbgp', uf, jnp.real(b_bar)),
                     jnp.einsum('blgh,gph->lbgp', uf, jnp.imag(b_bar)))
    a_elems = jnp.broadcast_to(lam_bar, (seq, 1, S5_GROUPS, S5_STATE))
    _, xs = lax.associative_scan(_s5_combine, (a_elems, bu), axis=0)
    y = (jnp.einsum('lbgp,ghp->blgh', jnp.real(xs), c_re.astype(jnp.float32))
         - jnp.einsum('lbgp,ghp->blgh', jnp.imag(xs), c_im.astype(jnp.float32)))
    y = y + d_skip.astype(jnp.float32).reshape(S5_GROUPS, S5_GROUP) * uf
    y = jax.nn.gelu(y.reshape(bsz, seq, D_INNER)).astype(h.dtype)
    y = y * jax.nn.sigmoid(y @ w_glu + b_glu)
    return (y * jax.nn.silu(z)) @ w_out


def _mla_mixer(h, positions, w_in, q_norm_g, w_uq, kv_norm_g, w_ukv, w_out):
    bsz, seq, _ = h.shape
    c_q, c_kv, k_r, z = jnp.split(
        h @ w_in, [MLA_Q_RANK, MLA_Q_RANK + MLA_KV_RANK, MLA_Q_RANK + MLA_KV_RANK + MLA_ROPE], axis=-1)
    q = (_rmsnorm(c_q, q_norm_g) @ w_uq).reshape(bsz, seq, MLA_HEADS, MLA_QK_DIM)
    q_nope, q_rope = q[..., :MLA_NOPE], q[..., MLA_NOPE:]
    kv = (_rmsnorm(c_kv, kv_norm_g) @ w_ukv).reshape(bsz, seq, MLA_HEADS, MLA_NOPE + MLA_V)
    k_nope, v = kv[..., :MLA_NOPE], kv[..., MLA_NOPE:]
    inv_freq = ROPE_THETA ** (-jnp.arange(0, MLA_ROPE, 2, dtype=jnp.float32) / MLA_ROPE)
    ang = positions.astype(jnp.float32)[..., None] * inv_freq
    cos, sin = jnp.cos(ang), jnp.sin(ang)
    q_rope = _rope(q_rope, cos[:, :, None], sin[:, :, None])
    k_r = _rope(k_r, cos, sin)
    n_blk = seq // ATTN_QBLOCK

    def to_blocks(t):
        return t.reshape(bsz, n_blk, ATTN_QBLOCK, *t.shape[2:]).swapaxes(0, 1)

    kpos = jnp.arange(seq)

    def attend(args):
        qn_b, qr_b, blk = args
        s = (jnp.einsum('bqhd,bkhd->bhqk', qn_b, k_nope)
             + jnp.einsum('bqhd,bkd->bhqk', qr_b, k_r)).astype(jnp.float32) * MLA_SCALE
        qpos = blk * ATTN_QBLOCK + jnp.arange(ATTN_QBLOCK)
        s = jnp.where(kpos[None, :] <= qpos[:, None], s, NEG_INF)
        p = jax.nn.softmax(s, axis=-1).astype(v.dtype)
        return jnp.einsum('bhqk,bkhd->bqhd', p, v)

    o = lax.map(attend, (to_blocks(q_nope), to_blocks(q_rope), jnp.arange(n_blk)))
    o = o.swapaxes(0, 1).reshape(bsz, seq, MLA_HEADS * MLA_V)
    return (o * jax.nn.silu(z)) @ w_out


def _gain(key, n):
    return 1.0 + 0.02 * jax.random.normal(key, (n,), jnp.float32)


def _normal(key, shape, scale):
    return jax.random.normal(key, shape, jnp.float32) * scale


def _gmlp_params(key, p):
    k = jax.random.split(key, 7)
    return {
        p + 'norm_g': _gain(k[0], D_MODEL),
        p + 'w_in': _normal(k[1], (D_MODEL, 3 * D_INNER), D_MODEL ** -0.5),
        p + 'ln_g': _gain(k[2], D_INNER),
        p + 'ln_b': _normal(k[3], (D_INNER,), 0.02),
        p + 'w_s': _normal(k[4], (GMLP_GROUPS, GMLP_CHUNK, GMLP_CHUNK), GMLP_CHUNK ** -0.5),
        p + 'b_s': 1.0 + _normal(k[5], (GMLP_GROUPS, GMLP_CHUNK), 0.02),
        p + 'w_out': _normal(k[6], (D_INNER, D_MODEL), D_INNER ** -0.5),
    }


def _s5_params(key, p):
    k = jax.random.split(key, 14)
    n = jnp.arange(S5_STATE, dtype=jnp.float32)
    return {
        p + 'norm_g': _gain(k[0], D_MODEL),
        p + 'w_in': _normal(k[1], (D_MODEL, 2 * D_INNER), D_MODEL ** -0.5),
        p + 'a_re': -0.5 + _normal(k[2], (S5_GROUPS, S5_STATE), 0.01),
        p + 'a_im': math.pi * n[None, :] + _normal(k[3], (S5_GROUPS, S5_STATE), 0.01),
        p + 'log_step': jax.random.uniform(k[4], (S5_GROUPS,), jnp.float32,
                                           math.log(S5_DT_MIN), math.log(S5_DT_MAX)),
        p + 'b_re': _normal(k[5], (S5_GROUPS, S5_STATE, S5_GROUP), (2 * S5_GROUP) ** -0.5),
        p + 'b_im': _normal(k[6], (S5_GROUPS, S5_STATE, S5_GROUP), (2 * S5_GROUP) ** -0.5),
        p + 'c_re': _normal(k[7], (S5_GROUPS, S5_GROUP, S5_STATE), (2 * S5_STATE) ** -0.5),
        p + 'c_im': _normal(k[8], (S5_GROUPS, S5_GROUP, S5_STATE), (2 * S5_STATE) ** -0.5),
        p + 'd_skip': _normal(k[9], (D_INNER,), 1.0),
        p + 'w_glu': _normal(k[10], (D_INNER, D_INNER), D_INNER ** -0.5),
        p + 'b_glu': _normal(k[11], (D_INNER,), 0.02),
        p + 'w_out': _normal(k[12], (D_INNER, D_MODEL), D_INNER ** -0.5),
    }


def _mla_params(key, p):
    k = jax.random.split(key, 7)
    return {
        p + 'norm_g': _gain(k[0], D_MODEL),
        p + 'w_in': _normal(k[1], (D_MODEL, MLA_Q_RANK + MLA_KV_RANK + MLA_ROPE + D_INNER), D_MODEL ** -0.5),
        p + 'q_norm_g': _gain(k[2], MLA_Q_RANK),
        p + 'w_uq': _normal(k[3], (MLA_Q_RANK, MLA_HEADS * MLA_QK_DIM), MLA_Q_RANK ** -0.5),
        p + 'kv_norm_g': _gain(k[4], MLA_KV_RANK),
        p + 'w_ukv': _normal(k[5], (MLA_KV_RANK, MLA_HEADS * (MLA_NOPE + MLA_V)), MLA_KV_RANK ** -0.5),
        p + 'w_out': _normal(k[6], (MLA_HEADS * MLA_V, D_MODEL), D_INNER ** -0.5),
    }


def setup_inputs(seed: int = 0) -> dict:
    key = jax.random.key(seed)
    keys = jax.random.split(key, DEPTH + 4)
    x = jax.random.normal(keys[0], (BATCH, SEQ, D_MODEL), jnp.float32)
    offset = jax.random.randint(keys[1], (BATCH, 1), 0, 1024, dtype=jnp.int32)
    positions = offset + jnp.arange(SEQ, dtype=jnp.int32)[None, :]
    inputs = {'x': x, 'positions': positions}
    makers = (_gmlp_params, _s5_params, _mla_params)
    for i in range(DEPTH):
        inputs.update(makers[i % N_MIXERS](keys[2 + i], 'l%d_' % i))
    inputs['final_norm_g'] = _gain(keys[2 + DEPTH], D_MODEL)
    return inputs


def reference(x, positions,
              l0_norm_g, l0_w_in, l0_ln_g, l0_ln_b, l0_w_s, l0_b_s, l0_w_out,
              l1_norm_g, l1_w_in, l1_a_re, l1_a_im, l1_log_step, l1_b_re, l1_b_im, l1_c_re, l1_c_im,
              l1_d_skip, l1_w_glu, l1_b_glu, l1_w_out,
              l2_norm_g, l2_w_in, l2_q_norm_g, l2_w_uq, l2_kv_norm_g, l2_w_ukv, l2_w_out,
              l3_norm_g, l3_w_in, l3_ln_g, l3_ln_b, l3_w_s, l3_b_s, l3_w_out,
              final_norm_g):
    layer_params = (
        (l0_norm_g, (l0_w_in, l0_ln_g, l0_ln_b, l0_w_s, l0_b_s, l0_w_out)),
        (l1_norm_g, (l1_w_in, l1_a_re, l1_a_im, l1_log_step, l1_b_re, l1_b_im, l1_c_re, l1_c_im,
                     l1_d_skip, l1_w_glu, l1_b_glu, l1_w_out)),
        (l2_norm_g, (l2_w_in, l2_q_norm_g, l2_w_uq, l2_kv_norm_g, l2_w_ukv, l2_w_out)),
        (l3_norm_g, (l3_w_in, l3_ln_g, l3_ln_b, l3_w_s, l3_b_s, l3_w_out)),
    )
    h = x
    for i in range(DEPTH):
        norm_g, p = layer_params[i]
        hn = _rmsnorm(h, norm_g)
        kind = i % N_MIXERS
        if kind == 0:
            y = _gmlp_mixer(hn, *p)
        elif kind == 1:
            y = _s5_mixer(hn, *p)
        else:
            y = _mla_mixer(hn, positions, *p)
        h = h + y
    return _rmsnorm(h, final_norm_g)
```

```python
import contextlib
import numpy as np
import concourse.bass as bass
import concourse.mybir as mybir
from concourse.bass_utils import run_bass_kernel_spmd

F32 = mybir.dt.float32
BF16 = mybir.dt.bfloat16
I32 = mybir.dt.int32
ALU = mybir.AluOpType
AF = mybir.ActivationFunctionType
AX = mybir.AxisListType

N_DMA_SEMS = 12


class Buf:
    __slots__ = ("w", "r", "name")

    def __init__(self, name=""):
        self.w = None
        self.r = []
        self.name = name


class Tile:
    def __init__(self, t, nb, name):
        self.t = t
        self.b = [Buf(f"{name}.{i}") for i in range(nb)]

    def __getitem__(self, idx):
        return self.t[idx]


class MK:
    ENG = ("pe", "act", "dve", "pool", "sp")

    def __init__(self):
        self.nc = bass.Bass("TRN2", target_bir_lowering=False)
        self.stack = contextlib.ExitStack()
        self.ops = {e: [] for e in self.ENG}
        self.cnt = {}
        self.sems = {}
        self.waited = {e: {} for e in self.ENG}
        for e in ("pe", "act", "dve", "pool"):
            self._mksem("c_" + e)
        for i in range(N_DMA_SEMS):
            self._mksem(f"d{i}")
        self.dma_rr = 0
        self.n_ops = 0
        self.stacks = [self.stack]
        self.uid = 0

    def _mksem(self, key):
        self.sems[key] = self.stack.enter_context(self.nc.semaphore(key))
        self.cnt[key] = 0

    def dram(self, name, shape, dt, kind):
        return self.nc.dram_tensor(name, list(shape), dt, kind=kind).ap()

    def stack_push(self, st):
        self.stacks.append(st)

    def stack_pop(self):
        self.stacks.pop().close()

    def sbuf(self, name, shape, dt, nb=1):
        self.uid += 1
        t = self.stacks[-1].enter_context(self.nc.sbuf_tensor(f"{name}_{self.uid}", list(shape), dt))
        return Tile(t, nb, name)

    def psum(self, name, shape, dt=F32, nb=1):
        self.uid += 1
        t = self.stacks[-1].enter_context(self.nc.psum_tensor(f"{name}_{self.uid}", list(shape), dt))
        return Tile(t, nb, name)

    def barrier(self):
        for eng in self.ENG:
            waits = []
            for k, v in self.cnt.items():
                if v > self.waited[eng].get(k, 0):
                    self.waited[eng][k] = v
                    waits.append((k, v))
            if waits:
                self.ops[eng].append((None, waits, None, 0))

    def _deps(self, eng, reads, writes):
        need = {}
        def add(tok):
            if tok is None:
                return
            k, v = tok
            if eng == "pe" and k == "c_pe":
                return
            if need.get(k, 0) < v:
                need[k] = v
        for b in reads:
            add(b.w)
        for b in writes:
            add(b.w)
            for tok in b.r:
                add(tok)
        out = []
        wd = self.waited[eng]
        for k, v in need.items():
            if wd.get(k, 0) < v:
                wd[k] = v
                out.append((k, v))
        return out

    def _commit(self, tok, reads, writes):
        for b in writes:
            b.w = tok
            b.r = []
        for b in reads:
            if b not in writes:
                b.r.append(tok)

    def op(self, eng, fn, reads=(), writes=()):
        reads = list(reads); writes = list(writes)
        waits = self._deps(eng, reads, writes)
        key = "c_" + eng
        self.cnt[key] += 1
        tok = (key, self.cnt[key])
        self.ops[eng].append((fn, waits, key, 1))
        self._commit(tok, reads, writes)
        self.n_ops += 1
        return tok

    def dma(self, eng, out, in_, reads=(), writes=(), **kw):
        reads = list(reads); writes = list(writes)
        i = self.dma_rr; self.dma_rr = (self.dma_rr + 1) % N_DMA_SEMS
        key = f"d{i}"
        waits = self._deps(eng, reads, writes)
        prev = self.cnt[key]
        if prev and self.waited[eng].get(key, 0) < prev:
            self.waited[eng][key] = prev
            waits.append((key, prev))
        self.cnt[key] += 16
        tok = (key, self.cnt[key])
        def fn(e, out=out, in_=in_, kw=kw):
            return e.dma_start(out=out, in_=in_, **kw)
        self.ops[eng].append((fn, waits, key, 16))
        self._commit(tok, reads, writes)
        self.n_ops += 1
        return tok

    def collective(self, kind, ins, outs, rbufs, wbufs):
        if "cc" not in self.sems:
            self._mksem("cc")
        waits = self._deps("pool", list(rbufs), list(wbufs))
        self.cnt["cc"] += 1
        tok = ("cc", self.cnt["cc"])
        def fn(e, ins=ins, outs=outs, kind=kind):
            return e.collective_compute(kind, ALU.bypass, replica_groups=[[0, 1], [2, 3], [4, 5], [6, 7]],
                                        ins=list(ins), outs=list(outs))
        self.ops["pool"].append((fn, waits, "cc", 1))
        self._commit(tok, list(rbufs), list(wbufs))
        return tok

    def wait_all(self, eng, bufs):
        waits = self._deps(eng, list(bufs), [])
        self.ops[eng].append((None, waits, None, 0))

    def build(self):
        nc = self.nc
        sems = self.sems
        ops = self.ops
        with nc.Block() as block:
            def emit(e, lst):
                for fn, waits, key, amt in lst:
                    for k, v in waits:
                        e.wait_ge(sems[k], v)
                    if fn is not None:
                        ins = fn(e)
                        ins.then_inc(sems[key], amt)

            @block.tensor
            def _(e):
                emit(e, ops["pe"])

            @block.scalar
            def _(e):
                emit(e, ops["act"])

            @block.vector
            def _(e):
                emit(e, ops["dve"])

            @block.gpsimd
            def _(e):
                emit(e, ops["pool"])

            @block.sync
            def _(e):
                emit(e, ops["sp"])
        self.stack.close()
        return nc


NTOK = 2048
NBLK = 16
D = 1024
DI = 2048
EPS = 1e-6


class PsumPool:
    def __init__(self, K, n):
        self.tiles = [K.psum(f"pp{i}", [128, 512], F32) for i in range(n)]
        self.i = 0

    def next(self):
        t = self.tiles[self.i]
        self.i = (self.i + 1) % len(self.tiles)
        return t


def bcast_load(K, dst, src_1d, n, eng="sp"):
    K.dma(eng, dst[:], src_1d.partition_broadcast(128), writes=dst.b)


def emit_consts(K):
    C = {}
    C["ident_f"] = K.sbuf("ident_f", [128, 128], F32)
    C["ident"] = K.sbuf("ident", [128, 128], BF16)
    idf = C["ident_f"]; idb = C["ident"]
    K.op("pool", lambda e: e.memset(idf[:], 1.0), writes=idf.b)
    K.op("pool", lambda e: e.affine_select(out=idf[:], in_=idf[:], pattern=[[-1, 128]],
                                           compare_op=ALU.is_equal, fill=0.0, base=0,
                                           channel_multiplier=1), reads=idf.b, writes=idf.b)
    K.op("dve", lambda e: e.tensor_copy(out=idb[:], in_=idf[:]), reads=idf.b, writes=idb.b)
    C["ones_bf"] = K.sbuf("ones_bf", [128, 128], BF16)
    ob = C["ones_bf"]
    K.op("dve", lambda e: e.memset(ob[:], 1.0), writes=ob.b)
    return C


def rmsnorm_block(K, W, hsrc, gB, out_bf, tag):
    h_ap, h_b = hsrc
    o_ap, o_b = out_bf
    junk = W["junk"]; ss = W["ss"]; rstd = W["rstd"]
    K.op("act", lambda e: e.activation(out=junk[:, 0:D], in_=h_ap, func=AF.Square, accum_out=ss[:, 0:1]),
         reads=h_b, writes=junk.b + ss.b)
    K.op("dve", lambda e: e.tensor_scalar(out=rstd[:, 0:1], in0=ss[:, 0:1], scalar1=1.0 / D, scalar2=EPS,
                                          op0=ALU.mult, op1=ALU.add), reads=ss.b, writes=rstd.b)
    K.op("act", lambda e: e.sqrt(out=rstd[:, 0:1], in_=rstd[:, 0:1]), reads=rstd.b, writes=rstd.b)
    K.op("dve", lambda e: e.reciprocal(out=rstd[:, 0:1], in_=rstd[:, 0:1]), reads=rstd.b, writes=rstd.b)
    K.op("dve", lambda e: e.scalar_tensor_tensor(out=o_ap, in0=h_ap, scalar=rstd[:, 0:1], in1=gB[:],
                                                 op0=ALU.mult, op1=ALU.mult),
         reads=h_b + rstd.b + gB.b, writes=o_b)


def norm_transpose_sb(K, C, W, P, h, sb, gB, hnT):
    junk = W["junk"]; ss = W["ss"]; rstd = W["rstd"]; hn = W["hn"]; tp = W["tp"]
    for j in range(4):
        blk = sb * 4 + j
        K.op("act", lambda e, j=j, blk=blk: e.activation(out=junk[:, 0:D], in_=h[:, blk, :], func=AF.Square,
                                                        accum_out=ss[:, 4 + j:5 + j]),
             reads=[h.b[blk]], writes=junk.b + ss.b)
    K.op("dve", lambda e: e.tensor_scalar(out=rstd[:, 4:8], in0=ss[:, 4:8], scalar1=1.0 / D, scalar2=EPS,
                                          op0=ALU.mult, op1=ALU.add), reads=ss.b, writes=rstd.b)
    K.op("act", lambda e: e.sqrt(out=rstd[:, 4:8], in_=rstd[:, 4:8]), reads=rstd.b, writes=rstd.b)
    K.op("dve", lambda e: e.reciprocal(out=rstd[:, 4:8], in_=rstd[:, 4:8]), reads=rstd.b, writes=rstd.b)
    for j in range(4):
        blk = sb * 4 + j
        K.op("dve", lambda e, j=j, blk=blk: e.scalar_tensor_tensor(out=hn[:], in0=h[:, blk, :], scalar=rstd[:, 4 + j:5 + j],
                                                                  in1=gB[:], op0=ALU.mult, op1=ALU.mult),
             reads=[h.b[blk]] + rstd.b + gB.b, writes=hn.b)
        for k in range(8):
            K.op("pe", lambda e, k=k: e.transpose(tp[:, k * 128:(k + 1) * 128], hn[:, k * 128:(k + 1) * 128],
                                                  C["ident"][:]),
                 reads=hn.b + C["ident"].b, writes=tp.b)
        K.op("act", lambda e, j=j: e.copy(out=hnT[:, :, j * 128:(j + 1) * 128],
                                           in_=tp[:].rearrange("p (k t) -> p k t", k=8)),
             reads=tp.b, writes=[hnT.b[j]])


def emit_gmlp(K, C, P, h, prm, lname):
    nc = K.nc
    L = contextlib.ExitStack()
    K.stack_push(L)
    P = PsumPool(K, 6)
    w_in, ln_g, ln_b, w_s, b_s, w_out, norm_g = (prm[lname + s] for s in
                                                 ("w_in", "ln_g", "ln_b", "w_s", "b_s", "w_out", "norm_g"))
    W = {}
    W["junk"] = K.sbuf("junk", [128, 1024], F32)
    W["ss"] = K.sbuf("ss", [128, 8], F32)
    W["rstd"] = K.sbuf("rstd", [128, 8], F32)
    W["hn"] = K.sbuf("hn", [128, D], BF16)
    W["tp"] = K.psum("tp", [128, 1024], BF16)
    gB = K.sbuf("gB", [128, D], F32)
    lgc = K.sbuf("lgc", [128, 16], F32)
    lbc = K.sbuf("lbc", [128, 16], F32)
    bias2 = K.sbuf("bias2", [128, 16, 128], F32)
    bsB = K.sbuf("bsB", [128, 8, 128], F32)
    WsT = K.sbuf("WsT", [128, 8, 128], BF16)
    hnTs = [K.sbuf(f"hnT{i}", [128, 8, 512], BF16, nb=4) for i in range(2)]
    NWB = 4
    wbuf = [K.sbuf(f"wbuf{i}", [128, 8, 512], BF16) for i in range(NWB)]
    wo_tiles = [K.sbuf(f"wobuf{i}", [128, 16, 256], BF16) for i in range(2)]
    uT = K.sbuf("uT", [128, 16, 512], BF16, nb=16)
    szT = K.sbuf("szT", [128, 16, 512], BF16, nb=16)
    vtok = K.sbuf("vtok", [128, 4, DI], BF16, nb=4)
    st1 = K.sbuf("st1", [128, 4, 4], F32, nb=4)
    st2 = K.sbuf("st2", [128, 4, 4], F32, nb=4)
    mv = K.sbuf("mv", [128, 8], F32)
    t1 = [K.sbuf(f"t1_{i}", [128, 512], F32) for i in range(2)]

    bcast_load(K, gB, norm_g, D)
    K.dma("sp", lgc[:], ln_g.rearrange("(c r) -> r c", r=128), writes=lgc.b, allow_slow_non_contiguous=True)
    K.dma("sp", lbc[:], ln_b.rearrange("(c r) -> r c", r=128), writes=lbc.b, allow_slow_non_contiguous=True)
    K.dma("sp", bsB[:].rearrange("p g t -> p (g t)"), b_s.rearrange("g t -> (g t)").partition_broadcast(128),
          writes=bsB.b)
    wsf_t = W["hn"]
    wsf = wsf_t[:].rearrange("p (g s) -> p g s", g=8)
    K.dma("pool", wsf, w_s.rearrange("g t s -> t g s"), writes=wsf_t.b)
    tp = W["tp"]
    for g in range(8):
        K.op("pe", lambda e, g=g: e.transpose(tp[:, g * 128:(g + 1) * 128], wsf[:, g, :], C["ident"][:]),
             reads=wsf_t.b + C["ident"].b, writes=tp.b)
    K.op("dve", lambda e: e.tensor_copy(out=WsT[:].rearrange("p g t -> p (g t)"), in_=tp[:]),
         reads=tp.b, writes=WsT.b)
    for g in range(8):
        K.op("pool", lambda e, g=g: e.affine_select(out=WsT[:, g, :], in_=WsT[:, g, :], pattern=[[1, 128]],
                                                    compare_op=ALU.is_ge, fill=0.0, base=0,
                                                    channel_multiplier=-1), reads=WsT.b, writes=WsT.b)

    psr = P.next()
    for g in range(8):
        K.op("pe", lambda e, g=g: e.matmul(psr[:, 0:128], lhsT=C["ones_bf"][:], rhs=WsT[:, g, :], start=True, stop=True),
             reads=C["ones_bf"].b + WsT.b, writes=psr.b)
        for fl in range(2):
            fc = 2 * g + fl
            K.op("dve", lambda e, g=g, fc=fc: e.scalar_tensor_tensor(
                out=bias2[:, fc, :], in0=psr[:, 0:128], scalar=lbc[:, fc:fc + 1], in1=bsB[:, g, :],
                op0=ALU.mult, op1=ALU.add), reads=psr.b + lbc.b + bsB.b, writes=bias2.b)
    w_in_v = w_in.rearrange("(k p) n -> p k n", p=128)
    w_out_v = w_out.rearrange("(k p) n -> p k n", p=128)
    CG_ORDER = (4, 5, 6, 7, 0, 1, 2, 3, 8, 9, 10, 11)
    wseq = [cg for _ in range(4) for cg in CG_ORDER]
    wst = {"issued": 0}

    def issue_w(upto):
        while wst["issued"] < min(upto, len(wseq)):
            i = wst["issued"]; cgi = wseq[i]
            wbi = wbuf[i % NWB]
            K.dma("pool", wbi[:], w_in_v[:, :, cgi * 512:(cgi + 1) * 512], writes=wbi.b)
            wst["issued"] += 1
    issue_w(NWB - 1)
    nload = 0
    for sb in range(4):
        hnT = hnTs[sb % 2]
        if sb == 0:
            norm_transpose_sb(K, C, W, P, h, 0, gB, hnT)
        def emit_ln():
            for j in range(4):
                K.op("dve", lambda e, j=j: e.reduce_sum(out=mv[:, 0:1], in_=st1[:, j, :], axis=AX.X),
                     reads=[st1.b[j]], writes=mv.b)
                K.op("dve", lambda e, j=j: e.reduce_sum(out=mv[:, 1:2], in_=st2[:, j, :], axis=AX.X),
                     reads=[st2.b[j]], writes=mv.b)
                K.op("dve", lambda e: e.tensor_scalar(out=mv[:, 2:4], in0=mv[:, 0:2], scalar1=1.0 / DI, scalar2=None,
                                                      op0=ALU.mult), reads=mv.b, writes=mv.b)
                K.op("dve", lambda e: e.tensor_tensor(out=mv[:, 4:5], in0=mv[:, 2:3], in1=mv[:, 2:3], op=ALU.mult),
                     reads=mv.b, writes=mv.b)
                K.op("dve", lambda e: e.tensor_tensor(out=mv[:, 5:6], in0=mv[:, 3:4], in1=mv[:, 4:5], op=ALU.subtract),
                     reads=mv.b, writes=mv.b)
                K.op("dve", lambda e: e.tensor_scalar(out=mv[:, 6:7], in0=mv[:, 5:6], scalar1=EPS, scalar2=None,
                                                      op0=ALU.add), reads=mv.b, writes=mv.b)
                K.op("act", lambda e: e.sqrt(out=mv[:, 6:7], in_=mv[:, 6:7]), reads=mv.b, writes=mv.b)
                K.op("dve", lambda e: e.reciprocal(out=mv[:, 6:7], in_=mv[:, 6:7]), reads=mv.b, writes=mv.b)
                K.op("dve", lambda e, j=j: e.tensor_scalar(out=vtok[:, j, :], in0=vtok[:, j, :], scalar1=mv[:, 2:3],
                                                           scalar2=mv[:, 6:7], op0=ALU.subtract, op1=ALU.mult),
                     reads=[vtok.b[j]] + mv.b, writes=[vtok.b[j]])
        def emit_spatial():
            nt = 0
            for jp in range(2):
                for g in range(8):
                    ps = P.next()
                    for fl in range(2):
                        for jl in range(2):
                            fc = 2 * g + fl; j = 2 * jp + jl
                            K.op("pe", lambda e, fc=fc, j=j, fl=fl, jl=jl, ps=ps, g=g: e.matmul(
                                ps[:, (fl * 2 + jl) * 128:(fl * 2 + jl + 1) * 128],
                                lhsT=vtok[:, j, fc * 128:(fc + 1) * 128], rhs=WsT[:, g, :], start=True, stop=True),
                                reads=[vtok.b[j]] + WsT.b, writes=ps.b)
                    tt = t1[nt % 2]; nt += 1
                    for fl in range(2):
                        fc = 2 * g + fl
                        K.op("dve", lambda e, ps=ps, tt=tt, fc=fc, fl=fl: e.scalar_tensor_tensor(
                            out=tt[:, fl * 256:(fl + 1) * 256].rearrange("p (a t) -> p a t", a=2),
                            in0=ps[:, fl * 256:(fl + 1) * 256].rearrange("p (a t) -> p a t", a=2),
                            scalar=lgc[:, fc:fc + 1], in1=bias2[:, fc:fc + 1, :].to_broadcast([128, 2, 128]),
                            op0=ALU.mult, op1=ALU.add), reads=ps.b + lgc.b + bias2.b, writes=tt.b)
                    usl = uT[:, 2 * g:2 * g + 2, jp * 256:(jp + 1) * 256]
                    K.op("pool", lambda e, tt=tt, usl=usl: e.tensor_tensor(
                        out=usl, in0=tt[:].rearrange("p (a t) -> p a t", a=2), in1=usl, op=ALU.mult),
                        reads=tt.b + [uT.b[2 * g], uT.b[2 * g + 1]], writes=[uT.b[2 * g], uT.b[2 * g + 1]])
        for cg in CG_ORDER:
            issue_w(nload + NWB)
            wb = wbuf[nload % NWB]; nload += 1
            if cg == 9:
                K.dma("pool", wo_tiles[0][:], w_out_v[:, :, 0:256], writes=wo_tiles[0].b)
            kind = cg // 4
            if kind != 1:
                dstT = uT if kind == 0 else szT
                fn = AF.Gelu_apprx_tanh if kind == 0 else AF.Silu
                for sub in range(4):
                    fc = (cg % 4) * 4 + sub
                    ps = P.next()
                    for k in range(8):
                        K.op("pe", lambda e, k=k, sub=sub, ps=ps, wb=wb, hnT=hnT: e.matmul(
                            ps[:], lhsT=wb[:, k, sub * 128:(sub + 1) * 128], rhs=hnT[:, k, :],
                            start=(k == 0), stop=(k == 7)), reads=wb.b + hnT.b, writes=ps.b)
                    K.op("act", lambda e, fc=fc, ps=ps, dstT=dstT, fn=fn: e.activation(
                        out=dstT[:, fc, :], in_=ps[:], func=fn), reads=ps.b, writes=[dstT.b[fc]])
            else:
                cgv = cg % 4
                for j in range(4):
                    ps = P.next()
                    for k in range(8):
                        K.op("pe", lambda e, k=k, j=j, ps=ps, wb=wb, hnT=hnT: e.matmul(
                            ps[:], lhsT=hnT[:, k, j * 128:(j + 1) * 128], rhs=wb[:, k, :],
                            start=(k == 0), stop=(k == 7)), reads=wb.b + [hnT.b[j]], writes=ps.b)
                    K.op("act", lambda e, j=j, ps=ps, cgv=cgv: e.activation(
                        out=vtok[:, j, cgv * 512:(cgv + 1) * 512], in_=ps[:], func=AF.Gelu_apprx_tanh,
                        accum_out=st1[:, j, cgv:cgv + 1]), reads=ps.b, writes=[vtok.b[j], st1.b[j]])
                    K.op("act", lambda e, j=j, cgv=cgv: e.activation(
                        out=W["junk"][:, 0:512], in_=vtok[:, j, cgv * 512:(cgv + 1) * 512], func=AF.Square,
                        accum_out=st2[:, j, cgv:cgv + 1]), reads=[vtok.b[j]], writes=W["junk"].b + [st2.b[j]])
                if cg == 7:
                    emit_ln()
            if cg == 3:
                emit_spatial()
        for fc in range(16):
            K.op("dve", lambda e, fc=fc: e.tensor_tensor(out=uT[:, fc, :], in0=uT[:, fc, :], in1=szT[:, fc, :], op=ALU.mult),
                 reads=[uT.b[fc], szT.b[fc]], writes=[uT.b[fc]])
        for cgo in range(4):
            if cgo == 1 and sb + 1 < 4:
                norm_transpose_sb(K, C, W, P, h, sb + 1, gB, hnTs[(sb + 1) % 2])
            wo = wo_tiles[cgo % 2]
            if cgo + 1 < 4:
                wn = wo_tiles[(cgo + 1) % 2]
                K.dma("pool", wn[:], w_out_v[:, :, (cgo + 1) * 256:(cgo + 2) * 256], writes=wn.b)
            for j in range(4):
                blk = sb * 4 + j
                ps = P.next()
                for k in range(16):
                    K.op("pe", lambda e, k=k, j=j, ps=ps, wo=wo: e.matmul(
                        ps[:, 0:256], lhsT=uT[:, k, j * 128:(j + 1) * 128], rhs=wo[:, k, :],
                        start=(k == 0), stop=(k == 15)), reads=wo.b + [uT.b[k]], writes=ps.b)
                hs = h[:, blk, cgo * 256:(cgo + 1) * 256]
                K.op("dve", lambda e, hs=hs, ps=ps: e.tensor_tensor(out=hs, in0=hs, in1=ps[:, 0:256], op=ALU.add),
                     reads=ps.b + [h.b[blk]], writes=[h.b[blk]])
    K.barrier()
    K.stack_pop()


import math as _math

PI = _math.pi


def tt(K, eng, out, in0, in1, op, reads, writes):
    return K.op(eng, lambda e: e.tensor_tensor(out=out, in0=in0, in1=in1, op=op), reads=reads, writes=writes)


def ts(K, eng, out, in0, s1, op0, reads, writes, s2=None, op1=None):
    if op1 is None:
        return K.op(eng, lambda e: e.tensor_scalar(out=out, in0=in0, scalar1=s1, scalar2=None, op0=op0),
                    reads=reads, writes=writes)
    return K.op(eng, lambda e: e.tensor_scalar(out=out, in0=in0, scalar1=s1, scalar2=s2, op0=op0, op1=op1),
                reads=reads, writes=writes)


def stt(K, eng, out, in0, scalar, in1, op0, op1, reads, writes):
    return K.op(eng, lambda e: e.scalar_tensor_tensor(out=out, in0=in0, scalar=scalar, in1=in1, op0=op0, op1=op1),
                reads=reads, writes=writes)


def act(K, out, in_, func, reads, writes, **kw):
    return K.op("act", lambda e: e.activation(out=out, in_=in_, func=func, **kw), reads=reads, writes=writes)


def emit_s5_prep(K, C, prm, p, S, cmask, PRE):
    L = contextlib.ExitStack(); K.stack_push(L)
    a_re, a_im, log_step = prm[p + "a_re"], prm[p + "a_im"], prm[p + "log_step"]
    shp = [128, 64]
    def T(name, shape=shp, dt=F32):
        return K.sbuf(name, shape, dt)
    are, aim, lst = PRE["are"], PRE["aim"], PRE["lst"]
    step, dr, th, mag, imag = T("step"), T("dr"), T("th"), T("mag"), T("imag")
    act(K, step[:], lst[:], AF.Exp, lst.b, step.b)
    tt(K, "dve", dr[:], are[:], step[:], ALU.mult, are.b + step.b, dr.b)
    tt(K, "dve", th[:], aim[:], step[:], ALU.mult, aim.b + step.b, th.b)
    act(K, mag[:], dr[:], AF.Exp, dr.b, mag.b)
    act(K, imag[:], dr[:], AF.Exp, dr.b, imag.b, scale=-1.0)
    sn, cs, t1, t2 = T("sn"), T("cs"), T("t1"), T("t2")
    hpi = K.sbuf("hpi", [128, 1], F32)
    K.op("dve", lambda e: e.memset(hpi[:], PI / 2), writes=hpi.b)
    act(K, sn[:], th[:], AF.Sin, th.b, sn.b, scale=1.0 / 16)
    act(K, cs[:], th[:], AF.Sin, th.b + hpi.b, cs.b, scale=1.0 / 16, bias=hpi[:, 0:1])
    for _ in range(4):
        tt(K, "dve", t1[:], cs[:], cs[:], ALU.mult, cs.b, t1.b)
        tt(K, "dve", t2[:], sn[:], sn[:], ALU.mult, sn.b, t2.b)
        stt(K, "dve", sn[:], cs[:], 2.0, sn[:], ALU.mult, ALU.mult, cs.b + sn.b, sn.b)
        tt(K, "dve", cs[:], t1[:], t2[:], ALU.subtract, t1.b + t2.b, cs.b)
    zr, zi, wr, wi = T("zr"), T("zi"), T("wr"), T("wi")
    tt(K, "dve", zr[:], mag[:], cs[:], ALU.mult, mag.b + cs.b, zr.b)
    tt(K, "dve", zi[:], mag[:], sn[:], ALU.mult, mag.b + sn.b, zi.b)
    tt(K, "dve", wr[:], imag[:], cs[:], ALU.mult, imag.b + cs.b, wr.b)
    stt(K, "dve", wi[:], imag[:], -1.0, sn[:], ALU.mult, ALU.mult, imag.b + sn.b, wi.b)
    nr, den, kr, ki, tmp = T("nr"), T("den"), T("kr"), T("ki"), T("tmpk")
    ts(K, "dve", nr[:], zr[:], -1.0, ALU.add, zr.b, nr.b)
    tt(K, "dve", den[:], are[:], are[:], ALU.mult, are.b, den.b)
    tt(K, "dve", tmp[:], aim[:], aim[:], ALU.mult, aim.b, tmp.b)
    tt(K, "dve", den[:], den[:], tmp[:], ALU.add, den.b + tmp.b, den.b)
    K.op("dve", lambda e: e.reciprocal(out=den[:], in_=den[:]), reads=den.b, writes=den.b)
    tt(K, "dve", kr[:], nr[:], are[:], ALU.mult, nr.b + are.b, kr.b)
    tt(K, "dve", tmp[:], zi[:], aim[:], ALU.mult, zi.b + aim.b, tmp.b)
    tt(K, "dve", kr[:], kr[:], tmp[:], ALU.add, kr.b + tmp.b, kr.b)
    tt(K, "dve", kr[:], kr[:], den[:], ALU.mult, kr.b + den.b, kr.b)
    tt(K, "dve", ki[:], zi[:], are[:], ALU.mult, zi.b + are.b, ki.b)
    tt(K, "dve", tmp[:], nr[:], aim[:], ALU.mult, nr.b + aim.b, tmp.b)
    tt(K, "dve", ki[:], ki[:], tmp[:], ALU.subtract, ki.b + tmp.b, ki.b)
    tt(K, "dve", ki[:], ki[:], den[:], ALU.mult, ki.b + den.b, ki.b)

    tp = K.psum("tp5", [128, 1024], BF16)
    L2 = contextlib.ExitStack(); K.stack_push(L2)
    bnr = K.sbuf("bnr", [128, 64, 16], F32); bni = K.sbuf("bni", [128, 64, 16], F32)
    K.dma("sp", bnr[:], prm[p + "b_re"].rearrange("(j gl) p h -> (gl p) j h", gl=2), writes=bnr.b)
    K.dma("sp", bni[:], prm[p + "b_im"].rearrange("(j gl) p h -> (gl p) j h", gl=2), writes=bni.b)
    bbr = K.sbuf("bbr", [128, 64, 16], F32); bbi = K.sbuf("bbi", [128, 64, 16], F32)
    btmp = K.sbuf("btmp", [128, 64, 16], F32)
    krb = kr[:].unsqueeze(2).to_broadcast([128, 64, 16]); kib = ki[:].unsqueeze(2).to_broadcast([128, 64, 16])
    tt(K, "dve", bbr[:], bnr[:], krb, ALU.mult, bnr.b + kr.b, bbr.b)
    tt(K, "dve", btmp[:], bni[:], kib, ALU.mult, bni.b + ki.b, btmp.b)
    tt(K, "dve", bbr[:], bbr[:], btmp[:], ALU.subtract, bbr.b + btmp.b, bbr.b)
    tt(K, "dve", bbi[:], bni[:], krb, ALU.mult, bni.b + kr.b, bbi.b)
    tt(K, "dve", btmp[:], bnr[:], kib, ALU.mult, bnr.b + ki.b, btmp.b)
    tt(K, "dve", bbi[:], bbi[:], btmp[:], ALU.add, bbi.b + btmp.b, bbi.b)
    pin = K.sbuf("pin", [128, 64, 128], BF16)
    stg = [K.sbuf(f"stg{i}", [128, 8, 128], BF16) for i in range(2)]
    ns = 0
    for ri, src in enumerate((bbr, bbi)):
        K.op("pool", lambda e: e.memset(pin[:], 0.0), writes=pin.b)
        for q in range(4):
            for gl in range(2):
                K.op("dve", lambda e, q=q, gl=gl, src=src: e.tensor_copy(
                    out=pin[gl * 64:(gl + 1) * 64, q::4, 32 * q + 16 * gl:32 * q + 16 * gl + 16],
                    in_=src[gl * 64:(gl + 1) * 64, q::4, :]), reads=src.b, writes=pin.b)
        for i8 in range(8):
            for jl in range(8):
                j = i8 * 8 + jl
                K.op("pe", lambda e, j=j, jl=jl: e.transpose(tp[:, jl * 128:(jl + 1) * 128], pin[:, j, :],
                                                             C["ident"][:]),
                     reads=pin.b + C["ident"].b, writes=tp.b)
            sg = stg[ns % 2]; ns += 1
            K.op("dve", lambda e, sg=sg: e.tensor_copy(out=sg[:].rearrange("r j c -> r (j c)"), in_=tp[:]),
                 reads=tp.b, writes=sg.b)
            K.dma("sp", S["bpad"][i8 * 8:(i8 + 1) * 8, ri].rearrange("j r c -> r j c"), sg[:], reads=sg.b,
                  writes=[S["bpad_b"]])
    K.barrier(); K.stack_pop()

    L3 = contextlib.ExitStack(); K.stack_push(L3)
    cin = K.sbuf("cin", [128, 16, 128], BF16)
    cn = K.sbuf("cn", [128, 16, 64], F32)
    for ri, (nm, sgn, dst) in enumerate((("c_re", 1.0, S["CTr"]), ("c_im", -1.0, S["CTn"]), ("c_re", -1.0, S["CTrn"]))):
        K.dma("sp", cn[:], prm[p + nm].rearrange("(jj q gl) h c -> (q gl h) jj c", q=4, gl=2), writes=cn.b)
        for gl in range(2):
            ts(K, "dve", cin[:, :, gl * 64:(gl + 1) * 64], cn[:], cmask[:, gl:gl + 1], ALU.mult,
               cn.b + cmask.b, cin.b, s2=sgn, op1=ALU.mult)
        for i8 in range(2):
            for jl in range(8):
                jj = i8 * 8 + jl
                K.op("pe", lambda e, jj=jj, jl=jl: e.transpose(tp[:, jl * 128:(jl + 1) * 128], cin[:, jj, :],
                                                               C["ident"][:]),
                     reads=cin.b + C["ident"].b, writes=tp.b)
            K.op("dve", lambda e, i8=i8, dst=dst: e.tensor_copy(
                out=dst[:, i8 * 8:(i8 + 1) * 8, :].rearrange("r j c -> r (j c)"), in_=tp[:]),
                reads=tp.b, writes=dst.b)
    K.barrier(); K.stack_pop()

    L4 = contextlib.ExitStack(); K.stack_push(L4)
    NB = 2
    Lp = K.sbuf("Lp", [128, 64, 4, 32], F32)
    Hp = K.sbuf("Hp", [128, 64, 4, 16], F32)
    cur = K.sbuf("pcur", [128, 64, 4], F32)
    cur2 = K.sbuf("pcur2", [128, 64, 4], F32)
    ctmp = K.sbuf("pctmp", [128, 64, 4], F32)
    ptm = [K.sbuf(f"pptm{i}", [128, 64, 16], F32) for i in range(2)]
    for ti in range(1):
        eng = "dve" if ti == 0 else "pool"
        sr, si = (zr, zi) if ti == 0 else (wr, wi)
        cr_ = cur[:, :, 2 * ti:2 * ti + 1]; ci_ = cur[:, :, 2 * ti + 1:2 * ti + 2]
        K.op(eng, lambda e, cr_=cr_, sr=sr: e.tensor_copy(out=cr_, in_=sr[:].unsqueeze(2)), reads=sr.b, writes=cur.b)
        K.op(eng, lambda e, ci_=ci_, si=si: e.tensor_copy(out=ci_, in_=si[:].unsqueeze(2)), reads=si.b, writes=cur.b)
        pt = ptm[ti]
        for (Tt, nsteps) in ((Lp, 5), (Hp, 4)):
            Tr = Tt[:, :, 2 * ti, :]; Ti = Tt[:, :, 2 * ti + 1, :]
            K.op(eng, lambda e, Tr=Tr: e.memset(Tr[:, :, 0:1], 1.0), writes=Tt.b)
            K.op(eng, lambda e, Ti=Ti: e.memset(Ti[:, :, 0:1], 0.0), writes=Tt.b)
            for k in range(nsteps):
                n = 1 << k
                crb = cr_.to_broadcast([128, 64, n]); cib = ci_.to_broadcast([128, 64, n])
                A_r = Tr[:, :, 0:n]; A_i = Ti[:, :, 0:n]; O_r = Tr[:, :, n:2 * n]; O_i = Ti[:, :, n:2 * n]
                tm = pt[:, :, 0:n]
                tt(K, eng, tm, A_i, cib, ALU.mult, Tt.b + cur.b, pt.b)
                tt(K, eng, O_r, A_r, crb, ALU.mult, Tt.b + cur.b, Tt.b)
                tt(K, eng, O_r, O_r, tm, ALU.subtract, Tt.b + pt.b, Tt.b)
                tt(K, eng, tm, A_i, crb, ALU.mult, Tt.b + cur.b, pt.b)
                tt(K, eng, O_i, A_r, cib, ALU.mult, Tt.b + cur.b, Tt.b)
                tt(K, eng, O_i, O_i, tm, ALU.add, Tt.b + pt.b, Tt.b)
                c2r = cur2[:, :, 2 * ti:2 * ti + 1]; c2i = cur2[:, :, 2 * ti + 1:2 * ti + 2]
                t_a = ctmp[:, :, 2 * ti:2 * ti + 1]; t_b = ctmp[:, :, 2 * ti + 1:2 * ti + 2]
                tt(K, eng, t_a, cr_, cr_, ALU.mult, cur.b, ctmp.b)
                tt(K, eng, t_b, ci_, ci_, ALU.mult, cur.b, ctmp.b)
                tt(K, eng, c2r, t_a, t_b, ALU.subtract, ctmp.b, cur2.b)
                tt(K, eng, c2i, cr_, ci_, ALU.mult, cur.b, cur2.b)
                tt(K, eng, c2i, c2i, c2i, ALU.add, cur2.b, cur2.b)
                K.op(eng, lambda e, cr_=cr_, c2r=c2r: e.tensor_copy(out=cr_, in_=c2r), reads=cur2.b, writes=cur.b)
                K.op(eng, lambda e, ci_=ci_, c2i=c2i: e.tensor_copy(out=ci_, in_=c2i), reads=cur2.b, writes=cur.b)
        if ti == 0:
            K.op(eng, lambda e, cr_=cr_: e.tensor_copy(out=S["L5"][:, :, 0:1], in_=cr_), reads=cur.b, writes=S["L5"].b)
            K.op(eng, lambda e, ci_=ci_: e.tensor_copy(out=S["L5"][:, :, 1:2], in_=ci_), reads=cur.b, writes=S["L5"].b)
    bv = K.sbuf("pbv", [128, 32], F32); on32 = K.sbuf("pon32", [128, 32], F32)
    K.op("dve", lambda e: e.memset(on32[:], 1.0), writes=on32.b)
    K.op("dve", lambda e: e.tensor_tensor_scan(out=bv[:], data0=on32[:], data1=on32[:], initial=-1.0,
                                               op0=ALU.mult, op1=ALU.add), reads=on32.b, writes=bv.b)
    gL = K.sbuf("pgL", [128, 64, 32], F32); gH = K.sbuf("pgH", [128, 64, 16], F32); gHn = K.sbuf("pgHn", [128, 64, 16], F32)
    tt(K, "dve", gL[:], dr[:].unsqueeze(2).to_broadcast([128, 64, 32]), bv[:].unsqueeze(1).to_broadcast([128, 64, 32]),
       ALU.mult, dr.b + bv.b, gL.b)
    tt(K, "dve", gH[:], dr[:].unsqueeze(2).to_broadcast([128, 64, 16]), bv[:, 0:16].unsqueeze(1).to_broadcast([128, 64, 16]),
       ALU.mult, dr.b + bv.b, gH.b)
    act(K, gL[:], gL[:], AF.Exp, gL.b, gL.b, scale=-2.0)
    act(K, gH[:], gH[:], AF.Exp, gH.b, gH.b, scale=-64.0)
    ts(K, "dve", gHn[:], gH[:], -1.0, ALU.mult, gH.b, gHn.b)
    tab = [K.sbuf(f"ptab{i}", [128, NB, 4, 512], F32) for i in range(2)]
    otm = [K.sbuf(f"potm{i}", [128, NB, 512], F32) for i in range(2)]
    for bi in range(64 // NB):
        tb = tab[bi % 2]
        j0 = bi * NB
        shp4 = [128, NB, 16, 32]
        om = otm[0]
        Hr = Hp[:, j0:j0 + NB, 0, :].unsqueeze(3).to_broadcast(shp4)
        Hi = Hp[:, j0:j0 + NB, 1, :].unsqueeze(3).to_broadcast(shp4)
        Lr = Lp[:, j0:j0 + NB, 0, :].unsqueeze(2).to_broadcast(shp4)
        Li = Lp[:, j0:j0 + NB, 1, :].unsqueeze(2).to_broadcast(shp4)
        Tr = tb[:, :, 0, :].rearrange("r j (a b) -> r j a b", a=16)
        Ti = tb[:, :, 1, :].rearrange("r j (a b) -> r j a b", a=16)
        o4 = om[:].rearrange("r j (a b) -> r j a b", a=16)
        rd = Hp.b + Lp.b
        tt(K, "dve", o4, Hi, Li, ALU.mult, rd, om.b)
        tt(K, "dve", Tr, Hr, Lr, ALU.mult, rd, tb.b)
        tt(K, "dve", Tr, Tr, o4, ALU.subtract, tb.b + om.b, tb.b)
        tt(K, "dve", o4, Hi, Lr, ALU.mult, rd, om.b)
        tt(K, "dve", Ti, Hr, Li, ALU.mult, rd, tb.b)
        tt(K, "dve", Ti, Ti, o4, ALU.add, tb.b + om.b, tb.b)
        omp = otm[1]
        g4 = omp[:].rearrange("r j (a b) -> r j a b", a=16)
        gHb = gH[:, j0:j0 + NB, :].unsqueeze(3).to_broadcast(shp4)
        gHnb = gHn[:, j0:j0 + NB, :].unsqueeze(3).to_broadcast(shp4)
        gLb = gL[:, j0:j0 + NB, :].unsqueeze(2).to_broadcast(shp4)
        tt(K, "pool", g4, gHb, gLb, ALU.mult, gH.b + gL.b, omp.b)
        tt(K, "pool", tb[:, :, 2, :], tb[:, :, 0, :], omp[:], ALU.mult, tb.b + omp.b, tb.b)
        tt(K, "pool", g4, gHnb, gLb, ALU.mult, gHn.b + gL.b, omp.b)
        tt(K, "pool", tb[:, :, 3, :], tb[:, :, 1, :], omp[:], ALU.mult, tb.b + omp.b, tb.b)
        K.dma("sp", S["tabs"][j0:j0 + NB].rearrange("j t r c -> r j t c"), tb[:], reads=tb.b,
              writes=[S["tabs_b"]])
    K.barrier(); K.stack_pop()
    K.barrier(); K.stack_pop()


def emit_s5(K, C, h, prm, p, kflag, cmask, PRE):
    nc = K.nc
    Lall = contextlib.ExitStack(); K.stack_push(Lall)
    gB = K.sbuf("gB5", [128, D], F32)
    bcast_load(K, gB, prm[p + "norm_g"], D)
    dsk = PRE["dsk"]; bgl = PRE["bgl"]
    S = {}
    S["tabs"] = K.dram("s5_tabs", [64, 4, 128, 512], F32, "Internal")
    S["bpad"] = K.dram("s5_bpad", [64, 2, 128, 128], BF16, "Internal")
    S["tabs_b"] = Buf("tabs"); S["bpad_b"] = Buf("bpad")
    S["CTr"] = K.sbuf("CTr", [128, 16, 128], BF16)
    S["CTn"] = K.sbuf("CTn", [128, 16, 128], BF16)
    S["CTrn"] = K.sbuf("CTrn", [128, 16, 128], BF16)
    S["L5"] = K.sbuf("L5", [128, 64, 2], F32)
    emit_s5_prep(K, C, prm, p, S, cmask, PRE)
    uT = K.sbuf("uT5", [128, 16, NTOK], BF16, nb=64)

    w_in_v = prm[p + "w_in"].rearrange("(k r) n -> r k n", r=128)
    w_glu_v = prm[p + "w_glu"].rearrange("(k r) n -> r k n", r=128)
    w_out_v = prm[p + "w_out"].rearrange("(k r) n -> r k n", r=128)

    def mkW():
        W = {}
        W["junk"] = K.sbuf("junk5", [128, 1024], F32)
        W["ss"] = K.sbuf("ss5", [128, 8], F32)
        W["rstd"] = K.sbuf("rstd5", [128, 8], F32)
        W["hn"] = K.sbuf("hn5", [128, D], BF16)
        W["tp"] = K.psum("tpA", [128, 1024], BF16)
        return W

    LA = contextlib.ExitStack(); K.stack_push(LA)
    W = mkW()
    P = PsumPool(K, 6)
    hnTs = [K.sbuf(f"hnT5_{i}", [128, 8, 512], BF16, nb=4) for i in range(2)]
    NWA = 4
    wbuf = [K.sbuf(f"wbA{i}", [128, 8, 512], BF16) for i in range(NWA)]
    wstA = {"issued": 0}

    def issue_a(upto):
        while wstA["issued"] < min(upto, 16):
            i = wstA["issued"]; cgi = i % 4
            K.dma("pool", wbuf[i % NWA][:], w_in_v[:, :, cgi * 512:(cgi + 1) * 512], writes=wbuf[i % NWA].b)
            wstA["issued"] += 1
    issue_a(NWA - 1)
    nl = 0
    for sb in range(4):
        hnT = hnTs[sb % 2]
        if sb == 0:
            norm_transpose_sb(K, C, W, P, h, 0, gB, hnT)
        for cg in range(4):
            if cg == 2 and sb + 1 < 4:
                norm_transpose_sb(K, C, W, P, h, sb + 1, gB, hnTs[(sb + 1) % 2])
            issue_a(nl + NWA)
            wb = wbuf[nl % NWA]; nl += 1
            for sub in range(4):
                fc = cg * 4 + sub
                ps = P.next()
                for k in range(8):
                    K.op("pe", lambda e, k=k, sub=sub, ps=ps, wb=wb, hnT=hnT: e.matmul(
                        ps[:], lhsT=wb[:, k, sub * 128:(sub + 1) * 128], rhs=hnT[:, k, :],
                        start=(k == 0), stop=(k == 7)), reads=wb.b + hnT.b, writes=ps.b)
                K.op("act", lambda e, fc=fc, ps=ps, sb=sb: e.copy(out=uT[:, fc, sb * 512:(sb + 1) * 512], in_=ps[:]),
                     reads=ps.b, writes=[uT.b[fc * 4 + sb]])
    K.barrier(); K.stack_pop()

    LB = contextlib.ExitStack(); K.stack_push(LB)
    psBU = [K.psum(f"psBU{i}", [128, 512], F32) for i in range(4)]
    psY = [K.psum(f"psY{i}", [128, 512], F32) for i in range(4)]
    ones = K.sbuf("ones5", [128, 512], F32)
    K.op("dve", lambda e: e.memset(ones[:], 1.0), writes=ones.b)
    tabt = [K.sbuf(f"tabt{i}", [128, 4, 512], F32) for i in range(2)]
    bpt = [K.sbuf(f"bpt{i}", [128, 2, 128], BF16) for i in range(2)]
    cst = K.sbuf("cst", [128, 64, 5, 2], F32, nb=64)
    NW = 3
    wk = [[K.sbuf(f"wk{s}_{i}", [128, 512], F32) for i in range(4)] for s in range(NW)]
    pk = [[K.sbuf(f"pk{s}_{i}", [128, 512], BF16) for i in range(4)] for s in range(2)]
    tny = K.sbuf("tny", [128, 4], F32)
    nl5i = K.sbuf("nl5i", [128, 64], F32)
    ts(K, "dve", nl5i[:].unsqueeze(2), S["L5"][:, :, 1:2], -1.0, ALU.mult, S["L5"].b, nl5i.b)
    LPre = contextlib.ExitStack(); K.stack_push(LPre)
    fpre = K.sbuf("fpre", [128, 64, 2], F32)
    accA = K.sbuf("accA5", [128, 64, 4, 4], F32, nb=64)
    itp = 0
    for j in range(64):
        jj = j // 4
        tb = tabt[j % 2]; bp = bpt[j % 2]
        K.dma("sp", tb[:, 2:4, :], S["tabs"][j, 2:4].rearrange("t r c -> r t c"), reads=[S["tabs_b"]], writes=tb.b)
        K.dma("sp", bp[:], S["bpad"][j].rearrange("t r c -> r t c"), reads=[S["bpad_b"]], writes=bp.b)
        Mr, Mi = tb[:, 2, :], tb[:, 3, :]
        T2 = wk[j % 2][1]; T3 = wk[j % 2][2]
        tt(K, "pool", T2[:], Mi, Mr, ALU.subtract, tb.b, T2.b)
        tt(K, "pool", T3[:], Mr, Mi, ALU.add, tb.b, T3.b)
        for blk in range(4):
            jk = wk[2][0]
            pR = psBU[(2 * itp) % 4]; pI = psBU[(2 * itp + 1) % 4]; pS = psY[itp % 4]
            itp += 1
            ub = [uT.b[jj * 4 + blk]]
            usl = uT[:, jj, blk * 512:(blk + 1) * 512]
            K.op("pe", lambda e, pR=pR, bp=bp, usl=usl: e.matmul(pR[:], lhsT=bp[:, 0, :], rhs=usl, start=True, stop=True),
                 reads=bp.b + ub, writes=pR.b)
            K.op("pe", lambda e, pI=pI, bp=bp, usl=usl: e.matmul(pI[:], lhsT=bp[:, 1, :], rhs=usl, start=True, stop=True),
                 reads=bp.b + ub, writes=pI.b)
            K.op("pe", lambda e, pS=pS, bp=bp, usl=usl: e.matmul(pS[:], lhsT=bp[:, 0, :], rhs=usl, start=True, stop=False),
                 reads=bp.b + ub, writes=pS.b)
            K.op("pe", lambda e, pS=pS, bp=bp, usl=usl: e.matmul(pS[:], lhsT=bp[:, 1, :], rhs=usl, start=False, stop=True),
                 reads=bp.b + ub, writes=pS.b)
            for ai, (pp, mm, mb) in enumerate(((pS, Mr, tb.b), (pR, T2[:], T2.b), (pI, T3[:], T3.b))):
                K.op("dve", lambda e, pp=pp, mm=mm, ai=ai, jk=jk, j=j, blk=blk: e.scalar_tensor_tensor(
                    out=jk[:], in0=pp[:], scalar=1.0, in1=mm, op0=ALU.mult, op1=ALU.mult, accum_out=accA[:, j, blk, ai:ai + 1]),
                    reads=pp.b + mb, writes=jk.b + [accA.b[j]])
    vv = K.sbuf("vv5", [128, 64, 2], F32); cc = K.sbuf("cc5", [128, 64, 2], F32); t4 = K.sbuf("t45", [128, 64, 4], F32)
    l5r_all = S["L5"][:, :, 0:1]; l5i_all = S["L5"][:, :, 1:2]
    for blk in range(4):
        tt(K, "dve", vv[:, :, 0:1], accA[:, :, blk, 0:1], accA[:, :, blk, 2:3], ALU.subtract, accA.b, vv.b)
        tt(K, "dve", vv[:, :, 1:2], accA[:, :, blk, 0:1], accA[:, :, blk, 1:2], ALU.add, accA.b, vv.b)
        if blk > 0:
            tt(K, "dve", vv[:], vv[:], cc[:], ALU.add, vv.b + cc.b, vv.b)
        dst = cc if blk < 3 else fpre
        tt(K, "dve", t4[:, :, 0:1], vv[:, :, 0:1], l5r_all, ALU.mult, vv.b + S["L5"].b, t4.b)
        tt(K, "dve", t4[:, :, 1:2], vv[:, :, 1:2], l5i_all, ALU.mult, vv.b + S["L5"].b, t4.b)
        tt(K, "dve", t4[:, :, 2:3], vv[:, :, 0:1], l5i_all, ALU.mult, vv.b + S["L5"].b, t4.b)
        tt(K, "dve", t4[:, :, 3:4], vv[:, :, 1:2], l5r_all, ALU.mult, vv.b + S["L5"].b, t4.b)
        tt(K, "dve", dst[:, :, 0:1], t4[:, :, 0:1], t4[:, :, 1:2], ALU.subtract, t4.b, dst.b)
        tt(K, "dve", dst[:, :, 1:2], t4[:, :, 2:3], t4[:, :, 3:4], ALU.add, t4.b, dst.b)
    cxi = K.dram("s5_cxi", [128, 128], F32, "Internal")
    cxo = K.dram("s5_cxo", [256, 128], F32, "Internal")
    bxi = Buf("cxi"); bxo = Buf("cxo")
    K.dma("sp", cxi, fpre[:].rearrange("r j t -> r (j t)"), reads=fpre.b, writes=[bxi])
    K.collective("AllGather", [cxi], [cxo], [bxi], [bxo])
    cext = K.sbuf("cext", [128, 64, 2], F32)
    K.dma("sp", cext[:].rearrange("r j t -> r (j t)"), cxo[0:128, :], reads=[bxo], writes=cext.b)
    ts(K, "dve", cst[:, :, 0, :], cext[:], kflag[:, 0:1], ALU.mult, cext.b + kflag.b, cst.b)
    K.barrier(); K.stack_pop()
    ytmp = [K.sbuf(f"ytmp{i}", [128, 512], F32) for i in range(1)]
    NIT = 256
    ctx = {}

    def S1(t):
        j, blk = t // 4, t % 4
        jj = j // 4
        tb = tabt[j % 2]; bp = bpt[j % 2]

        def load_pair(jn):
            tbn = tabt[jn % 2]; bpn = bpt[jn % 2]
            K.dma("sp", tbn[:], S["tabs"][jn].rearrange("t r c -> r t c"), reads=[S["tabs_b"]], writes=tbn.b)
            K.dma("sp", bpn[:], S["bpad"][jn].rearrange("t r c -> r t c"), reads=[S["bpad_b"]], writes=bpn.b)
        if t == 0:
            load_pair(0)
        if blk == 1 and j + 1 < 64:
            load_pair(j + 1)
        a, b, c, d = wk[t % NW]
        pR = psBU[(2 * t) % 4]; pI = psBU[(2 * t + 1) % 4]
        Mr, Mi = tb[:, 2, :], tb[:, 3, :]
        ub = [uT.b[jj * 4 + blk]]
        usl = uT[:, jj, blk * 512:(blk + 1) * 512]
        K.op("pe", lambda e: e.matmul(pR[:], lhsT=bp[:, 0, :], rhs=usl, start=True, stop=True), reads=bp.b + ub, writes=pR.b)
        K.op("pe", lambda e: e.matmul(pI[:], lhsT=bp[:, 1, :], rhs=usl, start=True, stop=True), reads=bp.b + ub, writes=pI.b)
        tt(K, "dve", a[:], pR[:], Mr, ALU.mult, pR.b + tb.b, a.b)
        tt(K, "dve", b[:], pI[:], Mi, ALU.mult, pI.b + tb.b, b.b)
        tt(K, "dve", c[:], pR[:], Mi, ALU.mult, pR.b + tb.b, c.b)
        tt(K, "dve", d[:], pI[:], Mr, ALU.mult, pI.b + tb.b, d.b)

    def S2(t):
        a, b, c, d = wk[t % NW]
        tt(K, "pool", a[:], a[:], b[:], ALU.subtract, a.b + b.b, a.b)
        tt(K, "pool", c[:], c[:], d[:], ALU.add, c.b + d.b, c.b)

    def S3(t):
        j, blk = t // 4, t % 4
        a, b, c, d = wk[t % NW]
        cb = [cst.b[j]]
        K.op("dve", lambda e: e.tensor_tensor_scan(out=b[:], data0=ones[:], data1=a[:], initial=cst[:, j, blk, 0:1],
                                                   op0=ALU.mult, op1=ALU.add), reads=ones.b + a.b + cb, writes=b.b)
        K.op("dve", lambda e: e.tensor_tensor_scan(out=d[:], data0=ones[:], data1=c[:], initial=cst[:, j, blk, 1:2],
                                                   op0=ALU.mult, op1=ALU.add), reads=ones.b + c.b + cb, writes=d.b)
        l5r = S["L5"][:, j, 0:1]; l5i = S["L5"][:, j, 1:2]
        if blk < 3:
            ts(K, "dve", tny[:, 0:1], d[:, 511:512], nl5i[:, j:j + 1], ALU.mult, d.b + nl5i.b, tny.b)
            stt(K, "dve", cst[:, j, blk + 1, 0:1], b[:, 511:512], l5r, tny[:, 0:1], ALU.mult, ALU.add,
                b.b + tny.b + S["L5"].b, cb)
            ts(K, "dve", tny[:, 1:2], d[:, 511:512], l5r, ALU.mult, d.b + S["L5"].b, tny.b)
            stt(K, "dve", cst[:, j, blk + 1, 1:2], b[:, 511:512], l5i, tny[:, 1:2], ALU.mult, ALU.add,
                b.b + tny.b + S["L5"].b, cb)

    def S4(t):
        j = t // 4
        tb = tabt[j % 2]
        Dr, Di = tb[:, 0, :], tb[:, 1, :]
        a, b, c, d = wk[t % NW]; p0, p1, p2, p3 = pk[t % 2]
        tt(K, "pool", p0[:], b[:], Dr, ALU.mult, b.b + tb.b, p0.b)
        tt(K, "pool", p1[:], d[:], Di, ALU.mult, d.b + tb.b, p1.b)
        tt(K, "pool", p2[:], b[:], Di, ALU.mult, b.b + tb.b, p2.b)
        tt(K, "dve", p3[:], d[:], Dr, ALU.mult, d.b + tb.b, p3.b)

    def S5(t):
        j, blk = t // 4, t % 4
        jj, q = j // 4, j % 4
        p0, p1, p2, p3 = pk[t % 2]
        py = psY[blk]
        sl = slice(32 * q, 32 * q + 32)
        for i, (wt_, pp) in enumerate(((S["CTr"], p0), (S["CTrn"], p1), (S["CTn"], p2), (S["CTn"], p3))):
            K.op("pe", lambda e, wt_=wt_, pp=pp, i=i: e.matmul(py[sl, :], lhsT=wt_[:, jj, sl], rhs=pp[:],
                                                          start=(i == 0), stop=(i == 3), tile_position=(0, 32 * q)),
                 reads=wt_.b + pp.b, writes=py.b)
        if q == 3:
            yt = ytmp[0]
            usl = uT[:, jj, blk * 512:(blk + 1) * 512]; ub = [uT.b[jj * 4 + blk]]
            stt(K, "dve", yt[:], usl, dsk[:, jj:jj + 1], py[:], ALU.mult, ALU.add, ub + dsk.b + py.b, yt.b)
            act(K, usl, yt[:], AF.Gelu_apprx_tanh, yt.b, ub)

    for t in range(NIT + 2):
        if t < NIT:
            S1(t); S2(t)
        if 0 <= t - 1 < NIT:
            S3(t - 1); S4(t - 1)
        if 0 <= t - 2 < NIT:
            S5(t - 2)
    K.barrier(); K.stack_pop()

    LC = contextlib.ExitStack(); K.stack_push(LC)
    W = mkW()
    P = PsumPool(K, 6)
    hnT = K.sbuf("hnT5c", [128, 8, 512], BF16, nb=4)
    wbs = [K.sbuf(f"wbC{i}", [128, 4096], BF16) for i in range(3)]
    szT = K.sbuf("szT5", [128, 16, 512], BF16, nb=16)
    gT = szT
    gtmp = [K.sbuf(f"gtmp{i}", [128, 512], F32) for i in range(2)]
    nl = 0; nt = 0
    for sb in range(4):
        norm_transpose_sb(K, C, W, P, h, sb, gB, hnT)
        for cg in range(4):
            wt = wbs[nl % 3]; nl += 1
            wb = wt[:].rearrange("r (k n) -> r k n", k=8)
            K.dma("pool", wb, w_in_v[:, :, 2048 + cg * 512:2048 + (cg + 1) * 512], writes=wt.b)
            for sub in range(4):
                fc = cg * 4 + sub
                ps = P.next()
                for k in range(8):
                    K.op("pe", lambda e, k=k, sub=sub, ps=ps, wb=wb: e.matmul(
                        ps[:], lhsT=wb[:, k, sub * 128:(sub + 1) * 128], rhs=hnT[:, k, :],
                        start=(k == 0), stop=(k == 7)), reads=wt.b + hnT.b, writes=ps.b)
                act(K, szT[:, fc, :], ps[:], AF.Silu, ps.b, [szT.b[fc]])
        for cg in range(8):
            wt = wbs[nl % 3]; nl += 1
            wg = wt[:].rearrange("r (k n) -> r k n", k=16)
            K.dma("pool", wg, w_glu_v[:, :, cg * 256:(cg + 1) * 256], writes=wt.b)
            for sub in range(2):
                fc = cg * 2 + sub
                ps = P.next()
                for k in range(16):
                    K.op("pe", lambda e, k=k, sub=sub, ps=ps, wg=wg, sb=sb: e.matmul(
                        ps[:], lhsT=wg[:, k, sub * 128:(sub + 1) * 128], rhs=uT[:, k, sb * 512:(sb + 1) * 512],
                        start=(k == 0), stop=(k == 15)), reads=wt.b + [uT.b[k * 4 + sb]], writes=ps.b)
                gt = gtmp[nt % 2]; nt += 1
                act(K, gt[:], ps[:], AF.Sigmoid, ps.b + bgl.b, gt.b, bias=bgl[:, fc:fc + 1])
                tt(K, "dve", gt[:], gt[:], uT[:, fc, sb * 512:(sb + 1) * 512], ALU.mult,
                   gt.b + [uT.b[fc * 4 + sb]], gt.b)
                tt(K, "dve", szT[:, fc, :], gt[:], szT[:, fc, :], ALU.mult, gt.b + [szT.b[fc]], [szT.b[fc]])
        for cgo in range(4):
            wt = wbs[nl % 3]; nl += 1
            wo = wt[:].rearrange("r (k n) -> r k n", k=16)
            K.dma("pool", wo, w_out_v[:, :, cgo * 256:(cgo + 1) * 256], writes=wt.b)
            for jb in range(4):
                blk = sb * 4 + jb
                ps = P.next()
                for k in range(16):
                    K.op("pe", lambda e, k=k, jb=jb, ps=ps, wo=wo: e.matmul(
                        ps[:, 0:256], lhsT=gT[:, k, jb * 128:(jb + 1) * 128], rhs=wo[:, k, :],
                        start=(k == 0), stop=(k == 15)), reads=wt.b + [gT.b[k]], writes=ps.b)
                hs = h[:, blk, cgo * 256:(cgo + 1) * 256]
                tt(K, "dve", hs, hs, ps[:, 0:256], ALU.add, ps.b + [h.b[blk]], [h.b[blk]])
    K.barrier(); K.stack_pop()
    K.barrier(); K.stack_pop()


MLA_SCALE_F = 192 ** -0.5


def sincos_tables(K, ang, sin_out, cos_out, tmp, npi, ki, hpi):
    a_ap, a_b = ang; s_ap, s_b = sin_out; c_ap, c_b = cos_out; t_ap, t_b = tmp; k_ap, k_b = ki
    C1 = 6.28125; C2 = 2 * PI - C1
    np_ = s_ap.shape[0]
    ts(K, "dve", k_ap, a_ap, 1.0 / (2 * PI), ALU.mult, a_b, k_b)
    K.op("dve", lambda e: e.tensor_copy(out=t_ap, in_=k_ap), reads=k_b, writes=t_b)
    stt(K, "dve", a_ap, t_ap, -C1, a_ap, ALU.mult, ALU.add, t_b + a_b, a_b)
    stt(K, "dve", a_ap, t_ap, -C2, a_ap, ALU.mult, ALU.add, t_b + a_b, a_b)
    ts(K, "dve", a_ap, a_ap, 3.1415925, ALU.min, a_b, a_b, s2=-3.1415925, op1=ALU.max)
    act(K, s_ap, a_ap, AF.Sin, a_b, s_b)
    stt(K, "dve", t_ap, a_ap, -1.0, a_ap, ALU.mult, ALU.max, a_b, t_b)
    act(K, c_ap, t_ap, AF.Sin, t_b + hpi.b, c_b, scale=-1.0, bias=hpi[0:np_, 0:1])


def emit_mla(K, C, h, prm, p, pos_d, invf_d, kbias_d):
    Lall = contextlib.ExitStack(); K.stack_push(Lall)
    w_in = prm[p + "w_in"]; w_uq = prm[p + "w_uq"]; w_ukv = prm[p + "w_ukv"]; w_out = prm[p + "w_out"]
    w_in_v = w_in.rearrange("(k r) n -> r k n", r=128)
    w_out_v = w_out.rearrange("(k r) n -> r k n", r=128)
    gB = K.sbuf("gB2", [128, D], F32); bcast_load(K, gB, prm[p + "norm_g"], D)
    gq = K.sbuf("gq", [128, 384], F32); bcast_load(K, gq, prm[p + "q_norm_g"], 384)
    gkv = K.sbuf("gkv", [128, 128], F32); bcast_load(K, gkv, prm[p + "kv_norm_g"], 128)
    npi = None
    hpi = K.sbuf("hpi2", [128, 1], F32)
    K.op("dve", lambda e: e.memset(hpi[:], PI / 2), writes=hpi.b)
    kbias = K.sbuf("kbias", [128, 1], F32)
    K.dma("sp", kbias[:], kbias_d, writes=kbias.b)
    tri = K.sbuf("tri", [128, 128], BF16)
    trif = K.sbuf("trif", [128, 128], F32)
    K.op("pool", lambda e: e.memset(trif[:], 1.0), writes=trif.b)
    K.op("pool", lambda e: e.affine_select(out=trif[:], in_=trif[:], pattern=[[1, 128]], compare_op=ALU.is_ge,
                                           fill=0.0, base=0, channel_multiplier=-1), reads=trif.b, writes=trif.b)
    K.op("dve", lambda e: e.tensor_copy(out=tri[:], in_=trif[:]), reads=trif.b, writes=tri.b)
    wqn = K.sbuf("wqn", [128, 3, 16, 128], BF16)
    wqa = K.sbuf("wqa", [128, 3, 16, 64], BF16)
    wqb = K.sbuf("wqb", [128, 3, 16, 64], BF16)
    wuq_v = w_uq.rearrange("(k r) (hh e) -> r k hh e", r=128, e=192)
    wukv_v = w_ukv.rearrange("c (hh e) -> c hh e", e=256)
    wuv = K.sbuf("wuv", [128, 16, 128], BF16)
    wukT = K.sbuf("wukT", [128, 16, 128], BF16)
    wqi = K.sbuf("wqi", [128, 8, 384], BF16)
    W = {}
    W["junk"] = K.sbuf("junk2", [128, 1024], BF16)
    W["ss"] = K.sbuf("ss2", [128, 8], F32)
    W["rstd"] = K.sbuf("rstd2", [128, 8], F32)
    W["hn"] = K.sbuf("hn2", [128, D], BF16)
    W["tp"] = K.psum("tp2", [128, 1024], BF16)
    tp = W["tp"]
    posi = K.sbuf("posi", [128, NBLK], I32)
    posf = K.sbuf("posf", [128, NBLK], F32)
    K.dma("sp", posi[:], pos_d.rearrange("(j r) -> r j", r=128), writes=posi.b, allow_slow_non_contiguous=True)
    K.op("dve", lambda e: e.tensor_copy(out=posf[:], in_=posi[:]), reads=posi.b, writes=posf.b)
    invB = K.sbuf("invB", [128, 32], F32)
    K.dma("sp", invB[:], invf_d[0:32].partition_broadcast(128), writes=invB.b)
    invc = K.sbuf("invc", [64, 1], F32)
    K.dma("sp", invc[:, 0:1], invf_d.rearrange("(r o) -> r o", o=1), writes=invc.b)
    posBi = K.sbuf("posBi", [64, 512], I32)
    sgn = K.sbuf("sgn", [64, 1], F32)
    K.op("dve", lambda e: e.memset(sgn[0:32, :], -1.0), writes=sgn.b)
    K.op("dve", lambda e: e.memset(sgn[32:64, :], 1.0), writes=sgn.b)

    ckvT = K.sbuf("ckvT", [128, 2 * NTOK], BF16, nb=2)
    krT = K.sbuf("krT", [64, 2 * NTOK], BF16, nb=2)
    ckv_tok = K.sbuf("ckv_tok", [128, 2 * NBLK, 128], BF16, nb=2)

    hnT = K.sbuf("hnT2", [128, 8, 512], BF16, nb=4)
    P = PsumPool(K, 1)
    LA = contextlib.ExitStack(); K.stack_push(LA)
    wkv = K.sbuf("wkv", [128, 8, 192], BF16)
    K.dma("pool", wkv[:], w_in_v[:, :, 384:576], writes=wkv.b)
    for k in range(3):
        K.dma("pool", wqn[:, k], wuq_v[:, k, :, 0:128], writes=wqn.b)
        K.dma("pool", wqa[:, k], wuq_v[:, k, :, 128:192], writes=wqa.b)
        K.dma("pool", wqb[:, k, :, 0:32], wuq_v[:, k, :, 160:192], writes=wqb.b)
        K.dma("pool", wqb[:, k, :, 32:64], wuq_v[:, k, :, 128:160], writes=wqb.b)
    K.dma("pool", wuv[:], wukv_v[:, :, 128:256], writes=wuv.b)
    K.dma("pool", wqi[:], w_in_v[:, :, 0:384], writes=wqi.b)
    angt = K.sbuf("angt", [128, 32], F32); tmpt = K.sbuf("tmpt", [128, 32], F32)
    kit = K.sbuf("kit", [128, 32], I32)
    sint = K.sbuf("sint", [128, 32], F32); cost = K.sbuf("cost", [128, 32], F32)
    ckf = K.sbuf("ckf", [128, 128], BF16); krf = K.sbuf("krf", [128, 64], BF16)
    r1 = K.sbuf("r1", [128, 32], F32); r2 = K.sbuf("r2", [128, 32], F32)
    angA = K.sbuf("angA", [128, NBLK, 32], F32); tmpA = K.sbuf("tmpA", [128, NBLK, 32], F32)
    sinA = K.sbuf("sinA", [128, NBLK, 32], F32); cosA = K.sbuf("cosA", [128, NBLK, 32], F32)
    kiA = K.sbuf("kiA", [128, NBLK, 32], I32)
    tt(K, "dve", angA[:], invB[:].unsqueeze(1).to_broadcast([128, NBLK, 32]),
       posf[:].unsqueeze(2).to_broadcast([128, NBLK, 32]), ALU.mult, invB.b + posf.b, angA.b)
    fl = lambda t_: t_[:].rearrange("r j i -> r (j i)")
    sincos_tables(K, (fl(angA), angA.b), (fl(sinA), sinA.b), (fl(cosA), cosA.b), (fl(tmpA), tmpA.b), npi,
                  (fl(kiA), kiA.b), hpi)
    psA = [P.tiles[0]] + [K.psum(f"psA{i}", [128, 512], F32) for i in range(3)]
    ckf4 = K.sbuf("ckf4", [128, 4, 128], BF16); krf4 = K.sbuf("krf4", [128, 4, 64], BF16)
    r1q = K.sbuf("r1q", [128, 4, 32], F32); r2q = K.sbuf("r2q", [128, 4, 32], F32)
    for sb in range(4):
        norm_transpose_sb(K, C, W, P, h, sb, gB, hnT)
        ss = W["ss"]; rstd = W["rstd"]; junk = W["junk"]
        for j in range(4):
            ps = psA[j]
            for k in range(8):
                K.op("pe", lambda e, k=k, j=j, ps=ps: e.matmul(ps[:, 0:192], lhsT=hnT[:, k, j * 128:(j + 1) * 128],
                                                              rhs=wkv[:, k, :], start=(k == 0), stop=(k == 7)),
                     reads=wkv.b + [hnT.b[j]], writes=ps.b)
            act(K, junk[:, 0:128], ps[:, 0:128], AF.Square, ps.b, junk.b + ss.b, accum_out=ss[:, j:j + 1])
        ts(K, "dve", rstd[:, 0:4], ss[:, 0:4], 1.0 / 128, ALU.mult, ss.b, rstd.b, s2=EPS, op1=ALU.add)
        K.op("act", lambda e: e.sqrt(out=rstd[:, 0:4], in_=rstd[:, 0:4]), reads=rstd.b, writes=rstd.b)
        K.op("dve", lambda e: e.reciprocal(out=rstd[:, 0:4], in_=rstd[:, 0:4]), reads=rstd.b, writes=rstd.b)
        for j in range(4):
            blk = sb * 4 + j
            ps = psA[j]
            stt(K, "dve", ckf4[:, j, :], ps[:, 0:128], rstd[:, j:j + 1], gkv[:], ALU.mult, ALU.mult,
                ps.b + rstd.b + gkv.b, ckf4.b)
            sint_b = sinA[:, blk, :]; cost_b = cosA[:, blk, :]
            x1 = ps[:, 128:160]; x2 = ps[:, 160:192]
            tt(K, "dve", r1q[:, j, :], x1, cost_b, ALU.mult, ps.b + cosA.b, r1q.b)
            tt(K, "dve", r2q[:, j, :], x2, sint_b, ALU.mult, ps.b + sinA.b, r2q.b)
            tt(K, "dve", krf4[:, j, 0:32], r1q[:, j, :], r2q[:, j, :], ALU.subtract, r1q.b + r2q.b, krf4.b)
            tt(K, "dve", r1q[:, j, :], x2, cost_b, ALU.mult, ps.b + cosA.b, r1q.b)
            tt(K, "dve", r2q[:, j, :], x1, sint_b, ALU.mult, ps.b + sinA.b, r2q.b)
            tt(K, "dve", krf4[:, j, 32:64], r1q[:, j, :], r2q[:, j, :], ALU.add, r1q.b + r2q.b, krf4.b)
        K.op("pool", lambda e, sb=sb: e.tensor_copy(out=ckv_tok[:, NBLK + sb * 4:NBLK + sb * 4 + 4, :], in_=ckf4[:]),
             reads=ckf4.b, writes=[ckv_tok.b[1]])
        for j in range(4):
            K.op("pe", lambda e, j=j: e.transpose(tp[:, j * 128:(j + 1) * 128], ckf4[:, j, :], C["ident"][:]),
                 reads=ckf4.b + C["ident"].b, writes=tp.b)
        for j in range(4):
            K.op("pe", lambda e, j=j: e.transpose(tp[0:64, (4 + j) * 128:(5 + j) * 128], krf4[:, j, :], C["ident"][:]),
                 reads=krf4.b + C["ident"].b, writes=tp.b)
        K.op("act", lambda e, sb=sb: e.copy(out=ckvT[:, NTOK + sb * 512:NTOK + (sb + 1) * 512], in_=tp[:, 0:512]),
             reads=tp.b, writes=[ckvT.b[1]])
        K.op("act", lambda e, sb=sb: e.copy(out=krT[:, NTOK + sb * 512:NTOK + (sb + 1) * 512], in_=tp[0:64, 512:1024]),
             reads=tp.b, writes=[krT.b[1]])
    lxi = K.dram("mla_lxi", [320, NTOK], BF16, "Internal")
    lxo = K.dram("mla_lxo", [640, NTOK], BF16, "Internal")
    bli = Buf("lxi"); blo = Buf("lxo")
    K.dma("sp", lxi[0:128, :], ckvT[:, NTOK:2 * NTOK], reads=[ckvT.b[1]], writes=[bli])
    K.dma("sp", lxi[128:192, :], krT[:, NTOK:2 * NTOK], reads=[krT.b[1]], writes=[bli])
    K.dma("sp", lxi[192:320, :], ckv_tok[:, NBLK:2 * NBLK, :].rearrange("r j c -> r (j c)"), reads=[ckv_tok.b[1]],
          writes=[bli])
    K.collective("AllGather", [lxi], [lxo], [bli], [blo])
    K.dma("sp", ckvT[:, 0:NTOK], lxo[0:128, :], reads=[blo], writes=[ckvT.b[0]])
    K.dma("sp", krT[:, 0:NTOK], lxo[128:192, :], reads=[blo], writes=[krT.b[0]])
    K.dma("sp", ckv_tok[:, 0:NBLK, :].rearrange("r j c -> r (j c)"), lxo[192:320, :], reads=[blo], writes=[ckv_tok.b[0]])
    K.barrier(); K.stack_pop()
    Lw = contextlib.ExitStack(); K.stack_push(Lw)
    wuk = K.sbuf("wuk", [128, 16, 128], BF16)
    K.dma("pool", wuk[:], wukv_v[:, :, 0:128], writes=wuk.b)
    for i2 in range(2):
        for hl in range(8):
            hh = i2 * 8 + hl
            K.op("pe", lambda e, hh=hh, hl=hl: e.transpose(tp[:, hl * 128:(hl + 1) * 128], wuk[:, hh, :], C["ident"][:]),
                 reads=wuk.b + C["ident"].b, writes=tp.b)
        K.op("dve", lambda e, i2=i2: e.tensor_copy(out=wukT[:, i2 * 8:(i2 + 1) * 8, :].rearrange("r j c -> r (j c)"),
                                                   in_=tp[:]), reads=tp.b, writes=wukT.b)
    K.barrier(); K.stack_pop()

    LB = contextlib.ExitStack(); K.stack_push(LB)
    psS = [K.psum(f"psS{i}", [128, 512], F32) for i in range(2)]
    psO = [K.psum(f"psO{i}", [128, 512], F32) for i in range(2)]
    psL = [K.psum(f"psL{i}", [128, 512], F32) for i in range(2)]

    class _Pool2:
        def __init__(self, tiles):
            self.tiles = tiles; self.i = 0
        def next(self):
            t = self.tiles[self.i]; self.i = (self.i + 1) % len(self.tiles); return t
    PA = _Pool2(P.tiles)
    PB = _Pool2(P.tiles + psS + psO + [psL[0]])
    cqT = K.sbuf("cqT", [128, 3, 512], BF16, nb=4)
    cqf4 = K.sbuf("cqf4", [128, 4, 384], BF16)
    szT = K.sbuf("szT2", [128, 16, 512], BF16, nb=16)
    wbs = [K.sbuf(f"wb2_{i}", [128, 4096], BF16) for i in range(2)]
    wseqB = []
    for _sb in range(4):
        wseqB += [("z", c) for c in range(4)] + [("o", c) for c in range(4)]
    wstB = {"issued": 0, "used": 0}

    def issue_b(upto):
        while wstB["issued"] < min(upto, len(wseqB)):
            i = wstB["issued"]; kind, c = wseqB[i]
            wt_ = wbs[i % 2]
            if kind == "z":
                K.dma("pool", wt_[:].rearrange("r (k n) -> r k n", k=8), w_in_v[:, :, 576 + c * 512:576 + (c + 1) * 512],
                      writes=wt_.b)
            else:
                K.dma("pool", wt_[:].rearrange("r (k n) -> r k n", k=16), w_out_v[:, :, c * 256:(c + 1) * 256], writes=wt_.b)
            wstB["issued"] += 1

    def next_w():
        i = wstB["used"]; wstB["used"] += 1
        issue_b(i + 2)
        return wbs[i % 2]
    issue_b(1)
    angF = K.sbuf("angF", [64, 512], F32); tmpF = K.sbuf("tmpF", [64, 512], F32)
    sinF = K.sbuf("sinF", [64, 512], F32); cosF = K.sbuf("cosF", [64, 512], F32)
    posBf = angF
    qn = [K.sbuf(f"qn{i}", [128, 512], BF16) for i in range(2)]
    qp = [K.sbuf(f"qp{i}", [128, 512], BF16) for i in range(2)]
    qr = [K.sbuf(f"qr{i}", [64, 512], BF16) for i in range(2)]
    qra = angF; qrb = tmpF
    pT = [K.sbuf(f"pT{i}", [128, 512], BF16) for i in range(3)]
    rl = K.sbuf("rl", [128, 512], F32)
    oh = [K.sbuf(f"oh{i}", [128, 512], BF16) for i in range(2)]
    st = {"nl": 0, "npt": 0, "nS": 0}

    def q_prologue(hh):
        qn_, qp_, qr_ = qn[hh % 2], qp[hh % 2], qr[hh % 2]
        ps = PA.next()
        for k in range(3):
            K.op("pe", lambda e, k=k, ps=ps: e.matmul(ps[:], lhsT=wqn[:, k, hh, :], rhs=cqT[:, k, :],
                                                     start=(k == 0), stop=(k == 2)),
                 reads=wqn.b + cqT.b, writes=ps.b)
        K.op("act", lambda e, ps=ps: e.copy(out=qn_[:], in_=ps[:]), reads=ps.b, writes=qn_.b)
        ps3 = PA.next()
        for k in range(3):
            K.op("pe", lambda e, k=k, ps3=ps3: e.matmul(ps3[0:64, 0:512], lhsT=wqa[:, k, hh, :], rhs=cqT[:, k, :],
                                                       start=(k == 0), stop=(k == 2)),
                 reads=wqa.b + cqT.b, writes=ps3.b)
        tt(K, "dve", qra[:], ps3[0:64, :], cosF[:], ALU.mult, ps3.b + cosF.b, qra.b)
        ps2 = PA.next()
        K.op("pe", lambda e, ps2=ps2: e.matmul(ps2[:], lhsT=wukT[:, hh, :], rhs=qn_[:], start=True, stop=True),
             reads=wukT.b + qn_.b, writes=ps2.b)
        K.op("act", lambda e, ps2=ps2: e.copy(out=qp_[:], in_=ps2[:]), reads=ps2.b, writes=qp_.b)
        ps4 = PA.next()
        for k in range(3):
            K.op("pe", lambda e, k=k, ps4=ps4: e.matmul(ps4[0:64, 0:512], lhsT=wqb[:, k, hh, :], rhs=cqT[:, k, :],
                                                       start=(k == 0), stop=(k == 2)),
                 reads=wqb.b + cqT.b, writes=ps4.b)
        tt(K, "dve", qrb[:], ps4[0:64, :], sinF[:], ALU.mult, ps4.b + sinF.b, qrb.b)
        tt(K, "dve", qr_[:], qra[:], qrb[:], ALU.add, qra.b + qrb.b, qr_.b)

    def qk(hh, sb, kb):
        qp_, qr_ = qp[hh % 2], qr[hh % 2]
        half = 0 if kb < NBLK else 1
        jd = kb - (NBLK + sb * 4)
        c0 = jd * 128 if jd > 0 else 0
        pS = psS[st["nS"] % 2]; st["nS"] += 1
        K.op("pe", lambda e: e.matmul(pS[:, c0:512], lhsT=ckvT[:, kb * 128:(kb + 1) * 128], rhs=qp_[:, c0:512],
                                      start=True, stop=False), reads=[ckvT.b[half]] + qp_.b, writes=pS.b)
        K.op("pe", lambda e: e.matmul(pS[:, c0:512], lhsT=krT[:, kb * 128:(kb + 1) * 128], rhs=qr_[:, c0:512],
                                      start=False, stop=True), reads=[krT.b[half]] + qr_.b, writes=pS.b)
        return pS

    def softmax_pv(hh, sb, kb, pS, nkb):
        half = 0 if kb < NBLK else 1
        jd = kb - (NBLK + sb * 4)
        c0 = jd * 128 if jd > 0 else 0
        pt = pT[st["npt"] % 3]; st["npt"] += 1
        O = psO[hh % 2]; Lp = psL[hh % 2]
        if half == 0:
            act(K, pt[:, c0:512], pS[:, c0:512], AF.Exp, pS.b + kbias.b, pt.b, scale=MLA_SCALE_F, bias=kbias[:, 0:1])
        else:
            act(K, pt[:, c0:512], pS[:, c0:512], AF.Exp, pS.b, pt.b, scale=MLA_SCALE_F)
        if jd >= 0:
            tt(K, "pool", pt[:, jd * 128:(jd + 1) * 128], pt[:, jd * 128:(jd + 1) * 128], tri[:], ALU.mult,
               pt.b + tri.b, pt.b)
        K.op("pe", lambda e: e.matmul(O[:, c0:512], lhsT=ckv_tok[:, kb, :], rhs=pt[:, c0:512], start=(kb == 0),
                                      stop=(kb == nkb - 1)), reads=[ckv_tok.b[half]] + pt.b, writes=O.b)
        K.op("pe", lambda e: e.matmul(Lp[:, c0:512], lhsT=C["ones_bf"][:], rhs=pt[:, c0:512], start=(kb == 0),
                                      stop=(kb == nkb - 1)), reads=C["ones_bf"].b + pt.b, writes=Lp.b)

    def head_epilogue(hh):
        oh_ = oh[hh % 2]; O = psO[hh % 2]; Lp = psL[hh % 2]
        K.op("dve", lambda e: e.reciprocal(out=rl[:], in_=Lp[:]), reads=Lp.b, writes=rl.b)
        tt(K, "dve", oh_[:], O[:], rl[:], ALU.mult, O.b + rl.b, oh_.b)
        ps5 = PA.next()
        K.op("pe", lambda e: e.matmul(ps5[:], lhsT=wuv[:, hh, :], rhs=oh_[:], start=True, stop=True),
             reads=wuv.b + oh_.b, writes=ps5.b)
        tt(K, "dve", szT[:, hh, :], ps5[:], szT[:, hh, :], ALU.mult, ps5.b + [szT.b[hh]], [szT.b[hh]])

    for sb in range(4):
        norm_transpose_sb(K, C, W, P, h, sb, gB, hnT)
        K.dma("sp", posBi[:], pos_d[sb * 512:(sb + 1) * 512].partition_broadcast(64), writes=posBi.b)
        K.op("dve", lambda e: e.tensor_copy(out=posBf[:], in_=posBi[:]), reads=posBi.b, writes=posBf.b)
        ts(K, "dve", angF[:], angF[:], invc[:, 0:1], ALU.mult, angF.b + invc.b, angF.b)
        sincos_tables(K, (angF[:], angF.b), (sinF[:], sinF.b), (cosF[:], cosF.b), (tmpF[:], tmpF.b), npi,
                      (posBi[:], posBi.b), hpi)
        ts(K, "dve", sinF[:], sinF[:], sgn[:, 0:1], ALU.mult, sinF.b + sgn.b, sinF.b)
        ss = W["ss"]; rstd = W["rstd"]; junk = W["junk"]
        cq_ps = []
        for j in range(4):
            ps = PB.next(); cq_ps.append(ps)
            for k in range(8):
                K.op("pe", lambda e, k=k, j=j, ps=ps: e.matmul(ps[:, 0:384], lhsT=hnT[:, k, j * 128:(j + 1) * 128],
                                                              rhs=wqi[:, k, :], start=(k == 0), stop=(k == 7)),
                     reads=wqi.b + [hnT.b[j]], writes=ps.b)
            act(K, junk[:, 0:384], ps[:, 0:384], AF.Square, ps.b, junk.b + ss.b, accum_out=ss[:, j:j + 1])
        ts(K, "dve", rstd[:, 0:4], ss[:, 0:4], 1.0 / 384, ALU.mult, ss.b, rstd.b, s2=EPS, op1=ALU.add)
        K.op("act", lambda e: e.sqrt(out=rstd[:, 0:4], in_=rstd[:, 0:4]), reads=rstd.b, writes=rstd.b)
        K.op("dve", lambda e: e.reciprocal(out=rstd[:, 0:4], in_=rstd[:, 0:4]), reads=rstd.b, writes=rstd.b)
        for j in range(4):
            ps = cq_ps[j]
            stt(K, "dve", cqf4[:, j, :], ps[:, 0:384], rstd[:, j:j + 1], gq[:], ALU.mult, ALU.mult,
                ps.b + rstd.b + gq.b, cqf4.b)
        for cg in range(4):
            wt = next_w()
            wb = wt[:].rearrange("r (k n) -> r k n", k=8)
            for sub in range(4):
                fc = cg * 4 + sub
                ps = PB.next()
                for k in range(8):
                    K.op("pe", lambda e, k=k, sub=sub, ps=ps, wb=wb: e.matmul(
                        ps[:], lhsT=wb[:, k, sub * 128:(sub + 1) * 128], rhs=hnT[:, k, :],
                        start=(k == 0), stop=(k == 7)), reads=wt.b + hnT.b, writes=ps.b)
                act(K, szT[:, fc, :], ps[:], AF.Silu, ps.b, [szT.b[fc]])
        for j in range(4):
            for k in range(3):
                K.op("pe", lambda e, k=k, j=j: e.transpose(tp[:, k * 128:(k + 1) * 128], cqf4[:, j, k * 128:(k + 1) * 128],
                                                           C["ident"][:]), reads=cqf4.b + C["ident"].b, writes=tp.b)
            K.op("act", lambda e, j=j: e.copy(out=cqT[:, :, j * 128:(j + 1) * 128],
                                              in_=tp[:, 0:384].rearrange("r (k t) -> r k t", k=3)),
                 reads=tp.b, writes=[cqT.b[j]])
        nkb = NBLK + (sb + 1) * 4
        q_prologue(0)
        for hh in range(16):
            pS_next = qk(hh, sb, 0)
            for kb in range(nkb):
                pS_cur = pS_next
                if kb + 1 < nkb:
                    pS_next = qk(hh, sb, kb + 1)
                softmax_pv(hh, sb, kb, pS_cur, nkb)
                if kb == 3 and hh + 1 < 16:
                    q_prologue(hh + 1)
                if kb == 6 and hh > 0:
                    head_epilogue(hh - 1)
            if hh == 15:
                head_epilogue(15)
        for cgo in range(4):
            wt = next_w()
            wo = wt[:].rearrange("r (k n) -> r k n", k=16)
            for jb in range(4):
                blk = sb * 4 + jb
                ps = PB.next()
                for k in range(16):
                    K.op("pe", lambda e, k=k, jb=jb, ps=ps, wo=wo: e.matmul(
                        ps[:, 0:256], lhsT=szT[:, k, jb * 128:(jb + 1) * 128], rhs=wo[:, k, :],
                        start=(k == 0), stop=(k == 15)), reads=wt.b + [szT.b[k]], writes=ps.b)
                hs = h[:, blk, cgo * 256:(cgo + 1) * 256]
                tt(K, "dve", hs, hs, ps[:, 0:256], ALU.add, ps.b + [h.b[blk]], [h.b[blk]])
    K.barrier(); K.stack_pop()
    K.barrier(); K.stack_pop()


PARAM_SPECS = None


def param_specs():
    sp = {}
    def gm(p):
        sp[p + "norm_g"] = (1024,); sp[p + "w_in"] = (1024, 6144); sp[p + "ln_g"] = (2048,)
        sp[p + "ln_b"] = (2048,); sp[p + "w_s"] = (8, 128, 128); sp[p + "b_s"] = (8, 128)
        sp[p + "w_out"] = (2048, 1024)
    gm("l0_")
    p = "l1_"
    sp[p + "norm_g"] = (1024,); sp[p + "w_in"] = (1024, 4096)
    sp[p + "a_re"] = (128, 64); sp[p + "a_im"] = (128, 64); sp[p + "log_step"] = (128,)
    sp[p + "b_re"] = (128, 64, 16); sp[p + "b_im"] = (128, 64, 16)
    sp[p + "c_re"] = (128, 16, 64); sp[p + "c_im"] = (128, 16, 64)
    sp[p + "d_skip"] = (2048,); sp[p + "w_glu"] = (2048, 2048); sp[p + "b_glu"] = (2048,)
    sp[p + "w_out"] = (2048, 1024)
    p = "l2_"
    sp[p + "norm_g"] = (1024,); sp[p + "w_in"] = (1024, 2624); sp[p + "q_norm_g"] = (384,)
    sp[p + "w_uq"] = (384, 3072); sp[p + "kv_norm_g"] = (128,); sp[p + "w_ukv"] = (128, 4096)
    sp[p + "w_out"] = (2048, 1024)
    gm("l3_")
    sp["final_norm_g"] = (1024,)
    return sp


def build_program(layers=("l0",), final_norm=False):
    K = MK()
    x = K.dram("x", [NTOK, D], F32, "ExternalInput")
    out = K.dram("out", [NTOK, D], F32, "ExternalOutput")
    prm = {}
    need = set()
    for l in layers:
        need.add(l + "_")
    for name, shp in param_specs().items():
        if name[:3] in need or (final_norm and name == "final_norm_g"):
            prm[name] = K.dram(name, list(shp), F32, "ExternalInput")
    C = emit_consts(K)
    P = None
    cmask_d = K.dram("cmask", [128, 2], F32, "ExternalInput")
    cmask = K.sbuf("cmask_s", [128, 2], F32)
    K.dma("sp", cmask[:], cmask_d, writes=cmask.b)
    kflag_d = K.dram("kflag", [128, 1], F32, "ExternalInput")
    kflag = K.sbuf("kflag_s", [128, 1], F32)
    K.dma("sp", kflag[:], kflag_d, writes=kflag.b)
    pos_d = K.dram("pos", [NTOK], I32, "ExternalInput")
    invf_d = K.dram("invf", [64], F32, "ExternalInput")
    kbias_d = K.dram("kbias", [128, 1], F32, "ExternalInput")
    h = K.sbuf("h", [128, NBLK, D], F32, nb=NBLK)
    xv = x.rearrange("(j p) d -> p j d", p=128)
    for q in range(4):
        K.dma("sp", h[:, q * 4:(q + 1) * 4, :], xv[:, q * 4:(q + 1) * 4, :], writes=h.b[q * 4:(q + 1) * 4])
    PRE = {}
    if "l1" in layers:
        LPRE = contextlib.ExitStack(); K.stack_push(LPRE)
        p1 = "l1_"
        for nm in ("are", "aim", "lst"):
            PRE[nm] = K.sbuf("pre_" + nm, [128, 64], F32)
        K.dma("sp", PRE["are"][:], prm[p1 + "a_re"].rearrange("(j gl) p -> (gl p) j", gl=2), writes=PRE["are"].b,
              allow_slow_non_contiguous=True)
        K.dma("sp", PRE["aim"][:], prm[p1 + "a_im"].rearrange("(j gl) p -> (gl p) j", gl=2), writes=PRE["aim"].b,
              allow_slow_non_contiguous=True)
        lsv = prm[p1 + "log_step"].rearrange("(j gl) -> gl j", gl=2)
        for gl in range(2):
            K.dma("sp", PRE["lst"][gl * 64:(gl + 1) * 64, :], lsv[gl].partition_broadcast(64), writes=PRE["lst"].b,
                  allow_slow_non_contiguous=True)
        PRE["dsk"] = K.sbuf("pre_dsk", [128, 16], F32); PRE["bgl"] = K.sbuf("pre_bgl", [128, 16], F32)
        K.dma("sp", PRE["dsk"][:], prm[p1 + "d_skip"].rearrange("(c r) -> r c", r=128), writes=PRE["dsk"].b,
              allow_slow_non_contiguous=True)
        K.dma("sp", PRE["bgl"][:], prm[p1 + "b_glu"].rearrange("(c r) -> r c", r=128), writes=PRE["bgl"].b,
              allow_slow_non_contiguous=True)
    for l in layers:
        if l in ("l0", "l3"):
            emit_gmlp(K, C, P, h, prm, l + "_")
        elif l == "l2":
            emit_mla(K, C, h, prm, "l2_", pos_d, invf_d, kbias_d)
        elif l == "l1":
            emit_s5(K, C, h, prm, "l1_", kflag, cmask, PRE)
            K.barrier(); K.stack_pop()
    ov = out.rearrange("(j p) d -> p j d", p=128)
    if final_norm:
        emit_final_norm(K, h, prm["final_norm_g"], ov)
    else:
        for q in range(4):
            K.dma("sp", ov[:, q * 4:(q + 1) * 4, :], h[:, q * 4:(q + 1) * 4, :], reads=h.b[q * 4:(q + 1) * 4])
    K.barrier()
    return K.build(), sorted(prm.keys())


def emit_final_norm(K, h, g_d, ov):
    L = contextlib.ExitStack(); K.stack_push(L)
    gB = K.sbuf("gBf", [128, D], F32); bcast_load(K, gB, g_d, D)
    junk = [K.sbuf(f"junkf{i}", [128, D], F32) for i in range(2)]
    ss = K.sbuf("ssf", [128, NBLK], F32)
    ot = [K.sbuf(f"otf{i}", [128, D], F32) for i in range(3)]
    for blk in range(NBLK):
        jk = junk[blk % 2]
        act(K, jk[:], h[:, blk, :], AF.Square, [h.b[blk]], jk.b + ss.b, accum_out=ss[:, blk:blk + 1])
    ts(K, "dve", ss[:], ss[:], 1.0 / D, ALU.mult, ss.b, ss.b, s2=EPS, op1=ALU.add)
    K.op("act", lambda e: e.sqrt(out=ss[:], in_=ss[:]), reads=ss.b, writes=ss.b)
    K.op("dve", lambda e: e.reciprocal(out=ss[:], in_=ss[:]), reads=ss.b, writes=ss.b)
    for blk in range(NBLK):
        o = ot[blk % 3]
        stt(K, "dve", o[:], h[:, blk, :], ss[:, blk:blk + 1], gB[:], ALU.mult, ALU.mult, [h.b[blk]] + ss.b + gB.b, o.b)
        K.dma("sp", ov[:, blk, :], o[:], reads=o.b)
    K.barrier(); K.stack_pop()


_PROG = {}


def _get_prog(layers, final_norm):
    key = (tuple(layers), final_norm)
    if key not in _PROG:
        _PROG[key] = build_program(layers=layers, final_norm=final_norm)
    return _PROG[key]


def _aux_consts():
    cmask = np.zeros((128, 2), np.float32)
    for r in range(128):
        cmask[r, (r // 16) % 2] = 1.0
    invf = (np.float32(10000.0) ** (-np.arange(0, 64, 2, dtype=np.float32) / np.float32(64))).astype(np.float32)
    return cmask, np.concatenate([invf, invf])


def kernel(**inputs):
    inputs = {k: np.asarray(v) for k, v in inputs.items()}
    layers = ("l0", "l1", "l2", "l3")
    nc, names = _get_prog(layers, True)
    x = np.ascontiguousarray(inputs["x"], dtype=np.float32).reshape(8, NTOK, D)
    pos = np.ascontiguousarray(inputs["positions"]).astype(np.int32).reshape(8, NTOK)
    cmask, invf = _aux_consts()
    wts = {n: np.ascontiguousarray(inputs[n], dtype=np.float32) for n in names}

    in_maps = []
    for i in range(8):
        odd = float(i % 2)
        m = {"x": x[i], "cmask": cmask, "pos": pos[i], "invf": invf,
             "kbias": np.full((128, 1), 0.0 if odd else -30000.0, np.float32),
             "kflag": np.full((128, 1), odd, np.float32)}
        m.update(wts)
        in_maps.append(m)
    res = run_bass_kernel_spmd(nc, in_maps, core_ids=list(range(8))).results
    out = np.stack([np.asarray(r["out"]) for r in res]).reshape(4, 4096, D).astype(np.float32)
    return out
```

```python
import contextlib
import numpy as np
import concourse.bass as bass
import concourse.mybir as mybir
from concourse.bass_utils import run_bass_kernel_spmd

F32 = mybir.dt.float32
BF16 = mybir.dt.bfloat16
I32 = mybir.dt.int32
ALU = mybir.AluOpType
AF = mybir.ActivationFunctionType
AX = mybir.AxisListType

N_DMA_SEMS = 12


class Buf:
    __slots__ = ("w", "r", "name")

    def __init__(self, name=""):
        self.w = None
        self.r = []
        self.name = name


class Tile:
    def __init__(self, t, nb, name):
        self.t = t
        self.b = [Buf(f"{name}.{i}") for i in range(nb)]

    def __getitem__(self, idx):
        return self.t[idx]


class MK:
    ENG = ("pe", "act", "dve", "pool", "sp")

    def __init__(self):
        self.nc = bass.Bass("TRN2", target_bir_lowering=False)
        self.stack = contextlib.ExitStack()
        self.ops = {e: [] for e in self.ENG}
        self.cnt = {}
        self.sems = {}
        self.waited = {e: {} for e in self.ENG}
        for e in ("pe", "act", "dve", "pool"):
            self._mksem("c_" + e)
        for i in range(N_DMA_SEMS):
            self._mksem(f"d{i}")
        self.dma_rr = 0
        self.n_ops = 0
        self.stacks = [self.stack]
        self.uid = 0

    def _mksem(self, key):
        self.sems[key] = self.stack.enter_context(self.nc.semaphore(key))
        self.cnt[key] = 0

    def dram(self, name, shape, dt, kind):
        return self.nc.dram_tensor(name, list(shape), dt, kind=kind).ap()

    def stack_push(self, st):
        self.stacks.append(st)

    def stack_pop(self):
        self.stacks.pop().close()

    def sbuf(self, name, shape, dt, nb=1):
        self.uid += 1
        t = self.stacks[-1].enter_context(self.nc.sbuf_tensor(f"{name}_{self.uid}", list(shape), dt))
        return Tile(t, nb, name)

    def psum(self, name, shape, dt=F32, nb=1):
        self.uid += 1
        t = self.stacks[-1].enter_context(self.nc.psum_tensor(f"{name}_{self.uid}", list(shape), dt))
        return Tile(t, nb, name)

    def barrier(self):
        for eng in self.ENG:
            waits = []
            for k, v in self.cnt.items():
                if v > self.waited[eng].get(k, 0):
                    self.waited[eng][k] = v
                    waits.append((k, v))
            if waits:
                self.ops[eng].append((None, waits, None, 0))

    def _deps(self, eng, reads, writes):
        need = {}
        def add(tok):
            if tok is None:
                return
            k, v = tok
            if eng == "pe" and k == "c_pe":
                return
            if need.get(k, 0) < v:
                need[k] = v
        for b in reads:
            add(b.w)
        for b in writes:
            add(b.w)
            for tok in b.r:
                add(tok)
        out = []
        wd = self.waited[eng]
        for k, v in need.items():
            if wd.get(k, 0) < v:
                wd[k] = v
                out.append((k, v))
        return out

    def _commit(self, tok, reads, writes):
        for b in writes:
            b.w = tok
            b.r = []
        for b in reads:
            if b not in writes:
                b.r.append(tok)

    def op(self, eng, fn, reads=(), writes=()):
        reads = list(reads); writes = list(writes)
        waits = self._deps(eng, reads, writes)
        key = "c_" + eng
        self.cnt[key] += 1
        tok = (key, self.cnt[key])
        self.ops[eng].append((fn, waits, key, 1))
        self._commit(tok, reads, writes)
        self.n_ops += 1
        return tok

    def dma(self, eng, out, in_, reads=(), writes=(), **kw):
        reads = list(reads); writes = list(writes)
        i = self.dma_rr; self.dma_rr = (self.dma_rr + 1) % N_DMA_SEMS
        key = f"d{i}"
        waits = self._deps(eng, reads, writes)
        prev = self.cnt[key]
        if prev and self.waited[eng].get(key, 0) < prev:
            self.waited[eng][key] = prev
            waits.append((key, prev))
        self.cnt[key] += 16
        tok = (key, self.cnt[key])
        def fn(e, out=out, in_=in_, kw=kw):
            return e.dma_start(out=out, in_=in_, **kw)
        self.ops[eng].append((fn, waits, key, 16))
        self._commit(tok, reads, writes)
        self.n_ops += 1
        return tok

    def collective(self, kind, ins, outs, rbufs, wbufs):
        if "cc" not in self.sems:
            self._mksem("cc")
        waits = self._deps("pool", list(rbufs), list(wbufs))
        self.cnt["cc"] += 1
        tok = ("cc", self.cnt["cc"])
        def fn(e, ins=ins, outs=outs, kind=kind):
            return e.collective_compute(kind, ALU.bypass, replica_groups=[[0, 1], [2, 3], [4, 5], [6, 7]],
                                        ins=list(ins), outs=list(outs))
        self.ops["pool"].append((fn, waits, "cc", 1))
        self._commit(tok, list(rbufs), list(wbufs))
        return tok

    def wait_all(self, eng, bufs):
        waits = self._deps(eng, list(bufs), [])
        self.ops[eng].append((None, waits, None, 0))

    def build(self):
        nc = self.nc
        sems = self.sems
        ops = self.ops
        with nc.Block() as block:
            def emit(e, lst):
                for fn, waits, key, amt in lst:
                    for k, v in waits:
                        e.wait_ge(sems[k], v)
                    if fn is not None:
                        ins = fn(e)
                        ins.then_inc(sems[key], amt)

            @block.tensor
            def _(e):
                emit(e, ops["pe"])

            @block.scalar
            def _(e):
                emit(e, ops["act"])

            @block.vector
            def _(e):
                emit(e, ops["dve"])

            @block.gpsimd
            def _(e):
                emit(e, ops["pool"])

            @block.sync
            def _(e):
                emit(e, ops["sp"])
        self.stack.close()
        return nc


NTOK = 2048
NBLK = 16
D = 1024
DI = 2048
EPS = 1e-6


class PsumPool:
    def __init__(self, K, n):
        self.tiles = [K.psum(f"pp{i}", [128, 512], F32) for i in range(n)]
        self.i = 0

    def next(self):
        t = self.tiles[self.i]
        self.i = (self.i + 1) % len(self.tiles)
        return t


def bcast_load(K, dst, src_1d, n, eng="sp"):
    K.dma(eng, dst[:], src_1d.partition_broadcast(128), writes=dst.b)


def emit_consts(K):
    C = {}
    C["ident_f"] = K.sbuf("ident_f", [128, 128], F32)
    C["ident"] = K.sbuf("ident", [128, 128], BF16)
    idf = C["ident_f"]; idb = C["ident"]
    K.op("pool", lambda e: e.memset(idf[:], 1.0), writes=idf.b)
    K.op("pool", lambda e: e.affine_select(out=idf[:], in_=idf[:], pattern=[[-1, 128]],
                                           compare_op=ALU.is_equal, fill=0.0, base=0,
                                           channel_multiplier=1), reads=idf.b, writes=idf.b)
    K.op("dve", lambda e: e.tensor_copy(out=idb[:], in_=idf[:]), reads=idf.b, writes=idb.b)
    C["ones_bf"] = K.sbuf("ones_bf", [128, 128], BF16)
    ob = C["ones_bf"]
    K.op("dve", lambda e: e.memset(ob[:], 1.0), writes=ob.b)
    return C


def rmsnorm_block(K, W, hsrc, gB, out_bf, tag):
    h_ap, h_b = hsrc
    o_ap, o_b = out_bf
    junk = W["junk"]; ss = W["ss"]; rstd = W["rstd"]
    K.op("act", lambda e: e.activation(out=junk[:, 0:D], in_=h_ap, func=AF.Square, accum_out=ss[:, 0:1]),
         reads=h_b, writes=junk.b + ss.b)
    K.op("dve", lambda e: e.tensor_scalar(out=rstd[:, 0:1], in0=ss[:, 0:1], scalar1=1.0 / D, scalar2=EPS,
                                          op0=ALU.mult, op1=ALU.add), reads=ss.b, writes=rstd.b)
    K.op("act", lambda e: e.sqrt(out=rstd[:, 0:1], in_=rstd[:, 0:1]), reads=rstd.b, writes=rstd.b)
    K.op("dve", lambda e: e.reciprocal(out=rstd[:, 0:1], in_=rstd[:, 0:1]), reads=rstd.b, writes=rstd.b)
    K.op("dve", lambda e: e.scalar_tensor_tensor(out=o_ap, in0=h_ap, scalar=rstd[:, 0:1], in1=gB[:],
                                                 op0=ALU.mult, op1=ALU.mult),
         reads=h_b + rstd.b + gB.b, writes=o_b)


def norm_transpose_sb(K, C, W, P, h, sb, gB, hnT):
    junk = W["junk"]; ss = W["ss"]; rstd = W["rstd"]; hn = W["hn"]; tp = W["tp"]
    for j in range(4):
        blk = sb * 4 + j
        K.op("act", lambda e, j=j, blk=blk: e.activation(out=junk[:, 0:D], in_=h[:, blk, :], func=AF.Square,
                                                        accum_out=ss[:, 4 + j:5 + j]),
             reads=[h.b[blk]], writes=junk.b + ss.b)
    K.op("dve", lambda e: e.tensor_scalar(out=rstd[:, 4:8], in0=ss[:, 4:8], scalar1=1.0 / D, scalar2=EPS,
                                          op0=ALU.mult, op1=ALU.add), reads=ss.b, writes=rstd.b)
    K.op("act", lambda e: e.sqrt(out=rstd[:, 4:8], in_=rstd[:, 4:8]), reads=rstd.b, writes=rstd.b)
    K.op("dve", lambda e: e.reciprocal(out=rstd[:, 4:8], in_=rstd[:, 4:8]), reads=rstd.b, writes=rstd.b)
    for j in range(4):
        blk = sb * 4 + j
        K.op("dve", lambda e, j=j, blk=blk: e.scalar_tensor_tensor(out=hn[:], in0=h[:, blk, :], scalar=rstd[:, 4 + j:5 + j],
                                                                  in1=gB[:], op0=ALU.mult, op1=ALU.mult),
             reads=[h.b[blk]] + rstd.b + gB.b, writes=hn.b)
        for k in range(8):
            K.op("pe", lambda e, k=k: e.transpose(tp[:, k * 128:(k + 1) * 128], hn[:, k * 128:(k + 1) * 128],
                                                  C["ident"][:]),
                 reads=hn.b + C["ident"].b, writes=tp.b)
        K.op("act", lambda e, j=j: e.copy(out=hnT[:, :, j * 128:(j + 1) * 128],
                                           in_=tp[:].rearrange("p (k t) -> p k t", k=8)),
             reads=tp.b, writes=[hnT.b[j]])


def emit_gmlp(K, C, P, h, prm, lname):
    nc = K.nc
    L = contextlib.ExitStack()
    K.stack_push(L)
    P = PsumPool(K, 6)
    w_in, ln_g, ln_b, w_s, b_s, w_out, norm_g = (prm[lname + s] for s in
                                                 ("w_in", "ln_g", "ln_b", "w_s", "b_s", "w_out", "norm_g"))
    W = {}
    W["junk"] = K.sbuf("junk", [128, 1024], F32)
    W["ss"] = K.sbuf("ss", [128, 8], F32)
    W["rstd"] = K.sbuf("rstd", [128, 8], F32)
    W["hn"] = K.sbuf("hn", [128, D], BF16)
    W["tp"] = K.psum("tp", [128, 1024], BF16)
    gB = K.sbuf("gB", [128, D], F32)
    lgc = K.sbuf("lgc", [128, 16], F32)
    lbc = K.sbuf("lbc", [128, 16], F32)
    bias2 = K.sbuf("bias2", [128, 16, 128], F32)
    bsB = K.sbuf("bsB", [128, 8, 128], F32)
    WsT = K.sbuf("WsT", [128, 8, 128], BF16)
    hnTs = [K.sbuf(f"hnT{i}", [128, 8, 512], BF16, nb=4) for i in range(2)]
    NWB = 4
    wbuf = [K.sbuf(f"wbuf{i}", [128, 8, 512], BF16) for i in range(NWB)]
    wo_tiles = [K.sbuf(f"wobuf{i}", [128, 16, 256], BF16) for i in range(2)]
    uT = K.sbuf("uT", [128, 16, 512], BF16, nb=16)
    szT = K.sbuf("szT", [128, 16, 512], BF16, nb=16)
    vtok = K.sbuf("vtok", [128, 4, DI], BF16, nb=4)
    st1 = K.sbuf("st1", [128, 4, 4], F32, nb=4)
    st2 = K.sbuf("st2", [128, 4, 4], F32, nb=4)
    mv = K.sbuf("mv", [128, 8], F32)
    t1 = [K.sbuf(f"t1_{i}", [128, 512], F32) for i in range(2)]

    bcast_load(K, gB, norm_g, D)
    K.dma("sp", lgc[:], ln_g.rearrange("(c r) -> r c", r=128), writes=lgc.b, allow_slow_non_contiguous=True)
    K.dma("sp", lbc[:], ln_b.rearrange("(c r) -> r c", r=128), writes=lbc.b, allow_slow_non_contiguous=True)
    K.dma("sp", bsB[:].rearrange("p g t -> p (g t)"), b_s.rearrange("g t -> (g t)").partition_broadcast(128),
          writes=bsB.b)
    wsf_t = W["hn"]
    wsf = wsf_t[:].rearrange("p (g s) -> p g s", g=8)
    K.dma("pool", wsf, w_s.rearrange("g t s -> t g s"), writes=wsf_t.b)
    tp = W["tp"]
    for g in range(8):
        K.op("pe", lambda e, g=g: e.transpose(tp[:, g * 128:(g + 1) * 128], wsf[:, g, :], C["ident"][:]),
             reads=wsf_t.b + C["ident"].b, writes=tp.b)
    K.op("dve", lambda e: e.tensor_copy(out=WsT[:].rearrange("p g t -> p (g t)"), in_=tp[:]),
         reads=tp.b, writes=WsT.b)
    for g in range(8):
        K.op("pool", lambda e, g=g: e.affine_select(out=WsT[:, g, :], in_=WsT[:, g, :], pattern=[[1, 128]],
                                                    compare_op=ALU.is_ge, fill=0.0, base=0,
                                                    channel_multiplier=-1), reads=WsT.b, writes=WsT.b)

    psr = P.next()
    for g in range(8):
        K.op("pe", lambda e, g=g: e.matmul(psr[:, 0:128], lhsT=C["ones_bf"][:], rhs=WsT[:, g, :], start=True, stop=True),
             reads=C["ones_bf"].b + WsT.b, writes=psr.b)
        for fl in range(2):
            fc = 2 * g + fl
            K.op("dve", lambda e, g=g, fc=fc: e.scalar_tensor_tensor(
                out=bias2[:, fc, :], in0=psr[:, 0:128], scalar=lbc[:, fc:fc + 1], in1=bsB[:, g, :],
                op0=ALU.mult, op1=ALU.add), reads=psr.b + lbc.b + bsB.b, writes=bias2.b)
    w_in_v = w_in.rearrange("(k p) n -> p k n", p=128)
    w_out_v = w_out.rearrange("(k p) n -> p k n", p=128)
    CG_ORDER = (4, 5, 6, 7, 0, 1, 2, 3, 8, 9, 10, 11)
    wseq = [cg for _ in range(4) for cg in CG_ORDER]
    wst = {"issued": 0}

    def issue_w(upto):
        while wst["issued"] < min(upto, len(wseq)):
            i = wst["issued"]; cgi = wseq[i]
            wbi = wbuf[i % NWB]
            K.dma("pool", wbi[:], w_in_v[:, :, cgi * 512:(cgi + 1) * 512], writes=wbi.b)
            wst["issued"] += 1
    issue_w(NWB - 1)
    nload = 0
    for sb in range(4):
        hnT = hnTs[sb % 2]
        if sb == 0:
            norm_transpose_sb(K, C, W, P, h, 0, gB, hnT)
        def emit_ln():
            for j in range(4):
                K.op("dve", lambda e, j=j: e.reduce_sum(out=mv[:, 0:1], in_=st1[:, j, :], axis=AX.X),
                     reads=[st1.b[j]], writes=mv.b)
                K.op("dve", lambda e, j=j: e.reduce_sum(out=mv[:, 1:2], in_=st2[:, j, :], axis=AX.X),
                     reads=[st2.b[j]], writes=mv.b)
                K.op("dve", lambda e: e.tensor_scalar(out=mv[:, 2:4], in0=mv[:, 0:2], scalar1=1.0 / DI, scalar2=None,
                                                      op0=ALU.mult), reads=mv.b, writes=mv.b)
                K.op("dve", lambda e: e.tensor_tensor(out=mv[:, 4:5], in0=mv[:, 2:3], in1=mv[:, 2:3], op=ALU.mult),
                     reads=mv.b, writes=mv.b)
                K.op("dve", lambda e: e.tensor_tensor(out=mv[:, 5:6], in0=mv[:, 3:4], in1=mv[:, 4:5], op=ALU.subtract),
                     reads=mv.b, writes=mv.b)
                K.op("dve", lambda e: e.tensor_scalar(out=mv[:, 6:7], in0=mv[:, 5:6], scalar1=EPS, scalar2=None,
                                                      op0=ALU.add), reads=mv.b, writes=mv.b)
                K.op("act", lambda e: e.sqrt(out=mv[:, 6:7], in_=mv[:, 6:7]), reads=mv.b, writes=mv.b)
                K.op("dve", lambda e: e.reciprocal(out=mv[:, 6:7], in_=mv[:, 6:7]), reads=mv.b, writes=mv.b)
                K.op("dve", lambda e, j=j: e.tensor_scalar(out=vtok[:, j, :], in0=vtok[:, j, :], scalar1=mv[:, 2:3],
                                                           scalar2=mv[:, 6:7], op0=ALU.subtract, op1=ALU.mult),
                     reads=[vtok.b[j]] + mv.b, writes=[vtok.b[j]])
        def emit_spatial():
            nt = 0
            for jp in range(2):
                for g in range(8):
                    ps = P.next()
                    for fl in range(2):
                        for jl in range(2):
                            fc = 2 * g + fl; j = 2 * jp + jl
                            K.op("pe", lambda e, fc=fc, j=j, fl=fl, jl=jl, ps=ps, g=g: e.matmul(
                                ps[:, (fl * 2 + jl) * 128:(fl * 2 + jl + 1) * 128],
                                lhsT=vtok[:, j, fc * 128:(fc + 1) * 128], rhs=WsT[:, g, :], start=True, stop=True),
                                reads=[vtok.b[j]] + WsT.b, writes=ps.b)
                    tt = t1[nt % 2]; nt += 1
                    for fl in range(2):
                        fc = 2 * g + fl
                        K.op("dve", lambda e, ps=ps, tt=tt, fc=fc, fl=fl: e.scalar_tensor_tensor(
                            out=tt[:, fl * 256:(fl + 1) * 256].rearrange("p (a t) -> p a t", a=2),
                            in0=ps[:, fl * 256:(fl + 1) * 256].rearrange("p (a t) -> p a t", a=2),
                            scalar=lgc[:, fc:fc + 1], in1=bias2[:, fc:fc + 1, :].to_broadcast([128, 2, 128]),
                            op0=ALU.mult, op1=ALU.add), reads=ps.b + lgc.b + bias2.b, writes=tt.b)
                    usl = uT[:, 2 * g:2 * g + 2, jp * 256:(jp + 1) * 256]
                    K.op("pool", lambda e, tt=tt, usl=usl: e.tensor_tensor(
                        out=usl, in0=tt[:].rearrange("p (a t) -> p a t", a=2), in1=usl, op=ALU.mult),
                        reads=tt.b + [uT.b[2 * g], uT.b[2 * g + 1]], writes=[uT.b[2 * g], uT.b[2 * g + 1]])
        for cg in CG_ORDER:
            issue_w(nload + NWB)
            wb = wbuf[nload % NWB]; nload += 1
            if cg == 9:
                K.dma("pool", wo_tiles[0][:], w_out_v[:, :, 0:256], writes=wo_tiles[0].b)
            kind = cg // 4
            if kind != 1:
                dstT = uT if kind == 0 else szT
                fn = AF.Gelu_apprx_tanh if kind == 0 else AF.Silu
                for sub in range(4):
                    fc = (cg % 4) * 4 + sub
                    ps = P.next()
                    for k in range(8):
                        K.op("pe", lambda e, k=k, sub=sub, ps=ps, wb=wb, hnT=hnT: e.matmul(
                            ps[:], lhsT=wb[:, k, sub * 128:(sub + 1) * 128], rhs=hnT[:, k, :],
                            start=(k == 0), stop=(k == 7)), reads=wb.b + hnT.b, writes=ps.b)
                    K.op("act", lambda e, fc=fc, ps=ps, dstT=dstT, fn=fn: e.activation(
                        out=dstT[:, fc, :], in_=ps[:], func=fn), reads=ps.b, writes=[dstT.b[fc]])
            else:
                cgv = cg % 4
                for j in range(4):
                    ps = P.next()
                    for k in range(8):
                        K.op("pe", lambda e, k=k, j=j, ps=ps, wb=wb, hnT=hnT: e.matmul(
                            ps[:], lhsT=hnT[:, k, j * 128:(j + 1) * 128], rhs=wb[:, k, :],
                            start=(k == 0), stop=(k == 7)), reads=wb.b + [hnT.b[j]], writes=ps.b)
                    K.op("act", lambda e, j=j, ps=ps, cgv=cgv: e.activation(
                        out=vtok[:, j, cgv * 512:(cgv + 1) * 512], in_=ps[:], func=AF.Gelu_apprx_tanh,
                        accum_out=st1[:, j, cgv:cgv + 1]), reads=ps.b, writes=[vtok.b[j], st1.b[j]])
                    K.op("act", lambda e, j=j, cgv=cgv: e.activation(
                        out=W["junk"][:, 0:512], in_=vtok[:, j, cgv * 512:(cgv + 1) * 512], func=AF.Square,
                        accum_out=st2[:, j, cgv:cgv + 1]), reads=[vtok.b[j]], writes=W["junk"].b + [st2.b[j]])
                if cg == 7:
                    emit_ln()
            if cg == 3:
                emit_spatial()
        for fc in range(16):
            K.op("dve", lambda e, fc=fc: e.tensor_tensor(out=uT[:, fc, :], in0=uT[:, fc, :], in1=szT[:, fc, :], op=ALU.mult),
                 reads=[uT.b[fc], szT.b[fc]], writes=[uT.b[fc]])
        for cgo in range(4):
            if cgo == 1 and sb + 1 < 4:
                norm_transpose_sb(K, C, W, P, h, sb + 1, gB, hnTs[(sb + 1) % 2])
            wo = wo_tiles[cgo % 2]
            if cgo + 1 < 4:
                wn = wo_tiles[(cgo + 1) % 2]
                K.dma("pool", wn[:], w_out_v[:, :, (cgo + 1) * 256:(cgo + 2) * 256], writes=wn.b)
            for j in range(4):
                blk = sb * 4 + j
                ps = P.next()
                for k in range(16):
                    K.op("pe", lambda e, k=k, j=j, ps=ps, wo=wo: e.matmul(
                        ps[:, 0:256], lhsT=uT[:, k, j * 128:(j + 1) * 128], rhs=wo[:, k, :],
                        start=(k == 0), stop=(k == 15)), reads=wo.b + [uT.b[k]], writes=ps.b)
                hs = h[:, blk, cgo * 256:(cgo + 1) * 256]
                K.op("dve", lambda e, hs=hs, ps=ps: e.tensor_tensor(out=hs, in0=hs, in1=ps[:, 0:256], op=ALU.add),
                     reads=ps.b + [h.b[blk]], writes=[h.b[blk]])
    K.barrier()
    K.stack_pop()


import math as _math

PI = _math.pi


def tt(K, eng, out, in0, in1, op, reads, writes):
    return K.op(eng, lambda e: e.tensor_tensor(out=out, in0=in0, in1=in1, op=op), reads=reads, writes=writes)


def ts(K, eng, out, in0, s1, op0, reads, writes, s2=None, op1=None):
    if op1 is None:
        return K.op(eng, lambda e: e.tensor_scalar(out=out, in0=in0, scalar1=s1, scalar2=None, op0=op0),
                    reads=reads, writes=writes)
    return K.op(eng, lambda e: e.tensor_scalar(out=out, in0=in0, scalar1=s1, scalar2=s2, op0=op0, op1=op1),
                reads=reads, writes=writes)


def stt(K, eng, out, in0, scalar, in1, op0, op1, reads, writes):
    return K.op(eng, lambda e: e.scalar_tensor_tensor(out=out, in0=in0, scalar=scalar, in1=in1, op0=op0, op1=op1),
                reads=reads, writes=writes)


def act(K, out, in_, func, reads, writes, **kw):
    return K.op("act", lambda e: e.activation(out=out, in_=in_, func=func, **kw), reads=reads, writes=writes)


def emit_s5_prep(K, C, prm, p, S, cmask, PRE):
    L = contextlib.ExitStack(); K.stack_push(L)
    a_re, a_im, log_step = prm[p + "a_re"], prm[p + "a_im"], prm[p + "log_step"]
    shp = [128, 64]
    def T(name, shape=shp, dt=F32):
        return K.sbuf(name, shape, dt)
    are, aim, lst = PRE["are"], PRE["aim"], PRE["lst"]
    step, dr, th, mag, imag = T("step"), T("dr"), T("th"), T("mag"), T("imag")
    act(K, step[:], lst[:], AF.Exp, lst.b, step.b)
    tt(K, "dve", dr[:], are[:], step[:], ALU.mult, are.b + step.b, dr.b)
    tt(K, "dve", th[:], aim[:], step[:], ALU.mult, aim.b + step.b, th.b)
    act(K, mag[:], dr[:], AF.Exp, dr.b, mag.b)
    act(K, imag[:], dr[:], AF.Exp, dr.b, imag.b, scale=-1.0)
    sn, cs, t1, t2 = T("sn"), T("cs"), T("t1"), T("t2")
    hpi = K.sbuf("hpi", [128, 1], F32)
    K.op("dve", lambda e: e.memset(hpi[:], PI / 2), writes=hpi.b)
    act(K, sn[:], th[:], AF.Sin, th.b, sn.b, scale=1.0 / 16)
    act(K, cs[:], th[:], AF.Sin, th.b + hpi.b, cs.b, scale=1.0 / 16, bias=hpi[:, 0:1])
    for _ in range(4):
        tt(K, "dve", t1[:], cs[:], cs[:], ALU.mult, cs.b, t1.b)
        tt(K, "dve", t2[:], sn[:], sn[:], ALU.mult, sn.b, t2.b)
        stt(K, "dve", sn[:], cs[:], 2.0, sn[:], ALU.mult, ALU.mult, cs.b + sn.b, sn.b)
        tt(K, "dve", cs[:], t1[:], t2[:], ALU.subtract, t1.b + t2.b, cs.b)
    zr, zi, wr, wi = T("zr"), T("zi"), T("wr"), T("wi")
    tt(K, "dve", zr[:], mag[:], cs[:], ALU.mult, mag.b + cs.b, zr.b)
    tt(K, "dve", zi[:], mag[:], sn[:], ALU.mult, mag.b + sn.b, zi.b)
    tt(K, "dve", wr[:], imag[:], cs[:], ALU.mult, imag.b + cs.b, wr.b)
    stt(K, "dve", wi[:], imag[:], -1.0, sn[:], ALU.mult, ALU.mult, imag.b + sn.b, wi.b)
    nr, den, kr, ki, tmp = T("nr"), T("den"), T("kr"), T("ki"), T("tmpk")
    ts(K, "dve", nr[:], zr[:], -1.0, ALU.add, zr.b, nr.b)
    tt(K, "dve", den[:], are[:], are[:], ALU.mult, are.b, den.b)
    tt(K, "dve", tmp[:], aim[:], aim[:], ALU.mult, aim.b, tmp.b)
    tt(K, "dve", den[:], den[:], tmp[:], ALU.add, den.b + tmp.b, den.b)
    K.op("dve", lambda e: e.reciprocal(out=den[:], in_=den[:]), reads=den.b, writes=den.b)
    tt(K, "dve", kr[:], nr[:], are[:], ALU.mult, nr.b + are.b, kr.b)
    tt(K, "dve", tmp[:], zi[:], aim[:], ALU.mult, zi.b + aim.b, tmp.b)
    tt(K, "dve", kr[:], kr[:], tmp[:], ALU.add, kr.b + tmp.b, kr.b)
    tt(K, "dve", kr[:], kr[:], den[:], ALU.mult, kr.b + den.b, kr.b)
    tt(K, "dve", ki[:], zi[:], are[:], ALU.mult, zi.b + are.b, ki.b)
    tt(K, "dve", tmp[:], nr[:], aim[:], ALU.mult, nr.b + aim.b, tmp.b)
    tt(K, "dve", ki[:], ki[:], tmp[:], ALU.subtract, ki.b + tmp.b, ki.b)
    tt(K, "dve", ki[:], ki[:], den[:], ALU.mult, ki.b + den.b, ki.b)

    tp = K.psum("tp5", [128, 1024], BF16)
    L2 = contextlib.ExitStack(); K.stack_push(L2)
    bnr = K.sbuf("bnr", [128, 64, 16], F32); bni = K.sbuf("bni", [128, 64, 16], F32)
    K.dma("sp", bnr[:], prm[p + "b_re"].rearrange("(j gl) p h -> (gl p) j h", gl=2), writes=bnr.b)
    K.dma("sp", bni[:], prm[p + "b_im"].rearrange("(j gl) p h -> (gl p) j h", gl=2), writes=bni.b)
    bbr = K.sbuf("bbr", [128, 64, 16], F32); bbi = K.sbuf("bbi", [128, 64, 16], F32)
    btmp = K.sbuf("btmp", [128, 64, 16], F32)
    krb = kr[:].unsqueeze(2).to_broadcast([128, 64, 16]); kib = ki[:].unsqueeze(2).to_broadcast([128, 64, 16])
    tt(K, "dve", bbr[:], bnr[:], krb, ALU.mult, bnr.b + kr.b, bbr.b)
    tt(K, "dve", btmp[:], bni[:], kib, ALU.mult, bni.b + ki.b, btmp.b)
    tt(K, "dve", bbr[:], bbr[:], btmp[:], ALU.subtract, bbr.b + btmp.b, bbr.b)
    tt(K, "dve", bbi[:], bni[:], krb, ALU.mult, bni.b + kr.b, bbi.b)
    tt(K, "dve", btmp[:], bnr[:], kib, ALU.mult, bnr.b + ki.b, btmp.b)
    tt(K, "dve", bbi[:], bbi[:], btmp[:], ALU.add, bbi.b + btmp.b, bbi.b)
    pin = K.sbuf("pin", [128, 64, 128], BF16)
    stg = [K.sbuf(f"stg{i}", [128, 8, 128], BF16) for i in range(2)]
    ns = 0
    for ri, src in enumerate((bbr, bbi)):
        K.op("pool", lambda e: e.memset(pin[:], 0.0), writes=pin.b)
        for q in range(4):
            for gl in range(2):
                K.op("dve", lambda e, q=q, gl=gl, src=src: e.tensor_copy(
                    out=pin[gl * 64:(gl + 1) * 64, q::4, 32 * q + 16 * gl:32 * q + 16 * gl + 16],
                    in_=src[gl * 64:(gl + 1) * 64, q::4, :]), reads=src.b, writes=pin.b)
        for i8 in range(8):
            for jl in range(8):
                j = i8 * 8 + jl
                K.op("pe", lambda e, j=j, jl=jl: e.transpose(tp[:, jl * 128:(jl + 1) * 128], pin[:, j, :],
                                                             C["ident"][:]),
                     reads=pin.b + C["ident"].b, writes=tp.b)
            sg = stg[ns % 2]; ns += 1
            K.op("dve", lambda e, sg=sg: e.tensor_copy(out=sg[:].rearrange("r j c -> r (j c)"), in_=tp[:]),
                 reads=tp.b, writes=sg.b)
            K.dma("sp", S["bpad"][i8 * 8:(i8 + 1) * 8, ri].rearrange("j r c -> r j c"), sg[:], reads=sg.b,
                  writes=[S["bpad_b"]])
    K.barrier(); K.stack_pop()

    L3 = contextlib.ExitStack(); K.stack_push(L3)
    cin = K.sbuf("cin", [128, 16, 128], BF16)
    cn = K.sbuf("cn", [128, 16, 64], F32)
    for ri, (nm, sgn, dst) in enumerate((("c_re", 1.0, S["CTr"]), ("c_im", -1.0, S["CTn"]), ("c_re", -1.0, S["CTrn"]))):
        K.dma("sp", cn[:], prm[p + nm].rearrange("(jj q gl) h c -> (q gl h) jj c", q=4, gl=2), writes=cn.b)
        for gl in range(2):
            ts(K, "dve", cin[:, :, gl * 64:(gl + 1) * 64], cn[:], cmask[:, gl:gl + 1], ALU.mult,
               cn.b + cmask.b, cin.b, s2=sgn, op1=ALU.mult)
        for i8 in range(2):
            for jl in range(8):
                jj = i8 * 8 + jl
                K.op("pe", lambda e, jj=jj, jl=jl: e.transpose(tp[:, jl * 128:(jl + 1) * 128], cin[:, jj, :],
                                                               C["ident"][:]),
                     reads=cin.b + C["ident"].b, writes=tp.b)
            K.op("dve", lambda e, i8=i8, dst=dst: e.tensor_copy(
                out=dst[:, i8 * 8:(i8 + 1) * 8, :].rearrange("r j c -> r (j c)"), in_=tp[:]),
                reads=tp.b, writes=dst.b)
    K.barrier(); K.stack_pop()

    L4 = contextlib.ExitStack(); K.stack_push(L4)
    NB = 2
    Lp = K.sbuf("Lp", [128, 64, 4, 32], F32)
    Hp = K.sbuf("Hp", [128, 64, 4, 16], F32)
    cur = K.sbuf("pcur", [128, 64, 4], F32)
    cur2 = K.sbuf("pcur2", [128, 64, 4], F32)
    ctmp = K.sbuf("pctmp", [128, 64, 4], F32)
    ptm = [K.sbuf(f"pptm{i}", [128, 64, 16], F32) for i in range(2)]
    for ti in range(1):
        eng = "dve" if ti == 0 else "pool"
        sr, si = (zr, zi) if ti == 0 else (wr, wi)
        cr_ = cur[:, :, 2 * ti:2 * ti + 1]; ci_ = cur[:, :, 2 * ti + 1:2 * ti + 2]
        K.op(eng, lambda e, cr_=cr_, sr=sr: e.tensor_copy(out=cr_, in_=sr[:].unsqueeze(2)), reads=sr.b, writes=cur.b)
        K.op(eng, lambda e, ci_=ci_, si=si: e.tensor_copy(out=ci_, in_=si[:].unsqueeze(2)), reads=si.b, writes=cur.b)
        pt = ptm[ti]
        for (Tt, nsteps) in ((Lp, 5), (Hp, 4)):
            Tr = Tt[:, :, 2 * ti, :]; Ti = Tt[:, :, 2 * ti + 1, :]
            K.op(eng, lambda e, Tr=Tr: e.memset(Tr[:, :, 0:1], 1.0), writes=Tt.b)
            K.op(eng, lambda e, Ti=Ti: e.memset(Ti[:, :, 0:1], 0.0), writes=Tt.b)
            for k in range(nsteps):
                n = 1 << k
                crb = cr_.to_broadcast([128, 64, n]); cib = ci_.to_broadcast([128, 64, n])
                A_r = Tr[:, :, 0:n]; A_i = Ti[:, :, 0:n]; O_r = Tr[:, :, n:2 * n]; O_i = Ti[:, :, n:2 * n]
                tm = pt[:, :, 0:n]
                tt(K, eng, tm, A_i, cib, ALU.mult, Tt.b + cur.b, pt.b)
                tt(K, eng, O_r, A_r, crb, ALU.mult, Tt.b + cur.b, Tt.b)
                tt(K, eng, O_r, O_r, tm, ALU.subtract, Tt.b + pt.b, Tt.b)
                tt(K, eng, tm, A_i, crb, ALU.mult, Tt.b + cur.b, pt.b)
                tt(K, eng, O_i, A_r, cib, ALU.mult, Tt.b + cur.b, Tt.b)
                tt(K, eng, O_i, O_i, tm, ALU.add, Tt.b + pt.b, Tt.b)
                c2r = cur2[:, :, 2 * ti:2 * ti + 1]; c2i = cur2[:, :, 2 * ti + 1:2 * ti + 2]
                t_a = ctmp[:, :, 2 * ti:2 * ti + 1]; t_b = ctmp[:, :, 2 * ti + 1:2 * ti + 2]
                tt(K, eng, t_a, cr_, cr_, ALU.mult, cur.b, ctmp.b)
                tt(K, eng, t_b, ci_, ci_, ALU.mult, cur.b, ctmp.b)
                tt(K, eng, c2r, t_a, t_b, ALU.subtract, ctmp.b, cur2.b)
                tt(K, eng, c2i, cr_, ci_, ALU.mult, cur.b, cur2.b)
                tt(K, eng, c2i, c2i, c2i, ALU.add, cur2.b, cur2.b)
                K.op(eng, lambda e, cr_=cr_, c2r=c2r: e.tensor_copy(out=cr_, in_=c2r), reads=cur2.b, writes=cur.b)
                K.op(eng, lambda e, ci_=ci_, c2i=c2i: e.tensor_copy(out=ci_, in_=c2i), reads=cur2.b, writes=cur.b)
        if ti == 0:
            K.op(eng, lambda e, cr_=cr_: e.tensor_copy(out=S["L5"][:, :, 0:1], in_=cr_), reads=cur.b, writes=S["L5"].b)
            K.op(eng, lambda e, ci_=ci_: e.tensor_copy(out=S["L5"][:, :, 1:2], in_=ci_), reads=cur.b, writes=S["L5"].b)
    bv = K.sbuf("pbv", [128, 32], F32); on32 = K.sbuf("pon32", [128, 32], F32)
    K.op("dve", lambda e: e.memset(on32[:], 1.0), writes=on32.b)
    K.op("dve", lambda e: e.tensor_tensor_scan(out=bv[:], data0=on32[:], data1=on32[:], initial=-1.0,
                                               op0=ALU.mult, op1=ALU.add), reads=on32.b, writes=bv.b)
    gL = K.sbuf("pgL", [128, 64, 32], F32); gH = K.sbuf("pgH", [128, 64, 16], F32); gHn = K.sbuf("pgHn", [128, 64, 16], F32)
    tt(K, "dve", gL[:], dr[:].unsqueeze(2).to_broadcast([128, 64, 32]), bv[:].unsqueeze(1).to_broadcast([128, 64, 32]),
       ALU.mult, dr.b + bv.b, gL.b)
    tt(K, "dve", gH[:], dr[:].unsqueeze(2).to_broadcast([128, 64, 16]), bv[:, 0:16].unsqueeze(1).to_broadcast([128, 64, 16]),
       ALU.mult, dr.b + bv.b, gH.b)
    act(K, gL[:], gL[:], AF.Exp, gL.b, gL.b, scale=-2.0)
    act(K, gH[:], gH[:], AF.Exp, gH.b, gH.b, scale=-64.0)
    ts(K, "dve", gHn[:], gH[:], -1.0, ALU.mult, gH.b, gHn.b)
    tab = [K.sbuf(f"ptab{i}", [128, NB, 4, 512], F32) for i in range(2)]
    otm = [K.sbuf(f"potm{i}", [128, NB, 512], F32) for i in range(2)]
    for bi in range(64 // NB):
        tb = tab[bi % 2]
        j0 = bi * NB
        shp4 = [128, NB, 16, 32]
        om = otm[0]
        Hr = Hp[:, j0:j0 + NB, 0, :].unsqueeze(3).to_broadcast(shp4)
        Hi = Hp[:, j0:j0 + NB, 1, :].unsqueeze(3).to_broadcast(shp4)
        Lr = Lp[:, j0:j0 + NB, 0, :].unsqueeze(2).to_broadcast(shp4)
        Li = Lp[:, j0:j0 + NB, 1, :].unsqueeze(2).to_broadcast(shp4)
        Tr = tb[:, :, 0, :].rearrange("r j (a b) -> r j a b", a=16)
        Ti = tb[:, :, 1, :].rearrange("r j (a b) -> r j a b", a=16)
        o4 = om[:].rearrange("r j (a b) -> r j a b", a=16)
        rd = Hp.b + Lp.b
        tt(K, "dve", o4, Hi, Li, ALU.mult, rd, om.b)
        tt(K, "dve", Tr, Hr, Lr, ALU.mult, rd, tb.b)
        tt(K, "dve", Tr, Tr, o4, ALU.subtract, tb.b + om.b, tb.b)
        tt(K, "dve", o4, Hi, Lr, ALU.mult, rd, om.b)
        tt(K, "dve", Ti, Hr, Li, ALU.mult, rd, tb.b)
        tt(K, "dve", Ti, Ti, o4, ALU.add, tb.b + om.b, tb.b)
        omp = otm[1]
        g4 = omp[:].rearrange("r j (a b) -> r j a b", a=16)
        gHb = gH[:, j0:j0 + NB, :].unsqueeze(3).to_broadcast(shp4)
        gHnb = gHn[:, j0:j0 + NB, :].unsqueeze(3).to_broadcast(shp4)
        gLb = gL[:, j0:j0 + NB, :].unsqueeze(2).to_broadcast(shp4)
        tt(K, "pool", g4, gHb, gLb, ALU.mult, gH.b + gL.b, omp.b)
        tt(K, "pool", tb[:, :, 2, :], tb[:, :, 0, :], omp[:], ALU.mult, tb.b + omp.b, tb.b)
        tt(K, "pool", g4, gHnb, gLb, ALU.mult, gHn.b + gL.b, omp.b)
        tt(K, "pool", tb[:, :, 3, :], tb[:, :, 1, :], omp[:], ALU.mult, tb.b + omp.b, tb.b)
        K.dma("sp", S["tabs"][j0:j0 + NB].rearrange("j t r c -> r j t c"), tb[:], reads=tb.b,
              writes=[S["tabs_b"]])
    K.barrier(); K.stack_pop()
    K.barrier(); K.stack_pop()


def emit_s5(K, C, h, prm, p, kflag, cmask, PRE):
    nc = K.nc
    Lall = contextlib.ExitStack(); K.stack_push(Lall)
    gB = K.sbuf("gB5", [128, D], F32)
    bcast_load(K, gB, prm[p + "norm_g"], D)
    dsk = PRE["dsk"]; bgl = PRE["bgl"]
    S = {}
    S["tabs"] = K.dram("s5_tabs", [64, 4, 128, 512], F32, "Internal")
    S["bpad"] = K.dram("s5_bpad", [64, 2, 128, 128], BF16, "Internal")
    S["tabs_b"] = Buf("tabs"); S["bpad_b"] = Buf("bpad")
    S["CTr"] = K.sbuf("CTr", [128, 16, 128], BF16)
    S["CTn"] = K.sbuf("CTn", [128, 16, 128], BF16)
    S["CTrn"] = K.sbuf("CTrn", [128, 16, 128], BF16)
    S["L5"] = K.sbuf("L5", [128, 64, 2], F32)
    emit_s5_prep(K, C, prm, p, S, cmask, PRE)
    uT = K.sbuf("uT5", [128, 16, NTOK], BF16, nb=64)

    w_in_v = prm[p + "w_in"].rearrange("(k r) n -> r k n", r=128)
    w_glu_v = prm[p + "w_glu"].rearrange("(k r) n -> r k n", r=128)
    w_out_v = prm[p + "w_out"].rearrange("(k r) n -> r k n", r=128)

    def mkW():
        W = {}
        W["junk"] = K.sbuf("junk5", [128, 1024], F32)
        W["ss"] = K.sbuf("ss5", [128, 8], F32)
        W["rstd"] = K.sbuf("rstd5", [128, 8], F32)
        W["hn"] = K.sbuf("hn5", [128, D], BF16)
        W["tp"] = K.psum("tpA", [128, 1024], BF16)
        return W

    LA = contextlib.ExitStack(); K.stack_push(LA)
    W = mkW()
    P = PsumPool(K, 6)
    hnTs = [K.sbuf(f"hnT5_{i}", [128, 8, 512], BF16, nb=4) for i in range(2)]
    NWA = 4
    wbuf = [K.sbuf(f"wbA{i}", [128, 8, 512], BF16) for i in range(NWA)]
    wstA = {"issued": 0}

    def issue_a(upto):
        while wstA["issued"] < min(upto, 16):
            i = wstA["issued"]; cgi = i % 4
            K.dma("pool", wbuf[i % NWA][:], w_in_v[:, :, cgi * 512:(cgi + 1) * 512], writes=wbuf[i % NWA].b)
            wstA["issued"] += 1
    issue_a(NWA - 1)
    nl = 0
    for sb in range(4):
        hnT = hnTs[sb % 2]
        if sb == 0:
            norm_transpose_sb(K, C, W, P, h, 0, gB, hnT)
        for cg in range(4):
            if cg == 2 and sb + 1 < 4:
                norm_transpose_sb(K, C, W, P, h, sb + 1, gB, hnTs[(sb + 1) % 2])
            issue_a(nl + NWA)
            wb = wbuf[nl % NWA]; nl += 1
            for sub in range(4):
                fc = cg * 4 + sub
                ps = P.next()
                for k in range(8):
                    K.op("pe", lambda e, k=k, sub=sub, ps=ps, wb=wb, hnT=hnT: e.matmul(
                        ps[:], lhsT=wb[:, k, sub * 128:(sub + 1) * 128], rhs=hnT[:, k, :],
                        start=(k == 0), stop=(k == 7)), reads=wb.b + hnT.b, writes=ps.b)
                K.op("act", lambda e, fc=fc, ps=ps, sb=sb: e.copy(out=uT[:, fc, sb * 512:(sb + 1) * 512], in_=ps[:]),
                     reads=ps.b, writes=[uT.b[fc * 4 + sb]])
    K.barrier(); K.stack_pop()

    LB = contextlib.ExitStack(); K.stack_push(LB)
    psBU = [K.psum(f"psBU{i}", [128, 512], F32) for i in range(4)]
    psY = [K.psum(f"psY{i}", [128, 512], F32) for i in range(4)]
    ones = K.sbuf("ones5", [128, 512], F32)
    K.op("dve", lambda e: e.memset(ones[:], 1.0), writes=ones.b)
    tabt = [K.sbuf(f"tabt{i}", [128, 4, 512], F32) for i in range(2)]
    bpt = [K.sbuf(f"bpt{i}", [128, 2, 128], BF16) for i in range(2)]
    cst = K.sbuf("cst", [128, 64, 5, 2], F32, nb=64)
    NW = 3
    wk = [[K.sbuf(f"wk{s}_{i}", [128, 512], F32) for i in range(4)] for s in range(NW)]
    pk = [[K.sbuf(f"pk{s}_{i}", [128, 512], BF16) for i in range(4)] for s in range(2)]
    tny = K.sbuf("tny", [128, 4], F32)
    nl5i = K.sbuf("nl5i", [128, 64], F32)
    ts(K, "dve", nl5i[:].unsqueeze(2), S["L5"][:, :, 1:2], -1.0, ALU.mult, S["L5"].b, nl5i.b)
    LPre = contextlib.ExitStack(); K.stack_push(LPre)
    fpre = K.sbuf("fpre", [128, 64, 2], F32)
    accA = K.sbuf("accA5", [128, 64, 4, 4], F32, nb=64)
    itp = 0
    for j in range(64):
        jj = j // 4
        tb = tabt[j % 2]; bp = bpt[j % 2]

        def load_pre(jn):
            tbn = tabt[jn % 2]; bpn = bpt[jn % 2]
            K.dma("sp", tbn[:, 2:4, :], S["tabs"][jn, 2:4].rearrange("t r c -> r t c"), reads=[S["tabs_b"]], writes=tbn.b)
            K.dma("sp", bpn[:], S["bpad"][jn].rearrange("t r c -> r t c"), reads=[S["bpad_b"]], writes=bpn.b)
        if j == 0:
            load_pre(0)
        if j + 1 < 64:
            load_pre(j + 1)
        Mr, Mi = tb[:, 2, :], tb[:, 3, :]
        T2 = wk[j % 2][1]; T3 = wk[j % 2][2]
        tt(K, "pool", T2[:], Mi, Mr, ALU.subtract, tb.b, T2.b)
        tt(K, "pool", T3[:], Mr, Mi, ALU.add, tb.b, T3.b)
        for blk in range(4):
            jk = wk[2][0]
            pR = psBU[(2 * itp) % 4]; pI = psBU[(2 * itp + 1) % 4]; pS = psY[itp % 4]
            itp += 1
            ub = [uT.b[jj * 4 + blk]]
            usl = uT[:, jj, blk * 512:(blk + 1) * 512]
            K.op("pe", lambda e, pR=pR, bp=bp, usl=usl: e.matmul(pR[:], lhsT=bp[:, 0, :], rhs=usl, start=True, stop=True),
                 reads=bp.b + ub, writes=pR.b)
            K.op("pe", lambda e, pI=pI, bp=bp, usl=usl: e.matmul(pI[:], lhsT=bp[:, 1, :], rhs=usl, start=True, stop=True),
                 reads=bp.b + ub, writes=pI.b)
            K.op("pe", lambda e, pS=pS, bp=bp, usl=usl: e.matmul(pS[:], lhsT=bp[:, 0, :], rhs=usl, start=True, stop=False),
                 reads=bp.b + ub, writes=pS.b)
            K.op("pe", lambda e, pS=pS, bp=bp, usl=usl: e.matmul(pS[:], lhsT=bp[:, 1, :], rhs=usl, start=False, stop=True),
                 reads=bp.b + ub, writes=pS.b)
            for ai, (pp, mm, mb) in enumerate(((pS, Mr, tb.b), (pR, T2[:], T2.b), (pI, T3[:], T3.b))):
                K.op("dve", lambda e, pp=pp, mm=mm, ai=ai, jk=jk, j=j, blk=blk: e.scalar_tensor_tensor(
                    out=jk[:], in0=pp[:], scalar=1.0, in1=mm, op0=ALU.mult, op1=ALU.mult, accum_out=accA[:, j, blk, ai:ai + 1]),
                    reads=pp.b + mb, writes=jk.b + [accA.b[j]])
    vv = K.sbuf("vv5", [128, 64, 2], F32); cc = K.sbuf("cc5", [128, 64, 2], F32); t4 = K.sbuf("t45", [128, 64, 4], F32)
    l5r_all = S["L5"][:, :, 0:1]; l5i_all = S["L5"][:, :, 1:2]
    for blk in range(4):
        tt(K, "dve", vv[:, :, 0:1], accA[:, :, blk, 0:1], accA[:, :, blk, 2:3], ALU.subtract, accA.b, vv.b)
        tt(K, "dve", vv[:, :, 1:2], accA[:, :, blk, 0:1], accA[:, :, blk, 1:2], ALU.add, accA.b, vv.b)
        if blk > 0:
            tt(K, "dve", vv[:], vv[:], cc[:], ALU.add, vv.b + cc.b, vv.b)
        dst = cc if blk < 3 else fpre
        tt(K, "dve", t4[:, :, 0:1], vv[:, :, 0:1], l5r_all, ALU.mult, vv.b + S["L5"].b, t4.b)
        tt(K, "dve", t4[:, :, 1:2], vv[:, :, 1:2], l5i_all, ALU.mult, vv.b + S["L5"].b, t4.b)
        tt(K, "dve", t4[:, :, 2:3], vv[:, :, 0:1], l5i_all, ALU.mult, vv.b + S["L5"].b, t4.b)
        tt(K, "dve", t4[:, :, 3:4], vv[:, :, 1:2], l5r_all, ALU.mult, vv.b + S["L5"].b, t4.b)
        tt(K, "dve", dst[:, :, 0:1], t4[:, :, 0:1], t4[:, :, 1:2], ALU.subtract, t4.b, dst.b)
        tt(K, "dve", dst[:, :, 1:2], t4[:, :, 2:3], t4[:, :, 3:4], ALU.add, t4.b, dst.b)
    cxi = K.dram("s5_cxi", [128, 128], F32, "Internal")
    cxo = K.dram("s5_cxo", [256, 128], F32, "Internal")
    bxi = Buf("cxi"); bxo = Buf("cxo")
    K.dma("sp", cxi, fpre[:].rearrange("r j t -> r (j t)"), reads=fpre.b, writes=[bxi])
    K.collective("AllGather", [cxi], [cxo], [bxi], [bxo])
    cext = K.sbuf("cext", [128, 64, 2], F32)
    K.dma("sp", cext[:].rearrange("r j t -> r (j t)"), cxo[0:128, :], reads=[bxo], writes=cext.b)
    ts(K, "dve", cst[:, :, 0, :], cext[:], kflag[:, 0:1], ALU.mult, cext.b + kflag.b, cst.b)
    K.barrier(); K.stack_pop()
    ytmp = [K.sbuf(f"ytmp{i}", [128, 512], F32) for i in range(1)]
    NIT = 256
    ctx = {}

    def S1(t):
        j, blk = t // 4, t % 4
        jj = j // 4
        tb = tabt[j % 2]; bp = bpt[j % 2]

        def load_pair(jn):
            tbn = tabt[jn % 2]; bpn = bpt[jn % 2]
            K.dma("sp", tbn[:], S["tabs"][jn].rearrange("t r c -> r t c"), reads=[S["tabs_b"]], writes=tbn.b)
            K.dma("sp", bpn[:], S["bpad"][jn].rearrange("t r c -> r t c"), reads=[S["bpad_b"]], writes=bpn.b)
        if t == 0:
            load_pair(0)
        if blk == 1 and j + 1 < 64:
            load_pair(j + 1)
        a, b, c, d = wk[t % NW]
        pR = psBU[(2 * t) % 4]; pI = psBU[(2 * t + 1) % 4]
        Mr, Mi = tb[:, 2, :], tb[:, 3, :]
        ub = [uT.b[jj * 4 + blk]]
        usl = uT[:, jj, blk * 512:(blk + 1) * 512]
        K.op("pe", lambda e: e.matmul(pR[:], lhsT=bp[:, 0, :], rhs=usl, start=True, stop=True), reads=bp.b + ub, writes=pR.b)
        K.op("pe", lambda e: e.matmul(pI[:], lhsT=bp[:, 1, :], rhs=usl, start=True, stop=True), reads=bp.b + ub, writes=pI.b)
        tt(K, "dve", a[:], pR[:], Mr, ALU.mult, pR.b + tb.b, a.b)
        tt(K, "dve", b[:], pI[:], Mi, ALU.mult, pI.b + tb.b, b.b)
        tt(K, "dve", c[:], pR[:], Mi, ALU.mult, pR.b + tb.b, c.b)
        tt(K, "dve", d[:], pI[:], Mr, ALU.mult, pI.b + tb.b, d.b)

    def S2(t):
        a, b, c, d = wk[t % NW]
        tt(K, "pool", a[:], a[:], b[:], ALU.subtract, a.b + b.b, a.b)
        tt(K, "pool", c[:], c[:], d[:], ALU.add, c.b + d.b, c.b)

    def S3(t):
        j, blk = t // 4, t % 4
        a, b, c, d = wk[t % NW]
        cb = [cst.b[j]]
        K.op("dve", lambda e: e.tensor_tensor_scan(out=b[:], data0=ones[:], data1=a[:], initial=cst[:, j, blk, 0:1],
                                                   op0=ALU.mult, op1=ALU.add), reads=ones.b + a.b + cb, writes=b.b)
        K.op("dve", lambda e: e.tensor_tensor_scan(out=d[:], data0=ones[:], data1=c[:], initial=cst[:, j, blk, 1:2],
                                                   op0=ALU.mult, op1=ALU.add), reads=ones.b + c.b + cb, writes=d.b)
        l5r = S["L5"][:, j, 0:1]; l5i = S["L5"][:, j, 1:2]
        if blk < 3:
            ts(K, "dve", tny[:, 0:1], d[:, 511:512], nl5i[:, j:j + 1], ALU.mult, d.b + nl5i.b, tny.b)
            stt(K, "dve", cst[:, j, blk + 1, 0:1], b[:, 511:512], l5r, tny[:, 0:1], ALU.mult, ALU.add,
                b.b + tny.b + S["L5"].b, cb)
            ts(K, "dve", tny[:, 1:2], d[:, 511:512], l5r, ALU.mult, d.b + S["L5"].b, tny.b)
            stt(K, "dve", cst[:, j, blk + 1, 1:2], b[:, 511:512], l5i, tny[:, 1:2], ALU.mult, ALU.add,
                b.b + tny.b + S["L5"].b, cb)

    def S4(t):
        j = t // 4
        tb = tabt[j % 2]
        Dr, Di = tb[:, 0, :], tb[:, 1, :]
        a, b, c, d = wk[t % NW]; p0, p1, p2, p3 = pk[t % 2]
        tt(K, "pool", p0[:], b[:], Dr, ALU.mult, b.b + tb.b, p0.b)
        tt(K, "pool", p1[:], d[:], Di, ALU.mult, d.b + tb.b, p1.b)
        tt(K, "pool", p2[:], b[:], Di, ALU.mult, b.b + tb.b, p2.b)
        tt(K, "dve", p3[:], d[:], Dr, ALU.mult, d.b + tb.b, p3.b)

    def S5(t):
        j, blk = t // 4, t % 4
        jj, q = j // 4, j % 4
        p0, p1, p2, p3 = pk[t % 2]
        py = psY[blk]
        sl = slice(32 * q, 32 * q + 32)
        for i, (wt_, pp) in enumerate(((S["CTr"], p0), (S["CTrn"], p1), (S["CTn"], p2), (S["CTn"], p3))):
            K.op("pe", lambda e, wt_=wt_, pp=pp, i=i: e.matmul(py[sl, :], lhsT=wt_[:, jj, sl], rhs=pp[:],
                                                          start=(i == 0), stop=(i == 3), tile_position=(0, 32 * q)),
                 reads=wt_.b + pp.b, writes=py.b)
        if q == 3:
            yt = ytmp[0]
            usl = uT[:, jj, blk * 512:(blk + 1) * 512]; ub = [uT.b[jj * 4 + blk]]
            stt(K, "dve", yt[:], usl, dsk[:, jj:jj + 1], py[:], ALU.mult, ALU.add, ub + dsk.b + py.b, yt.b)
            act(K, usl, yt[:], AF.Gelu_apprx_tanh, yt.b, ub)

    for t in range(NIT + 2):
        if t < NIT:
            S1(t); S2(t)
        if 0 <= t - 1 < NIT:
            S3(t - 1); S4(t - 1)
        if 0 <= t - 2 < NIT:
            S5(t - 2)
    K.barrier(); K.stack_pop()

    LC = contextlib.ExitStack(); K.stack_push(LC)
    W = mkW()
    P = PsumPool(K, 6)
    hnT = K.sbuf("hnT5c", [128, 8, 512], BF16, nb=4)
    wbs = [K.sbuf(f"wbC{i}", [128, 4096], BF16) for i in range(3)]
    szT = K.sbuf("szT5", [128, 16, 512], BF16, nb=16)
    gT = szT
    gtmp = [K.sbuf(f"gtmp{i}", [128, 512], F32) for i in range(2)]
    nl = 0; nt = 0
    for sb in range(4):
        norm_transpose_sb(K, C, W, P, h, sb, gB, hnT)
        for cg in range(4):
            wt = wbs[nl % 3]; nl += 1
            wb = wt[:].rearrange("r (k n) -> r k n", k=8)
            K.dma("pool", wb, w_in_v[:, :, 2048 + cg * 512:2048 + (cg + 1) * 512], writes=wt.b)
            for sub in range(4):
                fc = cg * 4 + sub
                ps = P.next()
                for k in range(8):
                    K.op("pe", lambda e, k=k, sub=sub, ps=ps, wb=wb: e.matmul(
                        ps[:], lhsT=wb[:, k, sub * 128:(sub + 1) * 128], rhs=hnT[:, k, :],
                        start=(k == 0), stop=(k == 7)), reads=wt.b + hnT.b, writes=ps.b)
                act(K, szT[:, fc, :], ps[:], AF.Silu, ps.b, [szT.b[fc]])
        for cg in range(8):
            wt = wbs[nl % 3]; nl += 1
            wg = wt[:].rearrange("r (k n) -> r k n", k=16)
            K.dma("pool", wg, w_glu_v[:, :, cg * 256:(cg + 1) * 256], writes=wt.b)
            for sub in range(2):
                fc = cg * 2 + sub
                ps = P.next()
                for k in range(16):
                    K.op("pe", lambda e, k=k, sub=sub, ps=ps, wg=wg, sb=sb: e.matmul(
                        ps[:], lhsT=wg[:, k, sub * 128:(sub + 1) * 128], rhs=uT[:, k, sb * 512:(sb + 1) * 512],
                        start=(k == 0), stop=(k == 15)), reads=wt.b + [uT.b[k * 4 + sb]], writes=ps.b)
                gt = gtmp[nt % 2]; nt += 1
                act(K, gt[:], ps[:], AF.Sigmoid, ps.b + bgl.b, gt.b, bias=bgl[:, fc:fc + 1])
                tt(K, "dve", gt[:], gt[:], uT[:, fc, sb * 512:(sb + 1) * 512], ALU.mult,
                   gt.b + [uT.b[fc * 4 + sb]], gt.b)
                tt(K, "dve", szT[:, fc, :], gt[:], szT[:, fc, :], ALU.mult, gt.b + [szT.b[fc]], [szT.b[fc]])
        for cgo in range(4):
            wt = wbs[nl % 3]; nl += 1
            wo = wt[:].rearrange("r (k n) -> r k n", k=16)
            K.dma("pool", wo, w_out_v[:, :, cgo * 256:(cgo + 1) * 256], writes=wt.b)
            for jb in range(4):
                blk = sb * 4 + jb
                ps = P.next()
                for k in range(16):
                    K.op("pe", lambda e, k=k, jb=jb, ps=ps, wo=wo: e.matmul(
                        ps[:, 0:256], lhsT=gT[:, k, jb * 128:(jb + 1) * 128], rhs=wo[:, k, :],
                        start=(k == 0), stop=(k == 15)), reads=wt.b + [gT.b[k]], writes=ps.b)
                hs = h[:, blk, cgo * 256:(cgo + 1) * 256]
                tt(K, "dve", hs, hs, ps[:, 0:256], ALU.add, ps.b + [h.b[blk]], [h.b[blk]])
    K.barrier(); K.stack_pop()
    K.barrier(); K.stack_pop()


MLA_SCALE_F = 192 ** -0.5


def sincos_tables(K, ang, sin_out, cos_out, tmp, npi, ki, hpi):
    a_ap, a_b = ang; s_ap, s_b = sin_out; c_ap, c_b = cos_out; t_ap, t_b = tmp; k_ap, k_b = ki
    C1 = 6.28125; C2 = 2 * PI - C1
    np_ = s_ap.shape[0]
    ts(K, "dve", k_ap, a_ap, 1.0 / (2 * PI), ALU.mult, a_b, k_b)
    K.op("dve", lambda e: e.tensor_copy(out=t_ap, in_=k_ap), reads=k_b, writes=t_b)
    stt(K, "dve", a_ap, t_ap, -C1, a_ap, ALU.mult, ALU.add, t_b + a_b, a_b)
    stt(K, "dve", a_ap, t_ap, -C2, a_ap, ALU.mult, ALU.add, t_b + a_b, a_b)
    ts(K, "dve", a_ap, a_ap, 3.1415925, ALU.min, a_b, a_b, s2=-3.1415925, op1=ALU.max)
    act(K, s_ap, a_ap, AF.Sin, a_b, s_b)
    stt(K, "dve", t_ap, a_ap, -1.0, a_ap, ALU.mult, ALU.max, a_b, t_b)
    act(K, c_ap, t_ap, AF.Sin, t_b + hpi.b, c_b, scale=-1.0, bias=hpi[0:np_, 0:1])


def emit_mla(K, C, h, prm, p, pos_d, invf_d, kbias_d):
    Lall = contextlib.ExitStack(); K.stack_push(Lall)
    w_in = prm[p + "w_in"]; w_uq = prm[p + "w_uq"]; w_ukv = prm[p + "w_ukv"]; w_out = prm[p + "w_out"]
    w_in_v = w_in.rearrange("(k r) n -> r k n", r=128)
    w_out_v = w_out.rearrange("(k r) n -> r k n", r=128)
    gB = K.sbuf("gB2", [128, D], F32); bcast_load(K, gB, prm[p + "norm_g"], D)
    gq = K.sbuf("gq", [128, 384], F32); bcast_load(K, gq, prm[p + "q_norm_g"], 384)
    gkv = K.sbuf("gkv", [128, 128], F32); bcast_load(K, gkv, prm[p + "kv_norm_g"], 128)
    npi = None
    hpi = K.sbuf("hpi2", [128, 1], F32)
    K.op("dve", lambda e: e.memset(hpi[:], PI / 2), writes=hpi.b)
    kbias = K.sbuf("kbias", [128, 1], F32)
    K.dma("sp", kbias[:], kbias_d, writes=kbias.b)
    tri = K.sbuf("tri", [128, 128], BF16)
    trif = K.sbuf("trif", [128, 128], F32)
    K.op("pool", lambda e: e.memset(trif[:], 1.0), writes=trif.b)
    K.op("pool", lambda e: e.affine_select(out=trif[:], in_=trif[:], pattern=[[1, 128]], compare_op=ALU.is_ge,
                                           fill=0.0, base=0, channel_multiplier=-1), reads=trif.b, writes=trif.b)
    K.op("dve", lambda e: e.tensor_copy(out=tri[:], in_=trif[:]), reads=trif.b, writes=tri.b)
    wqn = K.sbuf("wqn", [128, 3, 16, 128], BF16)
    wqa = K.sbuf("wqa", [128, 3, 16, 64], BF16)
    wqb = K.sbuf("wqb", [128, 3, 16, 64], BF16)
    wuq_v = w_uq.rearrange("(k r) (hh e) -> r k hh e", r=128, e=192)
    wukv_v = w_ukv.rearrange("c (hh e) -> c hh e", e=256)
    wuv = K.sbuf("wuv", [128, 16, 128], BF16)
    wukT = K.sbuf("wukT", [128, 16, 128], BF16)
    wqi = K.sbuf("wqi", [128, 8, 384], BF16)
    W = {}
    W["junk"] = K.sbuf("junk2", [128, 1024], BF16)
    W["ss"] = K.sbuf("ss2", [128, 8], F32)
    W["rstd"] = K.sbuf("rstd2", [128, 8], F32)
    W["hn"] = K.sbuf("hn2", [128, D], BF16)
    W["tp"] = K.psum("tp2", [128, 1024], BF16)
    tp = W["tp"]
    posi = K.sbuf("posi", [128, NBLK], I32)
    posf = K.sbuf("posf", [128, NBLK], F32)
    K.dma("sp", posi[:], pos_d.rearrange("(j r) -> r j", r=128), writes=posi.b, allow_slow_non_contiguous=True)
    K.op("dve", lambda e: e.tensor_copy(out=posf[:], in_=posi[:]), reads=posi.b, writes=posf.b)
    invB = K.sbuf("invB", [128, 32], F32)
    K.dma("sp", invB[:], invf_d[0:32].partition_broadcast(128), writes=invB.b)
    invc = K.sbuf("invc", [64, 1], F32)
    K.dma("sp", invc[:, 0:1], invf_d.rearrange("(r o) -> r o", o=1), writes=invc.b)
    posBi = K.sbuf("posBi", [64, 512], I32)
    sgn = K.sbuf("sgn", [64, 1], F32)
    K.op("dve", lambda e: e.memset(sgn[0:32, :], -1.0), writes=sgn.b)
    K.op("dve", lambda e: e.memset(sgn[32:64, :], 1.0), writes=sgn.b)

    ckvT = K.sbuf("ckvT", [128, 2 * NTOK], BF16, nb=2)
    krT = K.sbuf("krT", [64, 2 * NTOK], BF16, nb=2)
    ckv_tok = K.sbuf("ckv_tok", [128, 2 * NBLK, 128], BF16, nb=2)

    hnT = K.sbuf("hnT2", [128, 8, 512], BF16, nb=4)
    P = PsumPool(K, 1)
    LA = contextlib.ExitStack(); K.stack_push(LA)
    wkv = K.sbuf("wkv", [128, 8, 192], BF16)
    K.dma("pool", wkv[:], w_in_v[:, :, 384:576], writes=wkv.b)
    for k in range(3):
        K.dma("pool", wqn[:, k], wuq_v[:, k, :, 0:128], writes=wqn.b)
        K.dma("pool", wqa[:, k], wuq_v[:, k, :, 128:192], writes=wqa.b)
        K.dma("pool", wqb[:, k, :, 0:32], wuq_v[:, k, :, 160:192], writes=wqb.b)
        K.dma("pool", wqb[:, k, :, 32:64], wuq_v[:, k, :, 128:160], writes=wqb.b)
    K.dma("pool", wuv[:], wukv_v[:, :, 128:256], writes=wuv.b)
    K.dma("pool", wqi[:], w_in_v[:, :, 0:384], writes=wqi.b)
    angt = K.sbuf("angt", [128, 32], F32); tmpt = K.sbuf("tmpt", [128, 32], F32)
    kit = K.sbuf("kit", [128, 32], I32)
    sint = K.sbuf("sint", [128, 32], F32); cost = K.sbuf("cost", [128, 32], F32)
    ckf = K.sbuf("ckf", [128, 128], BF16); krf = K.sbuf("krf", [128, 64], BF16)
    r1 = K.sbuf("r1", [128, 32], F32); r2 = K.sbuf("r2", [128, 32], F32)
    angA = K.sbuf("angA", [128, NBLK, 32], F32); tmpA = K.sbuf("tmpA", [128, NBLK, 32], F32)
    sinA = K.sbuf("sinA", [128, NBLK, 32], F32); cosA = K.sbuf("cosA", [128, NBLK, 32], F32)
    kiA = K.sbuf("kiA", [128, NBLK, 32], I32)
    tt(K, "dve", angA[:], invB[:].unsqueeze(1).to_broadcast([128, NBLK, 32]),
       posf[:].unsqueeze(2).to_broadcast([128, NBLK, 32]), ALU.mult, invB.b + posf.b, angA.b)
    fl = lambda t_: t_[:].rearrange("r j i -> r (j i)")
    sincos_tables(K, (fl(angA), angA.b), (fl(sinA), sinA.b), (fl(cosA), cosA.b), (fl(tmpA), tmpA.b), npi,
                  (fl(kiA), kiA.b), hpi)
    psA = [P.tiles[0]] + [K.psum(f"psA{i}", [128, 512], F32) for i in range(3)]
    ckf4 = K.sbuf("ckf4", [128, 4, 128], BF16); krf4 = K.sbuf("krf4", [128, 4, 64], BF16)
    r1q = K.sbuf("r1q", [128, 4, 32], F32); r2q = K.sbuf("r2q", [128, 4, 32], F32)
    for sb in range(4):
        norm_transpose_sb(K, C, W, P, h, sb, gB, hnT)
        ss = W["ss"]; rstd = W["rstd"]; junk = W["junk"]
        for j in range(4):
            ps = psA[j]
            for k in range(8):
                K.op("pe", lambda e, k=k, j=j, ps=ps: e.matmul(ps[:, 0:192], lhsT=hnT[:, k, j * 128:(j + 1) * 128],
                                                              rhs=wkv[:, k, :], start=(k == 0), stop=(k == 7)),
                     reads=wkv.b + [hnT.b[j]], writes=ps.b)
            act(K, junk[:, 0:128], ps[:, 0:128], AF.Square, ps.b, junk.b + ss.b, accum_out=ss[:, j:j + 1])
        ts(K, "dve", rstd[:, 0:4], ss[:, 0:4], 1.0 / 128, ALU.mult, ss.b, rstd.b, s2=EPS, op1=ALU.add)
        K.op("act", lambda e: e.sqrt(out=rstd[:, 0:4], in_=rstd[:, 0:4]), reads=rstd.b, writes=rstd.b)
        K.op("dve", lambda e: e.reciprocal(out=rstd[:, 0:4], in_=rstd[:, 0:4]), reads=rstd.b, writes=rstd.b)
        for j in range(4):
            blk = sb * 4 + j
            ps = psA[j]
            stt(K, "dve", ckf4[:, j, :], ps[:, 0:128], rstd[:, j:j + 1], gkv[:], ALU.mult, ALU.mult,
                ps.b + rstd.b + gkv.b, ckf4.b)
            sint_b = sinA[:, blk, :]; cost_b = cosA[:, blk, :]
            x1 = ps[:, 128:160]; x2 = ps[:, 160:192]
            tt(K, "dve", r1q[:, j, :], x1, cost_b, ALU.mult, ps.b + cosA.b, r1q.b)
            tt(K, "dve", r2q[:, j, :], x2, sint_b, ALU.mult, ps.b + sinA.b, r2q.b)
            tt(K, "dve", krf4[:, j, 0:32], r1q[:, j, :], r2q[:, j, :], ALU.subtract, r1q.b + r2q.b, krf4.b)
            tt(K, "dve", r1q[:, j, :], x2, cost_b, ALU.mult, ps.b + cosA.b, r1q.b)
            tt(K, "dve", r2q[:, j, :], x1, sint_b, ALU.mult, ps.b + sinA.b, r2q.b)
            tt(K, "dve", krf4[:, j, 32:64], r1q[:, j, :], r2q[:, j, :], ALU.add, r1q.b + r2q.b, krf4.b)
        K.op("pool", lambda e, sb=sb: e.tensor_copy(out=ckv_tok[:, NBLK + sb * 4:NBLK + sb * 4 + 4, :], in_=ckf4[:]),
             reads=ckf4.b, writes=[ckv_tok.b[1]])
        for j in range(4):
            K.op("pe", lambda e, j=j: e.transpose(tp[:, j * 128:(j + 1) * 128], ckf4[:, j, :], C["ident"][:]),
                 reads=ckf4.b + C["ident"].b, writes=tp.b)
        for j in range(4):
            K.op("pe", lambda e, j=j: e.transpose(tp[0:64, (4 + j) * 128:(5 + j) * 128], krf4[:, j, :], C["ident"][:]),
                 reads=krf4.b + C["ident"].b, writes=tp.b)
        K.op("act", lambda e, sb=sb: e.copy(out=ckvT[:, NTOK + sb * 512:NTOK + (sb + 1) * 512], in_=tp[:, 0:512]),
             reads=tp.b, writes=[ckvT.b[1]])
        K.op("act", lambda e, sb=sb: e.copy(out=krT[:, NTOK + sb * 512:NTOK + (sb + 1) * 512], in_=tp[0:64, 512:1024]),
             reads=tp.b, writes=[krT.b[1]])
    lxi = K.dram("mla_lxi", [320, NTOK], BF16, "Internal")
    lxo = K.dram("mla_lxo", [640, NTOK], BF16, "Internal")
    bli = Buf("lxi"); blo = Buf("lxo")
    K.dma("sp", lxi[0:128, :], ckvT[:, NTOK:2 * NTOK], reads=[ckvT.b[1]], writes=[bli])
    K.dma("sp", lxi[128:192, :], krT[:, NTOK:2 * NTOK], reads=[krT.b[1]], writes=[bli])
    K.dma("sp", lxi[192:320, :], ckv_tok[:, NBLK:2 * NBLK, :].rearrange("r j c -> r (j c)"), reads=[ckv_tok.b[1]],
          writes=[bli])
    K.collective("AllGather", [lxi], [lxo], [bli], [blo])
    K.dma("sp", ckvT[:, 0:NTOK], lxo[0:128, :], reads=[blo], writes=[ckvT.b[0]])
    K.dma("sp", krT[:, 0:NTOK], lxo[128:192, :], reads=[blo], writes=[krT.b[0]])
    K.dma("sp", ckv_tok[:, 0:NBLK, :].rearrange("r j c -> r (j c)"), lxo[192:320, :], reads=[blo], writes=[ckv_tok.b[0]])
    K.barrier(); K.stack_pop()
    Lw = contextlib.ExitStack(); K.stack_push(Lw)
    wuk = K.sbuf("wuk", [128, 16, 128], BF16)
    K.dma("pool", wuk[:], wukv_v[:, :, 0:128], writes=wuk.b)
    for i2 in range(2):
        for hl in range(8):
            hh = i2 * 8 + hl
            K.op("pe", lambda e, hh=hh, hl=hl: e.transpose(tp[:, hl * 128:(hl + 1) * 128], wuk[:, hh, :], C["ident"][:]),
                 reads=wuk.b + C["ident"].b, writes=tp.b)
        K.op("dve", lambda e, i2=i2: e.tensor_copy(out=wukT[:, i2 * 8:(i2 + 1) * 8, :].rearrange("r j c -> r (j c)"),
                                                   in_=tp[:]), reads=tp.b, writes=wukT.b)
    K.barrier(); K.stack_pop()

    LB = contextlib.ExitStack(); K.stack_push(LB)
    psS = [K.psum(f"psS{i}", [128, 512], F32) for i in range(2)]
    psO = [K.psum(f"psO{i}", [128, 512], F32) for i in range(2)]
    psL = [K.psum(f"psL{i}", [128, 512], F32) for i in range(2)]

    class _Pool2:
        def __init__(self, tiles):
            self.tiles = tiles; self.i = 0
        def next(self):
            t = self.tiles[self.i]; self.i = (self.i + 1) % len(self.tiles); return t
    PA = _Pool2(P.tiles)
    PB = _Pool2(P.tiles + psS + psO + [psL[0]])
    cqT = K.sbuf("cqT", [128, 3, 512], BF16, nb=4)
    cqf4 = K.sbuf("cqf4", [128, 4, 384], BF16)
    szT = K.sbuf("szT2", [128, 16, 512], BF16, nb=16)
    wbs = [K.sbuf(f"wb2_{i}", [128, 4096], BF16) for i in range(2)]
    wseqB = []
    for _sb in range(4):
        wseqB += [("z", c) for c in range(4)] + [("o", c) for c in range(4)]
    wstB = {"issued": 0, "used": 0}

    def issue_b(upto):
        while wstB["issued"] < min(upto, len(wseqB)):
            i = wstB["issued"]; kind, c = wseqB[i]
            wt_ = wbs[i % 2]
            if kind == "z":
                K.dma("pool", wt_[:].rearrange("r (k n) -> r k n", k=8), w_in_v[:, :, 576 + c * 512:576 + (c + 1) * 512],
                      writes=wt_.b)
            else:
                K.dma("pool", wt_[:].rearrange("r (k n) -> r k n", k=16), w_out_v[:, :, c * 256:(c + 1) * 256], writes=wt_.b)
            wstB["issued"] += 1

    def next_w():
        i = wstB["used"]; wstB["used"] += 1
        issue_b(i + 2)
        return wbs[i % 2]
    issue_b(1)
    angF = K.sbuf("angF", [64, 512], F32); tmpF = K.sbuf("tmpF", [64, 512], F32)
    sinF = K.sbuf("sinF", [64, 512], F32); cosF = K.sbuf("cosF", [64, 512], F32)
    posBf = angF
    qn = [K.sbuf(f"qn{i}", [128, 512], BF16) for i in range(2)]
    qp = [K.sbuf(f"qp{i}", [128, 512], BF16) for i in range(2)]
    qr = [K.sbuf(f"qr{i}", [64, 512], BF16) for i in range(2)]
    qra = angF; qrb = tmpF
    pT = [K.sbuf(f"pT{i}", [128, 512], BF16) for i in range(3)]
    rl = K.sbuf("rl", [128, 512], F32)
    oh = [K.sbuf(f"oh{i}", [128, 512], BF16) for i in range(2)]
    st = {"nl": 0, "npt": 0, "nS": 0}

    def q_prologue(hh):
        qn_, qp_, qr_ = qn[hh % 2], qp[hh % 2], qr[hh % 2]
        ps = PA.next()
        for k in range(3):
            K.op("pe", lambda e, k=k, ps=ps: e.matmul(ps[:], lhsT=wqn[:, k, hh, :], rhs=cqT[:, k, :],
                                                     start=(k == 0), stop=(k == 2)),
                 reads=wqn.b + cqT.b, writes=ps.b)
        K.op("act", lambda e, ps=ps: e.copy(out=qn_[:], in_=ps[:]), reads=ps.b, writes=qn_.b)
        ps3 = PA.next()
        for k in range(3):
            K.op("pe", lambda e, k=k, ps3=ps3: e.matmul(ps3[0:64, 0:512], lhsT=wqa[:, k, hh, :], rhs=cqT[:, k, :],
                                                       start=(k == 0), stop=(k == 2)),
                 reads=wqa.b + cqT.b, writes=ps3.b)
        tt(K, "dve", qra[:], ps3[0:64, :], cosF[:], ALU.mult, ps3.b + cosF.b, qra.b)
        ps2 = PA.next()
        K.op("pe", lambda e, ps2=ps2: e.matmul(ps2[:], lhsT=wukT[:, hh, :], rhs=qn_[:], start=True, stop=True),
             reads=wukT.b + qn_.b, writes=ps2.b)
        K.op("act", lambda e, ps2=ps2: e.copy(out=qp_[:], in_=ps2[:]), reads=ps2.b, writes=qp_.b)
        ps4 = PA.next()
        for k in range(3):
            K.op("pe", lambda e, k=k, ps4=ps4: e.matmul(ps4[0:64, 0:512], lhsT=wqb[:, k, hh, :], rhs=cqT[:, k, :],
                                                       start=(k == 0), stop=(k == 2)),
                 reads=wqb.b + cqT.b, writes=ps4.b)
        tt(K, "dve", qrb[:], ps4[0:64, :], sinF[:], ALU.mult, ps4.b + sinF.b, qrb.b)
        tt(K, "dve", qr_[:], qra[:], qrb[:], ALU.add, qra.b + qrb.b, qr_.b)

    def qk(hh, sb, kb):
        qp_, qr_ = qp[hh % 2], qr[hh % 2]
        half = 0 if kb < NBLK else 1
        jd = kb - (NBLK + sb * 4)
        c0 = jd * 128 if jd > 0 else 0
        pS = psS[st["nS"] % 2]; st["nS"] += 1
        K.op("pe", lambda e: e.matmul(pS[:, c0:512], lhsT=ckvT[:, kb * 128:(kb + 1) * 128], rhs=qp_[:, c0:512],
                                      start=True, stop=False), reads=[ckvT.b[half]] + qp_.b, writes=pS.b)
        K.op("pe", lambda e: e.matmul(pS[:, c0:512], lhsT=krT[:, kb * 128:(kb + 1) * 128], rhs=qr_[:, c0:512],
                                      start=False, stop=True), reads=[krT.b[half]] + qr_.b, writes=pS.b)
        return pS

    def softmax_pv(hh, sb, kb, pS, nkb):
        half = 0 if kb < NBLK else 1
        jd = kb - (NBLK + sb * 4)
        c0 = jd * 128 if jd > 0 else 0
        pt = pT[st["npt"] % 3]; st["npt"] += 1
        O = psO[hh % 2]; Lp = psL[hh % 2]
        if half == 0:
            act(K, pt[:, c0:512], pS[:, c0:512], AF.Exp, pS.b + kbias.b, pt.b, scale=MLA_SCALE_F, bias=kbias[:, 0:1])
        else:
            act(K, pt[:, c0:512], pS[:, c0:512], AF.Exp, pS.b, pt.b, scale=MLA_SCALE_F)
        if jd >= 0:
            tt(K, "pool", pt[:, jd * 128:(jd + 1) * 128], pt[:, jd * 128:(jd + 1) * 128], tri[:], ALU.mult,
               pt.b + tri.b, pt.b)
        K.op("pe", lambda e: e.matmul(O[:, c0:512], lhsT=ckv_tok[:, kb, :], rhs=pt[:, c0:512], start=(kb == 0),
                                      stop=(kb == nkb - 1)), reads=[ckv_tok.b[half]] + pt.b, writes=O.b)
        K.op("pe", lambda e: e.matmul(Lp[:, c0:512], lhsT=C["ones_bf"][:], rhs=pt[:, c0:512], start=(kb == 0),
                                      stop=(kb == nkb - 1)), reads=C["ones_bf"].b + pt.b, writes=Lp.b)

    def head_epilogue(hh):
        oh_ = oh[hh % 2]; O = psO[hh % 2]; Lp = psL[hh % 2]
        K.op("dve", lambda e: e.reciprocal(out=rl[:], in_=Lp[:]), reads=Lp.b, writes=rl.b)
        tt(K, "dve", oh_[:], O[:], rl[:], ALU.mult, O.b + rl.b, oh_.b)
        ps5 = PA.next()
        K.op("pe", lambda e: e.matmul(ps5[:], lhsT=wuv[:, hh, :], rhs=oh_[:], start=True, stop=True),
             reads=wuv.b + oh_.b, writes=ps5.b)
        tt(K, "dve", szT[:, hh, :], ps5[:], szT[:, hh, :], ALU.mult, ps5.b + [szT.b[hh]], [szT.b[hh]])

    for sb in range(4):
        norm_transpose_sb(K, C, W, P, h, sb, gB, hnT)
        K.dma("sp", posBi[:], pos_d[sb * 512:(sb + 1) * 512].partition_broadcast(64), writes=posBi.b)
        K.op("dve", lambda e: e.tensor_copy(out=posBf[:], in_=posBi[:]), reads=posBi.b, writes=posBf.b)
        ts(K, "dve", angF[:], angF[:], invc[:, 0:1], ALU.mult, angF.b + invc.b, angF.b)
        sincos_tables(K, (angF[:], angF.b), (sinF[:], sinF.b), (cosF[:], cosF.b), (tmpF[:], tmpF.b), npi,
                      (posBi[:], posBi.b), hpi)
        ts(K, "dve", sinF[:], sinF[:], sgn[:, 0:1], ALU.mult, sinF.b + sgn.b, sinF.b)
        ss = W["ss"]; rstd = W["rstd"]; junk = W["junk"]
        cq_ps = []
        for j in range(4):
            ps = PB.next(); cq_ps.append(ps)
            for k in range(8):
                K.op("pe", lambda e, k=k, j=j, ps=ps: e.matmul(ps[:, 0:384], lhsT=hnT[:, k, j * 128:(j + 1) * 128],
                                                              rhs=wqi[:, k, :], start=(k == 0), stop=(k == 7)),
                     reads=wqi.b + [hnT.b[j]], writes=ps.b)
            act(K, junk[:, 0:384], ps[:, 0:384], AF.Square, ps.b, junk.b + ss.b, accum_out=ss[:, j:j + 1])
        ts(K, "dve", rstd[:, 0:4], ss[:, 0:4], 1.0 / 384, ALU.mult, ss.b, rstd.b, s2=EPS, op1=ALU.add)
        K.op("act", lambda e: e.sqrt(out=rstd[:, 0:4], in_=rstd[:, 0:4]), reads=rstd.b, writes=rstd.b)
        K.op("dve", lambda e: e.reciprocal(out=rstd[:, 0:4], in_=rstd[:, 0:4]), reads=rstd.b, writes=rstd.b)
        for j in range(4):
            ps = cq_ps[j]
            stt(K, "dve", cqf4[:, j, :], ps[:, 0:384], rstd[:, j:j + 1], gq[:], ALU.mult, ALU.mult,
                ps.b + rstd.b + gq.b, cqf4.b)
        for cg in range(4):
            wt = next_w()
            wb = wt[:].rearrange("r (k n) -> r k n", k=8)
            for sub in range(4):
                fc = cg * 4 + sub
                ps = PB.next()
                for k in range(8):
                    K.op("pe", lambda e, k=k, sub=sub, ps=ps, wb=wb: e.matmul(
                        ps[:], lhsT=wb[:, k, sub * 128:(sub + 1) * 128], rhs=hnT[:, k, :],
                        start=(k == 0), stop=(k == 7)), reads=wt.b + hnT.b, writes=ps.b)
                act(K, szT[:, fc, :], ps[:], AF.Silu, ps.b, [szT.b[fc]])
        for j in range(4):
            for k in range(3):
                K.op("pe", lambda e, k=k, j=j: e.transpose(tp[:, k * 128:(k + 1) * 128], cqf4[:, j, k * 128:(k + 1) * 128],
                                                           C["ident"][:]), reads=cqf4.b + C["ident"].b, writes=tp.b)
            K.op("act", lambda e, j=j: e.copy(out=cqT[:, :, j * 128:(j + 1) * 128],
                                              in_=tp[:, 0:384].rearrange("r (k t) -> r k t", k=3)),
                 reads=tp.b, writes=[cqT.b[j]])
        nkb = NBLK + (sb + 1) * 4
        q_prologue(0)
        for hh in range(16):
            pS_next = qk(hh, sb, 0)
            for kb in range(nkb):
                pS_cur = pS_next
                if kb + 1 < nkb:
                    pS_next = qk(hh, sb, kb + 1)
                softmax_pv(hh, sb, kb, pS_cur, nkb)
                if kb == 3 and hh + 1 < 16:
                    q_prologue(hh + 1)
                if kb == 6 and hh > 0:
                    head_epilogue(hh - 1)
            if hh == 15:
                head_epilogue(15)
        for cgo in range(4):
            wt = next_w()
            wo = wt[:].rearrange("r (k n) -> r k n", k=16)
            for jb in range(4):
                blk = sb * 4 + jb
                ps = PB.next()
                for k in range(16):
                    K.op("pe", lambda e, k=k, jb=jb, ps=ps, wo=wo: e.matmul(
                        ps[:, 0:256], lhsT=szT[:, k, jb * 128:(jb + 1) * 128], rhs=wo[:, k, :],
                        start=(k == 0), stop=(k == 15)), reads=wt.b + [szT.b[k]], writes=ps.b)
                hs = h[:, blk, cgo * 256:(cgo + 1) * 256]
                tt(K, "dve", hs, hs, ps[:, 0:256], ALU.add, ps.b + [h.b[blk]], [h.b[blk]])
    K.barrier(); K.stack_pop()
    K.barrier(); K.stack_pop()


PARAM_SPECS = None


def param_specs():
    sp = {}
    def gm(p):
        sp[p + "norm_g"] = (1024,); sp[p + "w_in"] = (1024, 6144); sp[p + "ln_g"] = (2048,)
        sp[p + "ln_b"] = (2048,); sp[p + "w_s"] = (8, 128, 128); sp[p + "b_s"] = (8, 128)
        sp[p + "w_out"] = (2048, 1024)
    gm("l0_")
    p = "l1_"
    sp[p + "norm_g"] = (1024,); sp[p + "w_in"] = (1024, 4096)
    sp[p + "a_re"] = (128, 64); sp[p + "a_im"] = (128, 64); sp[p + "log_step"] = (128,)
    sp[p + "b_re"] = (128, 64, 16); sp[p + "b_im"] = (128, 64, 16)
    sp[p + "c_re"] = (128, 16, 64); sp[p + "c_im"] = (128, 16, 64)
    sp[p + "d_skip"] = (2048,); sp[p + "w_glu"] = (2048, 2048); sp[p + "b_glu"] = (2048,)
    sp[p + "w_out"] = (2048, 1024)
    p = "l2_"
    sp[p + "norm_g"] = (1024,); sp[p + "w_in"] = (1024, 2624); sp[p + "q_norm_g"] = (384,)
    sp[p + "w_uq"] = (384, 3072); sp[p + "kv_norm_g"] = (128,); sp[p + "w_ukv"] = (128, 4096)
    sp[p + "w_out"] = (2048, 1024)
    gm("l3_")
    sp["final_norm_g"] = (1024,)
    return sp


def build_program(layers=("l0",), final_norm=False):
    K = MK()
    x = K.dram("x", [NTOK, D], F32, "ExternalInput")
    out = K.dram("out", [NTOK, D], F32, "ExternalOutput")
    prm = {}
    need = set()
    for l in layers:
        need.add(l + "_")
    for name, shp in param_specs().items():
        if name[:3] in need or (final_norm and name == "final_norm_g"):
            prm[name] = K.dram(name, list(shp), F32, "ExternalInput")
    C = emit_consts(K)
    P = None
    cmask_d = K.dram("cmask", [128, 2], F32, "ExternalInput")
    cmask = K.sbuf("cmask_s", [128, 2], F32)
    K.dma("sp", cmask[:], cmask_d, writes=cmask.b)
    kflag_d = K.dram("kflag", [128, 1], F32, "ExternalInput")
    kflag = K.sbuf("kflag_s", [128, 1], F32)
    K.dma("sp", kflag[:], kflag_d, writes=kflag.b)
    pos_d = K.dram("pos", [NTOK], I32, "ExternalInput")
    invf_d = K.dram("invf", [64], F32, "ExternalInput")
    kbias_d = K.dram("kbias", [128, 1], F32, "ExternalInput")
    h = K.sbuf("h", [128, NBLK, D], F32, nb=NBLK)
    xv = x.rearrange("(j p) d -> p j d", p=128)
    for q in range(4):
        K.dma("sp", h[:, q * 4:(q + 1) * 4, :], xv[:, q * 4:(q + 1) * 4, :], writes=h.b[q * 4:(q + 1) * 4])
    PRE = {}
    if "l1" in layers:
        LPRE = contextlib.ExitStack(); K.stack_push(LPRE)
        p1 = "l1_"
        for nm in ("are", "aim", "lst"):
            PRE[nm] = K.sbuf("pre_" + nm, [128, 64], F32)
        K.dma("sp", PRE["are"][:], prm[p1 + "a_re"].rearrange("(j gl) p -> (gl p) j", gl=2), writes=PRE["are"].b,
              allow_slow_non_contiguous=True)
        K.dma("sp", PRE["aim"][:], prm[p1 + "a_im"].rearrange("(j gl) p -> (gl p) j", gl=2), writes=PRE["aim"].b,
              allow_slow_non_contiguous=True)
        lsv = prm[p1 + "log_step"].rearrange("(j gl) -> gl j", gl=2)
        for gl in range(2):
            K.dma("sp", PRE["lst"][gl * 64:(gl + 1) * 64, :], lsv[gl].partition_broadcast(64), writes=PRE["lst"].b,
                  allow_slow_non_contiguous=True)
        PRE["dsk"] = K.sbuf("pre_dsk", [128, 16], F32); PRE["bgl"] = K.sbuf("pre_bgl", [128, 16], F32)
        K.dma("sp", PRE["dsk"][:], prm[p1 + "d_skip"].rearrange("(c r) -> r c", r=128), writes=PRE["dsk"].b,
              allow_slow_non_contiguous=True)
        K.dma("sp", PRE["bgl"][:], prm[p1 + "b_glu"].rearrange("(c r) -> r c", r=128), writes=PRE["bgl"].b,
              allow_slow_non_contiguous=True)
    for l in layers:
        if l in ("l0", "l3"):
            emit_gmlp(K, C, P, h, prm, l + "_")
        elif l == "l2":
            emit_mla(K, C, h, prm, "l2_", pos_d, invf_d, kbias_d)
        elif l == "l1":
            emit_s5(K, C, h, prm, "l1_", kflag, cmask, PRE)
            K.barrier(); K.stack_pop()
    ov = out.rearrange("(j p) d -> p j d", p=128)
    if final_norm:
        emit_final_norm(K, h, prm["final_norm_g"], ov)
    else:
        for q in range(4):
            K.dma("sp", ov[:, q * 4:(q + 1) * 4, :], h[:, q * 4:(q + 1) * 4, :], reads=h.b[q * 4:(q + 1) * 4])
    K.barrier()
    return K.build(), sorted(prm.keys())


def emit_final_norm(K, h, g_d, ov):
    L = contextlib.ExitStack(); K.stack_push(L)
    gB = K.sbuf("gBf", [128, D], F32); bcast_load(K, gB, g_d, D)
    junk = [K.sbuf(f"junkf{i}", [128, D], F32) for i in range(2)]
    ss = K.sbuf("ssf", [128, NBLK], F32)
    ot = [K.sbuf(f"otf{i}", [128, D], F32) for i in range(3)]
    for blk in range(NBLK):
        jk = junk[blk % 2]
        act(K, jk[:], h[:, blk, :], AF.Square, [h.b[blk]], jk.b + ss.b, accum_out=ss[:, blk:blk + 1])
    ts(K, "dve", ss[:], ss[:], 1.0 / D, ALU.mult, ss.b, ss.b, s2=EPS, op1=ALU.add)
    K.op("act", lambda e: e.sqrt(out=ss[:], in_=ss[:]), reads=ss.b, writes=ss.b)
    K.op("dve", lambda e: e.reciprocal(out=ss[:], in_=ss[:]), reads=ss.b, writes=ss.b)
    for blk in range(NBLK):
        o = ot[blk % 3]
        stt(K, "dve", o[:], h[:, blk, :], ss[:, blk:blk + 1], gB[:], ALU.mult, ALU.mult, [h.b[blk]] + ss.b + gB.b, o.b)
        K.dma("sp", ov[:, blk, :], o[:], reads=o.b)
    K.barrier(); K.stack_pop()


_PROG = {}


def _get_prog(layers, final_norm):
    key = (tuple(layers), final_norm)
    if key not in _PROG:
        _PROG[key] = build_program(layers=layers, final_norm=final_norm)
    return _PROG[key]


def _aux_consts():
    cmask = np.zeros((128, 2), np.float32)
    for r in range(128):
        cmask[r, (r // 16) % 2] = 1.0
    invf = (np.float32(10000.0) ** (-np.arange(0, 64, 2, dtype=np.float32) / np.float32(64))).astype(np.float32)
    return cmask, np.concatenate([invf, invf])


def kernel(**inputs):
    inputs = {k: np.asarray(v) for k, v in inputs.items()}
    layers = ("l0", "l1", "l2", "l3")
    nc, names = _get_prog(layers, True)
    x = np.ascontiguousarray(inputs["x"], dtype=np.float32).reshape(8, NTOK, D)
    pos = np.ascontiguousarray(inputs["positions"]).astype(np.int32).reshape(8, NTOK)
    cmask, invf = _aux_consts()
    wts = {n: np.ascontiguousarray(inputs[n], dtype=np.float32) for n in names}

    in_maps = []
    for i in range(8):
        odd = float(i % 2)
        m = {"x": x[i], "cmask": cmask, "pos": pos[i], "invf": invf,
             "kbias": np.full((128, 1), 0.0 if odd else -30000.0, np.float32),
             "kflag": np.full((128, 1), odd, np.float32)}
        m.update(wts)
        in_maps.append(m)
    res = run_bass_kernel_spmd(nc, in_maps, core_ids=list(range(8))).results
    out = np.stack([np.asarray(r["out"]) for r in res]).reshape(4, 4096, D).astype(np.float32)
    return out
```

```python
import contextlib
import numpy as np
import concourse.bass as bass
import concourse.mybir as mybir
from concourse.bass_utils import run_bass_kernel_spmd

F32 = mybir.dt.float32
BF16 = mybir.dt.bfloat16
I32 = mybir.dt.int32
ALU = mybir.AluOpType
AF = mybir.ActivationFunctionType
AX = mybir.AxisListType

N_DMA_SEMS = 12


class Buf:
    __slots__ = ("w", "r", "name")

    def __init__(self, name=""):
        self.w = None
        self.r = []
        self.name = name


class Tile:
    def __init__(self, t, nb, name):
        self.t = t
        self.b = [Buf(f"{name}.{i}") for i in range(nb)]

    def __getitem__(self, idx):
        return self.t[idx]


class MK:
    ENG = ("pe", "act", "dve", "pool", "sp")

    def __init__(self):
        self.nc = bass.Bass("TRN2", target_bir_lowering=False)
        self.stack = contextlib.ExitStack()
        self.ops = {e: [] for e in self.ENG}
        self.cnt = {}
        self.sems = {}
        self.waited = {e: {} for e in self.ENG}
        for e in ("pe", "act", "dve", "pool"):
            self._mksem("c_" + e)
        for i in range(N_DMA_SEMS):
            self._mksem(f"d{i}")
        self.dma_rr = 0
        self.n_ops = 0
        self.stacks = [self.stack]
        self.uid = 0

    def _mksem(self, key):
        self.sems[key] = self.stack.enter_context(self.nc.semaphore(key))
        self.cnt[key] = 0

    def dram(self, name, shape, dt, kind):
        return self.nc.dram_tensor(name, list(shape), dt, kind=kind).ap()

    def stack_push(self, st):
        self.stacks.append(st)

    def stack_pop(self):
        self.stacks.pop().close()

    def sbuf(self, name, shape, dt, nb=1):
        self.uid += 1
        t = self.stacks[-1].enter_context(self.nc.sbuf_tensor(f"{name}_{self.uid}", list(shape), dt))
        return Tile(t, nb, name)

    def psum(self, name, shape, dt=F32, nb=1):
        self.uid += 1
        t = self.stacks[-1].enter_context(self.nc.psum_tensor(f"{name}_{self.uid}", list(shape), dt))
        return Tile(t, nb, name)

    def barrier(self):
        for eng in self.ENG:
            waits = []
            for k, v in self.cnt.items():
                if v > self.waited[eng].get(k, 0):
                    self.waited[eng][k] = v
                    waits.append((k, v))
            if waits:
                self.ops[eng].append((None, waits, None, 0))

    def _deps(self, eng, reads, writes):
        need = {}
        def add(tok):
            if tok is None:
                return
            k, v = tok
            if eng == "pe" and k == "c_pe":
                return
            if need.get(k, 0) < v:
                need[k] = v
        for b in reads:
            add(b.w)
        for b in writes:
            add(b.w)
            for tok in b.r:
                add(tok)
        out = []
        wd = self.waited[eng]
        for k, v in need.items():
            if wd.get(k, 0) < v:
                wd[k] = v
                out.append((k, v))
        return out

    def _commit(self, tok, reads, writes):
        for b in writes:
            b.w = tok
            b.r = []
        for b in reads:
            if b not in writes:
                b.r.append(tok)

    def op(self, eng, fn, reads=(), writes=()):
        reads = list(reads); writes = list(writes)
        waits = self._deps(eng, reads, writes)
        key = "c_" + eng
        self.cnt[key] += 1
        tok = (key, self.cnt[key])
        self.ops[eng].append((fn, waits, key, 1))
        self._commit(tok, reads, writes)
        self.n_ops += 1
        return tok

    def dma(self, eng, out, in_, reads=(), writes=(), **kw):
        reads = list(reads); writes = list(writes)
        i = self.dma_rr; self.dma_rr = (self.dma_rr + 1) % N_DMA_SEMS
        key = f"d{i}"
        waits = self._deps(eng, reads, writes)
        prev = self.cnt[key]
        if prev and self.waited[eng].get(key, 0) < prev:
            self.waited[eng][key] = prev
            waits.append((key, prev))
        self.cnt[key] += 16
        tok = (key, self.cnt[key])
        def fn(e, out=out, in_=in_, kw=kw):
            return e.dma_start(out=out, in_=in_, **kw)
        self.ops[eng].append((fn, waits, key, 16))
        self._commit(tok, reads, writes)
        self.n_ops += 1
        return tok

    def collective(self, kind, ins, outs, rbufs, wbufs):
        if "cc" not in self.sems:
            self._mksem("cc")
        waits = self._deps("pool", list(rbufs), list(wbufs))
        self.cnt["cc"] += 1
        tok = ("cc", self.cnt["cc"])
        def fn(e, ins=ins, outs=outs, kind=kind):
            return e.collective_compute(kind, ALU.bypass, replica_groups=[[0, 1], [2, 3], [4, 5], [6, 7]],
                                        ins=list(ins), outs=list(outs))
        self.ops["pool"].append((fn, waits, "cc", 1))
        self._commit(tok, list(rbufs), list(wbufs))
        return tok

    def wait_all(self, eng, bufs):
        waits = self._deps(eng, list(bufs), [])
        self.ops[eng].append((None, waits, None, 0))

    def build(self):
        nc = self.nc
        sems = self.sems
        ops = self.ops
        with nc.Block() as block:
            def emit(e, lst):
                for fn, waits, key, amt in lst:
                    for k, v in waits:
                        e.wait_ge(sems[k], v)
                    if fn is not None:
                        ins = fn(e)
                        ins.then_inc(sems[key], amt)

            @block.tensor
            def _(e):
                emit(e, ops["pe"])

            @block.scalar
            def _(e):
                emit(e, ops["act"])

            @block.vector
            def _(e):
                emit(e, ops["dve"])

            @block.gpsimd
            def _(e):
                emit(e, ops["pool"])

            @block.sync
            def _(e):
                emit(e, ops["sp"])
        self.stack.close()
        return nc


NTOK = 2048
NBLK = 16
D = 1024
DI = 2048
EPS = 1e-6


class PsumPool:
    def __init__(self, K, n):
        self.tiles = [K.psum(f"pp{i}", [128, 512], F32) for i in range(n)]
        self.i = 0

    def next(self):
        t = self.tiles[self.i]
        self.i = (self.i + 1) % len(self.tiles)
        return t


def bcast_load(K, dst, src_1d, n, eng="sp"):
    K.dma(eng, dst[:], src_1d.partition_broadcast(128), writes=dst.b)


def emit_consts(K):
    C = {}
    C["ident_f"] = K.sbuf("ident_f", [128, 128], F32)
    C["ident"] = K.sbuf("ident", [128, 128], BF16)
    idf = C["ident_f"]; idb = C["ident"]
    K.op("pool", lambda e: e.memset(idf[:], 1.0), writes=idf.b)
    K.op("pool", lambda e: e.affine_select(out=idf[:], in_=idf[:], pattern=[[-1, 128]],
                                           compare_op=ALU.is_equal, fill=0.0, base=0,
                                           channel_multiplier=1), reads=idf.b, writes=idf.b)
    K.op("dve", lambda e: e.tensor_copy(out=idb[:], in_=idf[:]), reads=idf.b, writes=idb.b)
    C["ones_bf"] = K.sbuf("ones_bf", [128, 128], BF16)
    ob = C["ones_bf"]
    K.op("dve", lambda e: e.memset(ob[:], 1.0), writes=ob.b)
    return C


def rmsnorm_block(K, W, hsrc, gB, out_bf, tag):
    h_ap, h_b = hsrc
    o_ap, o_b = out_bf
    junk = W["junk"]; ss = W["ss"]; rstd = W["rstd"]
    K.op("act", lambda e: e.activation(out=junk[:, 0:D], in_=h_ap, func=AF.Square, accum_out=ss[:, 0:1]),
         reads=h_b, writes=junk.b + ss.b)
    K.op("dve", lambda e: e.tensor_scalar(out=rstd[:, 0:1], in0=ss[:, 0:1], scalar1=1.0 / D, scalar2=EPS,
                                          op0=ALU.mult, op1=ALU.add), reads=ss.b, writes=rstd.b)
    K.op("act", lambda e: e.sqrt(out=rstd[:, 0:1], in_=rstd[:, 0:1]), reads=rstd.b, writes=rstd.b)
    K.op("dve", lambda e: e.reciprocal(out=rstd[:, 0:1], in_=rstd[:, 0:1]), reads=rstd.b, writes=rstd.b)
    K.op("dve", lambda e: e.scalar_tensor_tensor(out=o_ap, in0=h_ap, scalar=rstd[:, 0:1], in1=gB[:],
                                                 op0=ALU.mult, op1=ALU.mult),
         reads=h_b + rstd.b + gB.b, writes=o_b)


def norm_transpose_sb(K, C, W, P, h, sb, gB, hnT):
    junk = W["junk"]; ss = W["ss"]; rstd = W["rstd"]; hn = W["hn"]; tp = W["tp"]
    for j in range(4):
        blk = sb * 4 + j
        K.op("act", lambda e, j=j, blk=blk: e.activation(out=junk[:, 0:D], in_=h[:, blk, :], func=AF.Square,
                                                        accum_out=ss[:, 4 + j:5 + j]),
             reads=[h.b[blk]], writes=junk.b + ss.b)
    K.op("dve", lambda e: e.tensor_scalar(out=rstd[:, 4:8], in0=ss[:, 4:8], scalar1=1.0 / D, scalar2=EPS,
                                          op0=ALU.mult, op1=ALU.add), reads=ss.b, writes=rstd.b)
    K.op("act", lambda e: e.sqrt(out=rstd[:, 4:8], in_=rstd[:, 4:8]), reads=rstd.b, writes=rstd.b)
    K.op("dve", lambda e: e.reciprocal(out=rstd[:, 4:8], in_=rstd[:, 4:8]), reads=rstd.b, writes=rstd.b)
    for j in range(4):
        blk = sb * 4 + j
        K.op("dve", lambda e, j=j, blk=blk: e.scalar_tensor_tensor(out=hn[:], in0=h[:, blk, :], scalar=rstd[:, 4 + j:5 + j],
                                                                  in1=gB[:], op0=ALU.mult, op1=ALU.mult),
             reads=[h.b[blk]] + rstd.b + gB.b, writes=hn.b)
        for k in range(8):
            K.op("pe", lambda e, k=k: e.transpose(tp[:, k * 128:(k + 1) * 128], hn[:, k * 128:(k + 1) * 128],
                                                  C["ident"][:]),
                 reads=hn.b + C["ident"].b, writes=tp.b)
        K.op("act", lambda e, j=j: e.copy(out=hnT[:, :, j * 128:(j + 1) * 128],
                                           in_=tp[:].rearrange("p (k t) -> p k t", k=8)),
             reads=tp.b, writes=[hnT.b[j]])


def emit_gmlp(K, C, P, h, prm, lname):
    nc = K.nc
    L = contextlib.ExitStack()
    K.stack_push(L)
    P = PsumPool(K, 6)
    w_in, ln_g, ln_b, w_s, b_s, w_out, norm_g = (prm[lname + s] for s in
                                                 ("w_in", "ln_g", "ln_b", "w_s", "b_s", "w_out", "norm_g"))
    W = {}
    W["junk"] = K.sbuf("junk", [128, 1024], F32)
    W["ss"] = K.sbuf("ss", [128, 8], F32)
    W["rstd"] = K.sbuf("rstd", [128, 8], F32)
    W["hn"] = K.sbuf("hn", [128, D], BF16)
    W["tp"] = K.psum("tp", [128, 1024], BF16)
    gB = K.sbuf("gB", [128, D], F32)
    lgc = K.sbuf("lgc", [128, 16], F32)
    lbc = K.sbuf("lbc", [128, 16], F32)
    bias2 = K.sbuf("bias2", [128, 16, 128], F32)
    bsB = K.sbuf("bsB", [128, 8, 128], F32)
    WsT = K.sbuf("WsT", [128, 8, 128], BF16)
    hnTs = [K.sbuf(f"hnT{i}", [128, 8, 512], BF16, nb=4) for i in range(2)]
    NWB = 4
    wbuf = [K.sbuf(f"wbuf{i}", [128, 8, 512], BF16) for i in range(NWB)]
    wo_tiles = [K.sbuf(f"wobuf{i}", [128, 16, 256], BF16) for i in range(2)]
    uT = K.sbuf("uT", [128, 16, 512], BF16, nb=16)
    szT = K.sbuf("szT", [128, 16, 512], BF16, nb=16)
    vtok = K.sbuf("vtok", [128, 4, DI], BF16, nb=4)
    st1 = K.sbuf("st1", [128, 4, 4], F32, nb=4)
    st2 = K.sbuf("st2", [128, 4, 4], F32, nb=4)
    mv = K.sbuf("mv", [128, 8], F32)
    t1 = [K.sbuf(f"t1_{i}", [128, 512], F32) for i in range(2)]

    bcast_load(K, gB, norm_g, D)
    K.dma("sp", lgc[:], ln_g.rearrange("(c r) -> r c", r=128), writes=lgc.b, allow_slow_non_contiguous=True)
    K.dma("sp", lbc[:], ln_b.rearrange("(c r) -> r c", r=128), writes=lbc.b, allow_slow_non_contiguous=True)
    K.dma("sp", bsB[:].rearrange("p g t -> p (g t)"), b_s.rearrange("g t -> (g t)").partition_broadcast(128),
          writes=bsB.b)
    wsf_t = W["hn"]
    wsf = wsf_t[:].rearrange("p (g s) -> p g s", g=8)
    K.dma("pool", wsf, w_s.rearrange("g t s -> t g s"), writes=wsf_t.b)
    tp = W["tp"]
    for g in range(8):
        K.op("pe", lambda e, g=g: e.transpose(tp[:, g * 128:(g + 1) * 128], wsf[:, g, :], C["ident"][:]),
             reads=wsf_t.b + C["ident"].b, writes=tp.b)
    K.op("dve", lambda e: e.tensor_copy(out=WsT[:].rearrange("p g t -> p (g t)"), in_=tp[:]),
         reads=tp.b, writes=WsT.b)
    for g in range(8):
        K.op("pool", lambda e, g=g: e.affine_select(out=WsT[:, g, :], in_=WsT[:, g, :], pattern=[[1, 128]],
                                                    compare_op=ALU.is_ge, fill=0.0, base=0,
                                                    channel_multiplier=-1), reads=WsT.b, writes=WsT.b)

    psr = P.next()
    for g in range(8):
        K.op("pe", lambda e, g=g: e.matmul(psr[:, 0:128], lhsT=C["ones_bf"][:], rhs=WsT[:, g, :], start=True, stop=True),
             reads=C["ones_bf"].b + WsT.b, writes=psr.b)
        for fl in range(2):
            fc = 2 * g + fl
            K.op("dve", lambda e, g=g, fc=fc: e.scalar_tensor_tensor(
                out=bias2[:, fc, :], in0=psr[:, 0:128], scalar=lbc[:, fc:fc + 1], in1=bsB[:, g, :],
                op0=ALU.mult, op1=ALU.add), reads=psr.b + lbc.b + bsB.b, writes=bias2.b)
    w_in_v = w_in.rearrange("(k p) n -> p k n", p=128)
    w_out_v = w_out.rearrange("(k p) n -> p k n", p=128)
    CG_ORDER = (4, 5, 6, 7, 0, 1, 2, 3, 8, 9, 10, 11)
    wseq = [cg for _ in range(4) for cg in CG_ORDER]
    wst = {"issued": 0}

    def issue_w(upto):
        while wst["issued"] < min(upto, len(wseq)):
            i = wst["issued"]; cgi = wseq[i]
            wbi = wbuf[i % NWB]
            K.dma("pool", wbi[:], w_in_v[:, :, cgi * 512:(cgi + 1) * 512], writes=wbi.b)
            wst["issued"] += 1
    issue_w(NWB - 1)
    nload = 0
    for sb in range(4):
        hnT = hnTs[sb % 2]
        if sb == 0:
            norm_transpose_sb(K, C, W, P, h, 0, gB, hnT)
        def emit_ln():
            for j in range(4):
                K.op("dve", lambda e, j=j: e.reduce_sum(out=mv[:, 0:1], in_=st1[:, j, :], axis=AX.X),
                     reads=[st1.b[j]], writes=mv.b)
                K.op("dve", lambda e, j=j: e.reduce_sum(out=mv[:, 1:2], in_=st2[:, j, :], axis=AX.X),
                     reads=[st2.b[j]], writes=mv.b)
                K.op("dve", lambda e: e.tensor_scalar(out=mv[:, 2:4], in0=mv[:, 0:2], scalar1=1.0 / DI, scalar2=None,
                                                      op0=ALU.mult), reads=mv.b, writes=mv.b)
                K.op("dve", lambda e: e.tensor_tensor(out=mv[:, 4:5], in0=mv[:, 2:3], in1=mv[:, 2:3], op=ALU.mult),
                     reads=mv.b, writes=mv.b)
                K.op("dve", lambda e: e.tensor_tensor(out=mv[:, 5:6], in0=mv[:, 3:4], in1=mv[:, 4:5], op=ALU.subtract),
                     reads=mv.b, writes=mv.b)
                K.op("dve", lambda e: e.tensor_scalar(out=mv[:, 6:7], in0=mv[:, 5:6], scalar1=EPS, scalar2=None,
                                                      op0=ALU.add), reads=mv.b, writes=mv.b)
                K.op("act", lambda e: e.sqrt(out=mv[:, 6:7], in_=mv[:, 6:7]), reads=mv.b, writes=mv.b)
                K.op("dve", lambda e: e.reciprocal(out=mv[:, 6:7], in_=mv[:, 6:7]), reads=mv.b, writes=mv.b)
                K.op("dve", lambda e, j=j: e.tensor_scalar(out=vtok[:, j, :], in0=vtok[:, j, :], scalar1=mv[:, 2:3],
                                                           scalar2=mv[:, 6:7], op0=ALU.subtract, op1=ALU.mult),
                     reads=[vtok.b[j]] + mv.b, writes=[vtok.b[j]])
        def emit_spatial():
            nt = 0
            for jp in range(2):
                for g in range(8):
                    ps = P.next()
                    for fl in range(2):
                        for jl in range(2):
                            fc = 2 * g + fl; j = 2 * jp + jl
                            K.op("pe", lambda e, fc=fc, j=j, fl=fl, jl=jl, ps=ps, g=g: e.matmul(
                                ps[:, (fl * 2 + jl) * 128:(fl * 2 + jl + 1) * 128],
                                lhsT=vtok[:, j, fc * 128:(fc + 1) * 128], rhs=WsT[:, g, :], start=True, stop=True),
                                reads=[vtok.b[j]] + WsT.b, writes=ps.b)
                    tt = t1[nt % 2]; nt += 1
                    for fl in range(2):
                        fc = 2 * g + fl
                        K.op("dve", lambda e, ps=ps, tt=tt, fc=fc, fl=fl: e.scalar_tensor_tensor(
                            out=tt[:, fl * 256:(fl + 1) * 256].rearrange("p (a t) -> p a t", a=2),
                            in0=ps[:, fl * 256:(fl + 1) * 256].rearrange("p (a t) -> p a t", a=2),
                            scalar=lgc[:, fc:fc + 1], in1=bias2[:, fc:fc + 1, :].to_broadcast([128, 2, 128]),
                            op0=ALU.mult, op1=ALU.add), reads=ps.b + lgc.b + bias2.b, writes=tt.b)
                    usl = uT[:, 2 * g:2 * g + 2, jp * 256:(jp + 1) * 256]
                    K.op("pool", lambda e, tt=tt, usl=usl: e.tensor_tensor(
                        out=usl, in0=tt[:].rearrange("p (a t) -> p a t", a=2), in1=usl, op=ALU.mult),
                        reads=tt.b + [uT.b[2 * g], uT.b[2 * g + 1]], writes=[uT.b[2 * g], uT.b[2 * g + 1]])
        for cg in CG_ORDER:
            issue_w(nload + NWB)
            wb = wbuf[nload % NWB]; nload += 1
            if cg == 9:
                K.dma("pool", wo_tiles[0][:], w_out_v[:, :, 0:256], writes=wo_tiles[0].b)
            kind = cg // 4
            if kind != 1:
                dstT = uT if kind == 0 else szT
                fn = AF.Gelu_apprx_tanh if kind == 0 else AF.Silu
                for sub in range(4):
                    fc = (cg % 4) * 4 + sub
                    ps = P.next()
                    for k in range(8):
                        K.op("pe", lambda e, k=k, sub=sub, ps=ps, wb=wb, hnT=hnT: e.matmul(
                            ps[:], lhsT=wb[:, k, sub * 128:(sub + 1) * 128], rhs=hnT[:, k, :],
                            start=(k == 0), stop=(k == 7)), reads=wb.b + hnT.b, writes=ps.b)
                    K.op("act", lambda e, fc=fc, ps=ps, dstT=dstT, fn=fn: e.activation(
                        out=dstT[:, fc, :], in_=ps[:], func=fn), reads=ps.b, writes=[dstT.b[fc]])
            else:
                cgv = cg % 4
                for j in range(4):
                    ps = P.next()
                    for k in range(8):
                        K.op("pe", lambda e, k=k, j=j, ps=ps, wb=wb, hnT=hnT: e.matmul(
                            ps[:], lhsT=hnT[:, k, j * 128:(j + 1) * 128], rhs=wb[:, k, :],
                            start=(k == 0), stop=(k == 7)), reads=wb.b + [hnT.b[j]], writes=ps.b)
                    K.op("act", lambda e, j=j, ps=ps, cgv=cgv: e.activation(
                        out=vtok[:, j, cgv * 512:(cgv + 1) * 512], in_=ps[:], func=AF.Gelu_apprx_tanh,
                        accum_out=st1[:, j, cgv:cgv + 1]), reads=ps.b, writes=[vtok.b[j], st1.b[j]])
                    K.op("act", lambda e, j=j, cgv=cgv: e.activation(
                        out=W["junk"][:, 0:512], in_=vtok[:, j, cgv * 512:(cgv + 1) * 512], func=AF.Square,
                        accum_out=st2[:, j, cgv:cgv + 1]), reads=[vtok.b[j]], writes=W["junk"].b + [st2.b[j]])
                if cg == 7:
                    emit_ln()
            if cg == 3:
                emit_spatial()
        for fc in range(16):
            K.op("dve", lambda e, fc=fc: e.tensor_tensor(out=uT[:, fc, :], in0=uT[:, fc, :], in1=szT[:, fc, :], op=ALU.mult),
                 reads=[uT.b[fc], szT.b[fc]], writes=[uT.b[fc]])
        for cgo in range(4):
            if cgo == 1 and sb + 1 < 4:
                norm_transpose_sb(K, C, W, P, h, sb + 1, gB, hnTs[(sb + 1) % 2])
            wo = wo_tiles[cgo % 2]
            if cgo + 1 < 4:
                wn = wo_tiles[(cgo + 1) % 2]
                K.dma("pool", wn[:], w_out_v[:, :, (cgo + 1) * 256:(cgo + 2) * 256], writes=wn.b)
            for j in range(4):
                blk = sb * 4 + j
                ps = P.next()
                for k in range(16):
                    K.op("pe", lambda e, k=k, j=j, ps=ps, wo=wo: e.matmul(
                        ps[:, 0:256], lhsT=uT[:, k, j * 128:(j + 1) * 128], rhs=wo[:, k, :],
                        start=(k == 0), stop=(k == 15)), reads=wo.b + [uT.b[k]], writes=ps.b)
                hs = h[:, blk, cgo * 256:(cgo + 1) * 256]
                K.op("dve", lambda e, hs=hs, ps=ps: e.tensor_tensor(out=hs, in0=hs, in1=ps[:, 0:256], op=ALU.add),
                     reads=ps.b + [h.b[blk]], writes=[h.b[blk]])
    K.barrier()
    K.stack_pop()


import math as _math

PI = _math.pi


def tt(K, eng, out, in0, in1, op, reads, writes):
    return K.op(eng, lambda e: e.tensor_tensor(out=out, in0=in0, in1=in1, op=op), reads=reads, writes=writes)


def ts(K, eng, out, in0, s1, op0, reads, writes, s2=None, op1=None):
    if op1 is None:
        return K.op(eng, lambda e: e.tensor_scalar(out=out, in0=in0, scalar1=s1, scalar2=None, op0=op0),
                    reads=reads, writes=writes)
    return K.op(eng, lambda e: e.tensor_scalar(out=out, in0=in0, scalar1=s1, scalar2=s2, op0=op0, op1=op1),
                reads=reads, writes=writes)


def stt(K, eng, out, in0, scalar, in1, op0, op1, reads, writes):
    return K.op(eng, lambda e: e.scalar_tensor_tensor(out=out, in0=in0, scalar=scalar, in1=in1, op0=op0, op1=op1),
                reads=reads, writes=writes)


def act(K, out, in_, func, reads, writes, **kw):
    return K.op("act", lambda e: e.activation(out=out, in_=in_, func=func, **kw), reads=reads, writes=writes)


def emit_s5_prep(K, C, prm, p, S, cmask, PRE):
    L = contextlib.ExitStack(); K.stack_push(L)
    a_re, a_im, log_step = prm[p + "a_re"], prm[p + "a_im"], prm[p + "log_step"]
    shp = [128, 64]
    def T(name, shape=shp, dt=F32):
        return K.sbuf(name, shape, dt)
    are, aim, lst = PRE["are"], PRE["aim"], PRE["lst"]
    step, dr, th, mag, imag = T("step"), T("dr"), T("th"), T("mag"), T("imag")
    act(K, step[:], lst[:], AF.Exp, lst.b, step.b)
    tt(K, "dve", dr[:], are[:], step[:], ALU.mult, are.b + step.b, dr.b)
    tt(K, "dve", th[:], aim[:], step[:], ALU.mult, aim.b + step.b, th.b)
    act(K, mag[:], dr[:], AF.Exp, dr.b, mag.b)
    act(K, imag[:], dr[:], AF.Exp, dr.b, imag.b, scale=-1.0)
    sn, cs, t1, t2 = T("sn"), T("cs"), T("t1"), T("t2")
    hpi = K.sbuf("hpi", [128, 1], F32)
    K.op("dve", lambda e: e.memset(hpi[:], PI / 2), writes=hpi.b)
    act(K, sn[:], th[:], AF.Sin, th.b, sn.b, scale=1.0 / 16)
    act(K, cs[:], th[:], AF.Sin, th.b + hpi.b, cs.b, scale=1.0 / 16, bias=hpi[:, 0:1])
    for _ in range(4):
        tt(K, "dve", t1[:], cs[:], cs[:], ALU.mult, cs.b, t1.b)
        tt(K, "dve", t2[:], sn[:], sn[:], ALU.mult, sn.b, t2.b)
        stt(K, "dve", sn[:], cs[:], 2.0, sn[:], ALU.mult, ALU.mult, cs.b + sn.b, sn.b)
        tt(K, "dve", cs[:], t1[:], t2[:], ALU.subtract, t1.b + t2.b, cs.b)
    zr, zi, wr, wi = T("zr"), T("zi"), T("wr"), T("wi")
    tt(K, "dve", zr[:], mag[:], cs[:], ALU.mult, mag.b + cs.b, zr.b)
    tt(K, "dve", zi[:], mag[:], sn[:], ALU.mult, mag.b + sn.b, zi.b)
    tt(K, "dve", wr[:], imag[:], cs[:], ALU.mult, imag.b + cs.b, wr.b)
    stt(K, "dve", wi[:], imag[:], -1.0, sn[:], ALU.mult, ALU.mult, imag.b + sn.b, wi.b)
    nr, den, kr, ki, tmp = T("nr"), T("den"), T("kr"), T("ki"), T("tmpk")
    ts(K, "dve", nr[:], zr[:], -1.0, ALU.add, zr.b, nr.b)
    tt(K, "dve", den[:], are[:], are[:], ALU.mult, are.b, den.b)
    tt(K, "dve", tmp[:], aim[:], aim[:], ALU.mult, aim.b, tmp.b)
    tt(K, "dve", den[:], den[:], tmp[:], ALU.add, den.b + tmp.b, den.b)
    K.op("dve", lambda e: e.reciprocal(out=den[:], in_=den[:]), reads=den.b, writes=den.b)
    tt(K, "dve", kr[:], nr[:], are[:], ALU.mult, nr.b + are.b, kr.b)
    tt(K, "dve", tmp[:], zi[:], aim[:], ALU.mult, zi.b + aim.b, tmp.b)
    tt(K, "dve", kr[:], kr[:], tmp[:], ALU.add, kr.b + tmp.b, kr.b)
    tt(K, "dve", kr[:], kr[:], den[:], ALU.mult, kr.b + den.b, kr.b)
    tt(K, "dve", ki[:], zi[:], are[:], ALU.mult, zi.b + are.b, ki.b)
    tt(K, "dve", tmp[:], nr[:], aim[:], ALU.mult, nr.b + aim.b, tmp.b)
    tt(K, "dve", ki[:], ki[:], tmp[:], ALU.subtract, ki.b + tmp.b, ki.b)
    tt(K, "dve", ki[:], ki[:], den[:], ALU.mult, ki.b + den.b, ki.b)

    tp = K.psum("tp5", [128, 1024], BF16)
    L2 = contextlib.ExitStack(); K.stack_push(L2)
    bnr = K.sbuf("bnr", [128, 64, 16], F32); bni = K.sbuf("bni", [128, 64, 16], F32)
    K.dma("sp", bnr[:], prm[p + "b_re"].rearrange("(j gl) p h -> (gl p) j h", gl=2), writes=bnr.b)
    K.dma("sp", bni[:], prm[p + "b_im"].rearrange("(j gl) p h -> (gl p) j h", gl=2), writes=bni.b)
    bbr = K.sbuf("bbr", [128, 64, 16], F32); bbi = K.sbuf("bbi", [128, 64, 16], F32)
    btmp = K.sbuf("btmp", [128, 64, 16], F32)
    krb = kr[:].unsqueeze(2).to_broadcast([128, 64, 16]); kib = ki[:].unsqueeze(2).to_broadcast([128, 64, 16])
    tt(K, "dve", bbr[:], bnr[:], krb, ALU.mult, bnr.b + kr.b, bbr.b)
    tt(K, "dve", btmp[:], bni[:], kib, ALU.mult, bni.b + ki.b, btmp.b)
    tt(K, "dve", bbr[:], bbr[:], btmp[:], ALU.subtract, bbr.b + btmp.b, bbr.b)
    tt(K, "dve", bbi[:], bni[:], krb, ALU.mult, bni.b + kr.b, bbi.b)
    tt(K, "dve", btmp[:], bnr[:], kib, ALU.mult, bnr.b + ki.b, btmp.b)
    tt(K, "dve", bbi[:], bbi[:], btmp[:], ALU.add, bbi.b + btmp.b, bbi.b)
    pin = K.sbuf("pin", [128, 64, 128], BF16)
    stg = [K.sbuf(f"stg{i}", [128, 8, 128], BF16) for i in range(2)]
    ns = 0
    for ri, src in enumerate((bbr, bbi)):
        K.op("pool", lambda e: e.memset(pin[:], 0.0), writes=pin.b)
        for q in range(4):
            for gl in range(2):
                K.op("dve", lambda e, q=q, gl=gl, src=src: e.tensor_copy(
                    out=pin[gl * 64:(gl + 1) * 64, q::4, 32 * q + 16 * gl:32 * q + 16 * gl + 16],
                    in_=src[gl * 64:(gl + 1) * 64, q::4, :]), reads=src.b, writes=pin.b)
        for i8 in range(8):
            for jl in range(8):
                j = i8 * 8 + jl
                K.op("pe", lambda e, j=j, jl=jl: e.transpose(tp[:, jl * 128:(jl + 1) * 128], pin[:, j, :],
                                                             C["ident"][:]),
                     reads=pin.b + C["ident"].b, writes=tp.b)
            sg = stg[ns % 2]; ns += 1
            K.op("dve", lambda e, sg=sg: e.tensor_copy(out=sg[:].rearrange("r j c -> r (j c)"), in_=tp[:]),
                 reads=tp.b, writes=sg.b)
            K.dma("sp", S["bpad"][i8 * 8:(i8 + 1) * 8, ri].rearrange("j r c -> r j c"), sg[:], reads=sg.b,
                  writes=[S["bpad_b"]])
    K.barrier(); K.stack_pop()

    L3 = contextlib.ExitStack(); K.stack_push(L3)
    cin = K.sbuf("cin", [128, 16, 128], BF16)
    cn = K.sbuf("cn", [128, 16, 64], F32)
    for ri, (nm, sgn, dst) in enumerate((("c_re", 1.0, S["CTr"]), ("c_im", -1.0, S["CTn"]), ("c_re", -1.0, S["CTrn"]))):
        K.dma("sp", cn[:], prm[p + nm].rearrange("(jj q gl) h c -> (q gl h) jj c", q=4, gl=2), writes=cn.b)
        for gl in range(2):
            ts(K, "dve", cin[:, :, gl * 64:(gl + 1) * 64], cn[:], cmask[:, gl:gl + 1], ALU.mult,
               cn.b + cmask.b, cin.b, s2=sgn, op1=ALU.mult)
        for i8 in range(2):
            for jl in range(8):
                jj = i8 * 8 + jl
                K.op("pe", lambda e, jj=jj, jl=jl: e.transpose(tp[:, jl * 128:(jl + 1) * 128], cin[:, jj, :],
                                                               C["ident"][:]),
                     reads=cin.b + C["ident"].b, writes=tp.b)
            K.op("dve", lambda e, i8=i8, dst=dst: e.tensor_copy(
                out=dst[:, i8 * 8:(i8 + 1) * 8, :].rearrange("r j c -> r (j c)"), in_=tp[:]),
                reads=tp.b, writes=dst.b)
    K.barrier(); K.stack_pop()

    L4 = contextlib.ExitStack(); K.stack_push(L4)
    NB = 2
    Lp = K.sbuf("Lp", [128, 64, 4, 32], F32)
    Hp = K.sbuf("Hp", [128, 64, 4, 16], F32)
    cur = K.sbuf("pcur", [128, 64, 4], F32)
    cur2 = K.sbuf("pcur2", [128, 64, 4], F32)
    ctmp = K.sbuf("pctmp", [128, 64, 4], F32)
    ptm = [K.sbuf(f"pptm{i}", [128, 64, 16], F32) for i in range(2)]
    for ti in range(1):
        eng = "dve" if ti == 0 else "pool"
        sr, si = (zr, zi) if ti == 0 else (wr, wi)
        cr_ = cur[:, :, 2 * ti:2 * ti + 1]; ci_ = cur[:, :, 2 * ti + 1:2 * ti + 2]
        K.op(eng, lambda e, cr_=cr_, sr=sr: e.tensor_copy(out=cr_, in_=sr[:].unsqueeze(2)), reads=sr.b, writes=cur.b)
        K.op(eng, lambda e, ci_=ci_, si=si: e.tensor_copy(out=ci_, in_=si[:].unsqueeze(2)), reads=si.b, writes=cur.b)
        pt = ptm[ti]
        for (Tt, nsteps) in ((Lp, 5), (Hp, 4)):
            Tr = Tt[:, :, 2 * ti, :]; Ti = Tt[:, :, 2 * ti + 1, :]
            K.op(eng, lambda e, Tr=Tr: e.memset(Tr[:, :, 0:1], 1.0), writes=Tt.b)
            K.op(eng, lambda e, Ti=Ti: e.memset(Ti[:, :, 0:1], 0.0), writes=Tt.b)
            for k in range(nsteps):
                n = 1 << k
                crb = cr_.to_broadcast([128, 64, n]); cib = ci_.to_broadcast([128, 64, n])
                A_r = Tr[:, :, 0:n]; A_i = Ti[:, :, 0:n]; O_r = Tr[:, :, n:2 * n]; O_i = Ti[:, :, n:2 * n]
                tm = pt[:, :, 0:n]
                tt(K, eng, tm, A_i, cib, ALU.mult, Tt.b + cur.b, pt.b)
                tt(K, eng, O_r, A_r, crb, ALU.mult, Tt.b + cur.b, Tt.b)
                tt(K, eng, O_r, O_r, tm, ALU.subtract, Tt.b + pt.b, Tt.b)
                tt(K, eng, tm, A_i, crb, ALU.mult, Tt.b + cur.b, pt.b)
                tt(K, eng, O_i, A_r, cib, ALU.mult, Tt.b + cur.b, Tt.b)
                tt(K, eng, O_i, O_i, tm, ALU.add, Tt.b + pt.b, Tt.b)
                c2r = cur2[:, :, 2 * ti:2 * ti + 1]; c2i = cur2[:, :, 2 * ti + 1:2 * ti + 2]
                t_a = ctmp[:, :, 2 * ti:2 * ti + 1]; t_b = ctmp[:, :, 2 * ti + 1:2 * ti + 2]
                tt(K, eng, t_a, cr_, cr_, ALU.mult, cur.b, ctmp.b)
                tt(K, eng, t_b, ci_, ci_, ALU.mult, cur.b, ctmp.b)
                tt(K, eng, c2r, t_a, t_b, ALU.subtract, ctmp.b, cur2.b)
                tt(K, eng, c2i, cr_, ci_, ALU.mult, cur.b, cur2.b)
                tt(K, eng, c2i, c2i, c2i, ALU.add, cur2.b, cur2.b)
                K.op(eng, lambda e, cr_=cr_, c2r=c2r: e.tensor_copy(out=cr_, in_=c2r), reads=cur2.b, writes=cur.b)
                K.op(eng, lambda e, ci_=ci_, c2i=c2i: e.tensor_copy(out=ci_, in_=c2i), reads=cur2.b, writes=cur.b)
        if ti == 0:
            K.op(eng, lambda e, cr_=cr_: e.tensor_copy(out=S["L5"][:, :, 0:1], in_=cr_), reads=cur.b, writes=S["L5"].b)
            K.op(eng, lambda e, ci_=ci_: e.tensor_copy(out=S["L5"][:, :, 1:2], in_=ci_), reads=cur.b, writes=S["L5"].b)
    bv = K.sbuf("pbv", [128, 32], F32); on32 = K.sbuf("pon32", [128, 32], F32)
    K.op("dve", lambda e: e.memset(on32[:], 1.0), writes=on32.b)
    K.op("dve", lambda e: e.tensor_tensor_scan(out=bv[:], data0=on32[:], data1=on32[:], initial=-1.0,
                                               op0=ALU.mult, op1=ALU.add), reads=on32.b, writes=bv.b)
    gL = K.sbuf("pgL", [128, 64, 32], F32); gH = K.sbuf("pgH", [128, 64, 16], F32); gHn = K.sbuf("pgHn", [128, 64, 16], F32)
    tt(K, "dve", gL[:], dr[:].unsqueeze(2).to_broadcast([128, 64, 32]), bv[:].unsqueeze(1).to_broadcast([128, 64, 32]),
       ALU.mult, dr.b + bv.b, gL.b)
    tt(K, "dve", gH[:], dr[:].unsqueeze(2).to_broadcast([128, 64, 16]), bv[:, 0:16].unsqueeze(1).to_broadcast([128, 64, 16]),
       ALU.mult, dr.b + bv.b, gH.b)
    act(K, gL[:], gL[:], AF.Exp, gL.b, gL.b, scale=-2.0)
    act(K, gH[:], gH[:], AF.Exp, gH.b, gH.b, scale=-64.0)
    ts(K, "dve", gHn[:], gH[:], -1.0, ALU.mult, gH.b, gHn.b)
    tab = [K.sbuf(f"ptab{i}", [128, NB, 4, 512], F32) for i in range(2)]
    otm = [K.sbuf(f"potm{i}", [128, NB, 512], F32) for i in range(2)]
    for bi in range(64 // NB):
        tb = tab[bi % 2]
        j0 = bi * NB
        shp4 = [128, NB, 16, 32]
        om = otm[0]
        Hr = Hp[:, j0:j0 + NB, 0, :].unsqueeze(3).to_broadcast(shp4)
        Hi = Hp[:, j0:j0 + NB, 1, :].unsqueeze(3).to_broadcast(shp4)
        Lr = Lp[:, j0:j0 + NB, 0, :].unsqueeze(2).to_broadcast(shp4)
        Li = Lp[:, j0:j0 + NB, 1, :].unsqueeze(2).to_broadcast(shp4)
        Tr = tb[:, :, 0, :].rearrange("r j (a b) -> r j a b", a=16)
        Ti = tb[:, :, 1, :].rearrange("r j (a b) -> r j a b", a=16)
        o4 = om[:].rearrange("r j (a b) -> r j a b", a=16)
        rd = Hp.b + Lp.b
        tt(K, "dve", o4, Hi, Li, ALU.mult, rd, om.b)
        tt(K, "dve", Tr, Hr, Lr, ALU.mult, rd, tb.b)
        tt(K, "dve", Tr, Tr, o4, ALU.subtract, tb.b + om.b, tb.b)
        tt(K, "dve", o4, Hi, Lr, ALU.mult, rd, om.b)
        tt(K, "dve", Ti, Hr, Li, ALU.mult, rd, tb.b)
        tt(K, "dve", Ti, Ti, o4, ALU.add, tb.b + om.b, tb.b)
        omp = otm[1]
        g4 = omp[:].rearrange("r j (a b) -> r j a b", a=16)
        gHb = gH[:, j0:j0 + NB, :].unsqueeze(3).to_broadcast(shp4)
        gHnb = gHn[:, j0:j0 + NB, :].unsqueeze(3).to_broadcast(shp4)
        gLb = gL[:, j0:j0 + NB, :].unsqueeze(2).to_broadcast(shp4)
        tt(K, "pool", g4, gHb, gLb, ALU.mult, gH.b + gL.b, omp.b)
        tt(K, "pool", tb[:, :, 2, :], tb[:, :, 0, :], omp[:], ALU.mult, tb.b + omp.b, tb.b)
        tt(K, "pool", g4, gHnb, gLb, ALU.mult, gHn.b + gL.b, omp.b)
        tt(K, "pool", tb[:, :, 3, :], tb[:, :, 1, :], omp[:], ALU.mult, tb.b + omp.b, tb.b)
        K.dma("sp", S["tabs"][j0:j0 + NB].rearrange("j t r c -> r j t c"), tb[:], reads=tb.b,
              writes=[S["tabs_b"]])
    K.barrier(); K.stack_pop()
    K.barrier(); K.stack_pop()


def emit_s5(K, C, h, prm, p, kflag, cmask, PRE):
    nc = K.nc
    Lall = contextlib.ExitStack(); K.stack_push(Lall)
    gB = K.sbuf("gB5", [128, D], F32)
    bcast_load(K, gB, prm[p + "norm_g"], D)
    dsk = PRE["dsk"]; bgl = PRE["bgl"]
    S = {}
    S["tabs"] = K.dram("s5_tabs", [64, 4, 128, 512], F32, "Internal")
    S["bpad"] = K.dram("s5_bpad", [64, 2, 128, 128], BF16, "Internal")
    S["tabs_b"] = Buf("tabs"); S["bpad_b"] = Buf("bpad")
    S["CTr"] = K.sbuf("CTr", [128, 16, 128], BF16)
    S["CTn"] = K.sbuf("CTn", [128, 16, 128], BF16)
    S["CTrn"] = K.sbuf("CTrn", [128, 16, 128], BF16)
    S["L5"] = K.sbuf("L5", [128, 64, 2], F32)
    emit_s5_prep(K, C, prm, p, S, cmask, PRE)
    uT = K.sbuf("uT5", [128, 16, NTOK], BF16, nb=64)

    w_in_v = prm[p + "w_in"].rearrange("(k r) n -> r k n", r=128)
    w_glu_v = prm[p + "w_glu"].rearrange("(k r) n -> r k n", r=128)
    w_out_v = prm[p + "w_out"].rearrange("(k r) n -> r k n", r=128)

    def mkW():
        W = {}
        W["junk"] = K.sbuf("junk5", [128, 1024], F32)
        W["ss"] = K.sbuf("ss5", [128, 8], F32)
        W["rstd"] = K.sbuf("rstd5", [128, 8], F32)
        W["hn"] = K.sbuf("hn5", [128, D], BF16)
        W["tp"] = K.psum("tpA", [128, 1024], BF16)
        return W

    LA = contextlib.ExitStack(); K.stack_push(LA)
    W = mkW()
    P = PsumPool(K, 6)
    hnTs = [K.sbuf(f"hnT5_{i}", [128, 8, 512], BF16, nb=4) for i in range(2)]
    NWA = 4
    wbuf = [K.sbuf(f"wbA{i}", [128, 8, 512], BF16) for i in range(NWA)]
    wstA = {"issued": 0}

    def issue_a(upto):
        while wstA["issued"] < min(upto, 16):
            i = wstA["issued"]; cgi = i % 4
            K.dma("pool", wbuf[i % NWA][:], w_in_v[:, :, cgi * 512:(cgi + 1) * 512], writes=wbuf[i % NWA].b)
            wstA["issued"] += 1
    issue_a(NWA - 1)
    nl = 0
    for sb in range(4):
        hnT = hnTs[sb % 2]
        if sb == 0:
            norm_transpose_sb(K, C, W, P, h, 0, gB, hnT)
        for cg in range(4):
            if cg == 2 and sb + 1 < 4:
                norm_transpose_sb(K, C, W, P, h, sb + 1, gB, hnTs[(sb + 1) % 2])
            issue_a(nl + NWA)
            wb = wbuf[nl % NWA]; nl += 1
            for sub in range(4):
                fc = cg * 4 + sub
                ps = P.next()
                for k in range(8):
                    K.op("pe", lambda e, k=k, sub=sub, ps=ps, wb=wb, hnT=hnT: e.matmul(
                        ps[:], lhsT=wb[:, k, sub * 128:(sub + 1) * 128], rhs=hnT[:, k, :],
                        start=(k == 0), stop=(k == 7)), reads=wb.b + hnT.b, writes=ps.b)
                K.op("act", lambda e, fc=fc, ps=ps, sb=sb: e.copy(out=uT[:, fc, sb * 512:(sb + 1) * 512], in_=ps[:]),
                     reads=ps.b, writes=[uT.b[fc * 4 + sb]])
    K.barrier(); K.stack_pop()

    LB = contextlib.ExitStack(); K.stack_push(LB)
    psBU = [K.psum(f"psBU{i}", [128, 512], F32) for i in range(4)]
    psY = [K.psum(f"psY{i}", [128, 512], F32) for i in range(4)]
    ones = K.sbuf("ones5", [128, 512], F32)
    K.op("dve", lambda e: e.memset(ones[:], 1.0), writes=ones.b)
    tabt = [K.sbuf(f"tabt{i}", [128, 4, 512], F32) for i in range(2)]
    bpt = [K.sbuf(f"bpt{i}", [128, 2, 128], BF16) for i in range(2)]
    cst = K.sbuf("cst", [128, 64, 5, 2], F32, nb=64)
    NW = 3
    wk = [[K.sbuf(f"wk{s}_{i}", [128, 512], F32) for i in range(4)] for s in range(NW)]
    pk = [[K.sbuf(f"pk{s}_{i}", [128, 512], BF16) for i in range(4)] for s in range(2)]
    tny = K.sbuf("tny", [128, 4], F32)
    nl5i = K.sbuf("nl5i", [128, 64], F32)
    ts(K, "dve", nl5i[:].unsqueeze(2), S["L5"][:, :, 1:2], -1.0, ALU.mult, S["L5"].b, nl5i.b)
    LPre = contextlib.ExitStack(); K.stack_push(LPre)
    fpre = K.sbuf("fpre", [128, 64, 2], F32)
    accA = K.sbuf("accA5", [128, 64, 4, 4], F32, nb=64 * 12)
    itp = 0
    for j in range(64):
        jj = j // 4
        tb = tabt[j % 2]; bp = bpt[j % 2]
        K.dma("sp", tb[:, 2:4, :], S["tabs"][j, 2:4].rearrange("t r c -> r t c"), reads=[S["tabs_b"]], writes=tb.b)
        K.dma("sp", bp[:], S["bpad"][j].rearrange("t r c -> r t c"), reads=[S["bpad_b"]], writes=bp.b)
        Mr, Mi = tb[:, 2, :], tb[:, 3, :]
        T2 = wk[j % 2][1]; T3 = wk[j % 2][2]
        tt(K, "pool", T2[:], Mi, Mr, ALU.subtract, tb.b, T2.b)
        tt(K, "pool", T3[:], Mr, Mi, ALU.add, tb.b, T3.b)
        for blk in range(4):
            pR = psBU[(2 * itp) % 4]; pI = psBU[(2 * itp + 1) % 4]; pS = psY[itp % 4]
            itp += 1
            ub = [uT.b[jj * 4 + blk]]
            usl = uT[:, jj, blk * 512:(blk + 1) * 512]
            K.op("pe", lambda e, pR=pR, bp=bp, usl=usl: e.matmul(pR[:], lhsT=bp[:, 0, :], rhs=usl, start=True, stop=True),
                 reads=bp.b + ub, writes=pR.b)
            K.op("pe", lambda e, pI=pI, bp=bp, usl=usl: e.matmul(pI[:], lhsT=bp[:, 1, :], rhs=usl, start=True, stop=True),
                 reads=bp.b + ub, writes=pI.b)
            K.op("pe", lambda e, pS=pS, bp=bp, usl=usl: e.matmul(pS[:], lhsT=bp[:, 0, :], rhs=usl, start=True, stop=False),
                 reads=bp.b + ub, writes=pS.b)
            K.op("pe", lambda e, pS=pS, bp=bp, usl=usl: e.matmul(pS[:], lhsT=bp[:, 1, :], rhs=usl, start=False, stop=True),
                 reads=bp.b + ub, writes=pS.b)
            for ai, (pp, mm, mb) in enumerate(((pS, Mr, tb.b), (pR, T2[:], T2.b), (pI, T3[:], T3.b))):
                jk = wk[2][ai]
                K.op("dve", lambda e, pp=pp, mm=mm, ai=ai, jk=jk, j=j, blk=blk: e.scalar_tensor_tensor(
                    out=jk[:], in0=pp[:], scalar=1.0, in1=mm, op0=ALU.mult, op1=ALU.mult, accum_out=accA[:, j, blk, ai:ai + 1]),
                    reads=pp.b + mb, writes=jk.b + [accA.b[j * 12 + blk * 3 + ai]])
    vv = K.sbuf("vv5", [128, 64, 2], F32); cc = K.sbuf("cc5", [128, 64, 2], F32); t4 = K.sbuf("t45", [128, 64, 4], F32)
    l5r_all = S["L5"][:, :, 0:1]; l5i_all = S["L5"][:, :, 1:2]
    for blk in range(4):
        tt(K, "dve", vv[:, :, 0:1], accA[:, :, blk, 0:1], accA[:, :, blk, 2:3], ALU.subtract, accA.b, vv.b)
        tt(K, "dve", vv[:, :, 1:2], accA[:, :, blk, 0:1], accA[:, :, blk, 1:2], ALU.add, accA.b, vv.b)
        if blk > 0:
            tt(K, "dve", vv[:], vv[:], cc[:], ALU.add, vv.b + cc.b, vv.b)
        dst = cc if blk < 3 else fpre
        tt(K, "dve", t4[:, :, 0:1], vv[:, :, 0:1], l5r_all, ALU.mult, vv.b + S["L5"].b, t4.b)
        tt(K, "dve", t4[:, :, 1:2], vv[:, :, 1:2], l5i_all, ALU.mult, vv.b + S["L5"].b, t4.b)
        tt(K, "dve", t4[:, :, 2:3], vv[:, :, 0:1], l5i_all, ALU.mult, vv.b + S["L5"].b, t4.b)
        tt(K, "dve", t4[:, :, 3:4], vv[:, :, 1:2], l5r_all, ALU.mult, vv.b + S["L5"].b, t4.b)
        tt(K, "dve", dst[:, :, 0:1], t4[:, :, 0:1], t4[:, :, 1:2], ALU.subtract, t4.b, dst.b)
        tt(K, "dve", dst[:, :, 1:2], t4[:, :, 2:3], t4[:, :, 3:4], ALU.add, t4.b, dst.b)
    cxi = K.dram("s5_cxi", [128, 128], F32, "Internal")
    cxo = K.dram("s5_cxo", [256, 128], F32, "Internal")
    bxi = Buf("cxi"); bxo = Buf("cxo")
    K.dma("sp", cxi, fpre[:].rearrange("r j t -> r (j t)"), reads=fpre.b, writes=[bxi])
    K.collective("AllGather", [cxi], [cxo], [bxi], [bxo])
    cext = K.sbuf("cext", [128, 64, 2], F32)
    K.dma("sp", cext[:].rearrange("r j t -> r (j t)"), cxo[0:128, :], reads=[bxo], writes=cext.b)
    ts(K, "dve", cst[:, :, 0, :], cext[:], kflag[:, 0:1], ALU.mult, cext.b + kflag.b, cst.b)
    K.barrier(); K.stack_pop()
    ytmp = [K.sbuf(f"ytmp{i}", [128, 512], F32) for i in range(1)]
    NIT = 256
    ctx = {}

    def S1(t):
        j, blk = t // 4, t % 4
        jj = j // 4
        tb = tabt[j % 2]; bp = bpt[j % 2]

        def load_pair(jn):
            tbn = tabt[jn % 2]; bpn = bpt[jn % 2]
            K.dma("sp", tbn[:], S["tabs"][jn].rearrange("t r c -> r t c"), reads=[S["tabs_b"]], writes=tbn.b)
            K.dma("sp", bpn[:], S["bpad"][jn].rearrange("t r c -> r t c"), reads=[S["bpad_b"]], writes=bpn.b)
        if t == 0:
            load_pair(0)
        if blk == 1 and j + 1 < 64:
            load_pair(j + 1)
        a, b, c, d = wk[t % NW]
        pR = psBU[(2 * t) % 4]; pI = psBU[(2 * t + 1) % 4]
        Mr, Mi = tb[:, 2, :], tb[:, 3, :]
        ub = [uT.b[jj * 4 + blk]]
        usl = uT[:, jj, blk * 512:(blk + 1) * 512]
        K.op("pe", lambda e: e.matmul(pR[:], lhsT=bp[:, 0, :], rhs=usl, start=True, stop=True), reads=bp.b + ub, writes=pR.b)
        K.op("pe", lambda e: e.matmul(pI[:], lhsT=bp[:, 1, :], rhs=usl, start=True, stop=True), reads=bp.b + ub, writes=pI.b)
        tt(K, "dve", a[:], pR[:], Mr, ALU.mult, pR.b + tb.b, a.b)
        tt(K, "dve", b[:], pI[:], Mi, ALU.mult, pI.b + tb.b, b.b)
        tt(K, "dve", c[:], pR[:], Mi, ALU.mult, pR.b + tb.b, c.b)
        tt(K, "dve", d[:], pI[:], Mr, ALU.mult, pI.b + tb.b, d.b)

    def S2(t):
        a, b, c, d = wk[t % NW]
        tt(K, "pool", a[:], a[:], b[:], ALU.subtract, a.b + b.b, a.b)
        tt(K, "pool", c[:], c[:], d[:], ALU.add, c.b + d.b, c.b)

    def S3(t):
        j, blk = t // 4, t % 4
        a, b, c, d = wk[t % NW]
        cb = [cst.b[j]]
        K.op("dve", lambda e: e.tensor_tensor_scan(out=b[:], data0=ones[:], data1=a[:], initial=cst[:, j, blk, 0:1],
                                                   op0=ALU.mult, op1=ALU.add), reads=ones.b + a.b + cb, writes=b.b)
        K.op("dve", lambda e: e.tensor_tensor_scan(out=d[:], data0=ones[:], data1=c[:], initial=cst[:, j, blk, 1:2],
                                                   op0=ALU.mult, op1=ALU.add), reads=ones.b + c.b + cb, writes=d.b)
        l5r = S["L5"][:, j, 0:1]; l5i = S["L5"][:, j, 1:2]
        if blk < 3:
            ts(K, "dve", tny[:, 0:1], d[:, 511:512], nl5i[:, j:j + 1], ALU.mult, d.b + nl5i.b, tny.b)
            stt(K, "dve", cst[:, j, blk + 1, 0:1], b[:, 511:512], l5r, tny[:, 0:1], ALU.mult, ALU.add,
                b.b + tny.b + S["L5"].b, cb)
            ts(K, "dve", tny[:, 1:2], d[:, 511:512], l5r, ALU.mult, d.b + S["L5"].b, tny.b)
            stt(K, "dve", cst[:, j, blk + 1, 1:2], b[:, 511:512], l5i, tny[:, 1:2], ALU.mult, ALU.add,
                b.b + tny.b + S["L5"].b, cb)

    def S4(t):
        j = t // 4
        tb = tabt[j % 2]
        Dr, Di = tb[:, 0, :], tb[:, 1, :]
        a, b, c, d = wk[t % NW]; p0, p1, p2, p3 = pk[t % 2]
        tt(K, "pool", p0[:], b[:], Dr, ALU.mult, b.b + tb.b, p0.b)
        tt(K, "pool", p1[:], d[:], Di, ALU.mult, d.b + tb.b, p1.b)
        tt(K, "pool", p2[:], b[:], Di, ALU.mult, b.b + tb.b, p2.b)
        tt(K, "dve", p3[:], d[:], Dr, ALU.mult, d.b + tb.b, p3.b)

    def S5(t):
        j, blk = t // 4, t % 4
        jj, q = j // 4, j % 4
        p0, p1, p2, p3 = pk[t % 2]
        py = psY[blk]
        sl = slice(32 * q, 32 * q + 32)
        for i, (wt_, pp) in enumerate(((S["CTr"], p0), (S["CTrn"], p1), (S["CTn"], p2), (S["CTn"], p3))):
            K.op("pe", lambda e, wt_=wt_, pp=pp, i=i: e.matmul(py[sl, :], lhsT=wt_[:, jj, sl], rhs=pp[:],
                                                          start=(i == 0), stop=(i == 3), tile_position=(0, 32 * q)),
                 reads=wt_.b + pp.b, writes=py.b)
        if q == 3:
            yt = ytmp[0]
            usl = uT[:, jj, blk * 512:(blk + 1) * 512]; ub = [uT.b[jj * 4 + blk]]
            stt(K, "dve", yt[:], usl, dsk[:, jj:jj + 1], py[:], ALU.mult, ALU.add, ub + dsk.b + py.b, yt.b)
            act(K, usl, yt[:], AF.Gelu_apprx_tanh, yt.b, ub)

    for t in range(NIT + 2):
        if t < NIT:
            S1(t); S2(t)
        if 0 <= t - 1 < NIT:
            S3(t - 1); S4(t - 1)
        if 0 <= t - 2 < NIT:
            S5(t - 2)
    K.barrier(); K.stack_pop()

    LC = contextlib.ExitStack(); K.stack_push(LC)
    W = mkW()
    P = PsumPool(K, 6)
    hnT = K.sbuf("hnT5c", [128, 8, 512], BF16, nb=4)
    wbs = [K.sbuf(f"wbC{i}", [128, 4096], BF16) for i in range(3)]
    szT = K.sbuf("szT5", [128, 16, 512], BF16, nb=16)
    gT = szT
    gtmp = [K.sbuf(f"gtmp{i}", [128, 512], F32) for i in range(2)]
    nl = 0; nt = 0
    for sb in range(4):
        norm_transpose_sb(K, C, W, P, h, sb, gB, hnT)
        for cg in range(4):
            wt = wbs[nl % 3]; nl += 1
            wb = wt[:].rearrange("r (k n) -> r k n", k=8)
            K.dma("pool", wb, w_in_v[:, :, 2048 + cg * 512:2048 + (cg + 1) * 512], writes=wt.b)
            for sub in range(4):
                fc = cg * 4 + sub
                ps = P.next()
                for k in range(8):
                    K.op("pe", lambda e, k=k, sub=sub, ps=ps, wb=wb: e.matmul(
                        ps[:], lhsT=wb[:, k, sub * 128:(sub + 1) * 128], rhs=hnT[:, k, :],
                        start=(k == 0), stop=(k == 7)), reads=wt.b + hnT.b, writes=ps.b)
                act(K, szT[:, fc, :], ps[:], AF.Silu, ps.b, [szT.b[fc]])
        for cg in range(8):
            wt = wbs[nl % 3]; nl += 1
            wg = wt[:].rearrange("r (k n) -> r k n", k=16)
            K.dma("pool", wg, w_glu_v[:, :, cg * 256:(cg + 1) * 256], writes=wt.b)
            for sub in range(2):
                fc = cg * 2 + sub
                ps = P.next()
                for k in range(16):
                    K.op("pe", lambda e, k=k, sub=sub, ps=ps, wg=wg, sb=sb: e.matmul(
                        ps[:], lhsT=wg[:, k, sub * 128:(sub + 1) * 128], rhs=uT[:, k, sb * 512:(sb + 1) * 512],
                        start=(k == 0), stop=(k == 15)), reads=wt.b + [uT.b[k * 4 + sb]], writes=ps.b)
                gt = gtmp[nt % 2]; nt += 1
                act(K, gt[:], ps[:], AF.Sigmoid, ps.b + bgl.b, gt.b, bias=bgl[:, fc:fc + 1])
                tt(K, "dve", gt[:], gt[:], uT[:, fc, sb * 512:(sb + 1) * 512], ALU.mult,
                   gt.b + [uT.b[fc * 4 + sb]], gt.b)
                tt(K, "dve", szT[:, fc, :], gt[:], szT[:, fc, :], ALU.mult, gt.b + [szT.b[fc]], [szT.b[fc]])
        for cgo in range(4):
            wt = wbs[nl % 3]; nl += 1
            wo = wt[:].rearrange("r (k n) -> r k n", k=16)
            K.dma("pool", wo, w_out_v[:, :, cgo * 256:(cgo + 1) * 256], writes=wt.b)
            for jb in range(4):
                blk = sb * 4 + jb
                ps = P.next()
                for k in range(16):
                    K.op("pe", lambda e, k=k, jb=jb, ps=ps, wo=wo: e.matmul(
                        ps[:, 0:256], lhsT=gT[:, k, jb * 128:(jb + 1) * 128], rhs=wo[:, k, :],
                        start=(k == 0), stop=(k == 15)), reads=wt.b + [gT.b[k]], writes=ps.b)
                hs = h[:, blk, cgo * 256:(cgo + 1) * 256]
                tt(K, "dve", hs, hs, ps[:, 0:256], ALU.add, ps.b + [h.b[blk]], [h.b[blk]])
    K.barrier(); K.stack_pop()
    K.barrier(); K.stack_pop()


MLA_SCALE_F = 192 ** -0.5


def sincos_tables(K, ang, sin_out, cos_out, tmp, npi, ki, hpi):
    a_ap, a_b = ang; s_ap, s_b = sin_out; c_ap, c_b = cos_out; t_ap, t_b = tmp; k_ap, k_b = ki
    C1 = 6.28125; C2 = 2 * PI - C1
    np_ = s_ap.shape[0]
    ts(K, "dve", k_ap, a_ap, 1.0 / (2 * PI), ALU.mult, a_b, k_b)
    K.op("dve", lambda e: e.tensor_copy(out=t_ap, in_=k_ap), reads=k_b, writes=t_b)
    stt(K, "dve", a_ap, t_ap, -C1, a_ap, ALU.mult, ALU.add, t_b + a_b, a_b)
    stt(K, "dve", a_ap, t_ap, -C2, a_ap, ALU.mult, ALU.add, t_b + a_b, a_b)
    ts(K, "dve", a_ap, a_ap, 3.1415925, ALU.min, a_b, a_b, s2=-3.1415925, op1=ALU.max)
    act(K, s_ap, a_ap, AF.Sin, a_b, s_b)
    stt(K, "dve", t_ap, a_ap, -1.0, a_ap, ALU.mult, ALU.max, a_b, t_b)
    act(K, c_ap, t_ap, AF.Sin, t_b + hpi.b, c_b, scale=-1.0, bias=hpi[0:np_, 0:1])


def emit_mla(K, C, h, prm, p, pos_d, invf_d, kbias_d):
    Lall = contextlib.ExitStack(); K.stack_push(Lall)
    w_in = prm[p + "w_in"]; w_uq = prm[p + "w_uq"]; w_ukv = prm[p + "w_ukv"]; w_out = prm[p + "w_out"]
    w_in_v = w_in.rearrange("(k r) n -> r k n", r=128)
    w_out_v = w_out.rearrange("(k r) n -> r k n", r=128)
    gB = K.sbuf("gB2", [128, D], F32); bcast_load(K, gB, prm[p + "norm_g"], D)
    gq = K.sbuf("gq", [128, 384], F32); bcast_load(K, gq, prm[p + "q_norm_g"], 384)
    gkv = K.sbuf("gkv", [128, 128], F32); bcast_load(K, gkv, prm[p + "kv_norm_g"], 128)
    npi = None
    hpi = K.sbuf("hpi2", [128, 1], F32)
    K.op("dve", lambda e: e.memset(hpi[:], PI / 2), writes=hpi.b)
    kbias = K.sbuf("kbias", [128, 1], F32)
    K.dma("sp", kbias[:], kbias_d, writes=kbias.b)
    tri = K.sbuf("tri", [128, 128], BF16)
    trif = K.sbuf("trif", [128, 128], F32)
    K.op("pool", lambda e: e.memset(trif[:], 1.0), writes=trif.b)
    K.op("pool", lambda e: e.affine_select(out=trif[:], in_=trif[:], pattern=[[1, 128]], compare_op=ALU.is_ge,
                                           fill=0.0, base=0, channel_multiplier=-1), reads=trif.b, writes=trif.b)
    K.op("dve", lambda e: e.tensor_copy(out=tri[:], in_=trif[:]), reads=trif.b, writes=tri.b)
    wqn = K.sbuf("wqn", [128, 3, 16, 128], BF16)
    wqa = K.sbuf("wqa", [128, 3, 16, 64], BF16)
    wqb = K.sbuf("wqb", [128, 3, 16, 64], BF16)
    wuq_v = w_uq.rearrange("(k r) (hh e) -> r k hh e", r=128, e=192)
    wukv_v = w_ukv.rearrange("c (hh e) -> c hh e", e=256)
    wuv = K.sbuf("wuv", [128, 16, 128], BF16)
    wukT = K.sbuf("wukT", [128, 16, 128], BF16)
    wqi = K.sbuf("wqi", [128, 8, 384], BF16)
    W = {}
    W["junk"] = K.sbuf("junk2", [128, 1024], BF16)
    W["ss"] = K.sbuf("ss2", [128, 8], F32)
    W["rstd"] = K.sbuf("rstd2", [128, 8], F32)
    W["hn"] = K.sbuf("hn2", [128, D], BF16)
    W["tp"] = K.psum("tp2", [128, 1024], BF16)
    tp = W["tp"]
    posi = K.sbuf("posi", [128, NBLK], I32)
    posf = K.sbuf("posf", [128, NBLK], F32)
    K.dma("sp", posi[:], pos_d.rearrange("(j r) -> r j", r=128), writes=posi.b, allow_slow_non_contiguous=True)
    K.op("dve", lambda e: e.tensor_copy(out=posf[:], in_=posi[:]), reads=posi.b, writes=posf.b)
    invB = K.sbuf("invB", [128, 32], F32)
    K.dma("sp", invB[:], invf_d[0:32].partition_broadcast(128), writes=invB.b)
    invc = K.sbuf("invc", [64, 1], F32)
    K.dma("sp", invc[:, 0:1], invf_d.rearrange("(r o) -> r o", o=1), writes=invc.b)
    posBi = K.sbuf("posBi", [64, 512], I32)
    sgn = K.sbuf("sgn", [64, 1], F32)
    K.op("dve", lambda e: e.memset(sgn[0:32, :], -1.0), writes=sgn.b)
    K.op("dve", lambda e: e.memset(sgn[32:64, :], 1.0), writes=sgn.b)

    ckvT = K.sbuf("ckvT", [128, 2 * NTOK], BF16, nb=2)
    krT = K.sbuf("krT", [64, 2 * NTOK], BF16, nb=2)
    ckv_tok = K.sbuf("ckv_tok", [128, 2 * NBLK, 128], BF16, nb=2)

    hnT = K.sbuf("hnT2", [128, 8, 512], BF16, nb=4)
    P = PsumPool(K, 1)
    LA = contextlib.ExitStack(); K.stack_push(LA)
    wkv = K.sbuf("wkv", [128, 8, 192], BF16)
    K.dma("pool", wkv[:], w_in_v[:, :, 384:576], writes=wkv.b)
    for k in range(3):
        K.dma("pool", wqn[:, k], wuq_v[:, k, :, 0:128], writes=wqn.b)
        K.dma("pool", wqa[:, k], wuq_v[:, k, :, 128:192], writes=wqa.b)
        K.dma("pool", wqb[:, k, :, 0:32], wuq_v[:, k, :, 160:192], writes=wqb.b)
        K.dma("pool", wqb[:, k, :, 32:64], wuq_v[:, k, :, 128:160], writes=wqb.b)
    K.dma("pool", wuv[:], wukv_v[:, :, 128:256], writes=wuv.b)
    K.dma("pool", wqi[:], w_in_v[:, :, 0:384], writes=wqi.b)
    angt = K.sbuf("angt", [128, 32], F32); tmpt = K.sbuf("tmpt", [128, 32], F32)
    kit = K.sbuf("kit", [128, 32], I32)
    sint = K.sbuf("sint", [128, 32], F32); cost = K.sbuf("cost", [128, 32], F32)
    ckf = K.sbuf("ckf", [128, 128], BF16); krf = K.sbuf("krf", [128, 64], BF16)
    r1 = K.sbuf("r1", [128, 32], F32); r2 = K.sbuf("r2", [128, 32], F32)
    angA = K.sbuf("angA", [128, NBLK, 32], F32); tmpA = K.sbuf("tmpA", [128, NBLK, 32], F32)
    sinA = K.sbuf("sinA", [128, NBLK, 32], F32); cosA = K.sbuf("cosA", [128, NBLK, 32], F32)
    kiA = K.sbuf("kiA", [128, NBLK, 32], I32)
    tt(K, "dve", angA[:], invB[:].unsqueeze(1).to_broadcast([128, NBLK, 32]),
       posf[:].unsqueeze(2).to_broadcast([128, NBLK, 32]), ALU.mult, invB.b + posf.b, angA.b)
    fl = lambda t_: t_[:].rearrange("r j i -> r (j i)")
    sincos_tables(K, (fl(angA), angA.b), (fl(sinA), sinA.b), (fl(cosA), cosA.b), (fl(tmpA), tmpA.b), npi,
                  (fl(kiA), kiA.b), hpi)
    psA = [P.tiles[0]] + [K.psum(f"psA{i}", [128, 512], F32) for i in range(3)]
    ckf4 = K.sbuf("ckf4", [128, 4, 128], BF16); krf4 = K.sbuf("krf4", [128, 4, 64], BF16)
    r1q = K.sbuf("r1q", [128, 4, 32], F32); r2q = K.sbuf("r2q", [128, 4, 32], F32)
    for sb in range(4):
        norm_transpose_sb(K, C, W, P, h, sb, gB, hnT)
        ss = W["ss"]; rstd = W["rstd"]; junk = W["junk"]
        for j in range(4):
            ps = psA[j]
            for k in range(8):
                K.op("pe", lambda e, k=k, j=j, ps=ps: e.matmul(ps[:, 0:192], lhsT=hnT[:, k, j * 128:(j + 1) * 128],
                                                              rhs=wkv[:, k, :], start=(k == 0), stop=(k == 7)),
                     reads=wkv.b + [hnT.b[j]], writes=ps.b)
            act(K, junk[:, 0:128], ps[:, 0:128], AF.Square, ps.b, junk.b + ss.b, accum_out=ss[:, j:j + 1])
        ts(K, "dve", rstd[:, 0:4], ss[:, 0:4], 1.0 / 128, ALU.mult, ss.b, rstd.b, s2=EPS, op1=ALU.add)
        K.op("act", lambda e: e.sqrt(out=rstd[:, 0:4], in_=rstd[:, 0:4]), reads=rstd.b, writes=rstd.b)
        K.op("dve", lambda e: e.reciprocal(out=rstd[:, 0:4], in_=rstd[:, 0:4]), reads=rstd.b, writes=rstd.b)
        for j in range(4):
            blk = sb * 4 + j
            ps = psA[j]
            stt(K, "dve", ckf4[:, j, :], ps[:, 0:128], rstd[:, j:j + 1], gkv[:], ALU.mult, ALU.mult,
                ps.b + rstd.b + gkv.b, ckf4.b)
            sint_b = sinA[:, blk, :]; cost_b = cosA[:, blk, :]
            x1 = ps[:, 128:160]; x2 = ps[:, 160:192]
            tt(K, "dve", r1q[:, j, :], x1, cost_b, ALU.mult, ps.b + cosA.b, r1q.b)
            tt(K, "dve", r2q[:, j, :], x2, sint_b, ALU.mult, ps.b + sinA.b, r2q.b)
            tt(K, "dve", krf4[:, j, 0:32], r1q[:, j, :], r2q[:, j, :], ALU.subtract, r1q.b + r2q.b, krf4.b)
            tt(K, "dve", r1q[:, j, :], x2, cost_b, ALU.mult, ps.b + cosA.b, r1q.b)
            tt(K, "dve", r2q[:, j, :], x1, sint_b, ALU.mult, ps.b + sinA.b, r2q.b)
            tt(K, "dve", krf4[:, j, 32:64], r1q[:, j, :], r2q[:, j, :], ALU.add, r1q.b + r2q.b, krf4.b)
        K.op("pool", lambda e, sb=sb: e.tensor_copy(out=ckv_tok[:, NBLK + sb * 4:NBLK + sb * 4 + 4, :], in_=ckf4[:]),
             reads=ckf4.b, writes=[ckv_tok.b[1]])
        for j in range(4):
            K.op("pe", lambda e, j=j: e.transpose(tp[:, j * 128:(j + 1) * 128], ckf4[:, j, :], C["ident"][:]),
                 reads=ckf4.b + C["ident"].b, writes=tp.b)
        for j in range(4):
            K.op("pe", lambda e, j=j: e.transpose(tp[0:64, (4 + j) * 128:(5 + j) * 128], krf4[:, j, :], C["ident"][:]),
                 reads=krf4.b + C["ident"].b, writes=tp.b)
        K.op("act", lambda e, sb=sb: e.copy(out=ckvT[:, NTOK + sb * 512:NTOK + (sb + 1) * 512], in_=tp[:, 0:512]),
             reads=tp.b, writes=[ckvT.b[1]])
        K.op("act", lambda e, sb=sb: e.copy(out=krT[:, NTOK + sb * 512:NTOK + (sb + 1) * 512], in_=tp[0:64, 512:1024]),
             reads=tp.b, writes=[krT.b[1]])
    lxi = K.dram("mla_lxi", [320, NTOK], BF16, "Internal")
    lxo = K.dram("mla_lxo", [640, NTOK], BF16, "Internal")
    bli = Buf("lxi"); blo = Buf("lxo")
    K.dma("sp", lxi[0:128, :], ckvT[:, NTOK:2 * NTOK], reads=[ckvT.b[1]], writes=[bli])
    K.dma("sp", lxi[128:192, :], krT[:, NTOK:2 * NTOK], reads=[krT.b[1]], writes=[bli])
    K.dma("sp", lxi[192:320, :], ckv_tok[:, NBLK:2 * NBLK, :].rearrange("r j c -> r (j c)"), reads=[ckv_tok.b[1]],
          writes=[bli])
    K.collective("AllGather", [lxi], [lxo], [bli], [blo])
    K.dma("sp", ckvT[:, 0:NTOK], lxo[0:128, :], reads=[blo], writes=[ckvT.b[0]])
    K.dma("sp", krT[:, 0:NTOK], lxo[128:192, :], reads=[blo], writes=[krT.b[0]])
    K.dma("sp", ckv_tok[:, 0:NBLK, :].rearrange("r j c -> r (j c)"), lxo[192:320, :], reads=[blo], writes=[ckv_tok.b[0]])
    K.barrier(); K.stack_pop()
    Lw = contextlib.ExitStack(); K.stack_push(Lw)
    wuk = K.sbuf("wuk", [128, 16, 128], BF16)
    K.dma("pool", wuk[:], wukv_v[:, :, 0:128], writes=wuk.b)
    for i2 in range(2):
        for hl in range(8):
            hh = i2 * 8 + hl
            K.op("pe", lambda e, hh=hh, hl=hl: e.transpose(tp[:, hl * 128:(hl + 1) * 128], wuk[:, hh, :], C["ident"][:]),
                 reads=wuk.b + C["ident"].b, writes=tp.b)
        K.op("dve", lambda e, i2=i2: e.tensor_copy(out=wukT[:, i2 * 8:(i2 + 1) * 8, :].rearrange("r j c -> r (j c)"),
                                                   in_=tp[:]), reads=tp.b, writes=wukT.b)
    K.barrier(); K.stack_pop()

    LB = contextlib.ExitStack(); K.stack_push(LB)
    psS = [K.psum(f"psS{i}", [128, 512], F32) for i in range(2)]
    psO = [K.psum(f"psO{i}", [128, 512], F32) for i in range(2)]
    psL = [K.psum(f"psL{i}", [128, 512], F32) for i in range(2)]

    class _Pool2:
        def __init__(self, tiles):
            self.tiles = tiles; self.i = 0
        def next(self):
            t = self.tiles[self.i]; self.i = (self.i + 1) % len(self.tiles); return t
    PA = _Pool2(P.tiles)
    PB = _Pool2(P.tiles + psS + psO + [psL[0]])
    cqT = K.sbuf("cqT", [128, 3, 512], BF16, nb=4)
    cqf4 = K.sbuf("cqf4", [128, 4, 384], BF16)
    szT = K.sbuf("szT2", [128, 16, 512], BF16, nb=16)
    wbs = [K.sbuf(f"wb2_{i}", [128, 4096], BF16) for i in range(2)]
    wseqB = []
    for _sb in range(4):
        wseqB += [("z", c) for c in range(4)] + [("o", c) for c in range(4)]
    wstB = {"issued": 0, "used": 0}

    def issue_b(upto):
        while wstB["issued"] < min(upto, len(wseqB)):
            i = wstB["issued"]; kind, c = wseqB[i]
            wt_ = wbs[i % 2]
            if kind == "z":
                K.dma("pool", wt_[:].rearrange("r (k n) -> r k n", k=8), w_in_v[:, :, 576 + c * 512:576 + (c + 1) * 512],
                      writes=wt_.b)
            else:
                K.dma("pool", wt_[:].rearrange("r (k n) -> r k n", k=16), w_out_v[:, :, c * 256:(c + 1) * 256], writes=wt_.b)
            wstB["issued"] += 1

    def next_w():
        i = wstB["used"]; wstB["used"] += 1
        issue_b(i + 2)
        return wbs[i % 2]
    issue_b(1)
    angF = K.sbuf("angF", [64, 512], F32); tmpF = K.sbuf("tmpF", [64, 512], F32)
    sinF = K.sbuf("sinF", [64, 512], F32); cosF = K.sbuf("cosF", [64, 512], F32)
    posBf = angF
    qn = [K.sbuf(f"qn{i}", [128, 512], BF16) for i in range(2)]
    qp = [K.sbuf(f"qp{i}", [128, 512], BF16) for i in range(2)]
    qr = [K.sbuf(f"qr{i}", [64, 512], BF16) for i in range(2)]
    qra = angF; qrb = tmpF
    pT = [K.sbuf(f"pT{i}", [128, 512], BF16) for i in range(3)]
    rl = K.sbuf("rl", [128, 512], F32)
    oh = [K.sbuf(f"oh{i}", [128, 512], BF16) for i in range(2)]
    st = {"nl": 0, "npt": 0, "nS": 0}

    def q_prologue(hh):
        qn_, qp_, qr_ = qn[hh % 2], qp[hh % 2], qr[hh % 2]
        ps = PA.next()
        for k in range(3):
            K.op("pe", lambda e, k=k, ps=ps: e.matmul(ps[:], lhsT=wqn[:, k, hh, :], rhs=cqT[:, k, :],
                                                     start=(k == 0), stop=(k == 2)),
                 reads=wqn.b + cqT.b, writes=ps.b)
        K.op("act", lambda e, ps=ps: e.copy(out=qn_[:], in_=ps[:]), reads=ps.b, writes=qn_.b)
        ps3 = PA.next()
        for k in range(3):
            K.op("pe", lambda e, k=k, ps3=ps3: e.matmul(ps3[0:64, 0:512], lhsT=wqa[:, k, hh, :], rhs=cqT[:, k, :],
                                                       start=(k == 0), stop=(k == 2)),
                 reads=wqa.b + cqT.b, writes=ps3.b)
        tt(K, "dve", qra[:], ps3[0:64, :], cosF[:], ALU.mult, ps3.b + cosF.b, qra.b)
        ps2 = PA.next()
        K.op("pe", lambda e, ps2=ps2: e.matmul(ps2[:], lhsT=wukT[:, hh, :], rhs=qn_[:], start=True, stop=True),
             reads=wukT.b + qn_.b, writes=ps2.b)
        K.op("act", lambda e, ps2=ps2: e.copy(out=qp_[:], in_=ps2[:]), reads=ps2.b, writes=qp_.b)
        ps4 = PA.next()
        for k in range(3):
            K.op("pe", lambda e, k=k, ps4=ps4: e.matmul(ps4[0:64, 0:512], lhsT=wqb[:, k, hh, :], rhs=cqT[:, k, :],
                                                       start=(k == 0), stop=(k == 2)),
                 reads=wqb.b + cqT.b, writes=ps4.b)
        tt(K, "dve", qrb[:], ps4[0:64, :], sinF[:], ALU.mult, ps4.b + sinF.b, qrb.b)
        tt(K, "dve", qr_[:], qra[:], qrb[:], ALU.add, qra.b + qrb.b, qr_.b)

    def qk(hh, sb, kb):
        qp_, qr_ = qp[hh % 2], qr[hh % 2]
        half = 0 if kb < NBLK else 1
        jd = kb - (NBLK + sb * 4)
        c0 = jd * 128 if jd > 0 else 0
        pS = psS[st["nS"] % 2]; st["nS"] += 1
        K.op("pe", lambda e: e.matmul(pS[:, c0:512], lhsT=ckvT[:, kb * 128:(kb + 1) * 128], rhs=qp_[:, c0:512],
                                      start=True, stop=False), reads=[ckvT.b[half]] + qp_.b, writes=pS.b)
        K.op("pe", lambda e: e.matmul(pS[:, c0:512], lhsT=krT[:, kb * 128:(kb + 1) * 128], rhs=qr_[:, c0:512],
                                      start=False, stop=True), reads=[krT.b[half]] + qr_.b, writes=pS.b)
        return pS

    def softmax_pv(hh, sb, kb, pS, nkb):
        half = 0 if kb < NBLK else 1
        jd = kb - (NBLK + sb * 4)
        c0 = jd * 128 if jd > 0 else 0
        pt = pT[st["npt"] % 3]; st["npt"] += 1
        O = psO[hh % 2]; Lp = psL[hh % 2]
        if half == 0:
            act(K, pt[:, c0:512], pS[:, c0:512], AF.Exp, pS.b + kbias.b, pt.b, scale=MLA_SCALE_F, bias=kbias[:, 0:1])
        else:
            act(K, pt[:, c0:512], pS[:, c0:512], AF.Exp, pS.b, pt.b, scale=MLA_SCALE_F)
        if jd >= 0:
            tt(K, "pool", pt[:, jd * 128:(jd + 1) * 128], pt[:, jd * 128:(jd + 1) * 128], tri[:], ALU.mult,
               pt.b + tri.b, pt.b)
        K.op("pe", lambda e: e.matmul(O[:, c0:512], lhsT=ckv_tok[:, kb, :], rhs=pt[:, c0:512], start=(kb == 0),
                                      stop=(kb == nkb - 1)), reads=[ckv_tok.b[half]] + pt.b, writes=O.b)
        K.op("pe", lambda e: e.matmul(Lp[:, c0:512], lhsT=C["ones_bf"][:], rhs=pt[:, c0:512], start=(kb == 0),
                                      stop=(kb == nkb - 1)), reads=C["ones_bf"].b + pt.b, writes=Lp.b)

    def head_epilogue(hh):
        oh_ = oh[hh % 2]; O = psO[hh % 2]; Lp = psL[hh % 2]
        K.op("dve", lambda e: e.reciprocal(out=rl[:], in_=Lp[:]), reads=Lp.b, writes=rl.b)
        tt(K, "dve", oh_[:], O[:], rl[:], ALU.mult, O.b + rl.b, oh_.b)
        ps5 = PA.next()
        K.op("pe", lambda e: e.matmul(ps5[:], lhsT=wuv[:, hh, :], rhs=oh_[:], start=True, stop=True),
             reads=wuv.b + oh_.b, writes=ps5.b)
        tt(K, "dve", szT[:, hh, :], ps5[:], szT[:, hh, :], ALU.mult, ps5.b + [szT.b[hh]], [szT.b[hh]])

    for sb in range(4):
        norm_transpose_sb(K, C, W, P, h, sb, gB, hnT)
        K.dma("sp", posBi[:], pos_d[sb * 512:(sb + 1) * 512].partition_broadcast(64), writes=posBi.b)
        K.op("dve", lambda e: e.tensor_copy(out=posBf[:], in_=posBi[:]), reads=posBi.b, writes=posBf.b)
        ts(K, "dve", angF[:], angF[:], invc[:, 0:1], ALU.mult, angF.b + invc.b, angF.b)
        sincos_tables(K, (angF[:], angF.b), (sinF[:], sinF.b), (cosF[:], cosF.b), (tmpF[:], tmpF.b), npi,
                      (posBi[:], posBi.b), hpi)
        ts(K, "dve", sinF[:], sinF[:], sgn[:, 0:1], ALU.mult, sinF.b + sgn.b, sinF.b)
        ss = W["ss"]; rstd = W["rstd"]; junk = W["junk"]
        cq_ps = []
        for j in range(4):
            ps = PB.next(); cq_ps.append(ps)
            for k in range(8):
                K.op("pe", lambda e, k=k, j=j, ps=ps: e.matmul(ps[:, 0:384], lhsT=hnT[:, k, j * 128:(j + 1) * 128],
                                                              rhs=wqi[:, k, :], start=(k == 0), stop=(k == 7)),
                     reads=wqi.b + [hnT.b[j]], writes=ps.b)
            act(K, junk[:, 0:384], ps[:, 0:384], AF.Square, ps.b, junk.b + ss.b, accum_out=ss[:, j:j + 1])
        ts(K, "dve", rstd[:, 0:4], ss[:, 0:4], 1.0 / 384, ALU.mult, ss.b, rstd.b, s2=EPS, op1=ALU.add)
        K.op("act", lambda e: e.sqrt(out=rstd[:, 0:4], in_=rstd[:, 0:4]), reads=rstd.b, writes=rstd.b)
        K.op("dve", lambda e: e.reciprocal(out=rstd[:, 0:4], in_=rstd[:, 0:4]), reads=rstd.b, writes=rstd.b)
        for j in range(4):
            ps = cq_ps[j]
            stt(K, "dve", cqf4[:, j, :], ps[:, 0:384], rstd[:, j:j + 1], gq[:], ALU.mult, ALU.mult,
                ps.b + rstd.b + gq.b, cqf4.b)
        for cg in range(4):
            wt = next_w()
            wb = wt[:].rearrange("r (k n) -> r k n", k=8)
            for sub in range(4):
                fc = cg * 4 + sub
                ps = PB.next()
                for k in range(8):
                    K.op("pe", lambda e, k=k, sub=sub, ps=ps, wb=wb: e.matmul(
                        ps[:], lhsT=wb[:, k, sub * 128:(sub + 1) * 128], rhs=hnT[:, k, :],
                        start=(k == 0), stop=(k == 7)), reads=wt.b + hnT.b, writes=ps.b)
                act(K, szT[:, fc, :], ps[:], AF.Silu, ps.b, [szT.b[fc]])
        for j in range(4):
            for k in range(3):
                K.op("pe", lambda e, k=k, j=j: e.transpose(tp[:, k * 128:(k + 1) * 128], cqf4[:, j, k * 128:(k + 1) * 128],
                                                           C["ident"][:]), reads=cqf4.b + C["ident"].b, writes=tp.b)
            K.op("act", lambda e, j=j: e.copy(out=cqT[:, :, j * 128:(j + 1) * 128],
                                              in_=tp[:, 0:384].rearrange("r (k t) -> r k t", k=3)),
                 reads=tp.b, writes=[cqT.b[j]])
        nkb = NBLK + (sb + 1) * 4
        q_prologue(0)
        for hh in range(16):
            pS_next = qk(hh, sb, 0)
            for kb in range(nkb):
                pS_cur = pS_next
                if kb + 1 < nkb:
                    pS_next = qk(hh, sb, kb + 1)
                softmax_pv(hh, sb, kb, pS_cur, nkb)
                if kb == 3 and hh + 1 < 16:
                    q_prologue(hh + 1)
                if kb == 6 and hh > 0:
                    head_epilogue(hh - 1)
            if hh == 15:
                head_epilogue(15)
        for cgo in range(4):
            wt = next_w()
            wo = wt[:].rearrange("r (k n) -> r k n", k=16)
            for jb in range(4):
                blk = sb * 4 + jb
                ps = PB.next()
                for k in range(16):
                    K.op("pe", lambda e, k=k, jb=jb, ps=ps, wo=wo: e.matmul(
                        ps[:, 0:256], lhsT=szT[:, k, jb * 128:(jb + 1) * 128], rhs=wo[:, k, :],
                        start=(k == 0), stop=(k == 15)), reads=wt.b + [szT.b[k]], writes=ps.b)
                hs = h[:, blk, cgo * 256:(cgo + 1) * 256]
                tt(K, "dve", hs, hs, ps[:, 0:256], ALU.add, ps.b + [h.b[blk]], [h.b[blk]])
    K.barrier(); K.stack_pop()
    K.barrier(); K.stack_pop()


PARAM_SPECS = None


def param_specs():
    sp = {}
    def gm(p):
        sp[p + "norm_g"] = (1024,); sp[p + "w_in"] = (1024, 6144); sp[p + "ln_g"] = (2048,)
        sp[p + "ln_b"] = (2048,); sp[p + "w_s"] = (8, 128, 128); sp[p + "b_s"] = (8, 128)
        sp[p + "w_out"] = (2048, 1024)
    gm("l0_")
    p = "l1_"
    sp[p + "norm_g"] = (1024,); sp[p + "w_in"] = (1024, 4096)
    sp[p + "a_re"] = (128, 64); sp[p + "a_im"] = (128, 64); sp[p + "log_step"] = (128,)
    sp[p + "b_re"] = (128, 64, 16); sp[p + "b_im"] = (128, 64, 16)
    sp[p + "c_re"] = (128, 16, 64); sp[p + "c_im"] = (128, 16, 64)
    sp[p + "d_skip"] = (2048,); sp[p + "w_glu"] = (2048, 2048); sp[p + "b_glu"] = (2048,)
    sp[p + "w_out"] = (2048, 1024)
    p = "l2_"
    sp[p + "norm_g"] = (1024,); sp[p + "w_in"] = (1024, 2624); sp[p + "q_norm_g"] = (384,)
    sp[p + "w_uq"] = (384, 3072); sp[p + "kv_norm_g"] = (128,); sp[p + "w_ukv"] = (128, 4096)
    sp[p + "w_out"] = (2048, 1024)
    gm("l3_")
    sp["final_norm_g"] = (1024,)
    return sp


def build_program(layers=("l0",), final_norm=False):
    K = MK()
    x = K.dram("x", [NTOK, D], F32, "ExternalInput")
    out = K.dram("out", [NTOK, D], F32, "ExternalOutput")
    prm = {}
    need = set()
    for l in layers:
        need.add(l + "_")
    for name, shp in param_specs().items():
        if name[:3] in need or (final_norm and name == "final_norm_g"):
            prm[name] = K.dram(name, list(shp), F32, "ExternalInput")
    C = emit_consts(K)
    P = None
    cmask_d = K.dram("cmask", [128, 2], F32, "ExternalInput")
    cmask = K.sbuf("cmask_s", [128, 2], F32)
    K.dma("sp", cmask[:], cmask_d, writes=cmask.b)
    kflag_d = K.dram("kflag", [128, 1], F32, "ExternalInput")
    kflag = K.sbuf("kflag_s", [128, 1], F32)
    K.dma("sp", kflag[:], kflag_d, writes=kflag.b)
    pos_d = K.dram("pos", [NTOK], I32, "ExternalInput")
    invf_d = K.dram("invf", [64], F32, "ExternalInput")
    kbias_d = K.dram("kbias", [128, 1], F32, "ExternalInput")
    h = K.sbuf("h", [128, NBLK, D], F32, nb=NBLK)
    xv = x.rearrange("(j p) d -> p j d", p=128)
    for q in range(4):
        K.dma("sp", h[:, q * 4:(q + 1) * 4, :], xv[:, q * 4:(q + 1) * 4, :], writes=h.b[q * 4:(q + 1) * 4])
    PRE = {}
    if "l1" in layers:
        LPRE = contextlib.ExitStack(); K.stack_push(LPRE)
        p1 = "l1_"
        for nm in ("are", "aim", "lst"):
            PRE[nm] = K.sbuf("pre_" + nm, [128, 64], F32)
        K.dma("sp", PRE["are"][:], prm[p1 + "a_re"].rearrange("(j gl) p -> (gl p) j", gl=2), writes=PRE["are"].b,
              allow_slow_non_contiguous=True)
        K.dma("sp", PRE["aim"][:], prm[p1 + "a_im"].rearrange("(j gl) p -> (gl p) j", gl=2), writes=PRE["aim"].b,
              allow_slow_non_contiguous=True)
        lsv = prm[p1 + "log_step"].rearrange("(j gl) -> gl j", gl=2)
        for gl in range(2):
            K.dma("sp", PRE["lst"][gl * 64:(gl + 1) * 64, :], lsv[gl].partition_broadcast(64), writes=PRE["lst"].b,
                  allow_slow_non_contiguous=True)
        PRE["dsk"] = K.sbuf("pre_dsk", [128, 16], F32); PRE["bgl"] = K.sbuf("pre_bgl", [128, 16], F32)
        K.dma("sp", PRE["dsk"][:], prm[p1 + "d_skip"].rearrange("(c r) -> r c", r=128), writes=PRE["dsk"].b,
              allow_slow_non_contiguous=True)
        K.dma("sp", PRE["bgl"][:], prm[p1 + "b_glu"].rearrange("(c r) -> r c", r=128), writes=PRE["bgl"].b,
              allow_slow_non_contiguous=True)
    for l in layers:
        if l in ("l0", "l3"):
            emit_gmlp(K, C, P, h, prm, l + "_")
        elif l == "l2":
            emit_mla(K, C, h, prm, "l2_", pos_d, invf_d, kbias_d)
        elif l == "l1":
            emit_s5(K, C, h, prm, "l1_", kflag, cmask, PRE)
            K.barrier(); K.stack_pop()
    ov = out.rearrange("(j p) d -> p j d", p=128)
    if final_norm:
        emit_final_norm(K, h, prm["final_norm_g"], ov)
    else:
        for q in range(4):
            K.dma("sp", ov[:, q * 4:(q + 1) * 4, :], h[:, q * 4:(q + 1) * 4, :], reads=h.b[q * 4:(q + 1) * 4])
    K.barrier()
    return K.build(), sorted(prm.keys())


def emit_final_norm(K, h, g_d, ov):
    L = contextlib.ExitStack(); K.stack_push(L)
    gB = K.sbuf("gBf", [128, D], F32); bcast_load(K, gB, g_d, D)
    junk = [K.sbuf(f"junkf{i}", [128, D], F32) for i in range(2)]
    ss = K.sbuf("ssf", [128, NBLK], F32)
    ot = [K.sbuf(f"otf{i}", [128, D], F32) for i in range(3)]
    for blk in range(NBLK):
        jk = junk[blk % 2]
        act(K, jk[:], h[:, blk, :], AF.Square, [h.b[blk]], jk.b + ss.b, accum_out=ss[:, blk:blk + 1])
    ts(K, "dve", ss[:], ss[:], 1.0 / D, ALU.mult, ss.b, ss.b, s2=EPS, op1=ALU.add)
    K.op("act", lambda e: e.sqrt(out=ss[:], in_=ss[:]), reads=ss.b, writes=ss.b)
    K.op("dve", lambda e: e.reciprocal(out=ss[:], in_=ss[:]), reads=ss.b, writes=ss.b)
    for blk in range(NBLK):
        o = ot[blk % 3]
        stt(K, "dve", o[:], h[:, blk, :], ss[:, blk:blk + 1], gB[:], ALU.mult, ALU.mult, [h.b[blk]] + ss.b + gB.b, o.b)
        K.dma("sp", ov[:, blk, :], o[:], reads=o.b)
    K.barrier(); K.stack_pop()


_PROG = {}


def _get_prog(layers, final_norm):
    key = (tuple(layers), final_norm)
    if key not in _PROG:
        _PROG[key] = build_program(layers=layers, final_norm=final_norm)
    return _PROG[key]


def _aux_consts():
    cmask = np.zeros((128, 2), np.float32)
    for r in range(128):
        cmask[r, (r // 16) % 2] = 1.0
    invf = (np.float32(10000.0) ** (-np.arange(0, 64, 2, dtype=np.float32) / np.float32(64))).astype(np.float32)
    return cmask, np.concatenate([invf, invf])


def kernel(**inputs):
    inputs = {k: np.asarray(v) for k, v in inputs.items()}
    layers = ("l0", "l1", "l2", "l3")
    nc, names = _get_prog(layers, True)
    x = np.ascontiguousarray(inputs["x"], dtype=np.float32).reshape(8, NTOK, D)
    pos = np.ascontiguousarray(inputs["positions"]).astype(np.int32).reshape(8, NTOK)
    cmask, invf = _aux_consts()
    wts = {n: np.ascontiguousarray(inputs[n], dtype=np.float32) for n in names}

    in_maps = []
    for i in range(8):
        odd = float(i % 2)
        m = {"x": x[i], "cmask": cmask, "pos": pos[i], "invf": invf,
             "kbias": np.full((128, 1), 0.0 if odd else -30000.0, np.float32),
             "kflag": np.full((128, 1), odd, np.float32)}
        m.update(wts)
        in_maps.append(m)
    res = run_bass_kernel_spmd(nc, in_maps, core_ids=list(range(8))).results
    out = np.stack([np.asarray(r["out"]) for r in res]).reshape(4, 4096, D).astype(np.float32)
    return out
```

```python
import contextlib
import numpy as np
import concourse.bass as bass
import concourse.mybir as mybir
from concourse.bass_utils import run_bass_kernel_spmd

F32 = mybir.dt.float32
BF16 = mybir.dt.bfloat16
I32 = mybir.dt.int32
ALU = mybir.AluOpType
AF = mybir.ActivationFunctionType
AX = mybir.AxisListType

N_DMA_SEMS = 12


class Buf:
    __slots__ = ("w", "r", "name")

    def __init__(self, name=""):
        self.w = None
        self.r = []
        self.name = name


class Tile:
    def __init__(self, t, nb, name):
        self.t = t
        self.b = [Buf(f"{name}.{i}") for i in range(nb)]

    def __getitem__(self, idx):
        return self.t[idx]


class MK:
    ENG = ("pe", "act", "dve", "pool", "sp")

    def __init__(self):
        self.nc = bass.Bass("TRN2", target_bir_lowering=False)
        self.stack = contextlib.ExitStack()
        self.ops = {e: [] for e in self.ENG}
        self.cnt = {}
        self.sems = {}
        self.waited = {e: {} for e in self.ENG}
        for e in ("pe", "act", "dve", "pool"):
            self._mksem("c_" + e)
        for i in range(N_DMA_SEMS):
            self._mksem(f"d{i}")
        self.dma_rr = 0
        self.n_ops = 0
        self.stacks = [self.stack]
        self.uid = 0

    def _mksem(self, key):
        self.sems[key] = self.stack.enter_context(self.nc.semaphore(key))
        self.cnt[key] = 0

    def dram(self, name, shape, dt, kind):
        return self.nc.dram_tensor(name, list(shape), dt, kind=kind).ap()

    def stack_push(self, st):
        self.stacks.append(st)

    def stack_pop(self):
        self.stacks.pop().close()

    def sbuf(self, name, shape, dt, nb=1):
        self.uid += 1
        t = self.stacks[-1].enter_context(self.nc.sbuf_tensor(f"{name}_{self.uid}", list(shape), dt))
        return Tile(t, nb, name)

    def psum(self, name, shape, dt=F32, nb=1):
        self.uid += 1
        t = self.stacks[-1].enter_context(self.nc.psum_tensor(f"{name}_{self.uid}", list(shape), dt))
        return Tile(t, nb, name)

    def barrier(self):
        for eng in self.ENG:
            waits = []
            for k, v in self.cnt.items():
                if v > self.waited[eng].get(k, 0):
                    self.waited[eng][k] = v
                    waits.append((k, v))
            if waits:
                self.ops[eng].append((None, waits, None, 0))

    def _deps(self, eng, reads, writes):
        need = {}
        def add(tok):
            if tok is None:
                return
            k, v = tok
            if eng == "pe" and k == "c_pe":
                return
            if need.get(k, 0) < v:
                need[k] = v
        for b in reads:
            add(b.w)
        for b in writes:
            add(b.w)
            for tok in b.r:
                add(tok)
        out = []
        wd = self.waited[eng]
        for k, v in need.items():
            if wd.get(k, 0) < v:
                wd[k] = v
                out.append((k, v))
        return out

    def _commit(self, tok, reads, writes):
        for b in writes:
            b.w = tok
            b.r = []
        for b in reads:
            if b not in writes:
                b.r.append(tok)

    def op(self, eng, fn, reads=(), writes=()):
        reads = list(reads); writes = list(writes)
        waits = self._deps(eng, reads, writes)
        key = "c_" + eng
        self.cnt[key] += 1
        tok = (key, self.cnt[key])
        self.ops[eng].append((fn, waits, key, 1))
        self._commit(tok, reads, writes)
        self.n_ops += 1
        return tok

    def dma(self, eng, out, in_, reads=(), writes=(), **kw):
        reads = list(reads); writes = list(writes)
        i = self.dma_rr; self.dma_rr = (self.dma_rr + 1) % N_DMA_SEMS
        key = f"d{i}"
        waits = self._deps(eng, reads, writes)
        prev = self.cnt[key]
        if prev and self.waited[eng].get(key, 0) < prev:
            self.waited[eng][key] = prev
            waits.append((key, prev))
        self.cnt[key] += 16
        tok = (key, self.cnt[key])
        def fn(e, out=out, in_=in_, kw=kw):
            return e.dma_start(out=out, in_=in_, **kw)
        self.ops[eng].append((fn, waits, key, 16))
        self._commit(tok, reads, writes)
        self.n_ops += 1
        return tok

    def collective(self, kind, ins, outs, rbufs, wbufs):
        if "cc" not in self.sems:
            self._mksem("cc")
        waits = self._deps("pool", list(rbufs), list(wbufs))
        self.cnt["cc"] += 1
        tok = ("cc", self.cnt["cc"])
        def fn(e, ins=ins, outs=outs, kind=kind):
            return e.collective_compute(kind, ALU.bypass, replica_groups=[[0, 1], [2, 3], [4, 5], [6, 7]],
                                        ins=list(ins), outs=list(outs))
        self.ops["pool"].append((fn, waits, "cc", 1))
        self._commit(tok, list(rbufs), list(wbufs))
        return tok

    def wait_all(self, eng, bufs):
        waits = self._deps(eng, list(bufs), [])
        self.ops[eng].append((None, waits, None, 0))

    def build(self):
        nc = self.nc
        sems = self.sems
        ops = self.ops
        with nc.Block() as block:
            def emit(e, lst):
                for fn, waits, key, amt in lst:
                    for k, v in waits:
                        e.wait_ge(sems[k], v)
                    if fn is not None:
                        ins = fn(e)
                        ins.then_inc(sems[key], amt)

            @block.tensor
            def _(e):
                emit(e, ops["pe"])

            @block.scalar
            def _(e):
                emit(e, ops["act"])

            @block.vector
            def _(e):
                emit(e, ops["dve"])

            @block.gpsimd
            def _(e):
                emit(e, ops["pool"])

            @block.sync
            def _(e):
                emit(e, ops["sp"])
        self.stack.close()
        return nc


NTOK = 2048
NBLK = 16
D = 1024
DI = 2048
EPS = 1e-6


class PsumPool:
    def __init__(self, K, n):
        self.tiles = [K.psum(f"pp{i}", [128, 512], F32) for i in range(n)]
        self.i = 0

    def next(self):
        t = self.tiles[self.i]
        self.i = (self.i + 1) % len(self.tiles)
        return t


def bcast_load(K, dst, src_1d, n, eng="sp"):
    K.dma(eng, dst[:], src_1d.partition_broadcast(128), writes=dst.b)


def emit_consts(K):
    C = {}
    C["ident_f"] = K.sbuf("ident_f", [128, 128], F32)
    C["ident"] = K.sbuf("ident", [128, 128], BF16)
    idf = C["ident_f"]; idb = C["ident"]
    K.op("pool", lambda e: e.memset(idf[:], 1.0), writes=idf.b)
    K.op("pool", lambda e: e.affine_select(out=idf[:], in_=idf[:], pattern=[[-1, 128]],
                                           compare_op=ALU.is_equal, fill=0.0, base=0,
                                           channel_multiplier=1), reads=idf.b, writes=idf.b)
    K.op("dve", lambda e: e.tensor_copy(out=idb[:], in_=idf[:]), reads=idf.b, writes=idb.b)
    C["ones_bf"] = K.sbuf("ones_bf", [128, 128], BF16)
    ob = C["ones_bf"]
    K.op("dve", lambda e: e.memset(ob[:], 1.0), writes=ob.b)
    return C


def rmsnorm_block(K, W, hsrc, gB, out_bf, tag):
    h_ap, h_b = hsrc
    o_ap, o_b = out_bf
    junk = W["junk"]; ss = W["ss"]; rstd = W["rstd"]
    K.op("act", lambda e: e.activation(out=junk[:, 0:D], in_=h_ap, func=AF.Square, accum_out=ss[:, 0:1]),
         reads=h_b, writes=junk.b + ss.b)
    K.op("dve", lambda e: e.tensor_scalar(out=rstd[:, 0:1], in0=ss[:, 0:1], scalar1=1.0 / D, scalar2=EPS,
                                          op0=ALU.mult, op1=ALU.add), reads=ss.b, writes=rstd.b)
    K.op("act", lambda e: e.sqrt(out=rstd[:, 0:1], in_=rstd[:, 0:1]), reads=rstd.b, writes=rstd.b)
    K.op("dve", lambda e: e.reciprocal(out=rstd[:, 0:1], in_=rstd[:, 0:1]), reads=rstd.b, writes=rstd.b)
    K.op("dve", lambda e: e.scalar_tensor_tensor(out=o_ap, in0=h_ap, scalar=rstd[:, 0:1], in1=gB[:],
                                                 op0=ALU.mult, op1=ALU.mult),
         reads=h_b + rstd.b + gB.b, writes=o_b)


def norm_transpose_sb(K, C, W, P, h, sb, gB, hnT):
    junk = W["junk"]; ss = W["ss"]; rstd = W["rstd"]; hn = W["hn"]; tp = W["tp"]
    for j in range(4):
        blk = sb * 4 + j
        K.op("act", lambda e, j=j, blk=blk: e.activation(out=junk[:, 0:D], in_=h[:, blk, :], func=AF.Square,
                                                        accum_out=ss[:, 4 + j:5 + j]),
             reads=[h.b[blk]], writes=junk.b + ss.b)
    K.op("dve", lambda e: e.tensor_scalar(out=rstd[:, 4:8], in0=ss[:, 4:8], scalar1=1.0 / D, scalar2=EPS,
                                          op0=ALU.mult, op1=ALU.add), reads=ss.b, writes=rstd.b)
    K.op("act", lambda e: e.sqrt(out=rstd[:, 4:8], in_=rstd[:, 4:8]), reads=rstd.b, writes=rstd.b)
    K.op("dve", lambda e: e.reciprocal(out=rstd[:, 4:8], in_=rstd[:, 4:8]), reads=rstd.b, writes=rstd.b)
    for j in range(4):
        blk = sb * 4 + j
        K.op("dve", lambda e, j=j, blk=blk: e.scalar_tensor_tensor(out=hn[:], in0=h[:, blk, :], scalar=rstd[:, 4 + j:5 + j],
                                                                  in1=gB[:], op0=ALU.mult, op1=ALU.mult),
             reads=[h.b[blk]] + rstd.b + gB.b, writes=hn.b)
        for k in range(8):
            K.op("pe", lambda e, k=k: e.transpose(tp[:, k * 128:(k + 1) * 128], hn[:, k * 128:(k + 1) * 128],
                                                  C["ident"][:]),
                 reads=hn.b + C["ident"].b, writes=tp.b)
        K.op("act", lambda e, j=j: e.copy(out=hnT[:, :, j * 128:(j + 1) * 128],
                                           in_=tp[:].rearrange("p (k t) -> p k t", k=8)),
             reads=tp.b, writes=[hnT.b[j]])


def emit_gmlp(K, C, P, h, prm, lname):
    nc = K.nc
    L = contextlib.ExitStack()
    K.stack_push(L)
    P = PsumPool(K, 6)
    w_in, ln_g, ln_b, w_s, b_s, w_out, norm_g = (prm[lname + s] for s in
                                                 ("w_in", "ln_g", "ln_b", "w_s", "b_s", "w_out", "norm_g"))
    W = {}
    W["junk"] = K.sbuf("junk", [128, 1024], F32)
    W["ss"] = K.sbuf("ss", [128, 8], F32)
    W["rstd"] = K.sbuf("rstd", [128, 8], F32)
    W["hn"] = K.sbuf("hn", [128, D], BF16)
    W["tp"] = K.psum("tp", [128, 1024], BF16)
    gB = K.sbuf("gB", [128, D], F32)
    lgc = K.sbuf("lgc", [128, 16], F32)
    lbc = K.sbuf("lbc", [128, 16], F32)
    bias2 = K.sbuf("bias2", [128, 16, 128], F32)
    bsB = K.sbuf("bsB", [128, 8, 128], F32)
    WsT = K.sbuf("WsT", [128, 8, 128], BF16)
    hnTs = [K.sbuf(f"hnT{i}", [128, 8, 512], BF16, nb=4) for i in range(2)]
    NWB = 4
    wbuf = [K.sbuf(f"wbuf{i}", [128, 8, 512], BF16) for i in range(NWB)]
    wo_tiles = [K.sbuf(f"wobuf{i}", [128, 16, 256], BF16) for i in range(2)]
    uT = K.sbuf("uT", [128, 16, 512], BF16, nb=16)
    szT = K.sbuf("szT", [128, 16, 512], BF16, nb=16)
    vtok = K.sbuf("vtok", [128, 4, DI], BF16, nb=4)
    st1 = K.sbuf("st1", [128, 4, 4], F32, nb=4)
    st2 = K.sbuf("st2", [128, 4, 4], F32, nb=4)
    mv = K.sbuf("mv", [128, 8], F32)
    t1 = [K.sbuf(f"t1_{i}", [128, 512], F32) for i in range(2)]

    bcast_load(K, gB, norm_g, D)
    K.dma("sp", lgc[:], ln_g.rearrange("(c r) -> r c", r=128), writes=lgc.b, allow_slow_non_contiguous=True)
    K.dma("sp", lbc[:], ln_b.rearrange("(c r) -> r c", r=128), writes=lbc.b, allow_slow_non_contiguous=True)
    K.dma("sp", bsB[:].rearrange("p g t -> p (g t)"), b_s.rearrange("g t -> (g t)").partition_broadcast(128),
          writes=bsB.b)
    wsf_t = W["hn"]
    wsf = wsf_t[:].rearrange("p (g s) -> p g s", g=8)
    K.dma("pool", wsf, w_s.rearrange("g t s -> t g s"), writes=wsf_t.b)
    tp = W["tp"]
    for g in range(8):
        K.op("pe", lambda e, g=g: e.transpose(tp[:, g * 128:(g + 1) * 128], wsf[:, g, :], C["ident"][:]),
             reads=wsf_t.b + C["ident"].b, writes=tp.b)
    K.op("dve", lambda e: e.tensor_copy(out=WsT[:].rearrange("p g t -> p (g t)"), in_=tp[:]),
         reads=tp.b, writes=WsT.b)
    for g in range(8):
        K.op("pool", lambda e, g=g: e.affine_select(out=WsT[:, g, :], in_=WsT[:, g, :], pattern=[[1, 128]],
                                                    compare_op=ALU.is_ge, fill=0.0, base=0,
                                                    channel_multiplier=-1), reads=WsT.b, writes=WsT.b)

    psr = P.next()
    for g in range(8):
        K.op("pe", lambda e, g=g: e.matmul(psr[:, 0:128], lhsT=C["ones_bf"][:], rhs=WsT[:, g, :], start=True, stop=True),
             reads=C["ones_bf"].b + WsT.b, writes=psr.b)
        for fl in range(2):
            fc = 2 * g + fl
            K.op("dve", lambda e, g=g, fc=fc: e.scalar_tensor_tensor(
                out=bias2[:, fc, :], in0=psr[:, 0:128], scalar=lbc[:, fc:fc + 1], in1=bsB[:, g, :],
                op0=ALU.mult, op1=ALU.add), reads=psr.b + lbc.b + bsB.b, writes=bias2.b)
    w_in_v = w_in.rearrange("(k p) n -> p k n", p=128)
    w_out_v = w_out.rearrange("(k p) n -> p k n", p=128)
    CG_ORDER = (4, 5, 6, 7, 0, 1, 2, 3, 8, 9, 10, 11)
    wseq = [cg for _ in range(4) for cg in CG_ORDER]
    wst = {"issued": 0}

    def issue_w(upto):
        while wst["issued"] < min(upto, len(wseq)):
            i = wst["issued"]; cgi = wseq[i]
            wbi = wbuf[i % NWB]
            K.dma("pool", wbi[:], w_in_v[:, :, cgi * 512:(cgi + 1) * 512], writes=wbi.b)
            wst["issued"] += 1
    issue_w(NWB - 1)
    nload = 0
    for sb in range(4):
        hnT = hnTs[sb % 2]
        if sb == 0:
            norm_transpose_sb(K, C, W, P, h, 0, gB, hnT)
        def emit_ln():
            for j in range(4):
                K.op("dve", lambda e, j=j: e.reduce_sum(out=mv[:, 0:1], in_=st1[:, j, :], axis=AX.X),
                     reads=[st1.b[j]], writes=mv.b)
                K.op("dve", lambda e, j=j: e.reduce_sum(out=mv[:, 1:2], in_=st2[:, j, :], axis=AX.X),
                     reads=[st2.b[j]], writes=mv.b)
                K.op("dve", lambda e: e.tensor_scalar(out=mv[:, 2:4], in0=mv[:, 0:2], scalar1=1.0 / DI, scalar2=None,
                                                      op0=ALU.mult), reads=mv.b, writes=mv.b)
                K.op("dve", lambda e: e.tensor_tensor(out=mv[:, 4:5], in0=mv[:, 2:3], in1=mv[:, 2:3], op=ALU.mult),
                     reads=mv.b, writes=mv.b)
                K.op("dve", lambda e: e.tensor_tensor(out=mv[:, 5:6], in0=mv[:, 3:4], in1=mv[:, 4:5], op=ALU.subtract),
                     reads=mv.b, writes=mv.b)
                K.op("dve", lambda e: e.tensor_scalar(out=mv[:, 6:7], in0=mv[:, 5:6], scalar1=EPS, scalar2=None,
                                                      op0=ALU.add), reads=mv.b, writes=mv.b)
                K.op("act", lambda e: e.sqrt(out=mv[:, 6:7], in_=mv[:, 6:7]), reads=mv.b, writes=mv.b)
                K.op("dve", lambda e: e.reciprocal(out=mv[:, 6:7], in_=mv[:, 6:7]), reads=mv.b, writes=mv.b)
                K.op("dve", lambda e, j=j: e.tensor_scalar(out=vtok[:, j, :], in0=vtok[:, j, :], scalar1=mv[:, 2:3],
                                                           scalar2=mv[:, 6:7], op0=ALU.subtract, op1=ALU.mult),
                     reads=[vtok.b[j]] + mv.b, writes=[vtok.b[j]])
        def emit_spatial():
            nt = 0
            for jp in range(2):
                for g in range(8):
                    ps = P.next()
                    for fl in range(2):
                        for jl in range(2):
                            fc = 2 * g + fl; j = 2 * jp + jl
                            K.op("pe", lambda e, fc=fc, j=j, fl=fl, jl=jl, ps=ps, g=g: e.matmul(
                                ps[:, (fl * 2 + jl) * 128:(fl * 2 + jl + 1) * 128],
                                lhsT=vtok[:, j, fc * 128:(fc + 1) * 128], rhs=WsT[:, g, :], start=True, stop=True),
                                reads=[vtok.b[j]] + WsT.b, writes=ps.b)
                    tt = t1[nt % 2]; nt += 1
                    for fl in range(2):
                        fc = 2 * g + fl
                        K.op("dve", lambda e, ps=ps, tt=tt, fc=fc, fl=fl: e.scalar_tensor_tensor(
                            out=tt[:, fl * 256:(fl + 1) * 256].rearrange("p (a t) -> p a t", a=2),
                            in0=ps[:, fl * 256:(fl + 1) * 256].rearrange("p (a t) -> p a t", a=2),
                            scalar=lgc[:, fc:fc + 1], in1=bias2[:, fc:fc + 1, :].to_broadcast([128, 2, 128]),
                            op0=ALU.mult, op1=ALU.add), reads=ps.b + lgc.b + bias2.b, writes=tt.b)
                    usl = uT[:, 2 * g:2 * g + 2, jp * 256:(jp + 1) * 256]
                    K.op("pool", lambda e, tt=tt, usl=usl: e.tensor_tensor(
                        out=usl, in0=tt[:].rearrange("p (a t) -> p a t", a=2), in1=usl, op=ALU.mult),
                        reads=tt.b + [uT.b[2 * g], uT.b[2 * g + 1]], writes=[uT.b[2 * g], uT.b[2 * g + 1]])
        for cg in CG_ORDER:
            issue_w(nload + NWB)
            wb = wbuf[nload % NWB]; nload += 1
            if cg == 9:
                K.dma("pool", wo_tiles[0][:], w_out_v[:, :, 0:256], writes=wo_tiles[0].b)
            kind = cg // 4
            if kind != 1:
                dstT = uT if kind == 0 else szT
                fn = AF.Gelu_apprx_tanh if kind == 0 else AF.Silu
                for sub in range(4):
                    fc = (cg % 4) * 4 + sub
                    ps = P.next()
                    for k in range(8):
                        K.op("pe", lambda e, k=k, sub=sub, ps=ps, wb=wb, hnT=hnT: e.matmul(
                            ps[:], lhsT=wb[:, k, sub * 128:(sub + 1) * 128], rhs=hnT[:, k, :],
                            start=(k == 0), stop=(k == 7)), reads=wb.b + hnT.b, writes=ps.b)
                    K.op("act", lambda e, fc=fc, ps=ps, dstT=dstT, fn=fn: e.activation(
                        out=dstT[:, fc, :], in_=ps[:], func=fn), reads=ps.b, writes=[dstT.b[fc]])
            else:
                cgv = cg % 4
                for j in range(4):
                    ps = P.next()
                    for k in range(8):
                        K.op("pe", lambda e, k=k, j=j, ps=ps, wb=wb, hnT=hnT: e.matmul(
                            ps[:], lhsT=hnT[:, k, j * 128:(j + 1) * 128], rhs=wb[:, k, :],
                            start=(k == 0), stop=(k == 7)), reads=wb.b + [hnT.b[j]], writes=ps.b)
                    K.op("act", lambda e, j=j, ps=ps, cgv=cgv: e.activation(
                        out=vtok[:, j, cgv * 512:(cgv + 1) * 512], in_=ps[:], func=AF.Gelu_apprx_tanh,
                        accum_out=st1[:, j, cgv:cgv + 1]), reads=ps.b, writes=[vtok.b[j], st1.b[j]])
                    K.op("act", lambda e, j=j, cgv=cgv: e.activation(
                        out=W["junk"][:, 0:512], in_=vtok[:, j, cgv * 512:(cgv + 1) * 512], func=AF.Square,
                        accum_out=st2[:, j, cgv:cgv + 1]), reads=[vtok.b[j]], writes=W["junk"].b + [st2.b[j]])
                if cg == 7:
                    emit_ln()
            if cg == 3:
                emit_spatial()
        for fc in range(16):
            K.op("dve", lambda e, fc=fc: e.tensor_tensor(out=uT[:, fc, :], in0=uT[:, fc, :], in1=szT[:, fc, :], op=ALU.mult),
                 reads=[uT.b[fc], szT.b[fc]], writes=[uT.b[fc]])
        for cgo in range(4):
            if cgo == 1 and sb + 1 < 4:
                norm_transpose_sb(K, C, W, P, h, sb + 1, gB, hnTs[(sb + 1) % 2])
            wo = wo_tiles[cgo % 2]
            if cgo + 1 < 4:
                wn = wo_tiles[(cgo + 1) % 2]
                K.dma("pool", wn[:], w_out_v[:, :, (cgo + 1) * 256:(cgo + 2) * 256], writes=wn.b)
            for j in range(4):
                blk = sb * 4 + j
                ps = P.next()
                for k in range(16):
                    K.op("pe", lambda e, k=k, j=j, ps=ps, wo=wo: e.matmul(
                        ps[:, 0:256], lhsT=uT[:, k, j * 128:(j + 1) * 128], rhs=wo[:, k, :],
                        start=(k == 0), stop=(k == 15)), reads=wo.b + [uT.b[k]], writes=ps.b)
                hs = h[:, blk, cgo * 256:(cgo + 1) * 256]
                K.op("dve", lambda e, hs=hs, ps=ps: e.tensor_tensor(out=hs, in0=hs, in1=ps[:, 0:256], op=ALU.add),
                     reads=ps.b + [h.b[blk]], writes=[h.b[blk]])
    K.barrier()
    K.stack_pop()


import math as _math

PI = _math.pi


def tt(K, eng, out, in0, in1, op, reads, writes):
    return K.op(eng, lambda e: e.tensor_tensor(out=out, in0=in0, in1=in1, op=op), reads=reads, writes=writes)


def ts(K, eng, out, in0, s1, op0, reads, writes, s2=None, op1=None):
    if op1 is None:
        return K.op(eng, lambda e: e.tensor_scalar(out=out, in0=in0, scalar1=s1, scalar2=None, op0=op0),
                    reads=reads, writes=writes)
    return K.op(eng, lambda e: e.tensor_scalar(out=out, in0=in0, scalar1=s1, scalar2=s2, op0=op0, op1=op1),
                reads=reads, writes=writes)


def stt(K, eng, out, in0, scalar, in1, op0, op1, reads, writes):
    return K.op(eng, lambda e: e.scalar_tensor_tensor(out=out, in0=in0, scalar=scalar, in1=in1, op0=op0, op1=op1),
                reads=reads, writes=writes)


def act(K, out, in_, func, reads, writes, **kw):
    return K.op("act", lambda e: e.activation(out=out, in_=in_, func=func, **kw), reads=reads, writes=writes)


def emit_s5_prep(K, C, prm, p, S, cmask, PRE):
    L = contextlib.ExitStack(); K.stack_push(L)
    a_re, a_im, log_step = prm[p + "a_re"], prm[p + "a_im"], prm[p + "log_step"]
    shp = [128, 64]
    def T(name, shape=shp, dt=F32):
        return K.sbuf(name, shape, dt)
    are, aim, lst = PRE["are"], PRE["aim"], PRE["lst"]
    step, dr, th, mag, imag = T("step"), T("dr"), T("th"), T("mag"), T("imag")
    act(K, step[:], lst[:], AF.Exp, lst.b, step.b)
    tt(K, "dve", dr[:], are[:], step[:], ALU.mult, are.b + step.b, dr.b)
    tt(K, "dve", th[:], aim[:], step[:], ALU.mult, aim.b + step.b, th.b)
    act(K, mag[:], dr[:], AF.Exp, dr.b, mag.b)
    act(K, imag[:], dr[:], AF.Exp, dr.b, imag.b, scale=-1.0)
    sn, cs, t1, t2 = T("sn"), T("cs"), T("t1"), T("t2")
    hpi = K.sbuf("hpi", [128, 1], F32)
    K.op("dve", lambda e: e.memset(hpi[:], PI / 2), writes=hpi.b)
    act(K, sn[:], th[:], AF.Sin, th.b, sn.b, scale=1.0 / 16)
    act(K, cs[:], th[:], AF.Sin, th.b + hpi.b, cs.b, scale=1.0 / 16, bias=hpi[:, 0:1])
    for _ in range(4):
        tt(K, "dve", t1[:], cs[:], cs[:], ALU.mult, cs.b, t1.b)
        tt(K, "dve", t2[:], sn[:], sn[:], ALU.mult, sn.b, t2.b)
        stt(K, "dve", sn[:], cs[:], 2.0, sn[:], ALU.mult, ALU.mult, cs.b + sn.b, sn.b)
        tt(K, "dve", cs[:], t1[:], t2[:], ALU.subtract, t1.b + t2.b, cs.b)
    zr, zi, wr, wi = T("zr"), T("zi"), T("wr"), T("wi")
    tt(K, "dve", zr[:], mag[:], cs[:], ALU.mult, mag.b + cs.b, zr.b)
    tt(K, "dve", zi[:], mag[:], sn[:], ALU.mult, mag.b + sn.b, zi.b)
    tt(K, "dve", wr[:], imag[:], cs[:], ALU.mult, imag.b + cs.b, wr.b)
    stt(K, "dve", wi[:], imag[:], -1.0, sn[:], ALU.mult, ALU.mult, imag.b + sn.b, wi.b)
    nr, den, kr, ki, tmp = T("nr"), T("den"), T("kr"), T("ki"), T("tmpk")
    ts(K, "dve", nr[:], zr[:], -1.0, ALU.add, zr.b, nr.b)
    tt(K, "dve", den[:], are[:], are[:], ALU.mult, are.b, den.b)
    tt(K, "dve", tmp[:], aim[:], aim[:], ALU.mult, aim.b, tmp.b)
    tt(K, "dve", den[:], den[:], tmp[:], ALU.add, den.b + tmp.b, den.b)
    K.op("dve", lambda e: e.reciprocal(out=den[:], in_=den[:]), reads=den.b, writes=den.b)
    tt(K, "dve", kr[:], nr[:], are[:], ALU.mult, nr.b + are.b, kr.b)
    tt(K, "dve", tmp[:], zi[:], aim[:], ALU.mult, zi.b + aim.b, tmp.b)
    tt(K, "dve", kr[:], kr[:], tmp[:], ALU.add, kr.b + tmp.b, kr.b)
    tt(K, "dve", kr[:], kr[:], den[:], ALU.mult, kr.b + den.b, kr.b)
    tt(K, "dve", ki[:], zi[:], are[:], ALU.mult, zi.b + are.b, ki.b)
    tt(K, "dve", tmp[:], nr[:], aim[:], ALU.mult, nr.b + aim.b, tmp.b)
    tt(K, "dve", ki[:], ki[:], tmp[:], ALU.subtract, ki.b + tmp.b, ki.b)
    tt(K, "dve", ki[:], ki[:], den[:], ALU.mult, ki.b + den.b, ki.b)

    tp = K.psum("tp5", [128, 1024], BF16)
    L2 = contextlib.ExitStack(); K.stack_push(L2)
    bnr = K.sbuf("bnr", [128, 64, 16], F32); bni = K.sbuf("bni", [128, 64, 16], F32)
    K.dma("sp", bnr[:], prm[p + "b_re"].rearrange("(j gl) p h -> (gl p) j h", gl=2), writes=bnr.b)
    K.dma("sp", bni[:], prm[p + "b_im"].rearrange("(j gl) p h -> (gl p) j h", gl=2), writes=bni.b)
    bbr = K.sbuf("bbr", [128, 64, 16], F32); bbi = K.sbuf("bbi", [128, 64, 16], F32)
    btmp = K.sbuf("btmp", [128, 64, 16], F32)
    krb = kr[:].unsqueeze(2).to_broadcast([128, 64, 16]); kib = ki[:].unsqueeze(2).to_broadcast([128, 64, 16])
    tt(K, "dve", bbr[:], bnr[:], krb, ALU.mult, bnr.b + kr.b, bbr.b)
    tt(K, "dve", btmp[:], bni[:], kib, ALU.mult, bni.b + ki.b, btmp.b)
    tt(K, "dve", bbr[:], bbr[:], btmp[:], ALU.subtract, bbr.b + btmp.b, bbr.b)
    tt(K, "dve", bbi[:], bni[:], krb, ALU.mult, bni.b + kr.b, bbi.b)
    tt(K, "dve", btmp[:], bnr[:], kib, ALU.mult, bnr.b + ki.b, btmp.b)
    tt(K, "dve", bbi[:], bbi[:], btmp[:], ALU.add, bbi.b + btmp.b, bbi.b)
    pin = K.sbuf("pin", [128, 64, 128], BF16)
    stg = [K.sbuf(f"stg{i}", [128, 8, 128], BF16) for i in range(2)]
    ns = 0
    for ri, src in enumerate((bbr, bbi)):
        K.op("pool", lambda e: e.memset(pin[:], 0.0), writes=pin.b)
        for q in range(4):
            for gl in range(2):
                K.op("dve", lambda e, q=q, gl=gl, src=src: e.tensor_copy(
                    out=pin[gl * 64:(gl + 1) * 64, q::4, 32 * q + 16 * gl:32 * q + 16 * gl + 16],
                    in_=src[gl * 64:(gl + 1) * 64, q::4, :]), reads=src.b, writes=pin.b)
        for i8 in range(8):
            for jl in range(8):
                j = i8 * 8 + jl
                K.op("pe", lambda e, j=j, jl=jl: e.transpose(tp[:, jl * 128:(jl + 1) * 128], pin[:, j, :],
                                                             C["ident"][:]),
                     reads=pin.b + C["ident"].b, writes=tp.b)
            sg = stg[ns % 2]; ns += 1
            K.op("dve", lambda e, sg=sg: e.tensor_copy(out=sg[:].rearrange("r j c -> r (j c)"), in_=tp[:]),
                 reads=tp.b, writes=sg.b)
            K.dma("sp", S["bpad"][i8 * 8:(i8 + 1) * 8, ri].rearrange("j r c -> r j c"), sg[:], reads=sg.b,
                  writes=[S["bpad_b"]])
    K.barrier(); K.stack_pop()

    L3 = contextlib.ExitStack(); K.stack_push(L3)
    cin = K.sbuf("cin", [128, 16, 128], BF16)
    cn = K.sbuf("cn", [128, 16, 64], F32)
    for ri, (nm, sgn, dst) in enumerate((("c_re", 1.0, S["CTr"]), ("c_im", -1.0, S["CTn"]), ("c_re", -1.0, S["CTrn"]))):
        K.dma("sp", cn[:], prm[p + nm].rearrange("(jj q gl) h c -> (q gl h) jj c", q=4, gl=2), writes=cn.b)
        for gl in range(2):
            ts(K, "dve", cin[:, :, gl * 64:(gl + 1) * 64], cn[:], cmask[:, gl:gl + 1], ALU.mult,
               cn.b + cmask.b, cin.b, s2=sgn, op1=ALU.mult)
        for i8 in range(2):
            for jl in range(8):
                jj = i8 * 8 + jl
                K.op("pe", lambda e, jj=jj, jl=jl: e.transpose(tp[:, jl * 128:(jl + 1) * 128], cin[:, jj, :],
                                                               C["ident"][:]),
                     reads=cin.b + C["ident"].b, writes=tp.b)
            K.op("dve", lambda e, i8=i8, dst=dst: e.tensor_copy(
                out=dst[:, i8 * 8:(i8 + 1) * 8, :].rearrange("r j c -> r (j c)"), in_=tp[:]),
                reads=tp.b, writes=dst.b)
    K.barrier(); K.stack_pop()

    L4 = contextlib.ExitStack(); K.stack_push(L4)
    NB = 2
    Lp = K.sbuf("Lp", [128, 64, 4, 32], F32)
    Hp = K.sbuf("Hp", [128, 64, 4, 16], F32)
    cur = K.sbuf("pcur", [128, 64, 4], F32)
    cur2 = K.sbuf("pcur2", [128, 64, 4], F32)
    ctmp = K.sbuf("pctmp", [128, 64, 4], F32)
    ptm = [K.sbuf(f"pptm{i}", [128, 64, 16], F32) for i in range(2)]
    for ti in range(1):
        eng = "dve" if ti == 0 else "pool"
        sr, si = (zr, zi) if ti == 0 else (wr, wi)
        cr_ = cur[:, :, 2 * ti:2 * ti + 1]; ci_ = cur[:, :, 2 * ti + 1:2 * ti + 2]
        K.op(eng, lambda e, cr_=cr_, sr=sr: e.tensor_copy(out=cr_, in_=sr[:].unsqueeze(2)), reads=sr.b, writes=cur.b)
        K.op(eng, lambda e, ci_=ci_, si=si: e.tensor_copy(out=ci_, in_=si[:].unsqueeze(2)), reads=si.b, writes=cur.b)
        pt = ptm[ti]
        for (Tt, nsteps) in ((Lp, 5), (Hp, 4)):
            Tr = Tt[:, :, 2 * ti, :]; Ti = Tt[:, :, 2 * ti + 1, :]
            K.op(eng, lambda e, Tr=Tr: e.memset(Tr[:, :, 0:1], 1.0), writes=Tt.b)
            K.op(eng, lambda e, Ti=Ti: e.memset(Ti[:, :, 0:1], 0.0), writes=Tt.b)
            for k in range(nsteps):
                n = 1 << k
                crb = cr_.to_broadcast([128, 64, n]); cib = ci_.to_broadcast([128, 64, n])
                A_r = Tr[:, :, 0:n]; A_i = Ti[:, :, 0:n]; O_r = Tr[:, :, n:2 * n]; O_i = Ti[:, :, n:2 * n]
                tm = pt[:, :, 0:n]
                tt(K, eng, tm, A_i, cib, ALU.mult, Tt.b + cur.b, pt.b)
                tt(K, eng, O_r, A_r, crb, ALU.mult, Tt.b + cur.b, Tt.b)
                tt(K, eng, O_r, O_r, tm, ALU.subtract, Tt.b + pt.b, Tt.b)
                tt(K, eng, tm, A_i, crb, ALU.mult, Tt.b + cur.b, pt.b)
                tt(K, eng, O_i, A_r, cib, ALU.mult, Tt.b + cur.b, Tt.b)
                tt(K, eng, O_i, O_i, tm, ALU.add, Tt.b + pt.b, Tt.b)
                c2r = cur2[:, :, 2 * ti:2 * ti + 1]; c2i = cur2[:, :, 2 * ti + 1:2 * ti + 2]
                t_a = ctmp[:, :, 2 * ti:2 * ti + 1]; t_b = ctmp[:, :, 2 * ti + 1:2 * ti + 2]
                tt(K, eng, t_a, cr_, cr_, ALU.mult, cur.b, ctmp.b)
                tt(K, eng, t_b, ci_, ci_, ALU.mult, cur.b, ctmp.b)
                tt(K, eng, c2r, t_a, t_b, ALU.subtract, ctmp.b, cur2.b)
                tt(K, eng, c2i, cr_, ci_, ALU.mult, cur.b, cur2.b)
                tt(K, eng, c2i, c2i, c2i, ALU.add, cur2.b, cur2.b)
                K.op(eng, lambda e, cr_=cr_, c2r=c2r: e.tensor_copy(out=cr_, in_=c2r), reads=cur2.b, writes=cur.b)
                K.op(eng, lambda e, ci_=ci_, c2i=c2i: e.tensor_copy(out=ci_, in_=c2i), reads=cur2.b, writes=cur.b)
        if ti == 0:
            K.op(eng, lambda e, cr_=cr_: e.tensor_copy(out=S["L5"][:, :, 0:1], in_=cr_), reads=cur.b, writes=S["L5"].b)
            K.op(eng, lambda e, ci_=ci_: e.tensor_copy(out=S["L5"][:, :, 1:2], in_=ci_), reads=cur.b, writes=S["L5"].b)
    bv = K.sbuf("pbv", [128, 32], F32); on32 = K.sbuf("pon32", [128, 32], F32)
    K.op("dve", lambda e: e.memset(on32[:], 1.0), writes=on32.b)
    K.op("dve", lambda e: e.tensor_tensor_scan(out=bv[:], data0=on32[:], data1=on32[:], initial=-1.0,
                                               op0=ALU.mult, op1=ALU.add), reads=on32.b, writes=bv.b)
    gL = K.sbuf("pgL", [128, 64, 32], F32); gH = K.sbuf("pgH", [128, 64, 16], F32); gHn = K.sbuf("pgHn", [128, 64, 16], F32)
    tt(K, "dve", gL[:], dr[:].unsqueeze(2).to_broadcast([128, 64, 32]), bv[:].unsqueeze(1).to_broadcast([128, 64, 32]),
       ALU.mult, dr.b + bv.b, gL.b)
    tt(K, "dve", gH[:], dr[:].unsqueeze(2).to_broadcast([128, 64, 16]), bv[:, 0:16].unsqueeze(1).to_broadcast([128, 64, 16]),
       ALU.mult, dr.b + bv.b, gH.b)
    act(K, gL[:], gL[:], AF.Exp, gL.b, gL.b, scale=-2.0)
    act(K, gH[:], gH[:], AF.Exp, gH.b, gH.b, scale=-64.0)
    ts(K, "dve", gHn[:], gH[:], -1.0, ALU.mult, gH.b, gHn.b)
    tab = [K.sbuf(f"ptab{i}", [128, NB, 4, 512], F32, nb=4) for i in range(2)]
    otm = [K.sbuf(f"potm{i}", [128, NB, 512], F32) for i in range(3)]
    for bi in range(64 // NB):
        tb = tab[bi % 2]
        j0 = bi * NB
        shp4 = [128, NB, 16, 32]
        omA = otm[0]; omB = otm[1]
        Hr = Hp[:, j0:j0 + NB, 0, :].unsqueeze(3).to_broadcast(shp4)
        Hi = Hp[:, j0:j0 + NB, 1, :].unsqueeze(3).to_broadcast(shp4)
        Lr = Lp[:, j0:j0 + NB, 0, :].unsqueeze(2).to_broadcast(shp4)
        Li = Lp[:, j0:j0 + NB, 1, :].unsqueeze(2).to_broadcast(shp4)
        Tr = tb[:, :, 0, :].rearrange("r j (a b) -> r j a b", a=16)
        Ti = tb[:, :, 1, :].rearrange("r j (a b) -> r j a b", a=16)
        oA = omA[:].rearrange("r j (a b) -> r j a b", a=16)
        oB = omB[:].rearrange("r j (a b) -> r j a b", a=16)
        rd = Hp.b + Lp.b
        tt(K, "dve", oA, Hi, Li, ALU.mult, rd, omA.b)
        tt(K, "dve", Tr, Hr, Lr, ALU.mult, rd, [tb.b[0]])
        tt(K, "dve", oB, Hi, Lr, ALU.mult, rd, omB.b)
        tt(K, "dve", Ti, Hr, Li, ALU.mult, rd, [tb.b[1]])
        tt(K, "dve", Tr, Tr, oA, ALU.subtract, [tb.b[0]] + omA.b, [tb.b[0]])
        tt(K, "dve", Ti, Ti, oB, ALU.add, [tb.b[1]] + omB.b, [tb.b[1]])
        omp = otm[2]; omq = otm[2]
        g4 = omp[:].rearrange("r j (a b) -> r j a b", a=16)
        g4n = omq[:].rearrange("r j (a b) -> r j a b", a=16)
        gHb = gH[:, j0:j0 + NB, :].unsqueeze(3).to_broadcast(shp4)
        gHnb = gHn[:, j0:j0 + NB, :].unsqueeze(3).to_broadcast(shp4)
        gLb = gL[:, j0:j0 + NB, :].unsqueeze(2).to_broadcast(shp4)
        tt(K, "pool", g4, gHb, gLb, ALU.mult, gH.b + gL.b, omp.b)
        tt(K, "pool", tb[:, :, 2, :], tb[:, :, 0, :], omp[:], ALU.mult, [tb.b[0]] + omp.b, [tb.b[2]])
        tt(K, "pool", g4n, gHnb, gLb, ALU.mult, gHn.b + gL.b, omq.b)
        tt(K, "pool", tb[:, :, 3, :], tb[:, :, 1, :], omq[:], ALU.mult, [tb.b[1]] + omq.b, [tb.b[3]])
        K.dma("sp", S["tabs"][j0:j0 + NB].rearrange("j t r c -> r j t c"), tb[:], reads=tb.b,
              writes=[S["tabs_b"]])
    K.barrier(); K.stack_pop()
    K.barrier(); K.stack_pop()


def emit_s5(K, C, h, prm, p, kflag, cmask, PRE):
    nc = K.nc
    Lall = contextlib.ExitStack(); K.stack_push(Lall)
    gB = K.sbuf("gB5", [128, D], F32)
    bcast_load(K, gB, prm[p + "norm_g"], D)
    dsk = PRE["dsk"]; bgl = PRE["bgl"]
    S = {}
    S["tabs"] = K.dram("s5_tabs", [64, 4, 128, 512], F32, "Internal")
    S["bpad"] = K.dram("s5_bpad", [64, 2, 128, 128], BF16, "Internal")
    S["tabs_b"] = Buf("tabs"); S["bpad_b"] = Buf("bpad")
    S["CTr"] = K.sbuf("CTr", [128, 16, 128], BF16)
    S["CTn"] = K.sbuf("CTn", [128, 16, 128], BF16)
    S["CTrn"] = K.sbuf("CTrn", [128, 16, 128], BF16)
    S["L5"] = K.sbuf("L5", [128, 64, 2], F32)
    emit_s5_prep(K, C, prm, p, S, cmask, PRE)
    uT = K.sbuf("uT5", [128, 16, NTOK], BF16, nb=64)

    w_in_v = prm[p + "w_in"].rearrange("(k r) n -> r k n", r=128)
    w_glu_v = prm[p + "w_glu"].rearrange("(k r) n -> r k n", r=128)
    w_out_v = prm[p + "w_out"].rearrange("(k r) n -> r k n", r=128)

    def mkW():
        W = {}
        W["junk"] = K.sbuf("junk5", [128, 1024], F32)
        W["ss"] = K.sbuf("ss5", [128, 8], F32)
        W["rstd"] = K.sbuf("rstd5", [128, 8], F32)
        W["hn"] = K.sbuf("hn5", [128, D], BF16)
        W["tp"] = K.psum("tpA", [128, 1024], BF16)
        return W

    LA = contextlib.ExitStack(); K.stack_push(LA)
    W = mkW()
    P = PsumPool(K, 6)
    hnTs = [K.sbuf(f"hnT5_{i}", [128, 8, 512], BF16, nb=4) for i in range(2)]
    NWA = 4
    wbuf = [K.sbuf(f"wbA{i}", [128, 8, 512], BF16) for i in range(NWA)]
    wstA = {"issued": 0}

    def issue_a(upto):
        while wstA["issued"] < min(upto, 16):
            i = wstA["issued"]; cgi = i % 4
            K.dma("pool", wbuf[i % NWA][:], w_in_v[:, :, cgi * 512:(cgi + 1) * 512], writes=wbuf[i % NWA].b)
            wstA["issued"] += 1
    issue_a(NWA - 1)
    nl = 0
    for sb in range(4):
        hnT = hnTs[sb % 2]
        if sb == 0:
            norm_transpose_sb(K, C, W, P, h, 0, gB, hnT)
        for cg in range(4):
            if cg == 2 and sb + 1 < 4:
                norm_transpose_sb(K, C, W, P, h, sb + 1, gB, hnTs[(sb + 1) % 2])
            issue_a(nl + NWA)
            wb = wbuf[nl % NWA]; nl += 1
            for sub in range(4):
                fc = cg * 4 + sub
                ps = P.next()
                for k in range(8):
                    K.op("pe", lambda e, k=k, sub=sub, ps=ps, wb=wb, hnT=hnT: e.matmul(
                        ps[:], lhsT=wb[:, k, sub * 128:(sub + 1) * 128], rhs=hnT[:, k, :],
                        start=(k == 0), stop=(k == 7)), reads=wb.b + hnT.b, writes=ps.b)
                K.op("act", lambda e, fc=fc, ps=ps, sb=sb: e.copy(out=uT[:, fc, sb * 512:(sb + 1) * 512], in_=ps[:]),
                     reads=ps.b, writes=[uT.b[fc * 4 + sb]])
    K.barrier(); K.stack_pop()

    LB = contextlib.ExitStack(); K.stack_push(LB)
    psBU = [K.psum(f"psBU{i}", [128, 512], F32) for i in range(4)]
    psY = [K.psum(f"psY{i}", [128, 512], F32) for i in range(4)]
    ones = K.sbuf("ones5", [128, 512], F32)
    K.op("dve", lambda e: e.memset(ones[:], 1.0), writes=ones.b)
    tabt = [K.sbuf(f"tabt{i}", [128, 4, 512], F32) for i in range(2)]
    bpt = [K.sbuf(f"bpt{i}", [128, 2, 128], BF16) for i in range(2)]
    cst = K.sbuf("cst", [128, 64, 5, 2], F32, nb=64)
    NW = 3
    wk = [[K.sbuf(f"wk{s}_{i}", [128, 512], F32) for i in range(4)] for s in range(NW)]
    pk = [[K.sbuf(f"pk{s}_{i}", [128, 512], BF16) for i in range(4)] for s in range(2)]
    tny = K.sbuf("tny", [128, 4], F32)
    nl5i = K.sbuf("nl5i", [128, 64], F32)
    ts(K, "dve", nl5i[:].unsqueeze(2), S["L5"][:, :, 1:2], -1.0, ALU.mult, S["L5"].b, nl5i.b)
    LPre = contextlib.ExitStack(); K.stack_push(LPre)
    fpre = K.sbuf("fpre", [128, 64, 2], F32)
    accA = K.sbuf("accA5", [128, 64, 4, 4], F32, nb=64 * 12)
    itp = 0
    for j in range(64):
        jj = j // 4
        tb = tabt[j % 2]; bp = bpt[j % 2]
        K.dma("sp", tb[:, 2:4, :], S["tabs"][j, 2:4].rearrange("t r c -> r t c"), reads=[S["tabs_b"]], writes=tb.b)
        K.dma("sp", bp[:], S["bpad"][j].rearrange("t r c -> r t c"), reads=[S["bpad_b"]], writes=bp.b)
        Mr, Mi = tb[:, 2, :], tb[:, 3, :]
        T2 = wk[j % 2][1]; T3 = wk[j % 2][2]
        tt(K, "pool", T2[:], Mi, Mr, ALU.subtract, tb.b, T2.b)
        tt(K, "pool", T3[:], Mr, Mi, ALU.add, tb.b, T3.b)
        for blk in range(4):
            pR = psBU[(2 * itp) % 4]; pI = psBU[(2 * itp + 1) % 4]; pS = psY[itp % 4]
            itp += 1
            ub = [uT.b[jj * 4 + blk]]
            usl = uT[:, jj, blk * 512:(blk + 1) * 512]
            K.op("pe", lambda e, pR=pR, bp=bp, usl=usl: e.matmul(pR[:], lhsT=bp[:, 0, :], rhs=usl, start=True, stop=True),
                 reads=bp.b + ub, writes=pR.b)
            K.op("pe", lambda e, pI=pI, bp=bp, usl=usl: e.matmul(pI[:], lhsT=bp[:, 1, :], rhs=usl, start=True, stop=True),
                 reads=bp.b + ub, writes=pI.b)
            K.op("pe", lambda e, pS=pS, bp=bp, usl=usl: e.matmul(pS[:], lhsT=bp[:, 0, :], rhs=usl, start=True, stop=False),
                 reads=bp.b + ub, writes=pS.b)
            K.op("pe", lambda e, pS=pS, bp=bp, usl=usl: e.matmul(pS[:], lhsT=bp[:, 1, :], rhs=usl, start=False, stop=True),
                 reads=bp.b + ub, writes=pS.b)
            for ai, (pp, mm, mb) in enumerate(((pS, Mr, tb.b), (pR, T2[:], T2.b), (pI, T3[:], T3.b))):
                jk = wk[2][ai]
                K.op("dve", lambda e, pp=pp, mm=mm, ai=ai, jk=jk, j=j, blk=blk: e.scalar_tensor_tensor(
                    out=jk[:], in0=pp[:], scalar=1.0, in1=mm, op0=ALU.mult, op1=ALU.mult, accum_out=accA[:, j, blk, ai:ai + 1]),
                    reads=pp.b + mb, writes=jk.b + [accA.b[j * 12 + blk * 3 + ai]])
    vv = K.sbuf("vv5", [128, 64, 2], F32); cc = K.sbuf("cc5", [128, 64, 2], F32); t4 = K.sbuf("t45", [128, 64, 4], F32)
    l5r_all = S["L5"][:, :, 0:1]; l5i_all = S["L5"][:, :, 1:2]
    for blk in range(4):
        tt(K, "dve", vv[:, :, 0:1], accA[:, :, blk, 0:1], accA[:, :, blk, 2:3], ALU.subtract, accA.b, vv.b)
        tt(K, "dve", vv[:, :, 1:2], accA[:, :, blk, 0:1], accA[:, :, blk, 1:2], ALU.add, accA.b, vv.b)
        if blk > 0:
            tt(K, "dve", vv[:], vv[:], cc[:], ALU.add, vv.b + cc.b, vv.b)
        dst = cc if blk < 3 else fpre
        tt(K, "dve", t4[:, :, 0:1], vv[:, :, 0:1], l5r_all, ALU.mult, vv.b + S["L5"].b, t4.b)
        tt(K, "dve", t4[:, :, 1:2], vv[:, :, 1:2], l5i_all, ALU.mult, vv.b + S["L5"].b, t4.b)
        tt(K, "dve", t4[:, :, 2:3], vv[:, :, 0:1], l5i_all, ALU.mult, vv.b + S["L5"].b, t4.b)
        tt(K, "dve", t4[:, :, 3:4], vv[:, :, 1:2], l5r_all, ALU.mult, vv.b + S["L5"].b, t4.b)
        tt(K, "dve", dst[:, :, 0:1], t4[:, :, 0:1], t4[:, :, 1:2], ALU.subtract, t4.b, dst.b)
        tt(K, "dve", dst[:, :, 1:2], t4[:, :, 2:3], t4[:, :, 3:4], ALU.add, t4.b, dst.b)
    cxi = K.dram("s5_cxi", [128, 128], F32, "Internal")
    cxo = K.dram("s5_cxo", [256, 128], F32, "Internal")
    bxi = Buf("cxi"); bxo = Buf("cxo")
    K.dma("sp", cxi, fpre[:].rearrange("r j t -> r (j t)"), reads=fpre.b, writes=[bxi])
    K.collective("AllGather", [cxi], [cxo], [bxi], [bxo])
    cext = K.sbuf("cext", [128, 64, 2], F32)
    K.dma("sp", cext[:].rearrange("r j t -> r (j t)"), cxo[0:128, :], reads=[bxo], writes=cext.b)
    ts(K, "dve", cst[:, :, 0, :], cext[:], kflag[:, 0:1], ALU.mult, cext.b + kflag.b, cst.b)
    K.barrier(); K.stack_pop()
    ytmp = [K.sbuf(f"ytmp{i}", [128, 512], F32) for i in range(1)]
    NIT = 256
    ctx = {}

    def S1(t):
        j, blk = t // 4, t % 4
        jj = j // 4
        tb = tabt[j % 2]; bp = bpt[j % 2]

        def load_pair(jn):
            tbn = tabt[jn % 2]; bpn = bpt[jn % 2]
            K.dma("sp", tbn[:], S["tabs"][jn].rearrange("t r c -> r t c"), reads=[S["tabs_b"]], writes=tbn.b)
            K.dma("sp", bpn[:], S["bpad"][jn].rearrange("t r c -> r t c"), reads=[S["bpad_b"]], writes=bpn.b)
        if t == 0:
            load_pair(0)
        if blk == 1 and j + 1 < 64:
            load_pair(j + 1)
        a, b, c, d = wk[t % NW]
        pR = psBU[(2 * t) % 4]; pI = psBU[(2 * t + 1) % 4]
        Mr, Mi = tb[:, 2, :], tb[:, 3, :]
        ub = [uT.b[jj * 4 + blk]]
        usl = uT[:, jj, blk * 512:(blk + 1) * 512]
        K.op("pe", lambda e: e.matmul(pR[:], lhsT=bp[:, 0, :], rhs=usl, start=True, stop=True), reads=bp.b + ub, writes=pR.b)
        K.op("pe", lambda e: e.matmul(pI[:], lhsT=bp[:, 1, :], rhs=usl, start=True, stop=True), reads=bp.b + ub, writes=pI.b)
        tt(K, "dve", a[:], pR[:], Mr, ALU.mult, pR.b + tb.b, a.b)
        tt(K, "dve", b[:], pI[:], Mi, ALU.mult, pI.b + tb.b, b.b)
        tt(K, "dve", c[:], pR[:], Mi, ALU.mult, pR.b + tb.b, c.b)
        tt(K, "dve", d[:], pI[:], Mr, ALU.mult, pI.b + tb.b, d.b)

    def S2(t):
        a, b, c, d = wk[t % NW]
        tt(K, "pool", a[:], a[:], b[:], ALU.subtract, a.b + b.b, a.b)
        tt(K, "pool", c[:], c[:], d[:], ALU.add, c.b + d.b, c.b)

    def S3(t):
        j, blk = t // 4, t % 4
        a, b, c, d = wk[t % NW]
        cb = [cst.b[j]]
        K.op("dve", lambda e: e.tensor_tensor_scan(out=b[:], data0=ones[:], data1=a[:], initial=cst[:, j, blk, 0:1],
                                                   op0=ALU.mult, op1=ALU.add), reads=ones.b + a.b + cb, writes=b.b)
        K.op("dve", lambda e: e.tensor_tensor_scan(out=d[:], data0=ones[:], data1=c[:], initial=cst[:, j, blk, 1:2],
                                                   op0=ALU.mult, op1=ALU.add), reads=ones.b + c.b + cb, writes=d.b)
        l5r = S["L5"][:, j, 0:1]; l5i = S["L5"][:, j, 1:2]
        if blk < 3:
            ts(K, "dve", tny[:, 0:1], d[:, 511:512], nl5i[:, j:j + 1], ALU.mult, d.b + nl5i.b, tny.b)
            stt(K, "dve", cst[:, j, blk + 1, 0:1], b[:, 511:512], l5r, tny[:, 0:1], ALU.mult, ALU.add,
                b.b + tny.b + S["L5"].b, cb)
            ts(K, "dve", tny[:, 1:2], d[:, 511:512], l5r, ALU.mult, d.b + S["L5"].b, tny.b)
            stt(K, "dve", cst[:, j, blk + 1, 1:2], b[:, 511:512], l5i, tny[:, 1:2], ALU.mult, ALU.add,
                b.b + tny.b + S["L5"].b, cb)

    def S4(t):
        j = t // 4
        tb = tabt[j % 2]
        Dr, Di = tb[:, 0, :], tb[:, 1, :]
        a, b, c, d = wk[t % NW]; p0, p1, p2, p3 = pk[t % 2]
        tt(K, "pool", p0[:], b[:], Dr, ALU.mult, b.b + tb.b, p0.b)
        tt(K, "pool", p1[:], d[:], Di, ALU.mult, d.b + tb.b, p1.b)
        tt(K, "pool", p2[:], b[:], Di, ALU.mult, b.b + tb.b, p2.b)
        tt(K, "dve", p3[:], d[:], Dr, ALU.mult, d.b + tb.b, p3.b)

    def S5(t):
        j, blk = t // 4, t % 4
        jj, q = j // 4, j % 4
        p0, p1, p2, p3 = pk[t % 2]
        py = psY[blk]
        sl = slice(32 * q, 32 * q + 32)
        for i, (wt_, pp) in enumerate(((S["CTr"], p0), (S["CTrn"], p1), (S["CTn"], p2), (S["CTn"], p3))):
            K.op("pe", lambda e, wt_=wt_, pp=pp, i=i: e.matmul(py[sl, :], lhsT=wt_[:, jj, sl], rhs=pp[:],
                                                          start=(i == 0), stop=(i == 3), tile_position=(0, 32 * q)),
                 reads=wt_.b + pp.b, writes=py.b)
        if q == 3:
            yt = ytmp[0]
            usl = uT[:, jj, blk * 512:(blk + 1) * 512]; ub = [uT.b[jj * 4 + blk]]
            stt(K, "dve", yt[:], usl, dsk[:, jj:jj + 1], py[:], ALU.mult, ALU.add, ub + dsk.b + py.b, yt.b)
            act(K, usl, yt[:], AF.Gelu_apprx_tanh, yt.b, ub)

    for t in range(NIT + 2):
        if t < NIT:
            S1(t); S2(t)
        if 0 <= t - 1 < NIT:
            S3(t - 1); S4(t - 1)
        if 0 <= t - 2 < NIT:
            S5(t - 2)
    K.barrier(); K.stack_pop()

    LC = contextlib.ExitStack(); K.stack_push(LC)
    W = mkW()
    P = PsumPool(K, 6)
    hnT = K.sbuf("hnT5c", [128, 8, 512], BF16, nb=4)
    wbs = [K.sbuf(f"wbC{i}", [128, 4096], BF16) for i in range(3)]
    szT = K.sbuf("szT5", [128, 16, 512], BF16, nb=16)
    gT = szT
    gtmp = [K.sbuf(f"gtmp{i}", [128, 512], F32) for i in range(2)]
    nl = 0; nt = 0
    for sb in range(4):
        norm_transpose_sb(K, C, W, P, h, sb, gB, hnT)
        for cg in range(4):
            wt = wbs[nl % 3]; nl += 1
            wb = wt[:].rearrange("r (k n) -> r k n", k=8)
            K.dma("pool", wb, w_in_v[:, :, 2048 + cg * 512:2048 + (cg + 1) * 512], writes=wt.b)
            for sub in range(4):
                fc = cg * 4 + sub
                ps = P.next()
                for k in range(8):
                    K.op("pe", lambda e, k=k, sub=sub, ps=ps, wb=wb: e.matmul(
                        ps[:], lhsT=wb[:, k, sub * 128:(sub + 1) * 128], rhs=hnT[:, k, :],
                        start=(k == 0), stop=(k == 7)), reads=wt.b + hnT.b, writes=ps.b)
                act(K, szT[:, fc, :], ps[:], AF.Silu, ps.b, [szT.b[fc]])
        for cg in range(8):
            wt = wbs[nl % 3]; nl += 1
            wg = wt[:].rearrange("r (k n) -> r k n", k=16)
            K.dma("pool", wg, w_glu_v[:, :, cg * 256:(cg + 1) * 256], writes=wt.b)
            for sub in range(2):
                fc = cg * 2 + sub
                ps = P.next()
                for k in range(16):
                    K.op("pe", lambda e, k=k, sub=sub, ps=ps, wg=wg, sb=sb: e.matmul(
                        ps[:], lhsT=wg[:, k, sub * 128:(sub + 1) * 128], rhs=uT[:, k, sb * 512:(sb + 1) * 512],
                        start=(k == 0), stop=(k == 15)), reads=wt.b + [uT.b[k * 4 + sb]], writes=ps.b)
                gt = gtmp[nt % 2]; nt += 1
                act(K, gt[:], ps[:], AF.Sigmoid, ps.b + bgl.b, gt.b, bias=bgl[:, fc:fc + 1])
                tt(K, "dve", gt[:], gt[:], uT[:, fc, sb * 512:(sb + 1) * 512], ALU.mult,
                   gt.b + [uT.b[fc * 4 + sb]], gt.b)
                tt(K, "dve", szT[:, fc, :], gt[:], szT[:, fc, :], ALU.mult, gt.b + [szT.b[fc]], [szT.b[fc]])
        for cgo in range(4):
            wt = wbs[nl % 3]; nl += 1
            wo = wt[:].rearrange("r (k n) -> r k n", k=16)
            K.dma("pool", wo, w_out_v[:, :, cgo * 256:(cgo + 1) * 256], writes=wt.b)
            for jb in range(4):
                blk = sb * 4 + jb
                ps = P.next()
                for k in range(16):
                    K.op("pe", lambda e, k=k, jb=jb, ps=ps, wo=wo: e.matmul(
                        ps[:, 0:256], lhsT=gT[:, k, jb * 128:(jb + 1) * 128], rhs=wo[:, k, :],
                        start=(k == 0), stop=(k == 15)), reads=wt.b + [gT.b[k]], writes=ps.b)
                hs = h[:, blk, cgo * 256:(cgo + 1) * 256]
                tt(K, "dve", hs, hs, ps[:, 0:256], ALU.add, ps.b + [h.b[blk]], [h.b[blk]])
    K.barrier(); K.stack_pop()
    K.barrier(); K.stack_pop()


MLA_SCALE_F = 192 ** -0.5


def sincos_tables(K, ang, sin_out, cos_out, tmp, npi, ki, hpi):
    a_ap, a_b = ang; s_ap, s_b = sin_out; c_ap, c_b = cos_out; t_ap, t_b = tmp; k_ap, k_b = ki
    C1 = 6.28125; C2 = 2 * PI - C1
    np_ = s_ap.shape[0]
    ts(K, "dve", k_ap, a_ap, 1.0 / (2 * PI), ALU.mult, a_b, k_b)
    K.op("dve", lambda e: e.tensor_copy(out=t_ap, in_=k_ap), reads=k_b, writes=t_b)
    stt(K, "dve", a_ap, t_ap, -C1, a_ap, ALU.mult, ALU.add, t_b + a_b, a_b)
    stt(K, "dve", a_ap, t_ap, -C2, a_ap, ALU.mult, ALU.add, t_b + a_b, a_b)
    ts(K, "dve", a_ap, a_ap, 3.1415925, ALU.min, a_b, a_b, s2=-3.1415925, op1=ALU.max)
    act(K, s_ap, a_ap, AF.Sin, a_b, s_b)
    stt(K, "dve", t_ap, a_ap, -1.0, a_ap, ALU.mult, ALU.max, a_b, t_b)
    act(K, c_ap, t_ap, AF.Sin, t_b + hpi.b, c_b, scale=-1.0, bias=hpi[0:np_, 0:1])


def emit_mla(K, C, h, prm, p, pos_d, invf_d, kbias_d):
    Lall = contextlib.ExitStack(); K.stack_push(Lall)
    w_in = prm[p + "w_in"]; w_uq = prm[p + "w_uq"]; w_ukv = prm[p + "w_ukv"]; w_out = prm[p + "w_out"]
    w_in_v = w_in.rearrange("(k r) n -> r k n", r=128)
    w_out_v = w_out.rearrange("(k r) n -> r k n", r=128)
    gB = K.sbuf("gB2", [128, D], F32); bcast_load(K, gB, prm[p + "norm_g"], D)
    gq = K.sbuf("gq", [128, 384], F32); bcast_load(K, gq, prm[p + "q_norm_g"], 384)
    gkv = K.sbuf("gkv", [128, 128], F32); bcast_load(K, gkv, prm[p + "kv_norm_g"], 128)
    npi = None
    hpi = K.sbuf("hpi2", [128, 1], F32)
    K.op("dve", lambda e: e.memset(hpi[:], PI / 2), writes=hpi.b)
    kbias = K.sbuf("kbias", [128, 1], F32)
    K.dma("sp", kbias[:], kbias_d, writes=kbias.b)
    tri = K.sbuf("tri", [128, 128], BF16)
    trif = K.sbuf("trif", [128, 128], F32)
    K.op("pool", lambda e: e.memset(trif[:], 1.0), writes=trif.b)
    K.op("pool", lambda e: e.affine_select(out=trif[:], in_=trif[:], pattern=[[1, 128]], compare_op=ALU.is_ge,
                                           fill=0.0, base=0, channel_multiplier=-1), reads=trif.b, writes=trif.b)
    K.op("dve", lambda e: e.tensor_copy(out=tri[:], in_=trif[:]), reads=trif.b, writes=tri.b)
    wqn = K.sbuf("wqn", [128, 3, 16, 128], BF16)
    wqa = K.sbuf("wqa", [128, 3, 16, 64], BF16)
    wqb = K.sbuf("wqb", [128, 3, 16, 64], BF16)
    wuq_v = w_uq.rearrange("(k r) (hh e) -> r k hh e", r=128, e=192)
    wukv_v = w_ukv.rearrange("c (hh e) -> c hh e", e=256)
    wuv = K.sbuf("wuv", [128, 16, 128], BF16)
    wukT = K.sbuf("wukT", [128, 16, 128], BF16)
    wqi = K.sbuf("wqi", [128, 8, 384], BF16)
    W = {}
    W["junk"] = K.sbuf("junk2", [128, 1024], BF16)
    W["ss"] = K.sbuf("ss2", [128, 8], F32)
    W["rstd"] = K.sbuf("rstd2", [128, 8], F32)
    W["hn"] = K.sbuf("hn2", [128, D], BF16)
    W["tp"] = K.psum("tp2", [128, 1024], BF16)
    tp = W["tp"]
    posi = K.sbuf("posi", [128, NBLK], I32)
    posf = K.sbuf("posf", [128, NBLK], F32)
    K.dma("sp", posi[:], pos_d.rearrange("(j r) -> r j", r=128), writes=posi.b, allow_slow_non_contiguous=True)
    K.op("dve", lambda e: e.tensor_copy(out=posf[:], in_=posi[:]), reads=posi.b, writes=posf.b)
    invB = K.sbuf("invB", [128, 32], F32)
    K.dma("sp", invB[:], invf_d[0:32].partition_broadcast(128), writes=invB.b)
    invc = K.sbuf("invc", [64, 1], F32)
    K.dma("sp", invc[:, 0:1], invf_d.rearrange("(r o) -> r o", o=1), writes=invc.b)
    posBi = K.sbuf("posBi", [64, 512], I32)
    sgn = K.sbuf("sgn", [64, 1], F32)
    K.op("dve", lambda e: e.memset(sgn[0:32, :], -1.0), writes=sgn.b)
    K.op("dve", lambda e: e.memset(sgn[32:64, :], 1.0), writes=sgn.b)

    ckvT = K.sbuf("ckvT", [128, 2 * NTOK], BF16, nb=2)
    krT = K.sbuf("krT", [64, 2 * NTOK], BF16, nb=2)
    ckv_tok = K.sbuf("ckv_tok", [128, 2 * NBLK, 128], BF16, nb=2)

    hnT = K.sbuf("hnT2", [128, 8, 512], BF16, nb=4)
    P = PsumPool(K, 1)
    LA = contextlib.ExitStack(); K.stack_push(LA)
    wkv = K.sbuf("wkv", [128, 8, 192], BF16)
    K.dma("pool", wkv[:], w_in_v[:, :, 384:576], writes=wkv.b)
    for k in range(3):
        K.dma("pool", wqn[:, k], wuq_v[:, k, :, 0:128], writes=wqn.b)
        K.dma("pool", wqa[:, k], wuq_v[:, k, :, 128:192], writes=wqa.b)
        K.dma("pool", wqb[:, k, :, 0:32], wuq_v[:, k, :, 160:192], writes=wqb.b)
        K.dma("pool", wqb[:, k, :, 32:64], wuq_v[:, k, :, 128:160], writes=wqb.b)
    K.dma("pool", wuv[:], wukv_v[:, :, 128:256], writes=wuv.b)
    K.dma("pool", wqi[:], w_in_v[:, :, 0:384], writes=wqi.b)
    angt = K.sbuf("angt", [128, 32], F32); tmpt = K.sbuf("tmpt", [128, 32], F32)
    kit = K.sbuf("kit", [128, 32], I32)
    sint = K.sbuf("sint", [128, 32], F32); cost = K.sbuf("cost", [128, 32], F32)
    ckf = K.sbuf("ckf", [128, 128], BF16); krf = K.sbuf("krf", [128, 64], BF16)
    r1 = K.sbuf("r1", [128, 32], F32); r2 = K.sbuf("r2", [128, 32], F32)
    angA = K.sbuf("angA", [128, NBLK, 32], F32); tmpA = K.sbuf("tmpA", [128, NBLK, 32], F32)
    sinA = K.sbuf("sinA", [128, NBLK, 32], F32); cosA = K.sbuf("cosA", [128, NBLK, 32], F32)
    kiA = K.sbuf("kiA", [128, NBLK, 32], I32)
    tt(K, "dve", angA[:], invB[:].unsqueeze(1).to_broadcast([128, NBLK, 32]),
       posf[:].unsqueeze(2).to_broadcast([128, NBLK, 32]), ALU.mult, invB.b + posf.b, angA.b)
    fl = lambda t_: t_[:].rearrange("r j i -> r (j i)")
    sincos_tables(K, (fl(angA), angA.b), (fl(sinA), sinA.b), (fl(cosA), cosA.b), (fl(tmpA), tmpA.b), npi,
                  (fl(kiA), kiA.b), hpi)
    psA = [P.tiles[0]] + [K.psum(f"psA{i}", [128, 512], F32) for i in range(3)]
    ckf4 = K.sbuf("ckf4", [128, 4, 128], BF16); krf4 = K.sbuf("krf4", [128, 4, 64], BF16)
    r1q = K.sbuf("r1q", [128, 4, 32], F32); r2q = K.sbuf("r2q", [128, 4, 32], F32)
    for sb in range(4):
        norm_transpose_sb(K, C, W, P, h, sb, gB, hnT)
        ss = W["ss"]; rstd = W["rstd"]; junk = W["junk"]
        for j in range(4):
            ps = psA[j]
            for k in range(8):
                K.op("pe", lambda e, k=k, j=j, ps=ps: e.matmul(ps[:, 0:192], lhsT=hnT[:, k, j * 128:(j + 1) * 128],
                                                              rhs=wkv[:, k, :], start=(k == 0), stop=(k == 7)),
                     reads=wkv.b + [hnT.b[j]], writes=ps.b)
            act(K, junk[:, 0:128], ps[:, 0:128], AF.Square, ps.b, junk.b + ss.b, accum_out=ss[:, j:j + 1])
        ts(K, "dve", rstd[:, 0:4], ss[:, 0:4], 1.0 / 128, ALU.mult, ss.b, rstd.b, s2=EPS, op1=ALU.add)
        K.op("act", lambda e: e.sqrt(out=rstd[:, 0:4], in_=rstd[:, 0:4]), reads=rstd.b, writes=rstd.b)
        K.op("dve", lambda e: e.reciprocal(out=rstd[:, 0:4], in_=rstd[:, 0:4]), reads=rstd.b, writes=rstd.b)
        for j in range(4):
            blk = sb * 4 + j
            ps = psA[j]
            stt(K, "dve", ckf4[:, j, :], ps[:, 0:128], rstd[:, j:j + 1], gkv[:], ALU.mult, ALU.mult,
                ps.b + rstd.b + gkv.b, ckf4.b)
            sint_b = sinA[:, blk, :]; cost_b = cosA[:, blk, :]
            x1 = ps[:, 128:160]; x2 = ps[:, 160:192]
            tt(K, "dve", r1q[:, j, :], x1, cost_b, ALU.mult, ps.b + cosA.b, r1q.b)
            tt(K, "dve", r2q[:, j, :], x2, sint_b, ALU.mult, ps.b + sinA.b, r2q.b)
            tt(K, "dve", krf4[:, j, 0:32], r1q[:, j, :], r2q[:, j, :], ALU.subtract, r1q.b + r2q.b, krf4.b)
            tt(K, "dve", r1q[:, j, :], x2, cost_b, ALU.mult, ps.b + cosA.b, r1q.b)
            tt(K, "dve", r2q[:, j, :], x1, sint_b, ALU.mult, ps.b + sinA.b, r2q.b)
            tt(K, "dve", krf4[:, j, 32:64], r1q[:, j, :], r2q[:, j, :], ALU.add, r1q.b + r2q.b, krf4.b)
        K.op("pool", lambda e, sb=sb: e.tensor_copy(out=ckv_tok[:, NBLK + sb * 4:NBLK + sb * 4 + 4, :], in_=ckf4[:]),
             reads=ckf4.b, writes=[ckv_tok.b[1]])
        for j in range(4):
            K.op("pe", lambda e, j=j: e.transpose(tp[:, j * 128:(j + 1) * 128], ckf4[:, j, :], C["ident"][:]),
                 reads=ckf4.b + C["ident"].b, writes=tp.b)
        for j in range(4):
            K.op("pe", lambda e, j=j: e.transpose(tp[0:64, (4 + j) * 128:(5 + j) * 128], krf4[:, j, :], C["ident"][:]),
                 reads=krf4.b + C["ident"].b, writes=tp.b)
        K.op("act", lambda e, sb=sb: e.copy(out=ckvT[:, NTOK + sb * 512:NTOK + (sb + 1) * 512], in_=tp[:, 0:512]),
             reads=tp.b, writes=[ckvT.b[1]])
        K.op("act", lambda e, sb=sb: e.copy(out=krT[:, NTOK + sb * 512:NTOK + (sb + 1) * 512], in_=tp[0:64, 512:1024]),
             reads=tp.b, writes=[krT.b[1]])
    lxi = K.dram("mla_lxi", [320, NTOK], BF16, "Internal")
    lxo = K.dram("mla_lxo", [640, NTOK], BF16, "Internal")
    bli = Buf("lxi"); blo = Buf("lxo")
    K.dma("sp", lxi[0:128, :], ckvT[:, NTOK:2 * NTOK], reads=[ckvT.b[1]], writes=[bli])
    K.dma("sp", lxi[128:192, :], krT[:, NTOK:2 * NTOK], reads=[krT.b[1]], writes=[bli])
    K.dma("sp", lxi[192:320, :], ckv_tok[:, NBLK:2 * NBLK, :].rearrange("r j c -> r (j c)"), reads=[ckv_tok.b[1]],
          writes=[bli])
    K.collective("AllGather", [lxi], [lxo], [bli], [blo])
    K.dma("sp", ckvT[:, 0:NTOK], lxo[0:128, :], reads=[blo], writes=[ckvT.b[0]])
    K.dma("sp", krT[:, 0:NTOK], lxo[128:192, :], reads=[blo], writes=[krT.b[0]])
    K.dma("sp", ckv_tok[:, 0:NBLK, :].rearrange("r j c -> r (j c)"), lxo[192:320, :], reads=[blo], writes=[ckv_tok.b[0]])
    K.barrier(); K.stack_pop()
    Lw = contextlib.ExitStack(); K.stack_push(Lw)
    wuk = K.sbuf("wuk", [128, 16, 128], BF16)
    K.dma("pool", wuk[:], wukv_v[:, :, 0:128], writes=wuk.b)
    for i2 in range(2):
        for hl in range(8):
            hh = i2 * 8 + hl
            K.op("pe", lambda e, hh=hh, hl=hl: e.transpose(tp[:, hl * 128:(hl + 1) * 128], wuk[:, hh, :], C["ident"][:]),
                 reads=wuk.b + C["ident"].b, writes=tp.b)
        K.op("dve", lambda e, i2=i2: e.tensor_copy(out=wukT[:, i2 * 8:(i2 + 1) * 8, :].rearrange("r j c -> r (j c)"),
                                                   in_=tp[:]), reads=tp.b, writes=wukT.b)
    K.barrier(); K.stack_pop()

    LB = contextlib.ExitStack(); K.stack_push(LB)
    psS = [K.psum(f"psS{i}", [128, 512], F32) for i in range(2)]
    psO = [K.psum(f"psO{i}", [128, 512], F32) for i in range(2)]
    psL = [K.psum(f"psL{i}", [128, 512], F32) for i in range(2)]

    class _Pool2:
        def __init__(self, tiles):
            self.tiles = tiles; self.i = 0
        def next(self):
            t = self.tiles[self.i]; self.i = (self.i + 1) % len(self.tiles); return t
    PA = _Pool2(P.tiles)
    PB = _Pool2(P.tiles + psS + psO + [psL[0]])
    cqT = K.sbuf("cqT", [128, 3, 512], BF16, nb=4)
    cqf4 = K.sbuf("cqf4", [128, 4, 384], BF16)
    szT = K.sbuf("szT2", [128, 16, 512], BF16, nb=16)
    wbs = [K.sbuf(f"wb2_{i}", [128, 4096], BF16) for i in range(2)]
    wseqB = []
    for _sb in range(4):
        wseqB += [("z", c) for c in range(4)] + [("o", c) for c in range(4)]
    wstB = {"issued": 0, "used": 0}

    def issue_b(upto):
        while wstB["issued"] < min(upto, len(wseqB)):
            i = wstB["issued"]; kind, c = wseqB[i]
            wt_ = wbs[i % 2]
            if kind == "z":
                K.dma("pool", wt_[:].rearrange("r (k n) -> r k n", k=8), w_in_v[:, :, 576 + c * 512:576 + (c + 1) * 512],
                      writes=wt_.b)
            else:
                K.dma("pool", wt_[:].rearrange("r (k n) -> r k n", k=16), w_out_v[:, :, c * 256:(c + 1) * 256], writes=wt_.b)
            wstB["issued"] += 1

    def next_w():
        i = wstB["used"]; wstB["used"] += 1
        issue_b(i + 2)
        return wbs[i % 2]
    issue_b(1)
    angF = K.sbuf("angF", [64, 512], F32); tmpF = K.sbuf("tmpF", [64, 512], F32)
    sinF = K.sbuf("sinF", [64, 512], F32); cosF = K.sbuf("cosF", [64, 512], F32)
    posBf = angF
    qn = [K.sbuf(f"qn{i}", [128, 512], BF16) for i in range(2)]
    qp = [K.sbuf(f"qp{i}", [128, 512], BF16) for i in range(2)]
    qr = [K.sbuf(f"qr{i}", [64, 512], BF16) for i in range(2)]
    qra = angF; qrb = tmpF
    pT = [K.sbuf(f"pT{i}", [128, 512], BF16) for i in range(3)]
    rl = K.sbuf("rl", [128, 512], F32)
    oh = [K.sbuf(f"oh{i}", [128, 512], BF16) for i in range(2)]
    st = {"nl": 0, "npt": 0, "nS": 0}

    def q_prologue(hh):
        qn_, qp_, qr_ = qn[hh % 2], qp[hh % 2], qr[hh % 2]
        ps = PA.next()
        for k in range(3):
            K.op("pe", lambda e, k=k, ps=ps: e.matmul(ps[:], lhsT=wqn[:, k, hh, :], rhs=cqT[:, k, :],
                                                     start=(k == 0), stop=(k == 2)),
                 reads=wqn.b + cqT.b, writes=ps.b)
        K.op("act", lambda e, ps=ps: e.copy(out=qn_[:], in_=ps[:]), reads=ps.b, writes=qn_.b)
        ps3 = PA.next()
        for k in range(3):
            K.op("pe", lambda e, k=k, ps3=ps3: e.matmul(ps3[0:64, 0:512], lhsT=wqa[:, k, hh, :], rhs=cqT[:, k, :],
                                                       start=(k == 0), stop=(k == 2)),
                 reads=wqa.b + cqT.b, writes=ps3.b)
        tt(K, "dve", qra[:], ps3[0:64, :], cosF[:], ALU.mult, ps3.b + cosF.b, qra.b)
        ps2 = PA.next()
        K.op("pe", lambda e, ps2=ps2: e.matmul(ps2[:], lhsT=wukT[:, hh, :], rhs=qn_[:], start=True, stop=True),
             reads=wukT.b + qn_.b, writes=ps2.b)
        K.op("act", lambda e, ps2=ps2: e.copy(out=qp_[:], in_=ps2[:]), reads=ps2.b, writes=qp_.b)
        ps4 = PA.next()
        for k in range(3):
            K.op("pe", lambda e, k=k, ps4=ps4: e.matmul(ps4[0:64, 0:512], lhsT=wqb[:, k, hh, :], rhs=cqT[:, k, :],
                                                       start=(k == 0), stop=(k == 2)),
                 reads=wqb.b + cqT.b, writes=ps4.b)
        tt(K, "dve", qrb[:], ps4[0:64, :], sinF[:], ALU.mult, ps4.b + sinF.b, qrb.b)
        tt(K, "dve", qr_[:], qra[:], qrb[:], ALU.add, qra.b + qrb.b, qr_.b)

    def qk(hh, sb, kb):
        qp_, qr_ = qp[hh % 2], qr[hh % 2]
        half = 0 if kb < NBLK else 1
        jd = kb - (NBLK + sb * 4)
        c0 = jd * 128 if jd > 0 else 0
        pS = psS[st["nS"] % 2]; st["nS"] += 1
        K.op("pe", lambda e: e.matmul(pS[:, c0:512], lhsT=ckvT[:, kb * 128:(kb + 1) * 128], rhs=qp_[:, c0:512],
                                      start=True, stop=False), reads=[ckvT.b[half]] + qp_.b, writes=pS.b)
        K.op("pe", lambda e: e.matmul(pS[:, c0:512], lhsT=krT[:, kb * 128:(kb + 1) * 128], rhs=qr_[:, c0:512],
                                      start=False, stop=True), reads=[krT.b[half]] + qr_.b, writes=pS.b)
        return pS

    def softmax_pv(hh, sb, kb, pS, nkb):
        half = 0 if kb < NBLK else 1
        jd = kb - (NBLK + sb * 4)
        c0 = jd * 128 if jd > 0 else 0
        pt = pT[st["npt"] % 3]; st["npt"] += 1
        O = psO[hh % 2]; Lp = psL[hh % 2]
        if half == 0:
            act(K, pt[:, c0:512], pS[:, c0:512], AF.Exp, pS.b + kbias.b, pt.b, scale=MLA_SCALE_F, bias=kbias[:, 0:1])
        else:
            act(K, pt[:, c0:512], pS[:, c0:512], AF.Exp, pS.b, pt.b, scale=MLA_SCALE_F)
        if jd >= 0:
            tt(K, "pool", pt[:, jd * 128:(jd + 1) * 128], pt[:, jd * 128:(jd + 1) * 128], tri[:], ALU.mult,
               pt.b + tri.b, pt.b)
        K.op("pe", lambda e: e.matmul(O[:, c0:512], lhsT=ckv_tok[:, kb, :], rhs=pt[:, c0:512], start=(kb == 0),
                                      stop=(kb == nkb - 1)), reads=[ckv_tok.b[half]] + pt.b, writes=O.b)
        K.op("pe", lambda e: e.matmul(Lp[:, c0:512], lhsT=C["ones_bf"][:], rhs=pt[:, c0:512], start=(kb == 0),
                                      stop=(kb == nkb - 1)), reads=C["ones_bf"].b + pt.b, writes=Lp.b)

    def head_epilogue(hh):
        oh_ = oh[hh % 2]; O = psO[hh % 2]; Lp = psL[hh % 2]
        K.op("dve", lambda e: e.reciprocal(out=rl[:], in_=Lp[:]), reads=Lp.b, writes=rl.b)
        tt(K, "dve", oh_[:], O[:], rl[:], ALU.mult, O.b + rl.b, oh_.b)
        ps5 = PA.next()
        K.op("pe", lambda e: e.matmul(ps5[:], lhsT=wuv[:, hh, :], rhs=oh_[:], start=True, stop=True),
             reads=wuv.b + oh_.b, writes=ps5.b)
        tt(K, "dve", szT[:, hh, :], ps5[:], szT[:, hh, :], ALU.mult, ps5.b + [szT.b[hh]], [szT.b[hh]])

    for sb in range(4):
        norm_transpose_sb(K, C, W, P, h, sb, gB, hnT)
        K.dma("sp", posBi[:], pos_d[sb * 512:(sb + 1) * 512].partition_broadcast(64), writes=posBi.b)
        K.op("dve", lambda e: e.tensor_copy(out=posBf[:], in_=posBi[:]), reads=posBi.b, writes=posBf.b)
        ts(K, "dve", angF[:], angF[:], invc[:, 0:1], ALU.mult, angF.b + invc.b, angF.b)
        sincos_tables(K, (angF[:], angF.b), (sinF[:], sinF.b), (cosF[:], cosF.b), (tmpF[:], tmpF.b), npi,
                      (posBi[:], posBi.b), hpi)
        ts(K, "dve", sinF[:], sinF[:], sgn[:, 0:1], ALU.mult, sinF.b + sgn.b, sinF.b)
        ss = W["ss"]; rstd = W["rstd"]; junk = W["junk"]
        cq_ps = []
        for j in range(4):
            ps = PB.next(); cq_ps.append(ps)
            for k in range(8):
                K.op("pe", lambda e, k=k, j=j, ps=ps: e.matmul(ps[:, 0:384], lhsT=hnT[:, k, j * 128:(j + 1) * 128],
                                                              rhs=wqi[:, k, :], start=(k == 0), stop=(k == 7)),
                     reads=wqi.b + [hnT.b[j]], writes=ps.b)
            act(K, junk[:, 0:384], ps[:, 0:384], AF.Square, ps.b, junk.b + ss.b, accum_out=ss[:, j:j + 1])
        ts(K, "dve", rstd[:, 0:4], ss[:, 0:4], 1.0 / 384, ALU.mult, ss.b, rstd.b, s2=EPS, op1=ALU.add)
        K.op("act", lambda e: e.sqrt(out=rstd[:, 0:4], in_=rstd[:, 0:4]), reads=rstd.b, writes=rstd.b)
        K.op("dve", lambda e: e.reciprocal(out=rstd[:, 0:4], in_=rstd[:, 0:4]), reads=rstd.b, writes=rstd.b)
        for j in range(4):
            ps = cq_ps[j]
            stt(K, "dve", cqf4[:, j, :], ps[:, 0:384], rstd[:, j:j + 1], gq[:], ALU.mult, ALU.mult,
                ps.b + rstd.b + gq.b, cqf4.b)
        for cg in range(4):
            wt = next_w()
            wb = wt[:].rearrange("r (k n) -> r k n", k=8)
            for sub in range(4):
                fc = cg * 4 + sub
                ps = PB.next()
                for k in range(8):
                    K.op("pe", lambda e, k=k, sub=sub, ps=ps, wb=wb: e.matmul(
                        ps[:], lhsT=wb[:, k, sub * 128:(sub + 1) * 128], rhs=hnT[:, k, :],
                        start=(k == 0), stop=(k == 7)), reads=wt.b + hnT.b, writes=ps.b)
                act(K, szT[:, fc, :], ps[:], AF.Silu, ps.b, [szT.b[fc]])
        for j in range(4):
            for k in range(3):
                K.op("pe", lambda e, k=k, j=j: e.transpose(tp[:, k * 128:(k + 1) * 128], cqf4[:, j, k * 128:(k + 1) * 128],
                                                           C["ident"][:]), reads=cqf4.b + C["ident"].b, writes=tp.b)
            K.op("act", lambda e, j=j: e.copy(out=cqT[:, :, j * 128:(j + 1) * 128],
                                              in_=tp[:, 0:384].rearrange("r (k t) -> r k t", k=3)),
                 reads=tp.b, writes=[cqT.b[j]])
        nkb = NBLK + (sb + 1) * 4
        q_prologue(0)
        for hh in range(16):
            pS_next = qk(hh, sb, 0)
            for kb in range(nkb):
                pS_cur = pS_next
                if kb + 1 < nkb:
                    pS_next = qk(hh, sb, kb + 1)
                softmax_pv(hh, sb, kb, pS_cur, nkb)
                if kb == 3 and hh + 1 < 16:
                    q_prologue(hh + 1)
                if kb == 6 and hh > 0:
                    head_epilogue(hh - 1)
            if hh == 15:
                head_epilogue(15)
        for cgo in range(4):
            wt = next_w()
            wo = wt[:].rearrange("r (k n) -> r k n", k=16)
            for jb in range(4):
                blk = sb * 4 + jb
                ps = PB.next()
                for k in range(16):
                    K.op("pe", lambda e, k=k, jb=jb, ps=ps, wo=wo: e.matmul(
                        ps[:, 0:256], lhsT=szT[:, k, jb * 128:(jb + 1) * 128], rhs=wo[:, k, :],
                        start=(k == 0), stop=(k == 15)), reads=wt.b + [szT.b[k]], writes=ps.b)
                hs = h[:, blk, cgo * 256:(cgo + 1) * 256]
                tt(K, "dve", hs, hs, ps[:, 0:256], ALU.add, ps.b + [h.b[blk]], [h.b[blk]])
    K.barrier(); K.stack_pop()
    K.barrier(); K.stack_pop()


PARAM_SPECS = None


def param_specs():
    sp = {}
    def gm(p):
        sp[p + "norm_g"] = (1024,); sp[p + "w_in"] = (1024, 6144); sp[p + "ln_g"] = (2048,)
        sp[p + "ln_b"] = (2048,); sp[p + "w_s"] = (8, 128, 128); sp[p + "b_s"] = (8, 128)
        sp[p + "w_out"] = (2048, 1024)
    gm("l0_")
    p = "l1_"
    sp[p + "norm_g"] = (1024,); sp[p + "w_in"] = (1024, 4096)
    sp[p + "a_re"] = (128, 64); sp[p + "a_im"] = (128, 64); sp[p + "log_step"] = (128,)
    sp[p + "b_re"] = (128, 64, 16); sp[p + "b_im"] = (128, 64, 16)
    sp[p + "c_re"] = (128, 16, 64); sp[p + "c_im"] = (128, 16, 64)
    sp[p + "d_skip"] = (2048,); sp[p + "w_glu"] = (2048, 2048); sp[p + "b_glu"] = (2048,)
    sp[p + "w_out"] = (2048, 1024)
    p = "l2_"
    sp[p + "norm_g"] = (1024,); sp[p + "w_in"] = (1024, 2624); sp[p + "q_norm_g"] = (384,)
    sp[p + "w_uq"] = (384, 3072); sp[p + "kv_norm_g"] = (128,); sp[p + "w_ukv"] = (128, 4096)
    sp[p + "w_out"] = (2048, 1024)
    gm("l3_")
    sp["final_norm_g"] = (1024,)
    return sp


def build_program(layers=("l0",), final_norm=False):
    K = MK()
    x = K.dram("x", [NTOK, D], F32, "ExternalInput")
    out = K.dram("out", [NTOK, D], F32, "ExternalOutput")
    prm = {}
    need = set()
    for l in layers:
        need.add(l + "_")
    for name, shp in param_specs().items():
        if name[:3] in need or (final_norm and name == "final_norm_g"):
            prm[name] = K.dram(name, list(shp), F32, "ExternalInput")
    C = emit_consts(K)
    P = None
    cmask_d = K.dram("cmask", [128, 2], F32, "ExternalInput")
    cmask = K.sbuf("cmask_s", [128, 2], F32)
    K.dma("sp", cmask[:], cmask_d, writes=cmask.b)
    kflag_d = K.dram("kflag", [128, 1], F32, "ExternalInput")
    kflag = K.sbuf("kflag_s", [128, 1], F32)
    K.dma("sp", kflag[:], kflag_d, writes=kflag.b)
    pos_d = K.dram("pos", [NTOK], I32, "ExternalInput")
    invf_d = K.dram("invf", [64], F32, "ExternalInput")
    kbias_d = K.dram("kbias", [128, 1], F32, "ExternalInput")
    h = K.sbuf("h", [128, NBLK, D], F32, nb=NBLK)
    xv = x.rearrange("(j p) d -> p j d", p=128)
    for q in range(4):
        K.dma("sp", h[:, q * 4:(q + 1) * 4, :], xv[:, q * 4:(q + 1) * 4, :], writes=h.b[q * 4:(q + 1) * 4])
    PRE = {}
    if "l1" in layers:
        LPRE = contextlib.ExitStack(); K.stack_push(LPRE)
        p1 = "l1_"
        for nm in ("are", "aim", "lst"):
            PRE[nm] = K.sbuf("pre_" + nm, [128, 64], F32)
        K.dma("sp", PRE["are"][:], prm[p1 + "a_re"].rearrange("(j gl) p -> (gl p) j", gl=2), writes=PRE["are"].b,
              allow_slow_non_contiguous=True)
        K.dma("sp", PRE["aim"][:], prm[p1 + "a_im"].rearrange("(j gl) p -> (gl p) j", gl=2), writes=PRE["aim"].b,
              allow_slow_non_contiguous=True)
        lsv = prm[p1 + "log_step"].rearrange("(j gl) -> gl j", gl=2)
        for gl in range(2):
            K.dma("sp", PRE["lst"][gl * 64:(gl + 1) * 64, :], lsv[gl].partition_broadcast(64), writes=PRE["lst"].b,
                  allow_slow_non_contiguous=True)
        PRE["dsk"] = K.sbuf("pre_dsk", [128, 16], F32); PRE["bgl"] = K.sbuf("pre_bgl", [128, 16], F32)
        K.dma("sp", PRE["dsk"][:], prm[p1 + "d_skip"].rearrange("(c r) -> r c", r=128), writes=PRE["dsk"].b,
              allow_slow_non_contiguous=True)
        K.dma("sp", PRE["bgl"][:], prm[p1 + "b_glu"].rearrange("(c r) -> r c", r=128), writes=PRE["bgl"].b,
              allow_slow_non_contiguous=True)
    for l in layers:
        if l in ("l0", "l3"):
            emit_gmlp(K, C, P, h, prm, l + "_")
        elif l == "l2":
            emit_mla(K, C, h, prm, "l2_", pos_d, invf_d, kbias_d)
        elif l == "l1":
            emit_s5(K, C, h, prm, "l1_", kflag, cmask, PRE)
            K.barrier(); K.stack_pop()
    ov = out.rearrange("(j p) d -> p j d", p=128)
    if final_norm:
        emit_final_norm(K, h, prm["final_norm_g"], ov)
    else:
        for q in range(4):
            K.dma("sp", ov[:, q * 4:(q + 1) * 4, :], h[:, q * 4:(q + 1) * 4, :], reads=h.b[q * 4:(q + 1) * 4])
    K.barrier()
    return K.build(), sorted(prm.keys())


def emit_final_norm(K, h, g_d, ov):
    L = contextlib.ExitStack(); K.stack_push(L)
    gB = K.sbuf("gBf", [128, D], F32); bcast_load(K, gB, g_d, D)
    junk = [K.sbuf(f"junkf{i}", [128, D], F32) for i in range(2)]
    ss = K.sbuf("ssf", [128, NBLK], F32)
    ot = [K.sbuf(f"otf{i}", [128, D], F32) for i in range(3)]
    for blk in range(NBLK):
        jk = junk[blk % 2]
        act(K, jk[:], h[:, blk, :], AF.Square, [h.b[blk]], jk.b + ss.b, accum_out=ss[:, blk:blk + 1])
    ts(K, "dve", ss[:], ss[:], 1.0 / D, ALU.mult, ss.b, ss.b, s2=EPS, op1=ALU.add)
    K.op("act", lambda e: e.sqrt(out=ss[:], in_=ss[:]), reads=ss.b, writes=ss.b)
    K.op("dve", lambda e: e.reciprocal(out=ss[:], in_=ss[:]), reads=ss.b, writes=ss.b)
    for blk in range(NBLK):
        o = ot[blk % 3]
        stt(K, "dve", o[:], h[:, blk, :], ss[:, blk:blk + 1], gB[:], ALU.mult, ALU.mult, [h.b[blk]] + ss.b + gB.b, o.b)
        K.dma("sp", ov[:, blk, :], o[:], reads=o.b)
    K.barrier(); K.stack_pop()


_PROG = {}


def _get_prog(layers, final_norm):
    key = (tuple(layers), final_norm)
    if key not in _PROG:
        _PROG[key] = build_program(layers=layers, final_norm=final_norm)
    return _PROG[key]


def _aux_consts():
    cmask = np.zeros((128, 2), np.float32)
    for r in range(128):
        cmask[r, (r // 16) % 2] = 1.0
    invf = (np.float32(10000.0) ** (-np.arange(0, 64, 2, dtype=np.float32) / np.float32(64))).astype(np.float32)
    return cmask, np.concatenate([invf, invf])


def kernel(**inputs):
    inputs = {k: np.asarray(v) for k, v in inputs.items()}
    layers = ("l0", "l1", "l2", "l3")
    nc, names = _get_prog(layers, True)
    x = np.ascontiguousarray(inputs["x"], dtype=np.float32).reshape(8, NTOK, D)
    pos = np.ascontiguousarray(inputs["positions"]).astype(np.int32).reshape(8, NTOK)
    cmask, invf = _aux_consts()
    wts = {n: np.ascontiguousarray(inputs[n], dtype=np.float32) for n in names}

    in_maps = []
    for i in range(8):
        odd = float(i % 2)
        m = {"x": x[i], "cmask": cmask, "pos": pos[i], "invf": invf,
             "kbias": np.full((128, 1), 0.0 if odd else -30000.0, np.float32),
             "kflag": np.full((128, 1), odd, np.float32)}
        m.update(wts)
        in_maps.append(m)
    res = run_bass_kernel_spmd(nc, in_maps, core_ids=list(range(8))).results
    out = np.stack([np.asarray(r["out"]) for r in res]).reshape(4, 4096, D).astype(np.float32)
    return out
```

```python
import contextlib
import numpy as np
import concourse.bass as bass
import concourse.mybir as mybir
from concourse.bass_utils import run_bass_kernel_spmd

F32 = mybir.dt.float32
BF16 = mybir.dt.bfloat16
I32 = mybir.dt.int32
ALU = mybir.AluOpType
AF = mybir.ActivationFunctionType
AX = mybir.AxisListType

N_DMA_SEMS = 12


class Buf:
    __slots__ = ("w", "r", "name")

    def __init__(self, name=""):
        self.w = None
        self.r = []
        self.name = name


class Tile:
    def __init__(self, t, nb, name):
        self.t = t
        self.b = [Buf(f"{name}.{i}") for i in range(nb)]

    def __getitem__(self, idx):
        return self.t[idx]


class MK:
    ENG = ("pe", "act", "dve", "pool", "sp")

    def __init__(self):
        self.nc = bass.Bass("TRN2", target_bir_lowering=False)
        self.stack = contextlib.ExitStack()
        self.ops = {e: [] for e in self.ENG}
        self.cnt = {}
        self.sems = {}
        self.waited = {e: {} for e in self.ENG}
        for e in ("pe", "act", "dve", "pool"):
            self._mksem("c_" + e)
        for i in range(N_DMA_SEMS):
            self._mksem(f"d{i}")
        self.dma_rr = 0
        self.n_ops = 0
        self.stacks = [self.stack]
        self.uid = 0

    def _mksem(self, key):
        self.sems[key] = self.stack.enter_context(self.nc.semaphore(key))
        self.cnt[key] = 0

    def dram(self, name, shape, dt, kind):
        return self.nc.dram_tensor(name, list(shape), dt, kind=kind).ap()

    def stack_push(self, st):
        self.stacks.append(st)

    def stack_pop(self):
        self.stacks.pop().close()

    def sbuf(self, name, shape, dt, nb=1):
        self.uid += 1
        t = self.stacks[-1].enter_context(self.nc.sbuf_tensor(f"{name}_{self.uid}", list(shape), dt))
        return Tile(t, nb, name)

    def psum(self, name, shape, dt=F32, nb=1):
        self.uid += 1
        t = self.stacks[-1].enter_context(self.nc.psum_tensor(f"{name}_{self.uid}", list(shape), dt))
        return Tile(t, nb, name)

    def barrier(self):
        for eng in self.ENG:
            waits = []
            for k, v in self.cnt.items():
                if v > self.waited[eng].get(k, 0):
                    self.waited[eng][k] = v
                    waits.append((k, v))
            if waits:
                self.ops[eng].append((None, waits, None, 0))

    def _deps(self, eng, reads, writes):
        need = {}
        def add(tok):
            if tok is None:
                return
            k, v = tok
            if eng == "pe" and k == "c_pe":
                return
            if need.get(k, 0) < v:
                need[k] = v
        for b in reads:
            add(b.w)
        for b in writes:
            add(b.w)
            for tok in b.r:
                add(tok)
        out = []
        wd = self.waited[eng]
        for k, v in need.items():
            if wd.get(k, 0) < v:
                wd[k] = v
                out.append((k, v))
        return out

    def _commit(self, tok, reads, writes):
        for b in writes:
            b.w = tok
            b.r = []
        for b in reads:
            if b not in writes:
                b.r.append(tok)

    def op(self, eng, fn, reads=(), writes=()):
        reads = list(reads); writes = list(writes)
        waits = self._deps(eng, reads, writes)
        key = "c_" + eng
        self.cnt[key] += 1
        tok = (key, self.cnt[key])
        self.ops[eng].append((fn, waits, key, 1))
        self._commit(tok, reads, writes)
        self.n_ops += 1
        return tok

    def dma(self, eng, out, in_, reads=(), writes=(), **kw):
        reads = list(reads); writes = list(writes)
        i = self.dma_rr; self.dma_rr = (self.dma_rr + 1) % N_DMA_SEMS
        key = f"d{i}"
        waits = self._deps(eng, reads, writes)
        prev = self.cnt[key]
        if prev and self.waited[eng].get(key, 0) < prev:
            self.waited[eng][key] = prev
            waits.append((key, prev))
        self.cnt[key] += 16
        tok = (key, self.cnt[key])
        def fn(e, out=out, in_=in_, kw=kw):
            return e.dma_start(out=out, in_=in_, **kw)
        self.ops[eng].append((fn, waits, key, 16))
        self._commit(tok, reads, writes)
        self.n_ops += 1
        return tok

    def collective(self, kind, ins, outs, rbufs, wbufs):
        if "cc" not in self.sems:
            self._mksem("cc")
        waits = self._deps("pool", list(rbufs), list(wbufs))
        self.cnt["cc"] += 1
        tok = ("cc", self.cnt["cc"])
        def fn(e, ins=ins, outs=outs, kind=kind):
            return e.collective_compute(kind, ALU.bypass, replica_groups=[[0, 1], [2, 3], [4, 5], [6, 7]],
                                        ins=list(ins), outs=list(outs))
        self.ops["pool"].append((fn, waits, "cc", 1))
        self._commit(tok, list(rbufs), list(wbufs))
        return tok

    def wait_all(self, eng, bufs):
        waits = self._deps(eng, list(bufs), [])
        self.ops[eng].append((None, waits, None, 0))

    def build(self):
        nc = self.nc
        sems = self.sems
        ops = self.ops
        with nc.Block() as block:
            def emit(e, lst):
                for fn, waits, key, amt in lst:
                    for k, v in waits:
                        e.wait_ge(sems[k], v)
                    if fn is not None:
                        ins = fn(e)
                        ins.then_inc(sems[key], amt)

            @block.tensor
            def _(e):
                emit(e, ops["pe"])

            @block.scalar
            def _(e):
                emit(e, ops["act"])

            @block.vector
            def _(e):
                emit(e, ops["dve"])

            @block.gpsimd
            def _(e):
                emit(e, ops["pool"])

            @block.sync
            def _(e):
                emit(e, ops["sp"])
        self.stack.close()
        return nc


NTOK = 2048
NBLK = 16
D = 1024
DI = 2048
EPS = 1e-6


class PsumPool:
    def __init__(self, K, n):
        self.tiles = [K.psum(f"pp{i}", [128, 512], F32) for i in range(n)]
        self.i = 0

    def next(self):
        t = self.tiles[self.i]
        self.i = (self.i + 1) % len(self.tiles)
        return t


def bcast_load(K, dst, src_1d, n, eng="sp"):
    K.dma(eng, dst[:], src_1d.partition_broadcast(128), writes=dst.b)


def emit_consts(K):
    C = {}
    C["ident_f"] = K.sbuf("ident_f", [128, 128], F32)
    C["ident"] = K.sbuf("ident", [128, 128], BF16)
    idf = C["ident_f"]; idb = C["ident"]
    K.op("pool", lambda e: e.memset(idf[:], 1.0), writes=idf.b)
    K.op("pool", lambda e: e.affine_select(out=idf[:], in_=idf[:], pattern=[[-1, 128]],
                                           compare_op=ALU.is_equal, fill=0.0, base=0,
                                           channel_multiplier=1), reads=idf.b, writes=idf.b)
    K.op("dve", lambda e: e.tensor_copy(out=idb[:], in_=idf[:]), reads=idf.b, writes=idb.b)
    C["ones_bf"] = K.sbuf("ones_bf", [128, 128], BF16)
    ob = C["ones_bf"]
    K.op("dve", lambda e: e.memset(ob[:], 1.0), writes=ob.b)
    return C


def rmsnorm_block(K, W, hsrc, gB, out_bf, tag):
    h_ap, h_b = hsrc
    o_ap, o_b = out_bf
    junk = W["junk"]; ss = W["ss"]; rstd = W["rstd"]
    K.op("act", lambda e: e.activation(out=junk[:, 0:D], in_=h_ap, func=AF.Square, accum_out=ss[:, 0:1]),
         reads=h_b, writes=junk.b + ss.b)
    K.op("dve", lambda e: e.tensor_scalar(out=rstd[:, 0:1], in0=ss[:, 0:1], scalar1=1.0 / D, scalar2=EPS,
                                          op0=ALU.mult, op1=ALU.add), reads=ss.b, writes=rstd.b)
    K.op("act", lambda e: e.sqrt(out=rstd[:, 0:1], in_=rstd[:, 0:1]), reads=rstd.b, writes=rstd.b)
    K.op("dve", lambda e: e.reciprocal(out=rstd[:, 0:1], in_=rstd[:, 0:1]), reads=rstd.b, writes=rstd.b)
    K.op("dve", lambda e: e.scalar_tensor_tensor(out=o_ap, in0=h_ap, scalar=rstd[:, 0:1], in1=gB[:],
                                                 op0=ALU.mult, op1=ALU.mult),
         reads=h_b + rstd.b + gB.b, writes=o_b)


def norm_transpose_sb(K, C, W, P, h, sb, gB, hnT):
    junk = W["junk"]; ss = W["ss"]; rstd = W["rstd"]; hn = W["hn"]; tp = W["tp"]
    for j in range(4):
        blk = sb * 4 + j
        K.op("act", lambda e, j=j, blk=blk: e.activation(out=junk[:, 0:D], in_=h[:, blk, :], func=AF.Square,
                                                        accum_out=ss[:, 4 + j:5 + j]),
             reads=[h.b[blk]], writes=junk.b + ss.b)
    K.op("dve", lambda e: e.tensor_scalar(out=rstd[:, 4:8], in0=ss[:, 4:8], scalar1=1.0 / D, scalar2=EPS,
                                          op0=ALU.mult, op1=ALU.add), reads=ss.b, writes=rstd.b)
    K.op("act", lambda e: e.sqrt(out=rstd[:, 4:8], in_=rstd[:, 4:8]), reads=rstd.b, writes=rstd.b)
    K.op("dve", lambda e: e.reciprocal(out=rstd[:, 4:8], in_=rstd[:, 4:8]), reads=rstd.b, writes=rstd.b)
    for j in range(4):
        blk = sb * 4 + j
        K.op("dve", lambda e, j=j, blk=blk: e.scalar_tensor_tensor(out=hn[:], in0=h[:, blk, :], scalar=rstd[:, 4 + j:5 + j],
                                                                  in1=gB[:], op0=ALU.mult, op1=ALU.mult),
             reads=[h.b[blk]] + rstd.b + gB.b, writes=hn.b)
        for k in range(8):
            K.op("pe", lambda e, k=k: e.transpose(tp[:, k * 128:(k + 1) * 128], hn[:, k * 128:(k + 1) * 128],
                                                  C["ident"][:]),
                 reads=hn.b + C["ident"].b, writes=tp.b)
        K.op("act", lambda e, j=j: e.copy(out=hnT[:, :, j * 128:(j + 1) * 128],
                                           in_=tp[:].rearrange("p (k t) -> p k t", k=8)),
             reads=tp.b, writes=[hnT.b[j]])


def emit_gmlp(K, C, P, h, prm, lname):
    nc = K.nc
    L = contextlib.ExitStack()
    K.stack_push(L)
    P = PsumPool(K, 6)
    w_in, ln_g, ln_b, w_s, b_s, w_out, norm_g = (prm[lname + s] for s in
                                                 ("w_in", "ln_g", "ln_b", "w_s", "b_s", "w_out", "norm_g"))
    W = {}
    W["junk"] = K.sbuf("junk", [128, 1024], F32)
    W["ss"] = K.sbuf("ss", [128, 8], F32)
    W["rstd"] = K.sbuf("rstd", [128, 8], F32)
    W["hn"] = K.sbuf("hn", [128, D], BF16)
    W["tp"] = K.psum("tp", [128, 1024], BF16)
    gB = K.sbuf("gB", [128, D], F32)
    lgc = K.sbuf("lgc", [128, 16], F32)
    lbc = K.sbuf("lbc", [128, 16], F32)
    bias2 = K.sbuf("bias2", [128, 16, 128], F32)
    bsB = K.sbuf("bsB", [128, 8, 128], F32)
    WsT = K.sbuf("WsT", [128, 8, 128], BF16)
    hnTs = [K.sbuf(f"hnT{i}", [128, 8, 512], BF16, nb=4) for i in range(2)]
    NWB = 4
    wbuf = [K.sbuf(f"wbuf{i}", [128, 8, 512], BF16) for i in range(NWB)]
    wo_tiles = [K.sbuf(f"wobuf{i}", [128, 16, 256], BF16) for i in range(2)]
    uT = K.sbuf("uT", [128, 16, 512], BF16, nb=16)
    szT = K.sbuf("szT", [128, 16, 512], BF16, nb=16)
    vtok = K.sbuf("vtok", [128, 4, DI], BF16, nb=4)
    st1 = K.sbuf("st1", [128, 4, 4], F32, nb=4)
    st2 = K.sbuf("st2", [128, 4, 4], F32, nb=4)
    mv = K.sbuf("mv", [128, 8], F32)
    t1 = [K.sbuf(f"t1_{i}", [128, 512], F32) for i in range(2)]

    bcast_load(K, gB, norm_g, D)
    K.dma("sp", lgc[:], ln_g.rearrange("(c r) -> r c", r=128), writes=lgc.b, allow_slow_non_contiguous=True)
    K.dma("sp", lbc[:], ln_b.rearrange("(c r) -> r c", r=128), writes=lbc.b, allow_slow_non_contiguous=True)
    K.dma("sp", bsB[:].rearrange("p g t -> p (g t)"), b_s.rearrange("g t -> (g t)").partition_broadcast(128),
          writes=bsB.b)
    wsf_t = W["hn"]
    wsf = wsf_t[:].rearrange("p (g s) -> p g s", g=8)
    K.dma("pool", wsf, w_s.rearrange("g t s -> t g s"), writes=wsf_t.b)
    tp = W["tp"]
    for g in range(8):
        K.op("pe", lambda e, g=g: e.transpose(tp[:, g * 128:(g + 1) * 128], wsf[:, g, :], C["ident"][:]),
             reads=wsf_t.b + C["ident"].b, writes=tp.b)
    K.op("dve", lambda e: e.tensor_copy(out=WsT[:].rearrange("p g t -> p (g t)"), in_=tp[:]),
         reads=tp.b, writes=WsT.b)
    for g in range(8):
        K.op("pool", lambda e, g=g: e.affine_select(out=WsT[:, g, :], in_=WsT[:, g, :], pattern=[[1, 128]],
                                                    compare_op=ALU.is_ge, fill=0.0, base=0,
                                                    channel_multiplier=-1), reads=WsT.b, writes=WsT.b)

    psr = P.next()
    for g in range(8):
        K.op("pe", lambda e, g=g: e.matmul(psr[:, 0:128], lhsT=C["ones_bf"][:], rhs=WsT[:, g, :], start=True, stop=True),
             reads=C["ones_bf"].b + WsT.b, writes=psr.b)
        for fl in range(2):
            fc = 2 * g + fl
            K.op("dve", lambda e, g=g, fc=fc: e.scalar_tensor_tensor(
                out=bias2[:, fc, :], in0=psr[:, 0:128], scalar=lbc[:, fc:fc + 1], in1=bsB[:, g, :],
                op0=ALU.mult, op1=ALU.add), reads=psr.b + lbc.b + bsB.b, writes=bias2.b)
    w_in_v = w_in.rearrange("(k p) n -> p k n", p=128)
    w_out_v = w_out.rearrange("(k p) n -> p k n", p=128)
    CG_ORDER = (4, 5, 6, 7, 0, 1, 2, 3, 8, 9, 10, 11)
    wseq = [cg for _ in range(4) for cg in CG_ORDER]
    wst = {"issued": 0}

    def issue_w(upto):
        while wst["issued"] < min(upto, len(wseq)):
            i = wst["issued"]; cgi = wseq[i]
            wbi = wbuf[i % NWB]
            K.dma("pool", wbi[:], w_in_v[:, :, cgi * 512:(cgi + 1) * 512], writes=wbi.b)
            wst["issued"] += 1
    issue_w(NWB - 1)
    nload = 0
    for sb in range(4):
        hnT = hnTs[sb % 2]
        if sb == 0:
            norm_transpose_sb(K, C, W, P, h, 0, gB, hnT)
        def emit_ln():
            for j in range(4):
                K.op("dve", lambda e, j=j: e.reduce_sum(out=mv[:, 0:1], in_=st1[:, j, :], axis=AX.X),
                     reads=[st1.b[j]], writes=mv.b)
                K.op("dve", lambda e, j=j: e.reduce_sum(out=mv[:, 1:2], in_=st2[:, j, :], axis=AX.X),
                     reads=[st2.b[j]], writes=mv.b)
                K.op("dve", lambda e: e.tensor_scalar(out=mv[:, 2:4], in0=mv[:, 0:2], scalar1=1.0 / DI, scalar2=None,
                                                      op0=ALU.mult), reads=mv.b, writes=mv.b)
                K.op("dve", lambda e: e.tensor_tensor(out=mv[:, 4:5], in0=mv[:, 2:3], in1=mv[:, 2:3], op=ALU.mult),
                     reads=mv.b, writes=mv.b)
                K.op("dve", lambda e: e.tensor_tensor(out=mv[:, 5:6], in0=mv[:, 3:4], in1=mv[:, 4:5], op=ALU.subtract),
                     reads=mv.b, writes=mv.b)
                K.op("dve", lambda e: e.tensor_scalar(out=mv[:, 6:7], in0=mv[:, 5:6], scalar1=EPS, scalar2=None,
                                                      op0=ALU.add), reads=mv.b, writes=mv.b)
                K.op("act", lambda e: e.sqrt(out=mv[:, 6:7], in_=mv[:, 6:7]), reads=mv.b, writes=mv.b)
                K.op("dve", lambda e: e.reciprocal(out=mv[:, 6:7], in_=mv[:, 6:7]), reads=mv.b, writes=mv.b)
                K.op("dve", lambda e, j=j: e.tensor_scalar(out=vtok[:, j, :], in0=vtok[:, j, :], scalar1=mv[:, 2:3],
                                                           scalar2=mv[:, 6:7], op0=ALU.subtract, op1=ALU.mult),
                     reads=[vtok.b[j]] + mv.b, writes=[vtok.b[j]])
        def emit_spatial():
            nt = 0
            for jp in range(2):
                for g in range(8):
                    ps = P.next()
                    for fl in range(2):
                        for jl in range(2):
                            fc = 2 * g + fl; j = 2 * jp + jl
                            K.op("pe", lambda e, fc=fc, j=j, fl=fl, jl=jl, ps=ps, g=g: e.matmul(
                                ps[:, (fl * 2 + jl) * 128:(fl * 2 + jl + 1) * 128],
                                lhsT=vtok[:, j, fc * 128:(fc + 1) * 128], rhs=WsT[:, g, :], start=True, stop=True),
                                reads=[vtok.b[j]] + WsT.b, writes=ps.b)
                    tt = t1[nt % 2]; nt += 1
                    for fl in range(2):
                        fc = 2 * g + fl
                        K.op("dve", lambda e, ps=ps, tt=tt, fc=fc, fl=fl: e.scalar_tensor_tensor(
                            out=tt[:, fl * 256:(fl + 1) * 256].rearrange("p (a t) -> p a t", a=2),
                            in0=ps[:, fl * 256:(fl + 1) * 256].rearrange("p (a t) -> p a t", a=2),
                            scalar=lgc[:, fc:fc + 1], in1=bias2[:, fc:fc + 1, :].to_broadcast([128, 2, 128]),
                            op0=ALU.mult, op1=ALU.add), reads=ps.b + lgc.b + bias2.b, writes=tt.b)
                    usl = uT[:, 2 * g:2 * g + 2, jp * 256:(jp + 1) * 256]
                    K.op("pool", lambda e, tt=tt, usl=usl: e.tensor_tensor(
                        out=usl, in0=tt[:].rearrange("p (a t) -> p a t", a=2), in1=usl, op=ALU.mult),
                        reads=tt.b + [uT.b[2 * g], uT.b[2 * g + 1]], writes=[uT.b[2 * g], uT.b[2 * g + 1]])
        for cg in CG_ORDER:
            issue_w(nload + NWB)
            wb = wbuf[nload % NWB]; nload += 1
            if cg == 9:
                K.dma("pool", wo_tiles[0][:], w_out_v[:, :, 0:256], writes=wo_tiles[0].b)
            kind = cg // 4
            if kind != 1:
                dstT = uT if kind == 0 else szT
                fn = AF.Gelu_apprx_tanh if kind == 0 else AF.Silu
                for sub in range(4):
                    fc = (cg % 4) * 4 + sub
                    ps = P.next()
                    for k in range(8):
                        K.op("pe", lambda e, k=k, sub=sub, ps=ps, wb=wb, hnT=hnT: e.matmul(
                            ps[:], lhsT=wb[:, k, sub * 128:(sub + 1) * 128], rhs=hnT[:, k, :],
                            start=(k == 0), stop=(k == 7)), reads=wb.b + hnT.b, writes=ps.b)
                    K.op("act", lambda e, fc=fc, ps=ps, dstT=dstT, fn=fn: e.activation(
                        out=dstT[:, fc, :], in_=ps[:], func=fn), reads=ps.b, writes=[dstT.b[fc]])
            else:
                cgv = cg % 4
                for j in range(4):
                    ps = P.next()
                    for k in range(8):
                        K.op("pe", lambda e, k=k, j=j, ps=ps, wb=wb, hnT=hnT: e.matmul(
                            ps[:], lhsT=hnT[:, k, j * 128:(j + 1) * 128], rhs=wb[:, k, :],
                            start=(k == 0), stop=(k == 7)), reads=wb.b + [hnT.b[j]], writes=ps.b)
                    K.op("act", lambda e, j=j, ps=ps, cgv=cgv: e.activation(
                        out=vtok[:, j, cgv * 512:(cgv + 1) * 512], in_=ps[:], func=AF.Gelu_apprx_tanh,
                        accum_out=st1[:, j, cgv:cgv + 1]), reads=ps.b, writes=[vtok.b[j], st1.b[j]])
                    K.op("act", lambda e, j=j, cgv=cgv: e.activation(
                        out=W["junk"][:, 0:512], in_=vtok[:, j, cgv * 512:(cgv + 1) * 512], func=AF.Square,
                        accum_out=st2[:, j, cgv:cgv + 1]), reads=[vtok.b[j]], writes=W["junk"].b + [st2.b[j]])
                if cg == 7:
                    emit_ln()
            if cg == 3:
                emit_spatial()
        for fc in range(16):
            K.op("dve", lambda e, fc=fc: e.tensor_tensor(out=uT[:, fc, :], in0=uT[:, fc, :], in1=szT[:, fc, :], op=ALU.mult),
                 reads=[uT.b[fc], szT.b[fc]], writes=[uT.b[fc]])
        for cgo in range(4):
            if cgo == 1 and sb + 1 < 4:
                norm_transpose_sb(K, C, W, P, h, sb + 1, gB, hnTs[(sb + 1) % 2])
            wo = wo_tiles[cgo % 2]
            if cgo + 1 < 4:
                wn = wo_tiles[(cgo + 1) % 2]
                K.dma("pool", wn[:], w_out_v[:, :, (cgo + 1) * 256:(cgo + 2) * 256], writes=wn.b)
            for j in range(4):
                blk = sb * 4 + j
                ps = P.next()
                for k in range(16):
                    K.op("pe", lambda e, k=k, j=j, ps=ps, wo=wo: e.matmul(
                        ps[:, 0:256], lhsT=uT[:, k, j * 128:(j + 1) * 128], rhs=wo[:, k, :],
                        start=(k == 0), stop=(k == 15)), reads=wo.b + [uT.b[k]], writes=ps.b)
                hs = h[:, blk, cgo * 256:(cgo + 1) * 256]
                K.op("dve", lambda e, hs=hs, ps=ps: e.tensor_tensor(out=hs, in0=hs, in1=ps[:, 0:256], op=ALU.add),
                     reads=ps.b + [h.b[blk]], writes=[h.b[blk]])
    K.barrier()
    K.stack_pop()


import math as _math

PI = _math.pi


def tt(K, eng, out, in0, in1, op, reads, writes):
    return K.op(eng, lambda e: e.tensor_tensor(out=out, in0=in0, in1=in1, op=op), reads=reads, writes=writes)


def ts(K, eng, out, in0, s1, op0, reads, writes, s2=None, op1=None):
    if op1 is None:
        return K.op(eng, lambda e: e.tensor_scalar(out=out, in0=in0, scalar1=s1, scalar2=None, op0=op0),
                    reads=reads, writes=writes)
    return K.op(eng, lambda e: e.tensor_scalar(out=out, in0=in0, scalar1=s1, scalar2=s2, op0=op0, op1=op1),
                reads=reads, writes=writes)


def stt(K, eng, out, in0, scalar, in1, op0, op1, reads, writes):
    return K.op(eng, lambda e: e.scalar_tensor_tensor(out=out, in0=in0, scalar=scalar, in1=in1, op0=op0, op1=op1),
                reads=reads, writes=writes)


def act(K, out, in_, func, reads, writes, **kw):
    return K.op("act", lambda e: e.activation(out=out, in_=in_, func=func, **kw), reads=reads, writes=writes)


def emit_s5_prep(K, C, prm, p, S, cmask, PRE):
    L = contextlib.ExitStack(); K.stack_push(L)
    a_re, a_im, log_step = prm[p + "a_re"], prm[p + "a_im"], prm[p + "log_step"]
    shp = [128, 64]
    def T(name, shape=shp, dt=F32):
        return K.sbuf(name, shape, dt)
    are, aim, lst = PRE["are"], PRE["aim"], PRE["lst"]
    step, dr, th, mag, imag = T("step"), T("dr"), T("th"), T("mag"), T("imag")
    act(K, step[:], lst[:], AF.Exp, lst.b, step.b)
    tt(K, "dve", dr[:], are[:], step[:], ALU.mult, are.b + step.b, dr.b)
    tt(K, "dve", th[:], aim[:], step[:], ALU.mult, aim.b + step.b, th.b)
    act(K, mag[:], dr[:], AF.Exp, dr.b, mag.b)
    act(K, imag[:], dr[:], AF.Exp, dr.b, imag.b, scale=-1.0)
    sn, cs, t1, t2 = T("sn"), T("cs"), T("t1"), T("t2")
    hpi = K.sbuf("hpi", [128, 1], F32)
    K.op("dve", lambda e: e.memset(hpi[:], PI / 2), writes=hpi.b)
    act(K, sn[:], th[:], AF.Sin, th.b, sn.b, scale=1.0 / 16)
    act(K, cs[:], th[:], AF.Sin, th.b + hpi.b, cs.b, scale=1.0 / 16, bias=hpi[:, 0:1])
    for _ in range(4):
        tt(K, "dve", t1[:], cs[:], cs[:], ALU.mult, cs.b, t1.b)
        tt(K, "dve", t2[:], sn[:], sn[:], ALU.mult, sn.b, t2.b)
        stt(K, "dve", sn[:], cs[:], 2.0, sn[:], ALU.mult, ALU.mult, cs.b + sn.b, sn.b)
        tt(K, "dve", cs[:], t1[:], t2[:], ALU.subtract, t1.b + t2.b, cs.b)
    zr, zi, wr, wi = T("zr"), T("zi"), T("wr"), T("wi")
    tt(K, "dve", zr[:], mag[:], cs[:], ALU.mult, mag.b + cs.b, zr.b)
    tt(K, "dve", zi[:], mag[:], sn[:], ALU.mult, mag.b + sn.b, zi.b)
    tt(K, "dve", wr[:], imag[:], cs[:], ALU.mult, imag.b + cs.b, wr.b)
    stt(K, "dve", wi[:], imag[:], -1.0, sn[:], ALU.mult, ALU.mult, imag.b + sn.b, wi.b)
    nr, den, kr, ki, tmp = T("nr"), T("den"), T("kr"), T("ki"), T("tmpk")
    ts(K, "dve", nr[:], zr[:], -1.0, ALU.add, zr.b, nr.b)
    tt(K, "dve", den[:], are[:], are[:], ALU.mult, are.b, den.b)
    tt(K, "dve", tmp[:], aim[:], aim[:], ALU.mult, aim.b, tmp.b)
    tt(K, "dve", den[:], den[:], tmp[:], ALU.add, den.b + tmp.b, den.b)
    K.op("dve", lambda e: e.reciprocal(out=den[:], in_=den[:]), reads=den.b, writes=den.b)
    tt(K, "dve", kr[:], nr[:], are[:], ALU.mult, nr.b + are.b, kr.b)
    tt(K, "dve", tmp[:], zi[:], aim[:], ALU.mult, zi.b + aim.b, tmp.b)
    tt(K, "dve", kr[:], kr[:], tmp[:], ALU.add, kr.b + tmp.b, kr.b)
    tt(K, "dve", kr[:], kr[:], den[:], ALU.mult, kr.b + den.b, kr.b)
    tt(K, "dve", ki[:], zi[:], are[:], ALU.mult, zi.b + are.b, ki.b)
    tt(K, "dve", tmp[:], nr[:], aim[:], ALU.mult, nr.b + aim.b, tmp.b)
    tt(K, "dve", ki[:], ki[:], tmp[:], ALU.subtract, ki.b + tmp.b, ki.b)
    tt(K, "dve", ki[:], ki[:], den[:], ALU.mult, ki.b + den.b, ki.b)

    tp = K.psum("tp5", [128, 1024], BF16)
    L2 = contextlib.ExitStack(); K.stack_push(L2)
    bnr = K.sbuf("bnr", [128, 64, 16], F32); bni = K.sbuf("bni", [128, 64, 16], F32)
    K.dma("sp", bnr[:], prm[p + "b_re"].rearrange("(j gl) p h -> (gl p) j h", gl=2), writes=bnr.b)
    K.dma("sp", bni[:], prm[p + "b_im"].rearrange("(j gl) p h -> (gl p) j h", gl=2), writes=bni.b)
    bbr = K.sbuf("bbr", [128, 64, 16], F32); bbi = K.sbuf("bbi", [128, 64, 16], F32)
    btmp = K.sbuf("btmp", [128, 64, 16], F32)
    krb = kr[:].unsqueeze(2).to_broadcast([128, 64, 16]); kib = ki[:].unsqueeze(2).to_broadcast([128, 64, 16])
    tt(K, "dve", bbr[:], bnr[:], krb, ALU.mult, bnr.b + kr.b, bbr.b)
    tt(K, "dve", btmp[:], bni[:], kib, ALU.mult, bni.b + ki.b, btmp.b)
    tt(K, "dve", bbr[:], bbr[:], btmp[:], ALU.subtract, bbr.b + btmp.b, bbr.b)
    tt(K, "dve", bbi[:], bni[:], krb, ALU.mult, bni.b + kr.b, bbi.b)
    tt(K, "dve", btmp[:], bnr[:], kib, ALU.mult, bnr.b + ki.b, btmp.b)
    tt(K, "dve", bbi[:], bbi[:], btmp[:], ALU.add, bbi.b + btmp.b, bbi.b)
    pin = K.sbuf("pin", [128, 64, 128], BF16)
    stg = [K.sbuf(f"stg{i}", [128, 8, 128], BF16) for i in range(2)]
    ns = 0
    for ri, src in enumerate((bbr, bbi)):
        K.op("pool", lambda e: e.memset(pin[:], 0.0), writes=pin.b)
        for q in range(4):
            for gl in range(2):
                K.op("dve", lambda e, q=q, gl=gl, src=src: e.tensor_copy(
                    out=pin[gl * 64:(gl + 1) * 64, q::4, 32 * q + 16 * gl:32 * q + 16 * gl + 16],
                    in_=src[gl * 64:(gl + 1) * 64, q::4, :]), reads=src.b, writes=pin.b)
        for i8 in range(8):
            for jl in range(8):
                j = i8 * 8 + jl
                K.op("pe", lambda e, j=j, jl=jl: e.transpose(tp[:, jl * 128:(jl + 1) * 128], pin[:, j, :],
                                                             C["ident"][:]),
                     reads=pin.b + C["ident"].b, writes=tp.b)
            sg = stg[ns % 2]; ns += 1
            K.op("dve", lambda e, sg=sg: e.tensor_copy(out=sg[:].rearrange("r j c -> r (j c)"), in_=tp[:]),
                 reads=tp.b, writes=sg.b)
            K.dma("sp", S["bpad"][i8 * 8:(i8 + 1) * 8, ri].rearrange("j r c -> r j c"), sg[:], reads=sg.b,
                  writes=[S["bpad_b"]])
    K.barrier(); K.stack_pop()

    L3 = contextlib.ExitStack(); K.stack_push(L3)
    cin = K.sbuf("cin", [128, 16, 128], BF16)
    cn = K.sbuf("cn", [128, 16, 64], F32)
    for ri, (nm, sgn, dst) in enumerate((("c_re", 1.0, S["CTr"]), ("c_im", -1.0, S["CTn"]), ("c_re", -1.0, S["CTrn"]))):
        K.dma("sp", cn[:], prm[p + nm].rearrange("(jj q gl) h c -> (q gl h) jj c", q=4, gl=2), writes=cn.b)
        for gl in range(2):
            ts(K, "dve", cin[:, :, gl * 64:(gl + 1) * 64], cn[:], cmask[:, gl:gl + 1], ALU.mult,
               cn.b + cmask.b, cin.b, s2=sgn, op1=ALU.mult)
        for i8 in range(2):
            for jl in range(8):
                jj = i8 * 8 + jl
                K.op("pe", lambda e, jj=jj, jl=jl: e.transpose(tp[:, jl * 128:(jl + 1) * 128], cin[:, jj, :],
                                                               C["ident"][:]),
                     reads=cin.b + C["ident"].b, writes=tp.b)
            K.op("dve", lambda e, i8=i8, dst=dst: e.tensor_copy(
                out=dst[:, i8 * 8:(i8 + 1) * 8, :].rearrange("r j c -> r (j c)"), in_=tp[:]),
                reads=tp.b, writes=dst.b)
    K.barrier(); K.stack_pop()

    L4 = contextlib.ExitStack(); K.stack_push(L4)
    NB = 2
    Lp = K.sbuf("Lp", [128, 64, 4, 32], F32)
    Hp = K.sbuf("Hp", [128, 64, 4, 16], F32)
    cur = K.sbuf("pcur", [128, 64, 4], F32)
    cur2 = K.sbuf("pcur2", [128, 64, 4], F32)
    ctmp = K.sbuf("pctmp", [128, 64, 4], F32)
    ptm = [K.sbuf(f"pptm{i}", [128, 64, 16], F32) for i in range(2)]
    for ti in range(1):
        eng = "dve" if ti == 0 else "pool"
        sr, si = (zr, zi) if ti == 0 else (wr, wi)
        cr_ = cur[:, :, 2 * ti:2 * ti + 1]; ci_ = cur[:, :, 2 * ti + 1:2 * ti + 2]
        K.op(eng, lambda e, cr_=cr_, sr=sr: e.tensor_copy(out=cr_, in_=sr[:].unsqueeze(2)), reads=sr.b, writes=cur.b)
        K.op(eng, lambda e, ci_=ci_, si=si: e.tensor_copy(out=ci_, in_=si[:].unsqueeze(2)), reads=si.b, writes=cur.b)
        pt = ptm[ti]
        for (Tt, nsteps) in ((Lp, 5), (Hp, 4)):
            Tr = Tt[:, :, 2 * ti, :]; Ti = Tt[:, :, 2 * ti + 1, :]
            K.op(eng, lambda e, Tr=Tr: e.memset(Tr[:, :, 0:1], 1.0), writes=Tt.b)
            K.op(eng, lambda e, Ti=Ti: e.memset(Ti[:, :, 0:1], 0.0), writes=Tt.b)
            for k in range(nsteps):
                n = 1 << k
                crb = cr_.to_broadcast([128, 64, n]); cib = ci_.to_broadcast([128, 64, n])
                A_r = Tr[:, :, 0:n]; A_i = Ti[:, :, 0:n]; O_r = Tr[:, :, n:2 * n]; O_i = Ti[:, :, n:2 * n]
                tm = pt[:, :, 0:n]
                tt(K, eng, tm, A_i, cib, ALU.mult, Tt.b + cur.b, pt.b)
                tt(K, eng, O_r, A_r, crb, ALU.mult, Tt.b + cur.b, Tt.b)
                tt(K, eng, O_r, O_r, tm, ALU.subtract, Tt.b + pt.b, Tt.b)
                tt(K, eng, tm, A_i, crb, ALU.mult, Tt.b + cur.b, pt.b)
                tt(K, eng, O_i, A_r, cib, ALU.mult, Tt.b + cur.b, Tt.b)
                tt(K, eng, O_i, O_i, tm, ALU.add, Tt.b + pt.b, Tt.b)
                c2r = cur2[:, :, 2 * ti:2 * ti + 1]; c2i = cur2[:, :, 2 * ti + 1:2 * ti + 2]
                t_a = ctmp[:, :, 2 * ti:2 * ti + 1]; t_b = ctmp[:, :, 2 * ti + 1:2 * ti + 2]
                tt(K, eng, t_a, cr_, cr_, ALU.mult, cur.b, ctmp.b)
                tt(K, eng, t_b, ci_, ci_, ALU.mult, cur.b, ctmp.b)
                tt(K, eng, c2r, t_a, t_b, ALU.subtract, ctmp.b, cur2.b)
                tt(K, eng, c2i, cr_, ci_, ALU.mult, cur.b, cur2.b)
                tt(K, eng, c2i, c2i, c2i, ALU.add, cur2.b, cur2.b)
                K.op(eng, lambda e, cr_=cr_, c2r=c2r: e.tensor_copy(out=cr_, in_=c2r), reads=cur2.b, writes=cur.b)
                K.op(eng, lambda e, ci_=ci_, c2i=c2i: e.tensor_copy(out=ci_, in_=c2i), reads=cur2.b, writes=cur.b)
        if ti == 0:
            K.op(eng, lambda e, cr_=cr_: e.tensor_copy(out=S["L5"][:, :, 0:1], in_=cr_), reads=cur.b, writes=S["L5"].b)
            K.op(eng, lambda e, ci_=ci_: e.tensor_copy(out=S["L5"][:, :, 1:2], in_=ci_), reads=cur.b, writes=S["L5"].b)
    bv = K.sbuf("pbv", [128, 32], F32); on32 = K.sbuf("pon32", [128, 32], F32)
    K.op("dve", lambda e: e.memset(on32[:], 1.0), writes=on32.b)
    K.op("dve", lambda e: e.tensor_tensor_scan(out=bv[:], data0=on32[:], data1=on32[:], initial=-1.0,
                                               op0=ALU.mult, op1=ALU.add), reads=on32.b, writes=bv.b)
    gL = K.sbuf("pgL", [128, 64, 32], F32); gH = K.sbuf("pgH", [128, 64, 16], F32); gHn = K.sbuf("pgHn", [128, 64, 16], F32)
    tt(K, "dve", gL[:], dr[:].unsqueeze(2).to_broadcast([128, 64, 32]), bv[:].unsqueeze(1).to_broadcast([128, 64, 32]),
       ALU.mult, dr.b + bv.b, gL.b)
    tt(K, "dve", gH[:], dr[:].unsqueeze(2).to_broadcast([128, 64, 16]), bv[:, 0:16].unsqueeze(1).to_broadcast([128, 64, 16]),
       ALU.mult, dr.b + bv.b, gH.b)
    act(K, gL[:], gL[:], AF.Exp, gL.b, gL.b, scale=-2.0)
    act(K, gH[:], gH[:], AF.Exp, gH.b, gH.b, scale=-64.0)
    ts(K, "dve", gHn[:], gH[:], -1.0, ALU.mult, gH.b, gHn.b)
    tab = [K.sbuf(f"ptab{i}", [128, NB, 4, 512], F32, nb=4) for i in range(2)]
    otm = [K.sbuf(f"potm{i}", [128, NB, 512], F32) for i in range(3)]
    for bi in range(64 // NB):
        tb = tab[bi % 2]
        j0 = bi * NB
        shp4 = [128, NB, 16, 32]
        omA = otm[0]; omB = otm[1]
        Hr = Hp[:, j0:j0 + NB, 0, :].unsqueeze(3).to_broadcast(shp4)
        Hi = Hp[:, j0:j0 + NB, 1, :].unsqueeze(3).to_broadcast(shp4)
        Lr = Lp[:, j0:j0 + NB, 0, :].unsqueeze(2).to_broadcast(shp4)
        Li = Lp[:, j0:j0 + NB, 1, :].unsqueeze(2).to_broadcast(shp4)
        Tr = tb[:, :, 0, :].rearrange("r j (a b) -> r j a b", a=16)
        Ti = tb[:, :, 1, :].rearrange("r j (a b) -> r j a b", a=16)
        oA = omA[:].rearrange("r j (a b) -> r j a b", a=16)
        oB = omB[:].rearrange("r j (a b) -> r j a b", a=16)
        rd = Hp.b + Lp.b
        tt(K, "dve", oA, Hi, Li, ALU.mult, rd, omA.b)
        tt(K, "dve", Tr, Hr, Lr, ALU.mult, rd, [tb.b[0]])
        tt(K, "dve", oB, Hi, Lr, ALU.mult, rd, omB.b)
        tt(K, "dve", Ti, Hr, Li, ALU.mult, rd, [tb.b[1]])
        tt(K, "dve", Tr, Tr, oA, ALU.subtract, [tb.b[0]] + omA.b, [tb.b[0]])
        tt(K, "dve", Ti, Ti, oB, ALU.add, [tb.b[1]] + omB.b, [tb.b[1]])
        omp = otm[2]; omq = otm[2]
        g4 = omp[:].rearrange("r j (a b) -> r j a b", a=16)
        g4n = omq[:].rearrange("r j (a b) -> r j a b", a=16)
        gHb = gH[:, j0:j0 + NB, :].unsqueeze(3).to_broadcast(shp4)
        gHnb = gHn[:, j0:j0 + NB, :].unsqueeze(3).to_broadcast(shp4)
        gLb = gL[:, j0:j0 + NB, :].unsqueeze(2).to_broadcast(shp4)
        tt(K, "pool", g4, gHb, gLb, ALU.mult, gH.b + gL.b, omp.b)
        tt(K, "pool", tb[:, :, 2, :], tb[:, :, 0, :], omp[:], ALU.mult, [tb.b[0]] + omp.b, [tb.b[2]])
        tt(K, "pool", g4n, gHnb, gLb, ALU.mult, gHn.b + gL.b, omq.b)
        tt(K, "pool", tb[:, :, 3, :], tb[:, :, 1, :], omq[:], ALU.mult, [tb.b[1]] + omq.b, [tb.b[3]])
        K.dma("sp", S["tabs"][j0:j0 + NB].rearrange("j t r c -> r j t c"), tb[:], reads=tb.b,
              writes=[S["tabs_b"]])
    K.barrier(); K.stack_pop()
    K.barrier(); K.stack_pop()


def emit_s5(K, C, h, prm, p, kflag, cmask, PRE):
    nc = K.nc
    Lall = contextlib.ExitStack(); K.stack_push(Lall)
    gB = K.sbuf("gB5", [128, D], F32)
    bcast_load(K, gB, prm[p + "norm_g"], D)
    dsk = PRE["dsk"]; bgl = PRE["bgl"]
    S = {}
    S["tabs"] = K.dram("s5_tabs", [64, 4, 128, 512], F32, "Internal")
    S["bpad"] = K.dram("s5_bpad", [64, 2, 128, 128], BF16, "Internal")
    S["tabs_b"] = Buf("tabs"); S["bpad_b"] = Buf("bpad")
    S["CTr"] = K.sbuf("CTr", [128, 16, 128], BF16)
    S["CTn"] = K.sbuf("CTn", [128, 16, 128], BF16)
    S["CTrn"] = K.sbuf("CTrn", [128, 16, 128], BF16)
    S["L5"] = K.sbuf("L5", [128, 64, 2], F32)
    emit_s5_prep(K, C, prm, p, S, cmask, PRE)
    uT = K.sbuf("uT5", [128, 16, NTOK], BF16, nb=64)

    w_in_v = prm[p + "w_in"].rearrange("(k r) n -> r k n", r=128)
    w_glu_v = prm[p + "w_glu"].rearrange("(k r) n -> r k n", r=128)
    w_out_v = prm[p + "w_out"].rearrange("(k r) n -> r k n", r=128)

    def mkW():
        W = {}
        W["junk"] = K.sbuf("junk5", [128, 1024], F32)
        W["ss"] = K.sbuf("ss5", [128, 8], F32)
        W["rstd"] = K.sbuf("rstd5", [128, 8], F32)
        W["hn"] = K.sbuf("hn5", [128, D], BF16)
        W["tp"] = K.psum("tpA", [128, 1024], BF16)
        return W

    LA = contextlib.ExitStack(); K.stack_push(LA)
    W = mkW()
    P = PsumPool(K, 6)
    hnTs = [K.sbuf(f"hnT5_{i}", [128, 8, 512], BF16, nb=4) for i in range(2)]
    NWA = 4
    wbuf = [K.sbuf(f"wbA{i}", [128, 8, 512], BF16) for i in range(NWA)]
    wstA = {"issued": 0}

    def issue_a(upto):
        while wstA["issued"] < min(upto, 16):
            i = wstA["issued"]; cgi = i % 4
            K.dma("pool", wbuf[i % NWA][:], w_in_v[:, :, cgi * 512:(cgi + 1) * 512], writes=wbuf[i % NWA].b)
            wstA["issued"] += 1
    issue_a(NWA - 1)
    nl = 0
    for sb in range(4):
        hnT = hnTs[sb % 2]
        if sb == 0:
            norm_transpose_sb(K, C, W, P, h, 0, gB, hnT)
        for cg in range(4):
            if cg == 2 and sb + 1 < 4:
                norm_transpose_sb(K, C, W, P, h, sb + 1, gB, hnTs[(sb + 1) % 2])
            issue_a(nl + NWA)
            wb = wbuf[nl % NWA]; nl += 1
            for sub in range(4):
                fc = cg * 4 + sub
                ps = P.next()
                for k in range(8):
                    K.op("pe", lambda e, k=k, sub=sub, ps=ps, wb=wb, hnT=hnT: e.matmul(
                        ps[:], lhsT=wb[:, k, sub * 128:(sub + 1) * 128], rhs=hnT[:, k, :],
                        start=(k == 0), stop=(k == 7)), reads=wb.b + hnT.b, writes=ps.b)
                K.op("act", lambda e, fc=fc, ps=ps, sb=sb: e.copy(out=uT[:, fc, sb * 512:(sb + 1) * 512], in_=ps[:]),
                     reads=ps.b, writes=[uT.b[fc * 4 + sb]])
    K.barrier(); K.stack_pop()

    LB = contextlib.ExitStack(); K.stack_push(LB)
    psBU = [K.psum(f"psBU{i}", [128, 512], F32) for i in range(4)]
    psY = [K.psum(f"psY{i}", [128, 512], F32) for i in range(4)]
    ones = K.sbuf("ones5", [128, 512], F32)
    K.op("dve", lambda e: e.memset(ones[:], 1.0), writes=ones.b)
    tabt = [K.sbuf(f"tabt{i}", [128, 4, 512], F32) for i in range(2)]
    bpt = [K.sbuf(f"bpt{i}", [128, 2, 128], BF16) for i in range(2)]
    cst = K.sbuf("cst", [128, 64, 5, 2], F32, nb=64)
    NW = 3
    wk = [[K.sbuf(f"wk{s}_{i}", [128, 512], F32) for i in range(4)] for s in range(NW)]
    pk = [[K.sbuf(f"pk{s}_{i}", [128, 512], BF16) for i in range(4)] for s in range(2)]
    tny = K.sbuf("tny", [128, 4], F32, nb=4)
    nl5i = K.sbuf("nl5i", [128, 64], F32)
    ts(K, "dve", nl5i[:].unsqueeze(2), S["L5"][:, :, 1:2], -1.0, ALU.mult, S["L5"].b, nl5i.b)
    LPre = contextlib.ExitStack(); K.stack_push(LPre)
    fpre = K.sbuf("fpre", [128, 64, 2], F32)
    accA = K.sbuf("accA5", [128, 64, 4, 4], F32, nb=64 * 12)
    itp = 0
    for j in range(64):
        jj = j // 4
        tb = tabt[j % 2]; bp = bpt[j % 2]
        K.dma("sp", tb[:, 2:4, :], S["tabs"][j, 2:4].rearrange("t r c -> r t c"), reads=[S["tabs_b"]], writes=tb.b)
        K.dma("sp", bp[:], S["bpad"][j].rearrange("t r c -> r t c"), reads=[S["bpad_b"]], writes=bp.b)
        Mr, Mi = tb[:, 2, :], tb[:, 3, :]
        T2 = wk[j % 2][1]; T3 = wk[j % 2][2]
        tt(K, "pool", T2[:], Mi, Mr, ALU.subtract, tb.b, T2.b)
        tt(K, "pool", T3[:], Mr, Mi, ALU.add, tb.b, T3.b)
        for blk in range(4):
            pR = psBU[(2 * itp) % 4]; pI = psBU[(2 * itp + 1) % 4]; pS = psY[itp % 4]
            itp += 1
            ub = [uT.b[jj * 4 + blk]]
            usl = uT[:, jj, blk * 512:(blk + 1) * 512]
            K.op("pe", lambda e, pR=pR, bp=bp, usl=usl: e.matmul(pR[:], lhsT=bp[:, 0, :], rhs=usl, start=True, stop=True),
                 reads=bp.b + ub, writes=pR.b)
            K.op("pe", lambda e, pI=pI, bp=bp, usl=usl: e.matmul(pI[:], lhsT=bp[:, 1, :], rhs=usl, start=True, stop=True),
                 reads=bp.b + ub, writes=pI.b)
            K.op("pe", lambda e, pS=pS, bp=bp, usl=usl: e.matmul(pS[:], lhsT=bp[:, 0, :], rhs=usl, start=True, stop=False),
                 reads=bp.b + ub, writes=pS.b)
            K.op("pe", lambda e, pS=pS, bp=bp, usl=usl: e.matmul(pS[:], lhsT=bp[:, 1, :], rhs=usl, start=False, stop=True),
                 reads=bp.b + ub, writes=pS.b)
            for ai, (pp, mm, mb) in enumerate(((pS, Mr, tb.b), (pR, T2[:], T2.b), (pI, T3[:], T3.b))):
                jk = wk[2][ai]
                K.op("dve", lambda e, pp=pp, mm=mm, ai=ai, jk=jk, j=j, blk=blk: e.scalar_tensor_tensor(
                    out=jk[:], in0=pp[:], scalar=1.0, in1=mm, op0=ALU.mult, op1=ALU.mult, accum_out=accA[:, j, blk, ai:ai + 1]),
                    reads=pp.b + mb, writes=jk.b + [accA.b[j * 12 + blk * 3 + ai]])
    vv = K.sbuf("vv5", [128, 64, 2], F32); cc = K.sbuf("cc5", [128, 64, 2], F32); t4 = K.sbuf("t45", [128, 64, 4], F32)
    l5r_all = S["L5"][:, :, 0:1]; l5i_all = S["L5"][:, :, 1:2]
    for blk in range(4):
        tt(K, "dve", vv[:, :, 0:1], accA[:, :, blk, 0:1], accA[:, :, blk, 2:3], ALU.subtract, accA.b, vv.b)
        tt(K, "dve", vv[:, :, 1:2], accA[:, :, blk, 0:1], accA[:, :, blk, 1:2], ALU.add, accA.b, vv.b)
        if blk > 0:
            tt(K, "dve", vv[:], vv[:], cc[:], ALU.add, vv.b + cc.b, vv.b)
        dst = cc if blk < 3 else fpre
        tt(K, "dve", t4[:, :, 0:1], vv[:, :, 0:1], l5r_all, ALU.mult, vv.b + S["L5"].b, t4.b)
        tt(K, "dve", t4[:, :, 1:2], vv[:, :, 1:2], l5i_all, ALU.mult, vv.b + S["L5"].b, t4.b)
        tt(K, "dve", t4[:, :, 2:3], vv[:, :, 0:1], l5i_all, ALU.mult, vv.b + S["L5"].b, t4.b)
        tt(K, "dve", t4[:, :, 3:4], vv[:, :, 1:2], l5r_all, ALU.mult, vv.b + S["L5"].b, t4.b)
        tt(K, "dve", dst[:, :, 0:1], t4[:, :, 0:1], t4[:, :, 1:2], ALU.subtract, t4.b, dst.b)
        tt(K, "dve", dst[:, :, 1:2], t4[:, :, 2:3], t4[:, :, 3:4], ALU.add, t4.b, dst.b)
    cxi = K.dram("s5_cxi", [128, 128], F32, "Internal")
    cxo = K.dram("s5_cxo", [256, 128], F32, "Internal")
    bxi = Buf("cxi"); bxo = Buf("cxo")
    K.dma("sp", cxi, fpre[:].rearrange("r j t -> r (j t)"), reads=fpre.b, writes=[bxi])
    K.collective("AllGather", [cxi], [cxo], [bxi], [bxo])
    cext = K.sbuf("cext", [128, 64, 2], F32)
    K.dma("sp", cext[:].rearrange("r j t -> r (j t)"), cxo[0:128, :], reads=[bxo], writes=cext.b)
    ts(K, "dve", cst[:, :, 0, :], cext[:], kflag[:, 0:1], ALU.mult, cext.b + kflag.b, cst.b)
    K.barrier(); K.stack_pop()
    ytmp = [K.sbuf(f"ytmp{i}", [128, 512], F32) for i in range(1)]
    NIT = 256
    ctx = {}

    def S1(t):
        j, blk = t // 4, t % 4
        jj = j // 4
        tb = tabt[j % 2]; bp = bpt[j % 2]

        def load_pair(jn):
            tbn = tabt[jn % 2]; bpn = bpt[jn % 2]
            K.dma("sp", tbn[:], S["tabs"][jn].rearrange("t r c -> r t c"), reads=[S["tabs_b"]], writes=tbn.b)
            K.dma("sp", bpn[:], S["bpad"][jn].rearrange("t r c -> r t c"), reads=[S["bpad_b"]], writes=bpn.b)
        if t == 0:
            load_pair(0)
        if blk == 1 and j + 1 < 64:
            load_pair(j + 1)
        a, b, c, d = wk[t % NW]
        pR = psBU[(2 * t) % 4]; pI = psBU[(2 * t + 1) % 4]
        Mr, Mi = tb[:, 2, :], tb[:, 3, :]
        ub = [uT.b[jj * 4 + blk]]
        usl = uT[:, jj, blk * 512:(blk + 1) * 512]
        K.op("pe", lambda e: e.matmul(pR[:], lhsT=bp[:, 0, :], rhs=usl, start=True, stop=True), reads=bp.b + ub, writes=pR.b)
        K.op("pe", lambda e: e.matmul(pI[:], lhsT=bp[:, 1, :], rhs=usl, start=True, stop=True), reads=bp.b + ub, writes=pI.b)
        tt(K, "dve", a[:], pR[:], Mr, ALU.mult, pR.b + tb.b, a.b)
        tt(K, "dve", b[:], pI[:], Mi, ALU.mult, pI.b + tb.b, b.b)
        tt(K, "dve", c[:], pR[:], Mi, ALU.mult, pR.b + tb.b, c.b)
        tt(K, "dve", d[:], pI[:], Mr, ALU.mult, pI.b + tb.b, d.b)

    def S2(t):
        a, b, c, d = wk[t % NW]
        tt(K, "pool", a[:], a[:], b[:], ALU.subtract, a.b + b.b, a.b)
        tt(K, "pool", c[:], c[:], d[:], ALU.add, c.b + d.b, c.b)

    def S3(t):
        j, blk = t // 4, t % 4
        a, b, c, d = wk[t % NW]
        cb = [cst.b[j]]
        K.op("dve", lambda e: e.tensor_tensor_scan(out=b[:], data0=ones[:], data1=a[:], initial=cst[:, j, blk, 0:1],
                                                   op0=ALU.mult, op1=ALU.add), reads=ones.b + a.b + cb, writes=b.b)
        K.op("dve", lambda e: e.tensor_tensor_scan(out=d[:], data0=ones[:], data1=c[:], initial=cst[:, j, blk, 1:2],
                                                   op0=ALU.mult, op1=ALU.add), reads=ones.b + c.b + cb, writes=d.b)
        l5r = S["L5"][:, j, 0:1]; l5i = S["L5"][:, j, 1:2]
        if blk < 3:
            c0_ = 2 * (t % 2); c1_ = c0_ + 1
            ts(K, "dve", tny[:, c0_:c0_ + 1], d[:, 511:512], nl5i[:, j:j + 1], ALU.mult, d.b + nl5i.b, [tny.b[c0_]])
            ts(K, "dve", tny[:, c1_:c1_ + 1], d[:, 511:512], l5r, ALU.mult, d.b + S["L5"].b, [tny.b[c1_]])
            stt(K, "dve", cst[:, j, blk + 1, 0:1], b[:, 511:512], l5r, tny[:, c0_:c0_ + 1], ALU.mult, ALU.add,
                b.b + [tny.b[c0_]] + S["L5"].b, cb)
            stt(K, "dve", cst[:, j, blk + 1, 1:2], b[:, 511:512], l5i, tny[:, c1_:c1_ + 1], ALU.mult, ALU.add,
                b.b + [tny.b[c1_]] + S["L5"].b, cb)

    def S4(t):
        j = t // 4
        tb = tabt[j % 2]
        Dr, Di = tb[:, 0, :], tb[:, 1, :]
        a, b, c, d = wk[t % NW]; p0, p1, p2, p3 = pk[t % 2]
        tt(K, "pool", p0[:], b[:], Dr, ALU.mult, b.b + tb.b, p0.b)
        tt(K, "pool", p1[:], d[:], Di, ALU.mult, d.b + tb.b, p1.b)
        tt(K, "pool", p2[:], b[:], Di, ALU.mult, b.b + tb.b, p2.b)
        tt(K, "dve", p3[:], d[:], Dr, ALU.mult, d.b + tb.b, p3.b)

    def S5(t):
        j, blk = t // 4, t % 4
        jj, q = j // 4, j % 4
        p0, p1, p2, p3 = pk[t % 2]
        py = psY[blk]
        sl = slice(32 * q, 32 * q + 32)
        for i, (wt_, pp) in enumerate(((S["CTr"], p0), (S["CTrn"], p1), (S["CTn"], p2), (S["CTn"], p3))):
            K.op("pe", lambda e, wt_=wt_, pp=pp, i=i: e.matmul(py[sl, :], lhsT=wt_[:, jj, sl], rhs=pp[:],
                                                          start=(i == 0), stop=(i == 3), tile_position=(0, 32 * q)),
                 reads=wt_.b + pp.b, writes=py.b)
        if q == 3:
            yt = ytmp[0]
            usl = uT[:, jj, blk * 512:(blk + 1) * 512]; ub = [uT.b[jj * 4 + blk]]
            stt(K, "dve", yt[:], usl, dsk[:, jj:jj + 1], py[:], ALU.mult, ALU.add, ub + dsk.b + py.b, yt.b)
            act(K, usl, yt[:], AF.Gelu_apprx_tanh, yt.b, ub)

    for t in range(NIT + 2):
        if t < NIT:
            S1(t); S2(t)
        if 0 <= t - 1 < NIT:
            S3(t - 1); S4(t - 1)
        if 0 <= t - 2 < NIT:
            S5(t - 2)
    K.barrier(); K.stack_pop()

    LC = contextlib.ExitStack(); K.stack_push(LC)
    W = mkW()
    P = PsumPool(K, 6)
    hnT = K.sbuf("hnT5c", [128, 8, 512], BF16, nb=4)
    wbs = [K.sbuf(f"wbC{i}", [128, 4096], BF16) for i in range(3)]
    szT = K.sbuf("szT5", [128, 16, 512], BF16, nb=16)
    gT = szT
    gtmp = [K.sbuf(f"gtmp{i}", [128, 512], F32) for i in range(2)]
    nl = 0; nt = 0
    for sb in range(4):
        norm_transpose_sb(K, C, W, P, h, sb, gB, hnT)
        for cg in range(4):
            wt = wbs[nl % 3]; nl += 1
            wb = wt[:].rearrange("r (k n) -> r k n", k=8)
            K.dma("pool", wb, w_in_v[:, :, 2048 + cg * 512:2048 + (cg + 1) * 512], writes=wt.b)
            for sub in range(4):
                fc = cg * 4 + sub
                ps = P.next()
                for k in range(8):
                    K.op("pe", lambda e, k=k, sub=sub, ps=ps, wb=wb: e.matmul(
                        ps[:], lhsT=wb[:, k, sub * 128:(sub + 1) * 128], rhs=hnT[:, k, :],
                        start=(k == 0), stop=(k == 7)), reads=wt.b + hnT.b, writes=ps.b)
                act(K, szT[:, fc, :], ps[:], AF.Silu, ps.b, [szT.b[fc]])
        for cg in range(8):
            wt = wbs[nl % 3]; nl += 1
            wg = wt[:].rearrange("r (k n) -> r k n", k=16)
            K.dma("pool", wg, w_glu_v[:, :, cg * 256:(cg + 1) * 256], writes=wt.b)
            for sub in range(2):
                fc = cg * 2 + sub
                ps = P.next()
                for k in range(16):
                    K.op("pe", lambda e, k=k, sub=sub, ps=ps, wg=wg, sb=sb: e.matmul(
                        ps[:], lhsT=wg[:, k, sub * 128:(sub + 1) * 128], rhs=uT[:, k, sb * 512:(sb + 1) * 512],
                        start=(k == 0), stop=(k == 15)), reads=wt.b + [uT.b[k * 4 + sb]], writes=ps.b)
                gt = gtmp[nt % 2]; nt += 1
                act(K, gt[:], ps[:], AF.Sigmoid, ps.b + bgl.b, gt.b, bias=bgl[:, fc:fc + 1])
                tt(K, "dve", gt[:], gt[:], uT[:, fc, sb * 512:(sb + 1) * 512], ALU.mult,
                   gt.b + [uT.b[fc * 4 + sb]], gt.b)
                tt(K, "dve", szT[:, fc, :], gt[:], szT[:, fc, :], ALU.mult, gt.b + [szT.b[fc]], [szT.b[fc]])
        for cgo in range(4):
            wt = wbs[nl % 3]; nl += 1
            wo = wt[:].rearrange("r (k n) -> r k n", k=16)
            K.dma("pool", wo, w_out_v[:, :, cgo * 256:(cgo + 1) * 256], writes=wt.b)
            for jb in range(4):
                blk = sb * 4 + jb
                ps = P.next()
                for k in range(16):
                    K.op("pe", lambda e, k=k, jb=jb, ps=ps, wo=wo: e.matmul(
                        ps[:, 0:256], lhsT=gT[:, k, jb * 128:(jb + 1) * 128], rhs=wo[:, k, :],
                        start=(k == 0), stop=(k == 15)), reads=wt.b + [gT.b[k]], writes=ps.b)
                hs = h[:, blk, cgo * 256:(cgo + 1) * 256]
                tt(K, "dve", hs, hs, ps[:, 0:256], ALU.add, ps.b + [h.b[blk]], [h.b[blk]])
    K.barrier(); K.stack_pop()
    K.barrier(); K.stack_pop()


MLA_SCALE_F = 192 ** -0.5


def sincos_tables(K, ang, sin_out, cos_out, tmp, npi, ki, hpi):
    a_ap, a_b = ang; s_ap, s_b = sin_out; c_ap, c_b = cos_out; t_ap, t_b = tmp; k_ap, k_b = ki
    C1 = 6.28125; C2 = 2 * PI - C1
    np_ = s_ap.shape[0]
    ts(K, "dve", k_ap, a_ap, 1.0 / (2 * PI), ALU.mult, a_b, k_b)
    K.op("dve", lambda e: e.tensor_copy(out=t_ap, in_=k_ap), reads=k_b, writes=t_b)
    stt(K, "dve", a_ap, t_ap, -C1, a_ap, ALU.mult, ALU.add, t_b + a_b, a_b)
    stt(K, "dve", a_ap, t_ap, -C2, a_ap, ALU.mult, ALU.add, t_b + a_b, a_b)
    ts(K, "dve", a_ap, a_ap, 3.1415925, ALU.min, a_b, a_b, s2=-3.1415925, op1=ALU.max)
    act(K, s_ap, a_ap, AF.Sin, a_b, s_b)
    stt(K, "dve", t_ap, a_ap, -1.0, a_ap, ALU.mult, ALU.max, a_b, t_b)
    act(K, c_ap, t_ap, AF.Sin, t_b + hpi.b, c_b, scale=-1.0, bias=hpi[0:np_, 0:1])


def emit_mla(K, C, h, prm, p, pos_d, invf_d, kbias_d):
    Lall = contextlib.ExitStack(); K.stack_push(Lall)
    w_in = prm[p + "w_in"]; w_uq = prm[p + "w_uq"]; w_ukv = prm[p + "w_ukv"]; w_out = prm[p + "w_out"]
    w_in_v = w_in.rearrange("(k r) n -> r k n", r=128)
    w_out_v = w_out.rearrange("(k r) n -> r k n", r=128)
    gB = K.sbuf("gB2", [128, D], F32); bcast_load(K, gB, prm[p + "norm_g"], D)
    gq = K.sbuf("gq", [128, 384], F32); bcast_load(K, gq, prm[p + "q_norm_g"], 384)
    gkv = K.sbuf("gkv", [128, 128], F32); bcast_load(K, gkv, prm[p + "kv_norm_g"], 128)
    npi = None
    hpi = K.sbuf("hpi2", [128, 1], F32)
    K.op("dve", lambda e: e.memset(hpi[:], PI / 2), writes=hpi.b)
    kbias = K.sbuf("kbias", [128, 1], F32)
    K.dma("sp", kbias[:], kbias_d, writes=kbias.b)
    tri = K.sbuf("tri", [128, 128], BF16)
    trif = K.sbuf("trif", [128, 128], F32)
    K.op("pool", lambda e: e.memset(trif[:], 1.0), writes=trif.b)
    K.op("pool", lambda e: e.affine_select(out=trif[:], in_=trif[:], pattern=[[1, 128]], compare_op=ALU.is_ge,
                                           fill=0.0, base=0, channel_multiplier=-1), reads=trif.b, writes=trif.b)
    K.op("dve", lambda e: e.tensor_copy(out=tri[:], in_=trif[:]), reads=trif.b, writes=tri.b)
    wqn = K.sbuf("wqn", [128, 3, 16, 128], BF16)
    wqa = K.sbuf("wqa", [128, 3, 16, 64], BF16)
    wqb = K.sbuf("wqb", [128, 3, 16, 64], BF16)
    wuq_v = w_uq.rearrange("(k r) (hh e) -> r k hh e", r=128, e=192)
    wukv_v = w_ukv.rearrange("c (hh e) -> c hh e", e=256)
    wuv = K.sbuf("wuv", [128, 16, 128], BF16)
    wukT = K.sbuf("wukT", [128, 16, 128], BF16)
    wqi = K.sbuf("wqi", [128, 8, 384], BF16)
    W = {}
    W["junk"] = K.sbuf("junk2", [128, 1024], BF16)
    W["ss"] = K.sbuf("ss2", [128, 8], F32)
    W["rstd"] = K.sbuf("rstd2", [128, 8], F32)
    W["hn"] = K.sbuf("hn2", [128, D], BF16)
    W["tp"] = K.psum("tp2", [128, 1024], BF16)
    tp = W["tp"]
    posi = K.sbuf("posi", [128, NBLK], I32)
    posf = K.sbuf("posf", [128, NBLK], F32)
    K.dma("sp", posi[:], pos_d.rearrange("(j r) -> r j", r=128), writes=posi.b, allow_slow_non_contiguous=True)
    K.op("dve", lambda e: e.tensor_copy(out=posf[:], in_=posi[:]), reads=posi.b, writes=posf.b)
    invB = K.sbuf("invB", [128, 32], F32)
    K.dma("sp", invB[:], invf_d[0:32].partition_broadcast(128), writes=invB.b)
    invc = K.sbuf("invc", [64, 1], F32)
    K.dma("sp", invc[:, 0:1], invf_d.rearrange("(r o) -> r o", o=1), writes=invc.b)
    posBi = K.sbuf("posBi", [64, 512], I32)
    sgn = K.sbuf("sgn", [64, 1], F32)
    K.op("dve", lambda e: e.memset(sgn[0:32, :], -1.0), writes=sgn.b)
    K.op("dve", lambda e: e.memset(sgn[32:64, :], 1.0), writes=sgn.b)

    ckvT = K.sbuf("ckvT", [128, 2 * NTOK], BF16, nb=2)
    krT = K.sbuf("krT", [64, 2 * NTOK], BF16, nb=2)
    ckv_tok = K.sbuf("ckv_tok", [128, 2 * NBLK, 128], BF16, nb=2)

    hnT = K.sbuf("hnT2", [128, 8, 512], BF16, nb=4)
    P = PsumPool(K, 1)
    LA = contextlib.ExitStack(); K.stack_push(LA)
    wkv = K.sbuf("wkv", [128, 8, 192], BF16)
    K.dma("pool", wkv[:], w_in_v[:, :, 384:576], writes=wkv.b)
    for k in range(3):
        K.dma("pool", wqn[:, k], wuq_v[:, k, :, 0:128], writes=wqn.b)
        K.dma("pool", wqa[:, k], wuq_v[:, k, :, 128:192], writes=wqa.b)
        K.dma("pool", wqb[:, k, :, 0:32], wuq_v[:, k, :, 160:192], writes=wqb.b)
        K.dma("pool", wqb[:, k, :, 32:64], wuq_v[:, k, :, 128:160], writes=wqb.b)
    K.dma("pool", wuv[:], wukv_v[:, :, 128:256], writes=wuv.b)
    K.dma("pool", wqi[:], w_in_v[:, :, 0:384], writes=wqi.b)
    angt = K.sbuf("angt", [128, 32], F32); tmpt = K.sbuf("tmpt", [128, 32], F32)
    kit = K.sbuf("kit", [128, 32], I32)
    sint = K.sbuf("sint", [128, 32], F32); cost = K.sbuf("cost", [128, 32], F32)
    ckf = K.sbuf("ckf", [128, 128], BF16); krf = K.sbuf("krf", [128, 64], BF16)
    r1 = K.sbuf("r1", [128, 32], F32); r2 = K.sbuf("r2", [128, 32], F32)
    angA = K.sbuf("angA", [128, NBLK, 32], F32); tmpA = K.sbuf("tmpA", [128, NBLK, 32], F32)
    sinA = K.sbuf("sinA", [128, NBLK, 32], F32); cosA = K.sbuf("cosA", [128, NBLK, 32], F32)
    kiA = K.sbuf("kiA", [128, NBLK, 32], I32)
    tt(K, "dve", angA[:], invB[:].unsqueeze(1).to_broadcast([128, NBLK, 32]),
       posf[:].unsqueeze(2).to_broadcast([128, NBLK, 32]), ALU.mult, invB.b + posf.b, angA.b)
    fl = lambda t_: t_[:].rearrange("r j i -> r (j i)")
    sincos_tables(K, (fl(angA), angA.b), (fl(sinA), sinA.b), (fl(cosA), cosA.b), (fl(tmpA), tmpA.b), npi,
                  (fl(kiA), kiA.b), hpi)
    psA = [P.tiles[0]] + [K.psum(f"psA{i}", [128, 512], F32) for i in range(3)]
    ckf4 = K.sbuf("ckf4", [128, 4, 128], BF16); krf4 = K.sbuf("krf4", [128, 4, 64], BF16)
    r1q = K.sbuf("r1q", [128, 4, 32], F32); r2q = K.sbuf("r2q", [128, 4, 32], F32)
    for sb in range(4):
        norm_transpose_sb(K, C, W, P, h, sb, gB, hnT)
        ss = W["ss"]; rstd = W["rstd"]; junk = W["junk"]
        for j in range(4):
            ps = psA[j]
            for k in range(8):
                K.op("pe", lambda e, k=k, j=j, ps=ps: e.matmul(ps[:, 0:192], lhsT=hnT[:, k, j * 128:(j + 1) * 128],
                                                              rhs=wkv[:, k, :], start=(k == 0), stop=(k == 7)),
                     reads=wkv.b + [hnT.b[j]], writes=ps.b)
            act(K, junk[:, 0:128], ps[:, 0:128], AF.Square, ps.b, junk.b + ss.b, accum_out=ss[:, j:j + 1])
        ts(K, "dve", rstd[:, 0:4], ss[:, 0:4], 1.0 / 128, ALU.mult, ss.b, rstd.b, s2=EPS, op1=ALU.add)
        K.op("act", lambda e: e.sqrt(out=rstd[:, 0:4], in_=rstd[:, 0:4]), reads=rstd.b, writes=rstd.b)
        K.op("dve", lambda e: e.reciprocal(out=rstd[:, 0:4], in_=rstd[:, 0:4]), reads=rstd.b, writes=rstd.b)
        for j in range(4):
            blk = sb * 4 + j
            ps = psA[j]
            stt(K, "dve", ckf4[:, j, :], ps[:, 0:128], rstd[:, j:j + 1], gkv[:], ALU.mult, ALU.mult,
                ps.b + rstd.b + gkv.b, ckf4.b)
            sint_b = sinA[:, blk, :]; cost_b = cosA[:, blk, :]
            x1 = ps[:, 128:160]; x2 = ps[:, 160:192]
            tt(K, "dve", r1q[:, j, :], x1, cost_b, ALU.mult, ps.b + cosA.b, r1q.b)
            tt(K, "dve", r2q[:, j, :], x2, sint_b, ALU.mult, ps.b + sinA.b, r2q.b)
            tt(K, "dve", krf4[:, j, 0:32], r1q[:, j, :], r2q[:, j, :], ALU.subtract, r1q.b + r2q.b, krf4.b)
            tt(K, "dve", r1q[:, j, :], x2, cost_b, ALU.mult, ps.b + cosA.b, r1q.b)
            tt(K, "dve", r2q[:, j, :], x1, sint_b, ALU.mult, ps.b + sinA.b, r2q.b)
            tt(K, "dve", krf4[:, j, 32:64], r1q[:, j, :], r2q[:, j, :], ALU.add, r1q.b + r2q.b, krf4.b)
        K.op("pool", lambda e, sb=sb: e.tensor_copy(out=ckv_tok[:, NBLK + sb * 4:NBLK + sb * 4 + 4, :], in_=ckf4[:]),
             reads=ckf4.b, writes=[ckv_tok.b[1]])
        for j in range(4):
            K.op("pe", lambda e, j=j: e.transpose(tp[:, j * 128:(j + 1) * 128], ckf4[:, j, :], C["ident"][:]),
                 reads=ckf4.b + C["ident"].b, writes=tp.b)
        for j in range(4):
            K.op("pe", lambda e, j=j: e.transpose(tp[0:64, (4 + j) * 128:(5 + j) * 128], krf4[:, j, :], C["ident"][:]),
                 reads=krf4.b + C["ident"].b, writes=tp.b)
        K.op("act", lambda e, sb=sb: e.copy(out=ckvT[:, NTOK + sb * 512:NTOK + (sb + 1) * 512], in_=tp[:, 0:512]),
             reads=tp.b, writes=[ckvT.b[1]])
        K.op("act", lambda e, sb=sb: e.copy(out=krT[:, NTOK + sb * 512:NTOK + (sb + 1) * 512], in_=tp[0:64, 512:1024]),
             reads=tp.b, writes=[krT.b[1]])
    lxi = K.dram("mla_lxi", [320, NTOK], BF16, "Internal")
    lxo = K.dram("mla_lxo", [640, NTOK], BF16, "Internal")
    bli = Buf("lxi"); blo = Buf("lxo")
    K.dma("sp", lxi[0:128, :], ckvT[:, NTOK:2 * NTOK], reads=[ckvT.b[1]], writes=[bli])
    K.dma("sp", lxi[128:192, :], krT[:, NTOK:2 * NTOK], reads=[krT.b[1]], writes=[bli])
    K.dma("sp", lxi[192:320, :], ckv_tok[:, NBLK:2 * NBLK, :].rearrange("r j c -> r (j c)"), reads=[ckv_tok.b[1]],
          writes=[bli])
    K.collective("AllGather", [lxi], [lxo], [bli], [blo])
    K.dma("sp", ckvT[:, 0:NTOK], lxo[0:128, :], reads=[blo], writes=[ckvT.b[0]])
    K.dma("sp", krT[:, 0:NTOK], lxo[128:192, :], reads=[blo], writes=[krT.b[0]])
    K.dma("sp", ckv_tok[:, 0:NBLK, :].rearrange("r j c -> r (j c)"), lxo[192:320, :], reads=[blo], writes=[ckv_tok.b[0]])
    K.barrier(); K.stack_pop()
    Lw = contextlib.ExitStack(); K.stack_push(Lw)
    wuk = K.sbuf("wuk", [128, 16, 128], BF16)
    K.dma("pool", wuk[:], wukv_v[:, :, 0:128], writes=wuk.b)
    for i2 in range(2):
        for hl in range(8):
            hh = i2 * 8 + hl
            K.op("pe", lambda e, hh=hh, hl=hl: e.transpose(tp[:, hl * 128:(hl + 1) * 128], wuk[:, hh, :], C["ident"][:]),
                 reads=wuk.b + C["ident"].b, writes=tp.b)
        K.op("dve", lambda e, i2=i2: e.tensor_copy(out=wukT[:, i2 * 8:(i2 + 1) * 8, :].rearrange("r j c -> r (j c)"),
                                                   in_=tp[:]), reads=tp.b, writes=wukT.b)
    K.barrier(); K.stack_pop()

    LB = contextlib.ExitStack(); K.stack_push(LB)
    psS = [K.psum(f"psS{i}", [128, 512], F32) for i in range(2)]
    psO = [K.psum(f"psO{i}", [128, 512], F32) for i in range(2)]
    psL = [K.psum(f"psL{i}", [128, 512], F32) for i in range(2)]

    class _Pool2:
        def __init__(self, tiles):
            self.tiles = tiles; self.i = 0
        def next(self):
            t = self.tiles[self.i]; self.i = (self.i + 1) % len(self.tiles); return t
    PA = _Pool2(P.tiles)
    PB = _Pool2(P.tiles + psS + psO + [psL[0]])
    cqT = K.sbuf("cqT", [128, 3, 512], BF16, nb=4)
    cqf4 = K.sbuf("cqf4", [128, 4, 384], BF16)
    szT = K.sbuf("szT2", [128, 16, 512], BF16, nb=16)
    wbs = [K.sbuf(f"wb2_{i}", [128, 4096], BF16) for i in range(2)]
    wseqB = []
    for _sb in range(4):
        wseqB += [("z", c) for c in range(4)] + [("o", c) for c in range(4)]
    wstB = {"issued": 0, "used": 0}

    def issue_b(upto):
        while wstB["issued"] < min(upto, len(wseqB)):
            i = wstB["issued"]; kind, c = wseqB[i]
            wt_ = wbs[i % 2]
            if kind == "z":
                K.dma("pool", wt_[:].rearrange("r (k n) -> r k n", k=8), w_in_v[:, :, 576 + c * 512:576 + (c + 1) * 512],
                      writes=wt_.b)
            else:
                K.dma("pool", wt_[:].rearrange("r (k n) -> r k n", k=16), w_out_v[:, :, c * 256:(c + 1) * 256], writes=wt_.b)
            wstB["issued"] += 1

    def next_w():
        i = wstB["used"]; wstB["used"] += 1
        issue_b(i + 2)
        return wbs[i % 2]
    issue_b(1)
    angF = K.sbuf("angF", [64, 512], F32); tmpF = K.sbuf("tmpF", [64, 512], F32)
    sinF = K.sbuf("sinF", [64, 512], F32); cosF = K.sbuf("cosF", [64, 512], F32)
    posBf = angF
    qn = [K.sbuf(f"qn{i}", [128, 512], BF16) for i in range(2)]
    qp = [K.sbuf(f"qp{i}", [128, 512], BF16) for i in range(2)]
    qr = [K.sbuf(f"qr{i}", [64, 512], BF16) for i in range(2)]
    qra = angF; qrb = tmpF
    pT = [K.sbuf(f"pT{i}", [128, 512], BF16) for i in range(3)]
    rl = K.sbuf("rl", [128, 512], F32)
    oh = [K.sbuf(f"oh{i}", [128, 512], BF16) for i in range(2)]
    st = {"nl": 0, "npt": 0, "nS": 0}

    def q_prologue(hh):
        qn_, qp_, qr_ = qn[hh % 2], qp[hh % 2], qr[hh % 2]
        ps = PA.next()
        for k in range(3):
            K.op("pe", lambda e, k=k, ps=ps: e.matmul(ps[:], lhsT=wqn[:, k, hh, :], rhs=cqT[:, k, :],
                                                     start=(k == 0), stop=(k == 2)),
                 reads=wqn.b + cqT.b, writes=ps.b)
        K.op("act", lambda e, ps=ps: e.copy(out=qn_[:], in_=ps[:]), reads=ps.b, writes=qn_.b)
        ps3 = PA.next()
        for k in range(3):
            K.op("pe", lambda e, k=k, ps3=ps3: e.matmul(ps3[0:64, 0:512], lhsT=wqa[:, k, hh, :], rhs=cqT[:, k, :],
                                                       start=(k == 0), stop=(k == 2)),
                 reads=wqa.b + cqT.b, writes=ps3.b)
        tt(K, "dve", qra[:], ps3[0:64, :], cosF[:], ALU.mult, ps3.b + cosF.b, qra.b)
        ps2 = PA.next()
        K.op("pe", lambda e, ps2=ps2: e.matmul(ps2[:], lhsT=wukT[:, hh, :], rhs=qn_[:], start=True, stop=True),
             reads=wukT.b + qn_.b, writes=ps2.b)
        K.op("act", lambda e, ps2=ps2: e.copy(out=qp_[:], in_=ps2[:]), reads=ps2.b, writes=qp_.b)
        ps4 = PA.next()
        for k in range(3):
            K.op("pe", lambda e, k=k, ps4=ps4: e.matmul(ps4[0:64, 0:512], lhsT=wqb[:, k, hh, :], rhs=cqT[:, k, :],
                                                       start=(k == 0), stop=(k == 2)),
                 reads=wqb.b + cqT.b, writes=ps4.b)
        tt(K, "dve", qrb[:], ps4[0:64, :], sinF[:], ALU.mult, ps4.b + sinF.b, qrb.b)
        tt(K, "dve", qr_[:], qra[:], qrb[:], ALU.add, qra.b + qrb.b, qr_.b)

    def qk(hh, sb, kb):
        qp_, qr_ = qp[hh % 2], qr[hh % 2]
        half = 0 if kb < NBLK else 1
        jd = kb - (NBLK + sb * 4)
        c0 = jd * 128 if jd > 0 else 0
        pS = psS[st["nS"] % 2]; st["nS"] += 1
        K.op("pe", lambda e: e.matmul(pS[:, c0:512], lhsT=ckvT[:, kb * 128:(kb + 1) * 128], rhs=qp_[:, c0:512],
                                      start=True, stop=False), reads=[ckvT.b[half]] + qp_.b, writes=pS.b)
        K.op("pe", lambda e: e.matmul(pS[:, c0:512], lhsT=krT[:, kb * 128:(kb + 1) * 128], rhs=qr_[:, c0:512],
                                      start=False, stop=True), reads=[krT.b[half]] + qr_.b, writes=pS.b)
        return pS

    def softmax_pv(hh, sb, kb, pS, nkb):
        half = 0 if kb < NBLK else 1
        jd = kb - (NBLK + sb * 4)
        c0 = jd * 128 if jd > 0 else 0
        pt = pT[st["npt"] % 3]; st["npt"] += 1
        O = psO[hh % 2]; Lp = psL[hh % 2]
        if half == 0:
            act(K, pt[:, c0:512], pS[:, c0:512], AF.Exp, pS.b + kbias.b, pt.b, scale=MLA_SCALE_F, bias=kbias[:, 0:1])
        else:
            act(K, pt[:, c0:512], pS[:, c0:512], AF.Exp, pS.b, pt.b, scale=MLA_SCALE_F)
        if jd >= 0:
            tt(K, "pool", pt[:, jd * 128:(jd + 1) * 128], pt[:, jd * 128:(jd + 1) * 128], tri[:], ALU.mult,
               pt.b + tri.b, pt.b)
        K.op("pe", lambda e: e.matmul(O[:, c0:512], lhsT=ckv_tok[:, kb, :], rhs=pt[:, c0:512], start=(kb == 0),
                                      stop=(kb == nkb - 1)), reads=[ckv_tok.b[half]] + pt.b, writes=O.b)
        K.op("pe", lambda e: e.matmul(Lp[:, c0:512], lhsT=C["ones_bf"][:], rhs=pt[:, c0:512], start=(kb == 0),
                                      stop=(kb == nkb - 1)), reads=C["ones_bf"].b + pt.b, writes=Lp.b)

    def head_epilogue(hh):
        oh_ = oh[hh % 2]; O = psO[hh % 2]; Lp = psL[hh % 2]
        K.op("dve", lambda e: e.reciprocal(out=rl[:], in_=Lp[:]), reads=Lp.b, writes=rl.b)
        tt(K, "dve", oh_[:], O[:], rl[:], ALU.mult, O.b + rl.b, oh_.b)
        ps5 = PA.next()
        K.op("pe", lambda e: e.matmul(ps5[:], lhsT=wuv[:, hh, :], rhs=oh_[:], start=True, stop=True),
             reads=wuv.b + oh_.b, writes=ps5.b)
        tt(K, "dve", szT[:, hh, :], ps5[:], szT[:, hh, :], ALU.mult, ps5.b + [szT.b[hh]], [szT.b[hh]])

    for sb in range(4):
        norm_transpose_sb(K, C, W, P, h, sb, gB, hnT)
        K.dma("sp", posBi[:], pos_d[sb * 512:(sb + 1) * 512].partition_broadcast(64), writes=posBi.b)
        K.op("dve", lambda e: e.tensor_copy(out=posBf[:], in_=posBi[:]), reads=posBi.b, writes=posBf.b)
        ts(K, "dve", angF[:], angF[:], invc[:, 0:1], ALU.mult, angF.b + invc.b, angF.b)
        sincos_tables(K, (angF[:], angF.b), (sinF[:], sinF.b), (cosF[:], cosF.b), (tmpF[:], tmpF.b), npi,
                      (posBi[:], posBi.b), hpi)
        ts(K, "dve", sinF[:], sinF[:], sgn[:, 0:1], ALU.mult, sinF.b + sgn.b, sinF.b)
        ss = W["ss"]; rstd = W["rstd"]; junk = W["junk"]
        cq_ps = []
        for j in range(4):
            ps = PB.next(); cq_ps.append(ps)
            for k in range(8):
                K.op("pe", lambda e, k=k, j=j, ps=ps: e.matmul(ps[:, 0:384], lhsT=hnT[:, k, j * 128:(j + 1) * 128],
                                                              rhs=wqi[:, k, :], start=(k == 0), stop=(k == 7)),
                     reads=wqi.b + [hnT.b[j]], writes=ps.b)
            act(K, junk[:, 0:384], ps[:, 0:384], AF.Square, ps.b, junk.b + ss.b, accum_out=ss[:, j:j + 1])
        ts(K, "dve", rstd[:, 0:4], ss[:, 0:4], 1.0 / 384, ALU.mult, ss.b, rstd.b, s2=EPS, op1=ALU.add)
        K.op("act", lambda e: e.sqrt(out=rstd[:, 0:4], in_=rstd[:, 0:4]), reads=rstd.b, writes=rstd.b)
        K.op("dve", lambda e: e.reciprocal(out=rstd[:, 0:4], in_=rstd[:, 0:4]), reads=rstd.b, writes=rstd.b)
        for j in range(4):
            ps = cq_ps[j]
            stt(K, "dve", cqf4[:, j, :], ps[:, 0:384], rstd[:, j:j + 1], gq[:], ALU.mult, ALU.mult,
                ps.b + rstd.b + gq.b, cqf4.b)
        for cg in range(4):
            wt = next_w()
            wb = wt[:].rearrange("r (k n) -> r k n", k=8)
            for sub in range(4):
                fc = cg * 4 + sub
                ps = PB.next()
                for k in range(8):
                    K.op("pe", lambda e, k=k, sub=sub, ps=ps, wb=wb: e.matmul(
                        ps[:], lhsT=wb[:, k, sub * 128:(sub + 1) * 128], rhs=hnT[:, k, :],
                        start=(k == 0), stop=(k == 7)), reads=wt.b + hnT.b, writes=ps.b)
                act(K, szT[:, fc, :], ps[:], AF.Silu, ps.b, [szT.b[fc]])
        for j in range(4):
            for k in range(3):
                K.op("pe", lambda e, k=k, j=j: e.transpose(tp[:, k * 128:(k + 1) * 128], cqf4[:, j, k * 128:(k + 1) * 128],
                                                           C["ident"][:]), reads=cqf4.b + C["ident"].b, writes=tp.b)
            K.op("act", lambda e, j=j: e.copy(out=cqT[:, :, j * 128:(j + 1) * 128],
                                              in_=tp[:, 0:384].rearrange("r (k t) -> r k t", k=3)),
                 reads=tp.b, writes=[cqT.b[j]])
        nkb = NBLK + (sb + 1) * 4
        q_prologue(0)
        for hh in range(16):
            pS_next = qk(hh, sb, 0)
            for kb in range(nkb):
                pS_cur = pS_next
                if kb + 1 < nkb:
                    pS_next = qk(hh, sb, kb + 1)
                softmax_pv(hh, sb, kb, pS_cur, nkb)
                if kb == 3 and hh + 1 < 16:
                    q_prologue(hh + 1)
                if kb == 6 and hh > 0:
                    head_epilogue(hh - 1)
            if hh == 15:
                head_epilogue(15)
        for cgo in range(4):
            wt = next_w()
            wo = wt[:].rearrange("r (k n) -> r k n", k=16)
            for jb in range(4):
                blk = sb * 4 + jb
                ps = PB.next()
                for k in range(16):
                    K.op("pe", lambda e, k=k, jb=jb, ps=ps, wo=wo: e.matmul(
                        ps[:, 0:256], lhsT=szT[:, k, jb * 128:(jb + 1) * 128], rhs=wo[:, k, :],
                        start=(k == 0), stop=(k == 15)), reads=wt.b + [szT.b[k]], writes=ps.b)
                hs = h[:, blk, cgo * 256:(cgo + 1) * 256]
                tt(K, "dve", hs, hs, ps[:, 0:256], ALU.add, ps.b + [h.b[blk]], [h.b[blk]])
    K.barrier(); K.stack_pop()
    K.barrier(); K.stack_pop()


PARAM_SPECS = None


def param_specs():
    sp = {}
    def gm(p):
        sp[p + "norm_g"] = (1024,); sp[p + "w_in"] = (1024, 6144); sp[p + "ln_g"] = (2048,)
        sp[p + "ln_b"] = (2048,); sp[p + "w_s"] = (8, 128, 128); sp[p + "b_s"] = (8, 128)
        sp[p + "w_out"] = (2048, 1024)
    gm("l0_")
    p = "l1_"
    sp[p + "norm_g"] = (1024,); sp[p + "w_in"] = (1024, 4096)
    sp[p + "a_re"] = (128, 64); sp[p + "a_im"] = (128, 64); sp[p + "log_step"] = (128,)
    sp[p + "b_re"] = (128, 64, 16); sp[p + "b_im"] = (128, 64, 16)
    sp[p + "c_re"] = (128, 16, 64); sp[p + "c_im"] = (128, 16, 64)
    sp[p + "d_skip"] = (2048,); sp[p + "w_glu"] = (2048, 2048); sp[p + "b_glu"] = (2048,)
    sp[p + "w_out"] = (2048, 1024)
    p = "l2_"
    sp[p + "norm_g"] = (1024,); sp[p + "w_in"] = (1024, 2624); sp[p + "q_norm_g"] = (384,)
    sp[p + "w_uq"] = (384, 3072); sp[p + "kv_norm_g"] = (128,); sp[p + "w_ukv"] = (128, 4096)
    sp[p + "w_out"] = (2048, 1024)
    gm("l3_")
    sp["final_norm_g"] = (1024,)
    return sp


def build_program(layers=("l0",), final_norm=False):
    K = MK()
    x = K.dram("x", [NTOK, D], F32, "ExternalInput")
    out = K.dram("out", [NTOK, D], F32, "ExternalOutput")
    prm = {}
    need = set()
    for l in layers:
        need.add(l + "_")
    for name, shp in param_specs().items():
        if name[:3] in need or (final_norm and name == "final_norm_g"):
            prm[name] = K.dram(name, list(shp), F32, "ExternalInput")
    C = emit_consts(K)
    P = None
    cmask_d = K.dram("cmask", [128, 2], F32, "ExternalInput")
    cmask = K.sbuf("cmask_s", [128, 2], F32)
    K.dma("sp", cmask[:], cmask_d, writes=cmask.b)
    kflag_d = K.dram("kflag", [128, 1], F32, "ExternalInput")
    kflag = K.sbuf("kflag_s", [128, 1], F32)
    K.dma("sp", kflag[:], kflag_d, writes=kflag.b)
    pos_d = K.dram("pos", [NTOK], I32, "ExternalInput")
    invf_d = K.dram("invf", [64], F32, "ExternalInput")
    kbias_d = K.dram("kbias", [128, 1], F32, "ExternalInput")
    h = K.sbuf("h", [128, NBLK, D], F32, nb=NBLK)
    xv = x.rearrange("(j p) d -> p j d", p=128)
    for q in range(4):
        K.dma("sp", h[:, q * 4:(q + 1) * 4, :], xv[:, q * 4:(q + 1) * 4, :], writes=h.b[q * 4:(q + 1) * 4])
    PRE = {}
    if "l1" in layers:
        LPRE = contextlib.ExitStack(); K.stack_push(LPRE)
        p1 = "l1_"
        for nm in ("are", "aim", "lst"):
            PRE[nm] = K.sbuf("pre_" + nm, [128, 64], F32)
        K.dma("sp", PRE["are"][:], prm[p1 + "a_re"].rearrange("(j gl) p -> (gl p) j", gl=2), writes=PRE["are"].b,
              allow_slow_non_contiguous=True)
        K.dma("sp", PRE["aim"][:], prm[p1 + "a_im"].rearrange("(j gl) p -> (gl p) j", gl=2), writes=PRE["aim"].b,
              allow_slow_non_contiguous=True)
        lsv = prm[p1 + "log_step"].rearrange("(j gl) -> gl j", gl=2)
        for gl in range(2):
            K.dma("sp", PRE["lst"][gl * 64:(gl + 1) * 64, :], lsv[gl].partition_broadcast(64), writes=PRE["lst"].b,
                  allow_slow_non_contiguous=True)
        PRE["dsk"] = K.sbuf("pre_dsk", [128, 16], F32); PRE["bgl"] = K.sbuf("pre_bgl", [128, 16], F32)
        K.dma("sp", PRE["dsk"][:], prm[p1 + "d_skip"].rearrange("(c r) -> r c", r=128), writes=PRE["dsk"].b,
              allow_slow_non_contiguous=True)
        K.dma("sp", PRE["bgl"][:], prm[p1 + "b_glu"].rearrange("(c r) -> r c", r=128), writes=PRE["bgl"].b,
              allow_slow_non_contiguous=True)
    for l in layers:
        if l in ("l0", "l3"):
            emit_gmlp(K, C, P, h, prm, l + "_")
        elif l == "l2":
            emit_mla(K, C, h, prm, "l2_", pos_d, invf_d, kbias_d)
        elif l == "l1":
            emit_s5(K, C, h, prm, "l1_", kflag, cmask, PRE)
            K.barrier(); K.stack_pop()
    ov = out.rearrange("(j p) d -> p j d", p=128)
    if final_norm:
        emit_final_norm(K, h, prm["final_norm_g"], ov)
    else:
        for q in range(4):
            K.dma("sp", ov[:, q * 4:(q + 1) * 4, :], h[:, q * 4:(q + 1) * 4, :], reads=h.b[q * 4:(q + 1) * 4])
    K.barrier()
    return K.build(), sorted(prm.keys())


def emit_final_norm(K, h, g_d, ov):
    L = contextlib.ExitStack(); K.stack_push(L)
    gB = K.sbuf("gBf", [128, D], F32); bcast_load(K, gB, g_d, D)
    junk = [K.sbuf(f"junkf{i}", [128, D], F32) for i in range(2)]
    ss = K.sbuf("ssf", [128, NBLK], F32)
    ot = [K.sbuf(f"otf{i}", [128, D], F32) for i in range(3)]
    for blk in range(NBLK):
        jk = junk[blk % 2]
        act(K, jk[:], h[:, blk, :], AF.Square, [h.b[blk]], jk.b + ss.b, accum_out=ss[:, blk:blk + 1])
    ts(K, "dve", ss[:], ss[:], 1.0 / D, ALU.mult, ss.b, ss.b, s2=EPS, op1=ALU.add)
    K.op("act", lambda e: e.sqrt(out=ss[:], in_=ss[:]), reads=ss.b, writes=ss.b)
    K.op("dve", lambda e: e.reciprocal(out=ss[:], in_=ss[:]), reads=ss.b, writes=ss.b)
    for blk in range(NBLK):
        o = ot[blk % 3]
        stt(K, "dve", o[:], h[:, blk, :], ss[:, blk:blk + 1], gB[:], ALU.mult, ALU.mult, [h.b[blk]] + ss.b + gB.b, o.b)
        K.dma("sp", ov[:, blk, :], o[:], reads=o.b)
    K.barrier(); K.stack_pop()


_PROG = {}


def _get_prog(layers, final_norm):
    key = (tuple(layers), final_norm)
    if key not in _PROG:
        _PROG[key] = build_program(layers=layers, final_norm=final_norm)
    return _PROG[key]


def _aux_consts():
    cmask = np.zeros((128, 2), np.float32)
    for r in range(128):
        cmask[r, (r // 16) % 2] = 1.0
    invf = (np.float32(10000.0) ** (-np.arange(0, 64, 2, dtype=np.float32) / np.float32(64))).astype(np.float32)
    return cmask, np.concatenate([invf, invf])


def kernel(**inputs):
    inputs = {k: np.asarray(v) for k, v in inputs.items()}
    layers = ("l0", "l1", "l2", "l3")
    nc, names = _get_prog(layers, True)
    x = np.ascontiguousarray(inputs["x"], dtype=np.float32).reshape(8, NTOK, D)
    pos = np.ascontiguousarray(inputs["positions"]).astype(np.int32).reshape(8, NTOK)
    cmask, invf = _aux_consts()
    wts = {n: np.ascontiguousarray(inputs[n], dtype=np.float32) for n in names}

    in_maps = []
    for i in range(8):
        odd = float(i % 2)
        m = {"x": x[i], "cmask": cmask, "pos": pos[i], "invf": invf,
             "kbias": np.full((128, 1), 0.0 if odd else -30000.0, np.float32),
             "kflag": np.full((128, 1), odd, np.float32)}
        m.update(wts)
        in_maps.append(m)
    res = run_bass_kernel_spmd(nc, in_maps, core_ids=list(range(8))).results
    out = np.stack([np.asarray(r["out"]) for r in res]).reshape(4, 4096, D).astype(np.float32)
    return out
```
